# Optimizing a Trainium2 kernel written in Bass

```python
import math
import jax, jax.numpy as jnp
from jax import lax
import numpy as np

D_MODEL = 4096
BATCH = 4
SEQ = 2048
DEPTH = 4
DEC_BATCH = 8
DEC_SEQ = 8
PAST_LEN = 8192
PAGE_SIZE = 128

DH = 128
D_A = 3 * D_MODEL // 8
H_A = D_A // DH
A_BRANCHES = ((128, 1), (512, 4), (2048, 16))
WIN_MAX = 2048
BLK = 128
REL_BUCKETS = 32
REL_MAX_DIST = 2048
D_B = D_MODEL // 4
POOL_WINDOWS = (2, 4, 8, 16)
POOL_GROUPS = 4
CG = D_B // POOL_GROUPS
POOL_BUF = 15
D_C = D_MODEL - D_A - D_B
H_C = D_C // DH
DK = DH
DV = DH
GDN_CONV = 4
GDN_CHUNK = 64
D_MIX = D_A + D_B + D_C
D_FF = 256 * ((8 * D_MODEL // 3 + 255) // 256)
FFN_CONV = 3
N_IN = 3 * D_A + D_B + 4 * D_C + 2 * H_C
EPS = 1e-6
NEG_INF = -1e30

kernel_name = 'hymba_dilated_pool_gdn_decoder_step'


def rms_norm(x, gain):
    xf = x.astype(jnp.float32)
    y = xf * lax.rsqrt(jnp.mean(xf * xf, axis=-1, keepdims=True) + EPS)
    return (y * gain.astype(jnp.float32)).astype(x.dtype)


def rms_normalize(xf):
    return xf * lax.rsqrt(jnp.mean(xf * xf, axis=-1, keepdims=True) + EPS)


def l2_normalize(xf):
    return xf * lax.rsqrt(jnp.sum(xf * xf, axis=-1, keepdims=True) + EPS)


def causal_dwconv(ext, w):
    k = w.shape[0]
    t = ext.shape[1] - k + 1
    out = ext[:, 0:t] * w[0]
    for i in range(1, k):
        out = out + ext[:, i:i + t] * w[i]
    return out


def t5_bucket(dist):
    max_exact = REL_BUCKETS // 2
    d = jnp.maximum(dist, 1).astype(jnp.float32)
    large = max_exact + (jnp.log(d / max_exact) / math.log(REL_MAX_DIST / max_exact)
                         * (REL_BUCKETS - max_exact)).astype(jnp.int32)
    large = jnp.minimum(large, REL_BUCKETS - 1)
    return jnp.where(dist < max_exact, dist, large)


def branch_biases(rel_bias):
    out = []
    for window, dil in A_BRANCHES:
        nj = window // dil + 1
        buckets = t5_bucket(jnp.arange(nj, dtype=jnp.int32) * dil)
        out.append(rel_bias[buckets].T.astype(jnp.float32))
    return out


def band_dilated_attention(q, k, v, bias_hj, dil):
    b, s, h, dh = q.shape
    nj = bias_hj.shape[1]
    L = s // dil
    g = b * dil

    def to_classes(t):
        return t.reshape(b, L, dil, h, dh).transpose(0, 2, 1, 3, 4).reshape(g, L, h, dh)

    qc, kc, vc = to_classes(q), to_classes(k), to_classes(v)
    nb = -(-L // BLK)
    pad = nb * BLK - L
    qb = jnp.pad(qc, ((0, 0), (0, pad), (0, 0), (0, 0))).reshape(g, nb, BLK, h, dh)

    def band(t):
        tp = jnp.pad(t, ((0, 0), (BLK, pad), (0, 0), (0, 0))).reshape(g, nb + 1, BLK, h, dh)
        return jnp.concatenate([tp[:, :-1], tp[:, 1:]], axis=2)

    kb, vb = band(kc), band(vc)
    qi = jnp.arange(BLK)[:, None]
    ki = jnp.arange(2 * BLK)[None, :]
    rel = qi + BLK - ki
    key_pos = (jnp.arange(nb) * BLK)[:, None, None] - BLK + ki[None]
    valid = (rel >= 0) & (rel < nj) & (key_pos >= 0)
    bias = bias_hj[:, jnp.clip(rel, 0, nj - 1)]
    logits = jnp.einsum('gnqhd,gnkhd->gnhqk', qb, kb,
                        preferred_element_type=jnp.float32) * (DH ** -0.5) + bias[None, None]
    logits = jnp.where(valid[None, :, None], logits, NEG_INF)
    m = jnp.max(logits, axis=-1)
    p = jnp.exp(logits - m[..., None])
    den = jnp.sum(p, axis=-1)
    o = jnp.einsum('gnhqk,gnkhd->gnqhd', p, vb.astype(jnp.float32)) / jnp.swapaxes(den, 2, 3)[..., None]
    o = o.reshape(g, nb * BLK, h, dh)[:, :L].reshape(b, dil, L, h, dh).transpose(0, 2, 1, 3, 4).reshape(b, s, h, dh)

    def stat_back(t):
        t = jnp.swapaxes(t, 2, 3).reshape(g, nb * BLK, h)[:, :L]
        return t.reshape(b, dil, L, h).transpose(0, 2, 1, 3).reshape(b, s, h)

    return o, stat_back(m), stat_back(den)


def gathered_dilated_attention(q, k_all, v_all, bias_hj, dil, n_past):
    t = q.shape[1]
    nj = bias_hj.shape[1]
    idx = n_past + jnp.arange(t)[:, None] - jnp.arange(nj)[None, :] * dil
    valid = idx >= 0
    idx = jnp.maximum(idx, 0)
    kg = k_all[:, idx]
    vg = v_all[:, idx]
    logits = jnp.einsum('bthd,btjhd->bhtj', q, kg,
                        preferred_element_type=jnp.float32) * (DH ** -0.5) + bias_hj[None, :, None, :]
    logits = jnp.where(valid[None, None], logits, NEG_INF)
    m = jnp.max(logits, axis=-1)
    p = jnp.exp(logits - m[..., None])
    den = jnp.sum(p, axis=-1)
    o = jnp.einsum('bhtj,btjhd->bthd', p, vg.astype(jnp.float32)) / jnp.swapaxes(den, 1, 2)[..., None]
    return o, jnp.swapaxes(m, 1, 2), jnp.swapaxes(den, 1, 2)


def merge_branches(results):
    o_all = jnp.stack([r[0] for r in results])
    m_all = jnp.stack([r[1] for r in results])
    d_all = jnp.stack([r[2] for r in results])
    w = jnp.exp(m_all - jnp.max(m_all, axis=0, keepdims=True)) * d_all
    return jnp.sum(w[..., None] * o_all, axis=0) / jnp.sum(w, axis=0)[..., None]


def pooling_mixer(u, buf, n_valid, w_pool, scale):
    b, t, _ = u.shape
    p = buf.shape[1]
    ext = jnp.concatenate([buf, u], axis=1)
    cs = jnp.concatenate([jnp.zeros((b, 1, D_B), jnp.float32),
                          jnp.cumsum(ext.astype(jnp.float32), axis=1)], axis=1)
    pos = jnp.arange(t)
    hi = cs[:, p + 1:]
    diffs = []
    for gi, w in enumerate(POOL_WINDOWS):
        sl = slice(gi * CG, (gi + 1) * CG)
        lo = cs[:, p + 1 + pos - w, sl]
        cnt = jnp.minimum(w, n_valid + pos + 1).astype(jnp.float32)
        diffs.append((hi[..., sl] - lo) / cnt[:, None] - u[..., sl].astype(jnp.float32))
    d = jnp.stack(diffs, axis=2)
    y = jnp.einsum('btgc,gce->btge', d, w_pool.astype(jnp.float32))
    y = rms_normalize(y) * scale.astype(jnp.float32).reshape(POOL_GROUPS, CG)
    return y.reshape(b, t, D_B).astype(u.dtype), ext[:, -POOL_BUF:]


def chunk_gated_delta(q, k, v, beta, g, s0):
    b, t, h, _ = q.shape
    c = min(GDN_CHUNK, t)
    n = -(-t // c)
    pad = n * c - t

    def chunks(x):
        x = jnp.pad(x, [(0, 0), (0, pad)] + [(0, 0)] * (x.ndim - 2))
        x = x.reshape((b, n, c) + x.shape[2:])
        return jnp.moveaxis(x, 3, 1)

    qc, kc, vc, bc, gc = (chunks(x) for x in (q, k, v, beta, g))
    gcum = jnp.cumsum(gc, axis=-1)
    tril = jnp.tril(jnp.ones((c, c), bool))
    strict = jnp.tril(jnp.ones((c, c), bool), -1)
    decay = jnp.exp(jnp.where(tril, gcum[..., :, None] - gcum[..., None, :], -jnp.inf))
    kb = kc * bc[..., None]
    m_mat = jnp.where(strict, jnp.einsum('bhnid,bhnjd->bhnij', kb, kc) * decay, 0.0)
    rhs = jnp.concatenate([vc * bc[..., None], kb * jnp.exp(gcum)[..., None]], axis=-1)
    sol = lax.linalg.triangular_solve(m_mat + jnp.eye(c, dtype=jnp.float32), rhs,
                                      left_side=True, lower=True, unit_diagonal=True)
    u_c, w_c = sol[..., :DV], sol[..., DV:]
    a_qk = jnp.einsum('bhnid,bhnjd->bhnij', qc, kc) * decay
    q_dec = qc * jnp.exp(gcum)[..., None]
    k_dec = kc * jnp.exp(gcum[..., -1:] - gcum)[..., None]
    g_last = jnp.exp(gcum[..., -1])

    def step(state, xs):
        u_n, w_n, q_n, k_n, a_n, gl_n = xs
        v_new = u_n - jnp.einsum('bhcd,bhde->bhce', w_n, state)
        o_n = jnp.einsum('bhcd,bhde->bhce', q_n, state) + jnp.einsum('bhij,bhje->bhie', a_n, v_new)
        state = state * gl_n[..., None, None] + jnp.einsum('bhcd,bhce->bhde', k_n, v_new)
        return state, o_n

    xs = tuple(jnp.moveaxis(x, 2, 0) for x in (u_c, w_c, q_dec, k_dec, a_qk, g_last))
    s_fin, o = lax.scan(step, s0, xs)
    o = jnp.moveaxis(o, 0, 2).reshape(b, h, n * c, DV)[:, :, :t]
    return jnp.swapaxes(o, 1, 2), s_fin


def gdn_mixer(qkv_pre, gate, beta_raw, a_raw, conv_buf, s0, conv_w, a_log, dt_bias, out_gain):
    b, t, _ = qkv_pre.shape
    ext = jnp.concatenate([conv_buf, qkv_pre], axis=1)
    qkv = jax.nn.silu(causal_dwconv(ext, conv_w).astype(jnp.float32))
    q, k, v = jnp.split(qkv, 3, axis=-1)
    q = l2_normalize(q.reshape(b, t, H_C, DK)) * (DK ** -0.5)
    k = l2_normalize(k.reshape(b, t, H_C, DK))
    v = v.reshape(b, t, H_C, DV)
    beta = jax.nn.sigmoid(beta_raw.astype(jnp.float32))
    g = -jnp.exp(a_log.astype(jnp.float32)) * jax.nn.softplus(a_raw.astype(jnp.float32) + dt_bias.astype(jnp.float32))
    o, s_new = chunk_gated_delta(q, k, v, beta, g, s0.astype(jnp.float32))
    o = rms_normalize(o) * out_gain.astype(jnp.float32) * jax.nn.silu(gate.astype(jnp.float32).reshape(b, t, H_C, DV))
    return o.reshape(b, t, D_C).astype(qkv_pre.dtype), s_new.astype(s0.dtype), ext[:, -(GDN_CONV - 1):]


def trunk_layer(x, past_kv, pool_buf, pool_valid, gconv_buf, gstate, fconv_buf, lw, biases):
    b, t, _ = x.shape
    xn = rms_norm(x, lw['norm_mix'])
    proj = xn @ lw['w_in']
    cuts = [D_A, 2 * D_A, 3 * D_A, 3 * D_A + D_B, 3 * D_A + D_B + 3 * D_C,
            3 * D_A + D_B + 4 * D_C, 3 * D_A + D_B + 4 * D_C + H_C]
    aq, ak, av, pu, cqkv, cgate, cbeta, ca = jnp.split(proj, cuts, axis=-1)
    q = rms_norm(aq.reshape(b, t, H_A, DH), lw['a_q_norm'])
    k = rms_norm(ak.reshape(b, t, H_A, DH), lw['a_k_norm'])
    v = av.reshape(b, t, H_A, DH)
    if past_kv is None:
        res = [band_dilated_attention(q, k, v, bias, dil) for (win, dil), bias in zip(A_BRANCHES, biases)]
    else:
        n_past = past_kv.shape[1]
        k_all = jnp.concatenate([past_kv[:, :, 0].astype(k.dtype), k], axis=1)
        v_all = jnp.concatenate([past_kv[:, :, 1].astype(v.dtype), v], axis=1)
        res = [gathered_dilated_attention(q, k_all, v_all, bias, dil, n_past)
               for (win, dil), bias in zip(A_BRANCHES, biases)]
    oa = merge_branches(res)
    ya = rms_norm(oa, lw['a_out_norm'].reshape(H_A, DH)).reshape(b, t, D_A).astype(x.dtype)
    kv_rows = jnp.stack([k, v], axis=2)[:, -min(WIN_MAX, t):]
    yb, pool_new = pooling_mixer(pu, pool_buf, pool_valid, lw['pool_w'], lw['pool_scale'])
    yc, gstate_new, gconv_new = gdn_mixer(cqkv, cgate, cbeta, ca, gconv_buf, gstate, lw['gdn_conv_w'],
                                          lw['gdn_a_log'], lw['gdn_dt_bias'], lw['gdn_out_norm'])
    h = x + jnp.concatenate([ya, yb, yc], axis=-1) @ lw['w_out']
    up = rms_norm(h, lw['norm_ffn']) @ lw['ffn_up']
    ext = jnp.concatenate([fconv_buf, up], axis=1)
    cv = causal_dwconv(ext, lw['ffn_conv_w']) + lw['ffn_conv_b']
    gt, vl = jnp.split(cv, 2, axis=-1)
    y = h + (jax.nn.silu(gt) * vl) @ lw['ffn_down']
    return y, (kv_rows, pool_new, gconv_new, gstate_new, ext[:, -(FFN_CONV - 1):])


def setup_inputs(seed: int = 0) -> dict:
    key = jax.random.key(seed)
    ks = jax.random.split(key, 32)
    f32 = jnp.float32

    def nrm(k, shape, scale):
        return jax.random.normal(k, shape, f32) * scale

    a_buf = min(WIN_MAX, PAST_LEN)
    dt = jnp.exp(jax.random.uniform(ks[16], (DEPTH, H_C), f32, math.log(1e-3), math.log(1e-1)))
    return {
        'x_prompt': nrm(ks[0], (BATCH, SEQ, D_MODEL), 1.0),
        'x_sample': nrm(ks[1], (DEC_BATCH, DEC_SEQ, D_MODEL), 1.0),
        'cache_attn_kv': nrm(ks[2], (DEPTH, DEC_BATCH, a_buf, 2, H_A, DH), 1.0),
        'state_pool': nrm(ks[3], (DEPTH, DEC_BATCH, POOL_BUF, D_B), 1.0),
        'state_gdn_conv': nrm(ks[4], (DEPTH, DEC_BATCH, GDN_CONV - 1, 3 * D_C), 1.0),
        'state_gdn': nrm(ks[5], (DEPTH, DEC_BATCH, H_C, DK, DV), 0.3),
        'state_ffn_conv': nrm(ks[6], (DEPTH, DEC_BATCH, FFN_CONV - 1, 2 * D_FF), 1.0),
        'rel_bias': nrm(ks[7], (REL_BUCKETS, H_A), 0.5),
        'norm_mix': 1.0 + nrm(ks[8], (DEPTH, D_MODEL), 0.02),
        'w_in': nrm(ks[9], (DEPTH, D_MODEL, N_IN), D_MODEL ** -0.5),
        'a_q_norm': 1.0 + nrm(ks[10], (DEPTH, DH), 0.02),
        'a_k_norm': 1.0 + nrm(ks[11], (DEPTH, DH), 0.02),
        'a_out_norm': 1.0 + nrm(ks[12], (DEPTH, D_A), 0.02),
        'pool_w': nrm(ks[13], (DEPTH, POOL_GROUPS, CG, CG), CG ** -0.5),
        'pool_scale': 1.0 + nrm(ks[14], (DEPTH, D_B), 0.1),
        'gdn_conv_w': nrm(ks[15], (DEPTH, GDN_CONV, 3 * D_C), GDN_CONV ** -0.5),
        'gdn_a_log': jnp.log(jax.random.uniform(ks[17], (DEPTH, H_C), f32, 1.0, 16.0)),
        'gdn_dt_bias': dt + jnp.log(-jnp.expm1(-dt)),
        'gdn_out_norm': 1.0 + nrm(ks[18], (DEPTH, DV), 0.02),
        'w_out': nrm(ks[19], (DEPTH, D_MIX, D_MODEL), D_MIX ** -0.5),
        'norm_ffn': 1.0 + nrm(ks[20], (DEPTH, D_MODEL), 0.02),
        'ffn_up': nrm(ks[21], (DEPTH, D_MODEL, 2 * D_FF), D_MODEL ** -0.5),
        'ffn_conv_w': nrm(ks[22], (DEPTH, FFN_CONV, 2 * D_FF), FFN_CONV ** -0.5),
        'ffn_conv_b': nrm(ks[23], (DEPTH, 2 * D_FF), 0.01),
        'ffn_down': nrm(ks[24], (DEPTH, D_FF, D_MODEL), D_FF ** -0.5),
    }


def reference(x_prompt, x_sample, cache_attn_kv, state_pool, state_gdn_conv, state_gdn, state_ffn_conv,
              rel_bias, norm_mix, w_in, a_q_norm, a_k_norm, a_out_norm, pool_w, pool_scale,
              gdn_conv_w, gdn_a_log, gdn_dt_bias, gdn_out_norm, w_out, norm_ffn,
              ffn_up, ffn_conv_w, ffn_conv_b, ffn_down):
    biases = branch_biases(rel_bias)
    bp = x_prompt.shape[0]
    dtp = x_prompt.dtype
    xp, xs = x_prompt, x_sample
    p_kv, p_pool, p_gconv, p_gdn, p_fconv = [], [], [], [], []
    s_kv, s_pool, s_gconv, s_gdn, s_fconv = [], [], [], [], []
    for l in range(DEPTH):
        lw = {'norm_mix': norm_mix[l], 'w_in': w_in[l], 'a_q_norm': a_q_norm[l], 'a_k_norm': a_k_norm[l],
              'a_out_norm': a_out_norm[l], 'pool_w': pool_w[l], 'pool_scale': pool_scale[l],
              'gdn_conv_w': gdn_conv_w[l], 'gdn_a_log': gdn_a_log[l], 'gdn_dt_bias': gdn_dt_bias[l],
              'gdn_out_norm': gdn_out_norm[l], 'w_out': w_out[l], 'norm_ffn': norm_ffn[l],
              'ffn_up': ffn_up[l], 'ffn_conv_w': ffn_conv_w[l], 'ffn_conv_b': ffn_conv_b[l],
              'ffn_down': ffn_down[l]}
        xp, (kv, pl, gcv, gst, fcv) = trunk_layer(
            xp, None, jnp.zeros((bp, POOL_BUF, D_B), dtp), 0,
            jnp.zeros((bp, GDN_CONV - 1, 3 * D_C), dtp), jnp.zeros((bp, H_C, DK, DV), dtp),
            jnp.zeros((bp, FFN_CONV - 1, 2 * D_FF), dtp), lw, biases)
        p_kv.append(kv); p_pool.append(pl); p_gconv.append(gcv); p_gdn.append(gst); p_fconv.append(fcv)
        xs, (kv, pl, gcv, gst, fcv) = trunk_layer(
            xs, cache_attn_kv[l], state_pool[l], state_pool.shape[2],
            state_gdn_conv[l], state_gdn[l], state_ffn_conv[l], lw, biases)
        s_kv.append(kv); s_pool.append(pl); s_gconv.append(gcv); s_gdn.append(gst); s_fconv.append(fcv)
    return (xp, xs,
            jnp.stack(p_kv), jnp.stack(s_kv),
            jnp.stack(p_pool), jnp.stack(s_pool),
            jnp.stack(p_gconv), jnp.stack(s_gconv),
            jnp.stack(p_gdn), jnp.stack(s_gdn),
            jnp.stack(p_fconv), jnp.stack(s_fconv))
```

```python
import math
from concourse.bass_utils import run_bass_kernel_spmd
import numpy as np
import concourse.bass as bass
import concourse.mybir as mybir

F32 = mybir.dt.float32
BF16 = mybir.dt.bfloat16
I32 = mybir.dt.int32
ALU = mybir.AluOpType
AF = mybir.ActivationFunctionType
AX = mybir.AxisListType

EPOCH = 30000
NSLOT = 20


class Buf:
    __slots__ = ("name", "w", "r", "psum")

    def __init__(self, name, psum=False):
        self.name = name
        self.psum = psum
        self.w = None
        self.r = {}


class Eng:
    def __init__(self, k, name, h, is_compute=True):
        self.k = k
        self.name = name
        self.h = h
        self.sems = []
        self.count = 0
        self.waited = {}
        self.is_compute = is_compute
        self.slots = []
        self.slot_val = []
        self.ndma = 0


class KB:
    def __init__(self, nc, stack):
        self.nc = nc
        self.stack = stack
        self.E = {}
        for name, h in (("pe", nc.tensor), ("act", nc.scalar), ("dve", nc.vector),
                        ("pool", nc.gpsimd), ("sp", nc.sync)):
            self.E[name] = Eng(self, name, h)
        self.n_sem = 0
        self.n_inst = 0
        for e in self.E.values():
            if e.name in ("sp", "act", "pool"):
                for i in range(NSLOT):
                    e.slots.append(self._sem(f"d_{e.name}_{i}"))
                    e.slot_val.append(0)

    def _sem(self, name):
        self.n_sem += 1
        return self.stack.enter_context(self.nc.semaphore(name))

    def sbuf(self, name, shape, dtype, stack=None):
        self.n_alloc = getattr(self, "n_alloc", 0) + 1
        return (stack or self.stack).enter_context(self.nc.sbuf_tensor(f"{name}_{self.n_alloc}", list(shape), dtype))

    def psum(self, name, shape, dtype, stack=None):
        self.n_alloc = getattr(self, "n_alloc", 0) + 1
        return (stack or self.stack).enter_context(self.nc.psum_tensor(f"{name}_{self.n_alloc}", list(shape), dtype))

    def dram(self, name, shape, dtype, kind="Internal"):
        return self.nc.dram_tensor(name, list(shape), dtype, kind=kind)

    def _eng_sem(self, e, seq):
        idx = seq // EPOCH
        while len(e.sems) <= idx:
            e.sems.append(self._sem(f"c_{e.name}_{len(e.sems)}"))
        return e.sems[idx], seq % EPOCH + 1

    def _wait_tok(self, x, tok):
        if tok is None:
            return
        if tok[0] == "E":
            _, en, seq = tok
            key = ("E", en)
            if x.waited.get(key, -1) >= seq:
                return
            e = self.E[en]
            sem, val = self._eng_sem(e, seq)
            x.h.wait_ge(sem, val)
            x.waited[key] = seq
        else:
            _, qn, slot, val = tok
            key = ("D", qn, slot)
            if x.waited.get(key, 0) >= val:
                return
            q = self.E[qn]
            x.h.wait_ge(q.slots[slot], val)
            x.waited[key] = val

    def _deps(self, x, reads, writes, same_eng_waw=True):
        toks = []
        for b in reads:
            if b.w is not None:
                toks.append(b.w)
            if b.psum:
                for t in b.r.values():
                    if not (t[0] == "E" and t[1] == x.name):
                        toks.append(t)
        for b in writes:
            if b.w is not None:
                if b.w[0] == "E" and b.w[1] == x.name and not same_eng_waw:
                    pass
                else:
                    toks.append(b.w)
            for t in b.r.values():
                if t[0] == "E" and t[1] == x.name:
                    continue
                toks.append(t)
        for t in toks:
            self._wait_tok(x, t)

    def _record(self, tok, reads, writes):
        for b in reads:
            if tok[0] == "E":
                b.r[("E", tok[1])] = tok
            else:
                b.r[("D", tok[1], tok[2])] = tok
        for b in writes:
            b.w = tok
            b.r = {}

    def op(self, eng, fn, reads=(), writes=()):
        x = self.E[eng]
        self._deps(x, reads, writes, same_eng_waw=(eng != "pe"))
        inst = fn(x.h)
        seq = x.count
        x.count += 1
        sem, val = self._eng_sem(x, seq)
        inst.then_inc(sem, 1)
        self._record(("E", eng, seq), reads, writes)
        self.n_inst += 1
        return inst

    def dma(self, q, out, in_, reads=(), writes=(), **kw):
        x = self.E[q]
        self._deps(x, reads, writes)
        slot = x.ndma % NSLOT
        x.ndma += 1
        if x.slot_val[slot] > 0:
            self._wait_tok(x, ("D", q, slot, x.slot_val[slot]))
        x.slot_val[slot] += 16
        inst = x.h.dma_start(out=out, in_=in_, **kw)
        inst.then_inc(x.slots[slot], 16)
        self._record(("D", q, slot, x.slot_val[slot]), reads, writes)
        self.n_inst += 1
        return inst

    def barrier(self):
        sp = self.E["sp"]
        for e in self.E.values():
            if e.slots:
                for s in range(NSLOT):
                    if e.slot_val[s] > 0:
                        self._wait_tok(sp, ("D", e.name, s, e.slot_val[s]))
        for e in self.E.values():
            if e.name != "sp" and e.count > 0:
                self._wait_tok(sp, ("E", e.name, e.count - 1))
        seq = sp.count
        sp.count += 1
        sem, val = self._eng_sem(sp, seq)
        sp.h.nop().then_inc(sem, 1)
        for e in self.E.values():
            if e.name != "sp":
                self._wait_tok(e, ("E", "sp", seq))
                for o in self.E.values():
                    if o.count > 0 and o.name != "sp":
                        e.waited[("E", o.name)] = max(e.waited.get(("E", o.name), -1),
                                                      o.count - 1 if o.name != e.name else -1)
                    for s in range(len(o.slots)):
                        e.waited[("D", o.name, s)] = o.slot_val[s]

    def finish(self):
        self.barrier()

from contextlib import ExitStack

EPS = 1e-6


class Ctx:
    def __init__(self, k, cfg, consts_dram):
        self.k = k
        self.cfg = cfg
        nc = k.nc
        self.ident = k.sbuf("ident", [128, 128], F32)
        self.Bident = Buf("ident")
        self.ones32 = k.sbuf("ones32", [128, 128], F32)
        self.onesb = k.sbuf("onesb", [128, 128], BF16)
        self.identb = k.sbuf("identb", [128, 128], BF16)
        self.Bconst = Buf("const")
        k.dma("sp", self.ident[:], consts_dram.ap()[:, 0:128], writes=[self.Bident])
        k.op("dve", lambda h: h.memset(self.ones32[:], 1.0), writes=[self.Bconst])
        k.op("dve", lambda h: h.memset(self.onesb[:], 1.0), writes=[self.Bconst])
        k.op("dve", lambda h: h.tensor_copy(self.identb[:], self.ident[:]), reads=[self.Bident], writes=[self.Bconst])
        self.banks = []
        self.Bbank = []
        for i in range(8):
            self.banks.append(k.psum(f"bank{i}", [128, 512], F32))
            self.Bbank.append(Buf(f"bank{i}", psum=True))
        self.rr = 0
        self.lc_tmp = k.sbuf("lc_tmp", [128, 128], F32)
        self.Blc_tmp = Buf("lc_tmp")

    def evac_eng(self):
        self.rr += 1
        return "act" if self.rr % 2 else "dve"


def copy_on(k, eng, out, in_, reads, writes):
    if eng == "act":
        return k.op("act", lambda h: h.activation(out, in_, AF.Copy), reads=reads, writes=writes)
    return k.op(eng, lambda h: h.tensor_copy(out, in_), reads=reads, writes=writes)


def load_cols(cx, st, vec_ap_rows, R, dst, Bdst, dst_cols=None):
    k = cx.k
    tmp, Bt = cx.lc_tmp, cx.Blc_tmp
    k.dma("sp", tmp[0:R, :], vec_ap_rows, writes=[Bt])
    bank, Bb = cx.banks[7], cx.Bbank[7]
    k.op("pe", lambda h: h.transpose(bank[:, 0:R], tmp[0:R, :], cx.ident[0:R, 0:R]), reads=[Bt, cx.Bident], writes=[Bb])
    d = dst if dst_cols is None else dst_cols
    k.op("dve", lambda h: h.tensor_copy(d, bank[:, 0:R]), reads=[Bb], writes=[Bdst])


def load_vec_cols(cx, st, vec_dram_ap_1d, n, dst, Bdst, col0=0):
    R = n // 128
    rows = vec_dram_ap_1d.rearrange("(r p) -> r p", p=128)
    r0 = 0
    while r0 < R:
        rr = min(128, R - r0)
        load_cols(cx, st, rows[r0:r0 + rr, :], rr, dst, Bdst, dst_cols=dst[:, col0 + r0:col0 + r0 + rr])
        r0 += rr


def phase_transpose_in(cx, x_dram, xT_dram, Tg, D):
    k = cx.k
    DC = D // 128
    with ExitStack() as st:
        xr = [k.sbuf(f"ti_x{i}", [128, D], F32, stack=st) for i in range(2)]
        Bxr = [Buf(f"ti_x{i}") for i in range(2)]
        stg = [k.sbuf(f"ti_s{i}", [128, 4, 128], F32, stack=st) for i in range(3)]
        Bstg = [Buf(f"ti_s{i}") for i in range(3)]
        xTv = xT_dram.ap().rearrange("(c p) t -> p c t", p=128)
        nb = (Tg + 127) // 128
        si = 0
        for tb in range(nb):
            nt = min(128, Tg - tb * 128)
            s = tb % 2
            k.dma("sp", xr[s][0:nt, :], x_dram.ap()[tb * 128:tb * 128 + nt, :], writes=[Bxr[s]])
            for c4 in range(DC // 4):
                b = (tb * (DC // 4) + c4) % 8
                bank, Bb = cx.banks[b], cx.Bbank[b]
                for j in range(4):
                    c = c4 * 4 + j
                    k.op("pe", lambda h: h.transpose(bank[:, j * 128:j * 128 + nt], xr[s][0:nt, c * 128:(c + 1) * 128],
                                                     cx.ident[0:nt, 0:nt]), reads=[Bxr[s], cx.Bident], writes=[Bb])
                g = si % 3
                si += 1
                copy_on(k, cx.evac_eng(), stg[g][:, :, 0:nt], bank[:, :].rearrange("p (j t) -> p j t", j=4)[:, :, 0:nt],
                        [Bb], [Bstg[g]])
                k.dma("sp", xTv[:, c4 * 4:(c4 + 1) * 4, tb * 128:tb * 128 + nt], stg[g][:, :, 0:nt], reads=[Bstg[g]])
    k.barrier()


def phase_transpose_out(cx, xT_dram, y_dram, Tg, D):
    k = cx.k
    DC = D // 128
    with ExitStack() as st:
        xin = [k.sbuf(f"to_x{i}", [128, 4, 128], F32, stack=st) for i in range(3)]
        Bxin = [Buf(f"to_x{i}") for i in range(3)]
        stg = [k.sbuf(f"to_s{i}", [128, 512], F32, stack=st) for i in range(3)]
        Bstg = [Buf(f"to_s{i}") for i in range(3)]
        xTv = xT_dram.ap().rearrange("(c p) t -> p c t", p=128)
        nb = (Tg + 127) // 128
        it = 0
        for tb in range(nb):
            nt = min(128, Tg - tb * 128)
            for c4 in range(DC // 4):
                g = it % 3
                b = it % 8
                it += 1
                bank, Bb = cx.banks[b], cx.Bbank[b]
                k.dma("sp", xin[g][:, :, 0:nt], xTv[:, c4 * 4:(c4 + 1) * 4, tb * 128:tb * 128 + nt], writes=[Bxin[g]])
                for j in range(4):
                    k.op("pe", lambda h: h.transpose(bank[0:nt, j * 128:(j + 1) * 128], xin[g][:, j, 0:nt], cx.ident[:, :]),
                         reads=[Bxin[g], cx.Bident], writes=[Bb])
                copy_on(k, cx.evac_eng(), stg[g][0:nt, :], bank[0:nt, :], [Bb], [Bstg[g]])
                k.dma("sp", y_dram.ap()[tb * 128:tb * 128 + nt, c4 * 512:(c4 + 1) * 512], stg[g][0:nt, :], reads=[Bstg[g]])
    k.barrier()


class NormScratch:
    def __init__(self, cx, st, TT, tag):
        k = cx.k
        self.xs = [k.sbuf(f"{tag}_xs{i}", [128, 4, TT], F32, stack=st) for i in range(2)]
        self.Bxs = [Buf(f"{tag}_xs{i}") for i in range(2)]
        self.sq = [k.sbuf(f"{tag}_sq{i}", [128, TT], F32, stack=st) for i in range(2)]
        self.Bsq = [Buf(f"{tag}_sq{i}") for i in range(2)]
        self.rstd = k.sbuf(f"{tag}_rstd", [128, TT], F32, stack=st)
        self.Brstd = Buf(f"{tag}_rstd")


def norm_tile(cx, ns, xT_dram, t0, TT, D, gain_cols, Bgain, out_bf, Bout, tag):
    k = cx.k
    DC = D // 128
    G = 4
    xs, Bxs, sq, Bsq, rstd, Brstd = ns.xs, ns.Bxs, ns.sq, ns.Bsq, ns.rstd, ns.Brstd
    xTv = xT_dram.ap().rearrange("(c p) t -> p c t", p=128)
    bank, Bb = cx.banks[6], cx.Bbank[6]
    it = 0
    for c4 in range(DC // G):
        g = it % 2
        it += 1
        k.dma("sp", xs[g][:, :, :], xTv[:, c4 * G:(c4 + 1) * G, t0:t0 + TT], writes=[Bxs[g]])
        for j in range(G):
            c = c4 * G + j
            q = c % 2
            k.op("act", lambda h: h.activation(sq[q][:, :], xs[g][:, j, :], AF.Square), reads=[Bxs[g]], writes=[Bsq[q]])
            k.op("pe", lambda h: h.matmul(bank[:, 0:TT], cx.ones32[:, :], sq[q][:, :], start=(c == 0), stop=(c == DC - 1)),
                 reads=[Bsq[q], cx.Bconst], writes=[Bb])
    k.op("act", lambda h: h.activation(rstd[:, :], bank[:, 0:TT], AF.Sqrt, bias=EPS, scale=1.0 / D), reads=[Bb], writes=[Brstd])
    k.op("dve", lambda h: h.reciprocal(rstd[:, :], rstd[:, :]), reads=[Brstd], writes=[Brstd])
    for c4 in range(DC // G):
        g = it % 2
        it += 1
        k.dma("sp", xs[g][:, :, :], xTv[:, c4 * G:(c4 + 1) * G, t0:t0 + TT], writes=[Bxs[g]])
        for j in range(G):
            c = c4 * G + j
            k.op("dve", lambda h: h.scalar_tensor_tensor(out_bf[:, c, 0:TT], xs[g][:, j, :], gain_cols[:, c:c + 1], rstd[:, :],
                                                         ALU.mult, ALU.mult), reads=[Bxs[g], Bgain, Brstd], writes=[Bout])


class WStream:
    def __init__(self, cx, st, KG=4, NW=3, tag="ws"):
        self.cx = cx
        k = cx.k
        self.KG = KG
        self.NW = NW
        self.w = [k.sbuf(f"{tag}_w{i}", [128, KG, 512], BF16, stack=st) for i in range(NW)]
        self.Bw = [Buf(f"{tag}_w{i}") for i in range(NW)]
        self.wi = 0
        self.bi = 0

    def run(self, W_ap2d, K, blocks, rhs_bf, Brhs, TT, evac):
        cx, k = self.cx, self.cx.k
        KC = K // 128
        KG = self.KG
        assert KC % KG == 0
        Wv = W_ap2d.rearrange("(kc p) n -> p kc n", p=128)
        for blk in blocks:
            chunks = []
            off = 0
            for (c0, ncol) in blk:
                o = 0
                while o < ncol:
                    r = min(128, ncol - o)
                    chunks.append((off + o, r, c0 + o))
                    o += r
                off += ncol
            assert off <= 512 and len(chunks) <= 4
            half = self.bi % 2
            self.bi += 1
            for kg in range(KC // KG):
                s = self.wi % self.NW
                self.wi += 1
                off = 0
                for (c0, ncol) in blk:
                    k.dma("pool", self.w[s][:, :, off:off + ncol], Wv[:, kg * KG:(kg + 1) * KG, c0:c0 + ncol], writes=[self.Bw[s]])
                    off += ncol
                for kl in range(KG):
                    kc = kg * KG + kl
                    for j, (woff, rows, _) in enumerate(chunks):
                        b = half * 4 + j
                        k.op("pe", lambda h: h.matmul(cx.banks[b][0:rows, 0:TT], self.w[s][:, kl, woff:woff + rows], rhs_bf[:, kc, 0:TT],
                                                      start=(kc == 0), stop=(kc == KC - 1)),
                             reads=[self.Bw[s], Brhs], writes=[cx.Bbank[b]])
            for j, (woff, rows, col0) in enumerate(chunks):
                b = half * 4 + j
                evac(j, cx.banks[b][0:rows, 0:TT], cx.Bbank[b], rows, col0)


def std_blocks(N):
    out = []
    c = 0
    while c < N:
        n = min(512, N - c)
        out.append([(c, n)])
        c += n
    return out


def phase_inproj(cx, xT_dram, projT_dram, Tg, TT, D, NIN, w_in_ap, gain_vec_ap):
    k = cx.k
    DC = D // 128
    with ExitStack() as st:
        gain = k.sbuf("ip_gain", [128, DC], F32, stack=st)
        Bgain = Buf("ip_gain")
        load_vec_cols(cx, st, gain_vec_ap, D, gain, Bgain)
        xnb = k.sbuf("ip_xnb", [128, DC, TT], BF16, stack=st)
        Bxnb = Buf("ip_xnb")
        stg = [k.sbuf(f"ip_stg{i}", [128, TT], F32, stack=st) for i in range(4)]
        Bstg = [Buf(f"ip_stg{i}") for i in range(4)]
        ws = WStream(cx, st, tag="ip")
        ns = NormScratch(cx, st, TT, "ipn")
        cnt = [0]
        for t0 in range(0, Tg, TT):
            norm_tile(cx, ns, xT_dram, t0, TT, D, gain, Bgain, xnb, Bxnb, "ipn")

            def evac(j, bank_ap, Bb, rows, col0):
                g = cnt[0] % 4
                cnt[0] += 1
                copy_on(k, cx.evac_eng(), stg[g][0:rows, :], bank_ap, [Bb], [Bstg[g]])
                k.dma("sp", projT_dram.ap()[col0:col0 + rows, t0:t0 + TT], stg[g][0:rows, :], reads=[Bstg[g]])
            ws.run(w_in_ap, D, std_blocks(NIN), xnb, Bxnb, TT, evac)
    k.barrier()


def phase_outproj(cx, xT_dram, mixT_dram, hT_dram, Tg, TT, D, DMIX, w_out_ap):
    k = cx.k
    MC = DMIX // 128
    with ExitStack() as st:
        mixb = k.sbuf("op_mixb", [128, MC, TT], BF16, stack=st)
        Bmixb = Buf("op_mixb")
        xres = [k.sbuf(f"op_x{i}", [128, TT], F32, stack=st) for i in range(4)]
        Bxres = [Buf(f"op_x{i}") for i in range(4)]
        ws = WStream(cx, st, tag="op")
        cnt = [0]
        mv = mixT_dram.ap().rearrange("(c p) t -> p c t", p=128)
        for t0 in range(0, Tg, TT):
            for m0 in range(0, MC, 8):
                m1 = min(MC, m0 + 8)
                k.dma("sp", mixb[:, m0:m1, 0:TT], mv[:, m0:m1, t0:t0 + TT], writes=[Bmixb])

            def evac(j, bank_ap, Bb, rows, col0):
                g = cnt[0] % 4
                cnt[0] += 1
                k.dma("sp", xres[g][0:rows, :], xT_dram.ap()[col0:col0 + rows, t0:t0 + TT], writes=[Bxres[g]])
                k.op("dve", lambda h: h.tensor_tensor(xres[g][0:rows, :], bank_ap, xres[g][0:rows, :], ALU.add),
                     reads=[Bb, Bxres[g]], writes=[Bxres[g]])
                k.dma("sp", hT_dram.ap()[col0:col0 + rows, t0:t0 + TT], xres[g][0:rows, :], reads=[Bxres[g]])
            ws.run(w_out_ap, DMIX, std_blocks(D), mixb, Bmixb, TT, evac)
    k.barrier()


def phase_ffn(cx, hT_dram, xT_dram, Tg, TT, D, DFF, gain_vec_ap, up_ap, convw_ap, convb_ap, down_ap,
              state_ap, out_state_ap):
    k = cx.k
    DC = D // 128
    FC = DFF // 128
    with ExitStack() as st:
        gain = k.sbuf("ff_gain", [128, DC], F32, stack=st)
        Bgain = Buf("ff_gain")
        load_vec_cols(cx, st, gain_vec_ap, D, gain, Bgain)
        cw = k.sbuf("ff_cw", [128, 3, 2 * FC], F32, stack=st)
        cb = k.sbuf("ff_cb", [128, 2 * FC], F32, stack=st)
        tails = k.sbuf("ff_tails", [128, 2 * FC, 2], F32, stack=st)
        tl2 = k.sbuf("ff_tl2", [128, 2, 2 * FC], F32, stack=st)
        Bcw, Bcb, Btails = Buf("ff_cw"), Buf("ff_cb"), Buf("ff_tails")
        for i in range(3):
            load_vec_cols(cx, st, convw_ap[i, :], 2 * DFF, cw[:, i, :], Bcw)
        load_vec_cols(cx, st, convb_ap, 2 * DFF, cb, Bcb)
        if state_ap is None:
            k.op("dve", lambda h: h.memset(tails[:, :, :], 0.0), writes=[Btails])
        else:
            for r in range(2):
                load_vec_cols(cx, st, state_ap[r, :], 2 * DFF, tl2[:, r, :], Btails)
            k.op("dve", lambda h: h.tensor_copy(tails[:, :, :], tl2[:, :, :].rearrange("p r c -> p c r")), reads=[Btails], writes=[Btails])
        hnb = k.sbuf("ff_hnb", [128, DC, TT], BF16, stack=st)
        Bhnb = Buf("ff_hnb")
        actb = k.sbuf("ff_actb", [128, FC, TT], BF16, stack=st)
        Bactb = Buf("ff_actb")
        NE = 4
        ext = [k.sbuf(f"ff_ext{i}", [128, TT + 2], F32, stack=st) for i in range(NE)]
        Bext = [Buf(f"ff_ext{i}") for i in range(NE)]
        acc = [k.sbuf(f"ff_acc{i}", [128, TT], F32, stack=st) for i in range(NE)]
        Bacc = [Buf(f"ff_acc{i}") for i in range(NE)]
        hres = [k.sbuf(f"ff_h{i}", [128, TT], F32, stack=st) for i in range(2)]
        Bhres = [Buf(f"ff_h{i}") for i in range(2)]
        ns = NormScratch(cx, st, TT, "ffn")
        ws = WStream(cx, st, KG=(4 if DC % 4 == 0 else 2), tag="ffu")
        wsd = WStream(cx, st, KG=(4 if FC % 4 == 0 else (2 if FC % 2 == 0 else 1)), tag="ffd")
        cnt = [0]
        ublocks = []
        f = 0
        while f < FC:
            n = min(2, FC - f)
            ublocks.append([(f * 128, n * 128), (DFF + f * 128, n * 128)])
            f += n
        for t0 in range(0, Tg, TT):
            norm_tile(cx, ns, hT_dram, t0, TT, D, gain, Bgain, hnb, Bhnb, "ffn")
            pend = {}

            def evac_up(j, bank_ap, Bb, rows, col0):
                ch = col0 // 128
                e = cnt[0] % NE
                cnt[0] += 1
                k.op("act", lambda h: h.activation(ext[e][:, 2:2 + TT], bank_ap, AF.Copy), reads=[Bb], writes=[Bext[e]])
                k.op("dve", lambda h: h.tensor_copy(ext[e][:, 0:2], tails[:, ch, :]), reads=[Btails], writes=[Bext[e]])
                k.op("dve", lambda h: h.tensor_copy(tails[:, ch, :], ext[e][:, TT:TT + 2]), reads=[Bext[e]], writes=[Btails])
                k.op("act", lambda h: h.activation(acc[e][:, :], ext[e][:, 2:2 + TT], AF.Identity, bias=cb[:, ch:ch + 1], scale=cw[:, 2, ch:ch + 1]),
                     reads=[Bext[e], Bcw, Bcb], writes=[Bacc[e]])
                k.op("dve", lambda h: h.scalar_tensor_tensor(acc[e][:, :], ext[e][:, 1:1 + TT], cw[:, 1, ch:ch + 1], acc[e][:, :], ALU.mult, ALU.add),
                     reads=[Bext[e], Bcw, Bacc[e]], writes=[Bacc[e]])
                k.op("dve", lambda h: h.scalar_tensor_tensor(acc[e][:, :], ext[e][:, 0:TT], cw[:, 0, ch:ch + 1], acc[e][:, :], ALU.mult, ALU.add),
                     reads=[Bext[e], Bcw, Bacc[e]], writes=[Bacc[e]])
                if ch < FC:
                    k.op("act", lambda h: h.activation(acc[e][:, :], acc[e][:, :], AF.Silu), reads=[Bacc[e]], writes=[Bacc[e]])
                    pend[ch] = e
                else:
                    ge = pend.pop(ch - FC)
                    k.op("dve", lambda h: h.tensor_tensor(actb[:, ch - FC, 0:TT], acc[ge][:, :], acc[e][:, :], ALU.mult),
                         reads=[Bacc[ge], Bacc[e]], writes=[Bactb])
            ws.run(up_ap, D, ublocks, hnb, Bhnb, TT, evac_up)

            def evac_dn(j, bank_ap, Bb, rows, col0):
                g = cnt[0] % 2
                cnt[0] += 1
                k.dma("sp", hres[g][0:rows, :], hT_dram.ap()[col0:col0 + rows, t0:t0 + TT], writes=[Bhres[g]])
                k.op("dve", lambda h: h.tensor_tensor(hres[g][0:rows, :], bank_ap, hres[g][0:rows, :], ALU.add),
                     reads=[Bb, Bhres[g]], writes=[Bhres[g]])
                k.dma("sp", xT_dram.ap()[col0:col0 + rows, t0:t0 + TT], hres[g][0:rows, :], reads=[Bhres[g]])
            wsd.run(down_ap, DFF, std_blocks(D), actb, Bactb, TT, evac_dn)
        k.op("dve", lambda h: h.tensor_copy(tl2[:, :, :], tails[:, :, :].rearrange("p c r -> p r c")), reads=[Btails], writes=[Btails])
        so = k.sbuf("ff_so", [128, 128], F32, stack=st)
        Bso = Buf("ff_so")
        for r in range(2):
            c0 = 0
            while c0 < 2 * FC:
                n = min(128, 2 * FC - c0)
                bank, Bb = cx.banks[7], cx.Bbank[7]
                k.op("pe", lambda h: h.transpose(bank[0:n, 0:128], tl2[:, r, c0:c0 + n], cx.ident[:, :]), reads=[Btails, cx.Bident], writes=[Bb])
                k.op("dve", lambda h: h.tensor_copy(so[0:n, :], bank[0:n, 0:128]), reads=[Bb], writes=[Bso])
                k.dma("sp", out_state_ap[r, c0 * 128:(c0 + n) * 128].rearrange("(c p) -> c p", p=128), so[0:n, :], reads=[Bso])
                c0 += n
    k.barrier()

import math
from contextlib import ExitStack

BRANCHES = ((128, 1), (512, 4), (2048, 16))
TOE_ML = 384
SMP_ML = 2064


def t5_bucket_np(dist):
    dist = np.asarray(dist, np.int64)
    d = np.maximum(dist, 1).astype(np.float32)
    large = 16 + (np.log(d / np.float32(16)) / np.float32(math.log(2048 / 16)) * np.float32(16)).astype(np.int32)
    large = np.minimum(large, 31)
    return np.where(dist < 16, dist, large)


def attn_onehots():
    ohp = np.zeros((3, 32, TOE_ML), np.float32)
    for bi, (w, d) in enumerate(BRANCHES):
        for m in range(TOE_ML):
            j = m - 127
            if 0 <= j <= w // d:
                ohp[bi, t5_bucket_np(j * d), m] = 1.0
    ohs = np.zeros((32, SMP_ML), np.float32)
    for m in range(SMP_ML):
        rel = m - 8
        if rel < 0:
            continue
        cnt = sum(1 for (w, d) in BRANCHES if rel % d == 0 and rel <= w)
        ohs[t5_bucket_np(rel), m] = cnt
    return ohp, ohs


def setup_attn_tables(cx, rel_bias_dram, ohp_dram, ohs_dram, Hp_dram, Hs_dram, HA, do_prompt=True, do_sample=True):
    k = cx.k
    with ExitStack() as st:
        rb = k.sbuf("at_rb", [32, HA], F32, stack=st)
        Brb = Buf("at_rb")
        k.dma("sp", rb[:, :], rel_bias_dram.ap(), writes=[Brb])
        k.op("act", lambda h: h.activation(rb[:, :], rb[:, :], AF.Exp), reads=[Brb], writes=[Brb])
        ohp = k.sbuf("at_ohp", [32, 3, TOE_ML], F32, stack=st)
        ohs = k.sbuf("at_ohs", [32, SMP_ML], F32, stack=st)
        Boh = Buf("at_oh")
        k.dma("sp", ohp[:, :, :], ohp_dram.ap().rearrange("b k m -> k b m"), writes=[Boh])
        k.dma("sp", ohs[:, :], ohs_dram.ap(), writes=[Boh])
        erb = [k.sbuf(f"at_erb{i}", [32, 128], F32, stack=st) for i in range(2)]
        Berb = [Buf(f"at_erb{i}") for i in range(2)]
        stg = [k.sbuf(f"at_stg{i}", [128, 512], F32, stack=st) for i in range(3)]
        Bstg = [Buf(f"at_stg{i}") for i in range(3)]
        it = 0
        for h in range(HA):
            e = h % 2
            k.op("dve", lambda hh: hh.tensor_scalar(erb[e][:, :], cx.ones32[0:32, :], rb[:, h:h + 1], None, ALU.mult),
                 reads=[Brb, cx.Bconst], writes=[Berb[e]])
            jobs = []
            if do_prompt:
                for bi in range(3):
                    jobs.append((ohp[:, bi, :], TOE_ML, Hp_dram.ap()[h, bi, :, :]))
            if do_sample:
                c0 = 0
                while c0 < SMP_ML:
                    n = min(512, SMP_ML - c0)
                    jobs.append((ohs[:, c0:c0 + n], n, Hs_dram.ap()[h, :, c0:c0 + n]))
                    c0 += n
            for (rhs, n, dst) in jobs:
                b = it % 8
                g = it % 3
                it += 1
                k.op("pe", lambda hh: hh.matmul(cx.banks[b][:, 0:n], erb[e][:, :], rhs, start=True, stop=True),
                     reads=[Berb[e], Boh], writes=[cx.Bbank[b]])
                copy_on(k, cx.evac_eng(), stg[g][:, 0:n], cx.banks[b][:, 0:n], [cx.Bbank[b]], [Bstg[g]])
                k.dma("sp", dst, stg[g][:, 0:n], reads=[Bstg[g]])
    k.barrier()


def head_norm(cx, src, Bsrc, n, TT_list, gcol, Bg, rs, Brs, sqt, Bsq, bank_i, eps_scale):
    k = cx.k
    for (c0, cn) in TT_list:
        k.op("act", lambda h: h.activation(sqt[:, 0:cn], src[:, c0:c0 + cn], AF.Square), reads=[Bsrc], writes=[Bsq])
        k.op("pe", lambda h: h.matmul(cx.banks[bank_i][:, 0:cn], cx.ones32[:, :], sqt[:, 0:cn], start=True, stop=True),
             reads=[Bsq, cx.Bconst], writes=[cx.Bbank[bank_i]])
        k.op("act", lambda h: h.activation(rs[:, c0:c0 + cn], cx.banks[bank_i][:, 0:cn], AF.Sqrt, bias=EPS, scale=eps_scale),
             reads=[cx.Bbank[bank_i]], writes=[Brs])
    k.op("dve", lambda h: h.reciprocal(rs[:, 0:n], rs[:, 0:n]), reads=[Brs], writes=[Brs])


def tiles_of(n, t=512):
    return [(c, min(t, n - c)) for c in range(0, n, t)]


def phase_attn_prompt(cx, projT, mixT, kv_out_ap, S, HA, Hp_dram, qn_ap, kn_ap, on_ap):
    k = cx.k
    DA = HA * 128
    NB = S // 128
    assert S % 2048 == 0 or S in (256, 512, 1024, 2048)
    with ExitStack() as st:
        gq = k.sbuf("ap_gq", [128, 1], F32, stack=st)
        gk = k.sbuf("ap_gk", [128, 1], F32, stack=st)
        go = k.sbuf("ap_go", [128, HA], F32, stack=st)
        Bg = Buf("ap_g")
        load_vec_cols(cx, st, qn_ap, 128, gq, Bg)
        load_vec_cols(cx, st, kn_ap, 128, gk, Bg)
        load_vec_cols(cx, st, on_ap, DA, go, Bg)
        k.op("dve", lambda h: h.tensor_scalar(gq[:, :], gq[:, :], 128.0 ** -0.5, None, ALU.mult), reads=[Bg], writes=[Bg])
        raw = [k.sbuf(f"ap_raw{i}", [128, S], F32, stack=st) for i in range(3)]
        Braw = [Buf(f"ap_raw{i}") for i in range(3)]
        rs = k.sbuf("ap_rs", [128, S], F32, stack=st)
        Brs = Buf("ap_rs")
        sqt = k.sbuf("ap_sq", [128, 512], F32, stack=st)
        Bsq = Buf("ap_sq")
        knf = k.sbuf("ap_knf", [128, S], F32, stack=st)
        Bknf = Buf("ap_knf")
        qb = [k.sbuf(f"ap_qb{i}", [128, S], BF16, stack=st) for i in range(3)]
        kb = [k.sbuf(f"ap_kb{i}", [128, S], BF16, stack=st) for i in range(3)]
        Bqb = [Buf(f"ap_qb{i}") for i in range(3)]
        Bkb = [Buf(f"ap_kb{i}") for i in range(3)]
        vperm = k.sbuf("ap_vperm", [128, S], F32, stack=st)
        Bvperm = Buf("ap_vperm")
        vtok = k.sbuf("ap_vtok", [128, 3, NB, 128], BF16, stack=st)
        Bvtok = Buf("ap_vtok")
        kvst = [k.sbuf(f"ap_kvst{i}", [128, 2, 128], F32, stack=st) for i in range(3)]
        Bkvst = [Buf(f"ap_kvst{i}") for i in range(3)]
        eb = k.sbuf("ap_eb", [128, 3, 2, 128], F32, stack=st)
        Beb = Buf("ap_eb")
        pe_ = [k.sbuf(f"ap_pe{i}", [128, 128], F32, stack=st) for i in range(3)]
        Bpe = [Buf(f"ap_pe{i}") for i in range(3)]
        pt = [k.sbuf(f"ap_pt{i}", [128, 128], BF16, stack=st) for i in range(3)]
        Bpt = [Buf(f"ap_pt{i}") for i in range(3)]
        oacc = k.sbuf("ap_oacc", [128, S], F32, stack=st)
        dacc = k.sbuf("ap_dacc", [128, S], F32, stack=st)
        Boacc, Bdacc = Buf("ap_oacc"), Buf("ap_dacc")
        yab = k.sbuf("ap_yab", [128, S], BF16, stack=st)
        Byab = Buf("ap_yab")
        T5 = tiles_of(S)
        it = 0
        for h in range(HA):
            for i in range(3):
                k.dma("sp", raw[i][:, :], projT.ap()[i * DA + h * 128:i * DA + (h + 1) * 128, 0:S], writes=[Braw[i]])
            for bi in range(3):
                for vi, off in enumerate((127, 255)):
                    src = bass.AP(Hp_dram, (h * 3 + bi) * 128 * TOE_ML + off, [[TOE_ML - 1, 128], [1, 128]])
                    k.dma("sp", eb[:, bi, vi, :], src, writes=[Beb])
            head_norm(cx, raw[0], Braw[0], S, T5, None, None, rs, Brs, sqt, Bsq, 6, 1.0 / 128)
            k.op("dve", lambda hh: hh.scalar_tensor_tensor(qb[0][:, :], raw[0][:, :], gq[:, 0:1], rs[:, :], ALU.mult, ALU.mult),
                 reads=[Braw[0], Bg, Brs], writes=[Bqb[0]])
            head_norm(cx, raw[1], Braw[1], S, T5, None, None, rs, Brs, sqt, Bsq, 6, 1.0 / 128)
            k.op("dve", lambda hh: hh.scalar_tensor_tensor(knf[:, :], raw[1][:, :], gk[:, 0:1], rs[:, :], ALU.mult, ALU.mult),
                 reads=[Braw[1], Bg, Brs], writes=[Bknf])
            k.op("act", lambda hh: hh.activation(kb[0][:, :], knf[:, :], AF.Copy), reads=[Bknf], writes=[Bkb[0]])
            for bi, d in ((1, 4), (2, 16)):
                k.op("dve", lambda hh: hh.tensor_copy(qb[bi][:, :].rearrange("p (r u) -> p r u", r=d),
                                                      qb[0][:, :].rearrange("p (u r) -> p r u", r=d)), reads=[Bqb[0]], writes=[Bqb[bi]])
                k.op("dve", lambda hh: hh.tensor_copy(kb[bi][:, :].rearrange("p (r u) -> p r u", r=d),
                                                      kb[0][:, :].rearrange("p (u r) -> p r u", r=d)), reads=[Bkb[0]], writes=[Bkb[bi]])
            for tb in range(NB):
                b = it % 4
                g = it % 3
                it += 1
                bank, Bb = cx.banks[b], cx.Bbank[b]
                k.op("pe", lambda hh: hh.transpose(bank[:, 0:128], knf[:, tb * 128:(tb + 1) * 128], cx.ident[:, :]),
                     reads=[Bknf, cx.Bident], writes=[Bb])
                k.op("pe", lambda hh: hh.transpose(bank[:, 128:256], raw[2][:, tb * 128:(tb + 1) * 128], cx.ident[:, :]),
                     reads=[Braw[2], cx.Bident], writes=[Bb])
                k.op("act", lambda hh: hh.activation(kvst[g][:, :, :], bank[:, 0:256].rearrange("p (a d) -> p a d", a=2), AF.Copy),
                     reads=[Bb], writes=[Bkvst[g]])
                k.op("dve", lambda hh: hh.tensor_copy(vtok[:, 0, tb, :], bank[:, 128:256]), reads=[Bb], writes=[Bvtok])
                k.dma("sp", kv_out_ap[tb * 128:(tb + 1) * 128, :, h, :], kvst[g][:, :, :], reads=[Bkvst[g]])
            for bi, d in ((1, 4), (2, 16)):
                k.op("dve", lambda hh: hh.tensor_copy(vperm[:, :].rearrange("p (r u) -> p r u", r=d),
                                                      raw[2][:, :].rearrange("p (u r) -> p r u", r=d)), reads=[Braw[2]], writes=[Bvperm])
                for tb in range(NB):
                    b = it % 4
                    it += 1
                    bank, Bb = cx.banks[b], cx.Bbank[b]
                    k.op("pe", lambda hh: hh.transpose(bank[:, 0:128], vperm[:, tb * 128:(tb + 1) * 128], cx.ident[:, :]),
                         reads=[Bvperm, cx.Bident], writes=[Bb])
                    copy_on(k, cx.evac_eng(), vtok[:, bi, tb, :], bank[:, 0:128], [Bb], [Bvtok])
            for bi, (w, d) in enumerate(BRANCHES):
                L = S // d
                nbc = max(1, L // 128)
                for Q in range(NB // 4):
                    ob, db = 4 + (Q % 2) * 2, 5 + (Q % 2) * 2
                    for jj in range(4):
                        B = Q * 4 + jj
                        n = B % nbc
                        kbs = ([B - 1] if n >= 1 else []) + [B]
                        for ki, KB_ in enumerate(kbs):
                            vi = 0 if KB_ == B else 1
                            sb = it % 4
                            g = it % 3
                            it += 1
                            k.op("pe", lambda hh: hh.matmul(cx.banks[sb][:, 0:128], kb[bi][:, KB_ * 128:(KB_ + 1) * 128],
                                                            qb[bi][:, B * 128:(B + 1) * 128], start=True, stop=True),
                                 reads=[Bkb[bi], Bqb[bi]], writes=[cx.Bbank[sb]])
                            k.op("act", lambda hh: hh.activation(pe_[g][:, :], cx.banks[sb][:, 0:128], AF.Exp),
                                 reads=[cx.Bbank[sb]], writes=[Bpe[g]])
                            k.op("dve", lambda hh: hh.tensor_tensor(pt[g][:, :], pe_[g][:, :], eb[:, bi, vi, :], ALU.mult),
                                 reads=[Bpe[g], Beb], writes=[Bpt[g]])
                            k.op("pe", lambda hh: hh.matmul(cx.banks[ob][:, jj * 128:(jj + 1) * 128], vtok[:, bi, KB_, :], pt[g][:, :],
                                                            start=(ki == 0), stop=(ki == len(kbs) - 1)),
                                 reads=[Bvtok, Bpt[g]], writes=[cx.Bbank[ob]])
                            k.op("pe", lambda hh: hh.matmul(cx.banks[db][:, jj * 128:(jj + 1) * 128], cx.onesb[:, :], pt[g][:, :],
                                                            start=(ki == 0), stop=(ki == len(kbs) - 1)),
                                 reads=[cx.Bconst, Bpt[g]], writes=[cx.Bbank[db]])
                    if d == 1:
                        ov = oacc[:, Q * 512:(Q + 1) * 512]
                        dv = dacc[:, Q * 512:(Q + 1) * 512]
                        k.op("act", lambda hh: hh.activation(ov, cx.banks[ob][:, :], AF.Copy), reads=[cx.Bbank[ob]], writes=[Boacc])
                        k.op("dve", lambda hh: hh.tensor_copy(dv, cx.banks[db][:, :]), reads=[cx.Bbank[db]], writes=[Bdacc])
                    else:
                        cpq = 512 // L if L < 512 else 1
                        if L >= 512:
                            r = Q // (L // 512)
                            u0 = (Q % (L // 512)) * 512
                            ov = oacc[:, :].rearrange("p (u r) -> p r u", r=d)[:, r, u0:u0 + 512]
                            dv = dacc[:, :].rearrange("p (u r) -> p r u", r=d)[:, r, u0:u0 + 512]
                            oi = cx.banks[ob][:, :]
                            di = cx.banks[db][:, :]
                        else:
                            ov = oacc[:, :].rearrange("p (u r) -> p r u", r=d)[:, Q * cpq:(Q + 1) * cpq, :]
                            dv = dacc[:, :].rearrange("p (u r) -> p r u", r=d)[:, Q * cpq:(Q + 1) * cpq, :]
                            oi = cx.banks[ob][:, :].rearrange("p (c u) -> p c u", c=cpq)
                            di = cx.banks[db][:, :].rearrange("p (c u) -> p c u", c=cpq)
                        k.op("dve", lambda hh: hh.tensor_tensor(ov, oi, ov, ALU.add), reads=[cx.Bbank[ob], Boacc], writes=[Boacc])
                        k.op("dve", lambda hh: hh.tensor_tensor(dv, di, dv, ALU.add), reads=[cx.Bbank[db], Bdacc], writes=[Bdacc])
            k.op("dve", lambda hh: hh.reciprocal(dacc[:, :], dacc[:, :]), reads=[Bdacc], writes=[Bdacc])
            k.op("dve", lambda hh: hh.tensor_tensor(oacc[:, :], oacc[:, :], dacc[:, :], ALU.mult), reads=[Boacc, Bdacc], writes=[Boacc])
            head_norm(cx, oacc, Boacc, S, T5, None, None, rs, Brs, sqt, Bsq, 6, 1.0 / 128)
            k.op("dve", lambda hh: hh.scalar_tensor_tensor(yab[:, :], oacc[:, :], go[:, h:h + 1], rs[:, :], ALU.mult, ALU.mult),
                 reads=[Boacc, Bg, Brs], writes=[Byab])
            k.dma("sp", mixT.ap()[h * 128:(h + 1) * 128, 0:S], yab[:, :], reads=[Byab])
    k.barrier()


def phase_attn_sample(cx, projT, mixT, kv_out_ap, cache_ap, T, P, HA, Hs_dram, qn_ap, kn_ap, on_ap):
    k = cx.k
    DA = HA * 128
    PB = P // 128
    with ExitStack() as st:
        gq = k.sbuf("as_gq", [128, 1], F32, stack=st)
        gk = k.sbuf("as_gk", [128, 1], F32, stack=st)
        go = k.sbuf("as_go", [128, HA], F32, stack=st)
        Bg = Buf("as_g")
        load_vec_cols(cx, st, qn_ap, 128, gq, Bg)
        load_vec_cols(cx, st, kn_ap, 128, gk, Bg)
        load_vec_cols(cx, st, on_ap, DA, go, Bg)
        k.op("dve", lambda h: h.tensor_scalar(gq[:, :], gq[:, :], 128.0 ** -0.5, None, ALU.mult), reads=[Bg], writes=[Bg])
        raw = [k.sbuf(f"as_raw{i}", [128, T], F32, stack=st) for i in range(3)]
        Braw = [Buf(f"as_raw{i}") for i in range(3)]
        rs = k.sbuf("as_rs", [128, T], F32, stack=st)
        Brs = Buf("as_rs")
        sqt = k.sbuf("as_sq", [128, T], F32, stack=st)
        Bsq = Buf("as_sq")
        knf = k.sbuf("as_knf", [128, T], F32, stack=st)
        Bknf = Buf("as_knf")
        qb = k.sbuf("as_qb", [128, T], BF16, stack=st)
        kbn = k.sbuf("as_kbn", [128, T], BF16, stack=st)
        Bqb, Bkbn = Buf("as_qb"), Buf("as_kbn")
        kvst = k.sbuf("as_kvst", [128, 2, 128], F32, stack=st)
        Bkvst = Buf("as_kvst")
        vnb = k.sbuf("as_vnb", [128, 128], BF16, stack=st)
        Bvnb = Buf("as_vnb")
        kc = [k.sbuf(f"as_kc{i}", [128, PB, 128], F32, stack=st) for i in range(2)]
        vc = [k.sbuf(f"as_vc{i}", [128, PB, 128], F32, stack=st) for i in range(2)]
        Bkc = [Buf(f"as_kc{i}") for i in range(2)]
        Bvc = [Buf(f"as_vc{i}") for i in range(2)]
        ktb = k.sbuf("as_ktb", [128, PB, 128], BF16, stack=st)
        vcb = k.sbuf("as_vcb", [128, PB, 128], BF16, stack=st)
        Bktb, Bvcb = Buf("as_ktb"), Buf("as_vcb")
        cs = k.sbuf("as_cs", [128, PB, T], F32, stack=st)
        cn = k.sbuf("as_cn", [128, T], F32, stack=st)
        Bcs = Buf("as_cs")
        pe_ = k.sbuf("as_pe", [128, PB, T], F32, stack=st)
        pn_ = k.sbuf("as_pn", [128, T], F32, stack=st)
        ptb = k.sbuf("as_ptb", [128, PB, T], BF16, stack=st)
        pnb = k.sbuf("as_pnb", [128, T], BF16, stack=st)
        Bpe, Bptb = Buf("as_pe"), Buf("as_ptb")
        oacc = k.sbuf("as_oacc", [128, T], F32, stack=st)
        dacc = k.sbuf("as_dacc", [128, T], F32, stack=st)
        Boacc = Buf("as_oacc")
        yab = k.sbuf("as_yab", [128, T], BF16, stack=st)
        Byab = Buf("as_yab")
        TL = [(0, T)]
        for h in range(HA):
            s = h % 2
            for i in range(3):
                k.dma("sp", raw[i][:, :], projT.ap()[i * DA + h * 128:i * DA + (h + 1) * 128, 0:T], writes=[Braw[i]])
            k.dma("sp", kc[s][:, :, :], cache_ap[:, 0, h, :].rearrange("(b p) d -> p b d", p=128), writes=[Bkc[s]])
            k.dma("sp", vc[s][:, :, :], cache_ap[:, 1, h, :].rearrange("(b p) d -> p b d", p=128), writes=[Bvc[s]])
            src = bass.AP(Hs_dram, h * 128 * SMP_ML + 8 + 128, [[SMP_ML - 1, 128], [128, PB], [1, T]])
            k.dma("sp", cs[:, :, :], src, writes=[Bcs])
            srcn = bass.AP(Hs_dram, h * 128 * SMP_ML + 8, [[SMP_ML - 1, T], [1, T]])
            k.dma("sp", cn[0:T, :], srcn, writes=[Bcs])
            head_norm(cx, raw[0], Braw[0], T, TL, None, None, rs, Brs, sqt, Bsq, 6, 1.0 / 128)
            k.op("dve", lambda hh: hh.scalar_tensor_tensor(qb[:, :], raw[0][:, :], gq[:, 0:1], rs[:, :], ALU.mult, ALU.mult),
                 reads=[Braw[0], Bg, Brs], writes=[Bqb])
            head_norm(cx, raw[1], Braw[1], T, TL, None, None, rs, Brs, sqt, Bsq, 6, 1.0 / 128)
            k.op("dve", lambda hh: hh.scalar_tensor_tensor(knf[:, :], raw[1][:, :], gk[:, 0:1], rs[:, :], ALU.mult, ALU.mult),
                 reads=[Braw[1], Bg, Brs], writes=[Bknf])
            k.op("act", lambda hh: hh.activation(kbn[:, :], knf[:, :], AF.Copy), reads=[Bknf], writes=[Bkbn])
            bank, Bb = cx.banks[0], cx.Bbank[0]
            k.op("pe", lambda hh: hh.transpose(bank[0:T, 0:128], knf[:, 0:T], cx.ident[:, :]), reads=[Bknf, cx.Bident], writes=[Bb])
            k.op("pe", lambda hh: hh.transpose(bank[0:T, 128:256], raw[2][:, 0:T], cx.ident[:, :]), reads=[Braw[2], cx.Bident], writes=[Bb])
            k.op("act", lambda hh: hh.activation(kvst[0:T, :, :], bank[0:T, 0:256].rearrange("p (a d) -> p a d", a=2), AF.Copy),
                 reads=[Bb], writes=[Bkvst])
            k.op("dve", lambda hh: hh.tensor_copy(vnb[0:T, :], bank[0:T, 128:256]), reads=[Bb], writes=[Bvnb])
            k.dma("sp", kv_out_ap[0:T, :, h, :], kvst[0:T, :, :], reads=[Bkvst])
            for b4 in range(PB // 4):
                bi_ = 1 + (b4 % 3)
                bank, Bb = cx.banks[bi_], cx.Bbank[bi_]
                for j in range(4):
                    blk = b4 * 4 + j
                    k.op("pe", lambda hh: hh.transpose(bank[:, j * 128:(j + 1) * 128], kc[s][:, blk, :], cx.ident[:, :]),
                         reads=[Bkc[s], cx.Bident], writes=[Bb])
                copy_on(k, cx.evac_eng(), ktb[:, b4 * 4:(b4 + 1) * 4, :], bank[:, :].rearrange("p (j t) -> p j t", j=4), [Bb], [Bktb])
            k.op("dve", lambda hh: hh.tensor_copy(vcb[:, :, :], vc[s][:, :, :]), reads=[Bvc[s]], writes=[Bvcb])
            sbank, Bsb = cx.banks[4], cx.Bbank[4]
            for blk in range(PB):
                k.op("pe", lambda hh: hh.matmul(sbank[:, blk * T:(blk + 1) * T], ktb[:, blk, :], qb[:, :], start=True, stop=True),
                     reads=[Bktb, Bqb], writes=[Bsb])
            k.op("pe", lambda hh: hh.matmul(sbank[0:T, PB * T:(PB + 1) * T], kbn[:, 0:T], qb[:, :], start=True, stop=True),
                 reads=[Bkbn, Bqb], writes=[Bsb])
            k.op("act", lambda hh: hh.activation(pe_[:, :, :], sbank[:, 0:PB * T].rearrange("p (b t) -> p b t", t=T), AF.Exp),
                 reads=[Bsb], writes=[Bpe])
            k.op("act", lambda hh: hh.activation(pn_[0:T, :], sbank[0:T, PB * T:(PB + 1) * T], AF.Exp), reads=[Bsb], writes=[Bpe])
            for blk in range(PB):
                k.op("dve", lambda hh: hh.tensor_tensor(ptb[:, blk, :], pe_[:, blk, :], cs[:, PB - 1 - blk, :], ALU.mult),
                     reads=[Bpe, Bcs], writes=[Bptb])
            k.op("dve", lambda hh: hh.tensor_tensor(pnb[0:T, :], pn_[0:T, :], cn[0:T, :], ALU.mult), reads=[Bpe, Bcs], writes=[Bptb])
            obank, Bob = cx.banks[5], cx.Bbank[5]
            for blk in range(PB):
                k.op("pe", lambda hh: hh.matmul(obank[:, 0:T], vcb[:, blk, :], ptb[:, blk, :], start=(blk == 0), stop=False),
                     reads=[Bvcb, Bptb], writes=[Bob])
            k.op("pe", lambda hh: hh.matmul(obank[:, 0:T], vnb[0:T, :], pnb[0:T, :], start=False, stop=True), reads=[Bvnb, Bptb], writes=[Bob])
            for blk in range(PB):
                k.op("pe", lambda hh: hh.matmul(obank[:, 128:128 + T], cx.onesb[:, :], ptb[:, blk, :], start=(blk == 0), stop=False),
                     reads=[cx.Bconst, Bptb], writes=[Bob])
            k.op("pe", lambda hh: hh.matmul(obank[:, 128:128 + T], cx.onesb[0:T, :], pnb[0:T, :], start=False, stop=True),
                 reads=[cx.Bconst, Bptb], writes=[Bob])
            k.op("dve", lambda hh: hh.reciprocal(dacc[:, :], obank[:, 128:128 + T]), reads=[Bob], writes=[Boacc])
            k.op("dve", lambda hh: hh.tensor_tensor(oacc[:, :], obank[:, 0:T], dacc[:, :], ALU.mult), reads=[Bob, Boacc], writes=[Boacc])
            head_norm(cx, oacc, Boacc, T, TL, None, None, rs, Brs, sqt, Bsq, 6, 1.0 / 128)
            k.op("dve", lambda hh: hh.scalar_tensor_tensor(yab[:, :], oacc[:, :], go[:, h:h + 1], rs[:, :], ALU.mult, ALU.mult),
                 reads=[Boacc, Bg, Brs], writes=[Byab])
            k.dma("sp", mixT.ap()[h * 128:(h + 1) * 128, 0:T], yab[:, :], reads=[Byab])
    k.barrier()


def phase_pool(cx, projT, mixT, T, DA, DB, pw_ap, pscale_ap, state_ap, out_state_ap, n_valid):
    k = cx.k
    WINS = (2, 4, 8, 16)
    NCH = DB // 128
    TT = min(512, T)
    with ExitStack() as st:
        psc = k.sbuf("pl_psc", [128, NCH], F32, stack=st)
        Bpsc = Buf("pl_psc")
        load_vec_cols(cx, st, pscale_ap, DB, psc, Bpsc)
        pwb = k.sbuf("pl_pwb", [128, 4, 2, 256], BF16, stack=st)
        Bpwb = Buf("pl_pwb")
        k.dma("pool", pwb[:, :, :, :], pw_ap.rearrange("g (cc p) e -> p g cc e", p=128), writes=[Bpwb])
        icnt = k.sbuf("pl_icnt", [128, 4, 16], F32, stack=st)
        Bicnt = Buf("pl_icnt")
        for gi, w in enumerate(WINS):
            for t in range(16):
                cnt = min(w, n_valid + t + 1)
                k.op("dve", lambda h: h.memset(icnt[:, gi, t:t + 1], 1.0 / cnt), writes=[Bicnt])
        ext = [k.sbuf(f"pl_ext{i}", [128, 15 + T], F32, stack=st) for i in range(2)]
        sA = [k.sbuf(f"pl_sA{i}", [128, 15 + T], F32, stack=st) for i in range(2)]
        sB = [k.sbuf(f"pl_sB{i}", [128, 15 + T], F32, stack=st) for i in range(2)]
        Bext = [Buf(f"pl_ext{i}") for i in range(2)]
        BsA = [Buf(f"pl_sA{i}") for i in range(2)]
        BsB = [Buf(f"pl_sB{i}") for i in range(2)]
        db_ = k.sbuf("pl_db", [128, 2, T], BF16, stack=st)
        Bdb = Buf("pl_db")
        tmp16 = k.sbuf("pl_t16", [128, 16], F32, stack=st)
        Bt16 = Buf("pl_t16")
        stt = k.sbuf("pl_stt", [15, DB], F32, stack=st)
        Bstt = Buf("pl_stt")
        sto = k.sbuf("pl_sto", [15, DB], F32, stack=st)
        Bsto = Buf("pl_sto")
        yf = [k.sbuf(f"pl_yf{i}", [128, TT], F32, stack=st) for i in range(2)]
        Byf = [Buf(f"pl_yf{i}") for i in range(2)]
        sqt = k.sbuf("pl_sq", [128, TT], F32, stack=st)
        Bsq = Buf("pl_sq")
        rs = k.sbuf("pl_rs", [128, TT], F32, stack=st)
        Brs = Buf("pl_rs")
        yb = [k.sbuf(f"pl_yb{i}", [128, TT], BF16, stack=st) for i in range(2)]
        Byb = [Buf(f"pl_yb{i}") for i in range(2)]
        if state_ap is not None:
            k.dma("sp", stt[:, :], state_ap, writes=[Bstt])
        it = 0
        for gi, w in enumerate(WINS):
            for cc in range(2):
                ch = gi * 2 + cc
                e = ch % 2
                k.dma("sp", ext[e][:, 15:15 + T], projT.ap()[3 * DA + ch * 128:3 * DA + (ch + 1) * 128, 0:T], writes=[Bext[e]])
                if state_ap is None:
                    k.op("dve", lambda h: h.memset(ext[e][:, 0:15], 0.0), writes=[Bext[e]])
                else:
                    bank, Bb = cx.banks[7], cx.Bbank[7]
                    k.op("pe", lambda h: h.transpose(bank[:, 0:15], stt[0:15, ch * 128:(ch + 1) * 128], cx.ident[0:15, 0:15]),
                         reads=[Bstt, cx.Bident], writes=[Bb])
                    k.op("dve", lambda h: h.tensor_copy(ext[e][:, 0:15], bank[:, 0:15]), reads=[Bb], writes=[Bext[e]])
                bank, Bb = cx.banks[7], cx.Bbank[7]
                k.op("pe", lambda h: h.transpose(bank[0:15, 0:128], ext[e][:, T:T + 15], cx.ident[:, :]), reads=[Bext[e], cx.Bident], writes=[Bb])
                k.op("dve", lambda h: h.tensor_copy(sto[0:15, ch * 128:(ch + 1) * 128], bank[0:15, 0:128]), reads=[Bb], writes=[Bsto])
                n = 15 + T
                cur, Bcur = ext[e], Bext[e]
                nxt = [(sA[e], BsA[e]), (sB[e], BsB[e])]
                sh = 1
                li = 0
                lo = 0
                while sh < w:
                    dst, Bdst = nxt[li % 2]
                    li += 1
                    k.op("dve", lambda h: h.tensor_tensor(dst[:, lo + sh:n], cur[:, lo + sh:n], cur[:, lo:n - sh], ALU.add), reads=[Bcur], writes=[Bdst])
                    cur, Bcur = dst, Bdst
                    lo += sh
                    sh *= 2
                k.op("dve", lambda h: h.scalar_tensor_tensor(db_[:, cc, :], cur[:, 15:15 + T], 1.0 / w, ext[e][:, 15:15 + T], ALU.mult, ALU.subtract),
                     reads=[Bcur, Bext[e]], writes=[Bdb])
                nf = min(16, T)
                k.op("dve", lambda h: h.tensor_tensor(tmp16[:, 0:nf], cur[:, 15:15 + nf], icnt[:, gi, 0:nf], ALU.mult), reads=[Bcur, Bicnt], writes=[Bt16])
                k.op("dve", lambda h: h.tensor_tensor(db_[:, cc, 0:nf], tmp16[:, 0:nf], ext[e][:, 15:15 + nf], ALU.subtract),
                     reads=[Bt16, Bext[e]], writes=[Bdb])
            for (t0, tn) in tiles_of(T, TT):
                for ec in range(2):
                    b = ec
                    for cc in range(2):
                        k.op("pe", lambda h: h.matmul(cx.banks[b][:, 0:tn], pwb[:, gi, cc, ec * 128:(ec + 1) * 128], db_[:, cc, t0:t0 + tn],
                                                      start=(cc == 0), stop=(cc == 1)), reads=[Bpwb, Bdb], writes=[cx.Bbank[b]])
                    k.op("act", lambda h: h.activation(yf[ec][:, 0:tn], cx.banks[b][:, 0:tn], AF.Copy), reads=[cx.Bbank[b]], writes=[Byf[ec]])
                    k.op("act", lambda h: h.activation(sqt[:, 0:tn], yf[ec][:, 0:tn], AF.Square), reads=[Byf[ec]], writes=[Bsq])
                    k.op("pe", lambda h: h.matmul(cx.banks[2][:, 0:tn], cx.ones32[:, :], sqt[:, 0:tn], start=(ec == 0), stop=(ec == 1)),
                         reads=[Bsq, cx.Bconst], writes=[cx.Bbank[2]])
                k.op("act", lambda h: h.activation(rs[:, 0:tn], cx.banks[2][:, 0:tn], AF.Sqrt, bias=EPS, scale=1.0 / 256), reads=[cx.Bbank[2]], writes=[Brs])
                k.op("dve", lambda h: h.reciprocal(rs[:, 0:tn], rs[:, 0:tn]), reads=[Brs], writes=[Brs])
                for ec in range(2):
                    ch = gi * 2 + ec
                    k.op("dve", lambda h: h.scalar_tensor_tensor(yb[ec][:, 0:tn], yf[ec][:, 0:tn], psc[:, ch:ch + 1], rs[:, 0:tn], ALU.mult, ALU.mult),
                         reads=[Byf[ec], Bpsc, Brs], writes=[Byb[ec]])
                    k.dma("sp", mixT.ap()[DA + ch * 128:DA + (ch + 1) * 128, t0:t0 + tn], yb[ec][:, 0:tn], reads=[Byb[ec]])
        k.dma("sp", out_state_ap, sto[0:15, :], reads=[Bsto])
    k.barrier()


def gdn_consts(HC):
    mU = np.triu(np.ones((128, 128), np.float32))
    mSU = np.triu(np.ones((128, 128), np.float32), 1)
    l128 = np.zeros((128, 128), np.float32); l128[127, :] = 1.0
    l8 = np.zeros((128, 128), np.float32); l8[7, 0:8] = 1.0
    sel = np.zeros((128, HC * 128), np.float32)
    for h in range(HC):
        sel[h, h * 128:(h + 1) * 128] = 1.0
    idx = np.arange(128)
    bd8 = (idx[:, None] // 8 == idx[None, :] // 8).astype(np.float32)
    lls = []
    for b in (8, 16, 32, 64):
        same = idx[:, None] // (2 * b) == idx[None, :] // (2 * b)
        ll = same & ((idx[:, None] % (2 * b)) >= b) & ((idx[None, :] % (2 * b)) < b)
        lls.append(ll.astype(np.float32))
    return np.concatenate([mU, mSU, l128, l8, sel, bd8] + lls, axis=1)


def phase_gdn(cx, projT, mixT, T, c, DA, DB, HC, gconst, Bgconst, convw_ap, alog_ap, dtb_ap, onorm_ap,
              cstate_ap, cstate_out_ap, s0_ap, s_out_ap, HG):
    k = cx.k
    DC = HC * 128
    base = 3 * DA + DB
    NCHK = T // c
    L = 2
    mU, mSU = gconst[:, 0:128], gconst[:, 128:256]
    lrow = gconst[:, 256:384] if c == 128 else gconst[:, 384:512]
    sel = gconst[:, 512:512 + HC * 128]
    o_ = 512 + HC * 128
    bd8 = gconst[:, o_:o_ + 128]
    LLm = [gconst[:, o_ + 128 * (i + 1):o_ + 128 * (i + 2)] for i in range(4)]
    with ExitStack() as st:
        cwc = k.sbuf("gd_cwc", [128, 4, 3 * HC], F32, stack=st)
        Bcwc = Buf("gd_cwc")
        for i in range(4):
            load_vec_cols(cx, st, convw_ap[i, :], 3 * DC, cwc[:, i, :], Bcwc)
        cst = k.sbuf("gd_cst", [128, 3, 3 * HC], F32, stack=st)
        Bcst = Buf("gd_cst")
        if cstate_ap is not None:
            for r in range(3):
                load_vec_cols(cx, st, cstate_ap[r, :], 3 * DC, cst[:, r, :], Bcst)
        else:
            k.op("dve", lambda h: h.memset(cst[:, :, :], 0.0), writes=[Bcst])
        tail = k.sbuf("gd_tail", [128, 3, 3 * HC], F32, stack=st)
        Btail = Buf("gd_tail")
        onc = k.sbuf("gd_onc", [128, 1], F32, stack=st)
        Bonc = Buf("gd_onc")
        load_vec_cols(cx, st, onorm_ap, 128, onc, Bonc)
        hv = k.sbuf("gd_hv", [HC, 2], F32, stack=st)
        Bhv = Buf("gd_hv")
        k.dma("sp", hv[:, 0:1], alog_ap.rearrange("(h o) -> h o", o=1), writes=[Bhv])
        k.dma("sp", hv[:, 1:2], dtb_ap.rearrange("(h o) -> h o", o=1), writes=[Bhv])
        negA = k.sbuf("gd_negA", [HC, 1], F32, stack=st)
        BnegA = Buf("gd_negA")
        k.op("act", lambda h: h.activation(negA[:, :], hv[:, 0:1], AF.Exp), reads=[Bhv], writes=[BnegA])
        k.op("dve", lambda h: h.tensor_scalar(negA[:, :], negA[:, :], -1.0, None, ALU.mult), reads=[BnegA], writes=[BnegA])
        betaT = k.sbuf("gd_betaT", [HC, T], F32, stack=st)
        gT = k.sbuf("gd_gT", [HC, T], F32, stack=st)
        gcT = k.sbuf("gd_gcT", [HC, T], F32, stack=st)
        rm = k.sbuf("gd_rm", [HC, T], F32, stack=st)
        Bbeta, BgT, BgcT, Brm = Buf("gd_betaT"), Buf("gd_gT"), Buf("gd_gcT"), Buf("gd_rm")
        k.dma("sp", betaT[:, :], projT.ap()[base + 4 * DC:base + 4 * DC + HC, 0:T], writes=[Bbeta])
        k.dma("sp", gT[:, :], projT.ap()[base + 4 * DC + HC:base + 4 * DC + 2 * HC, 0:T], writes=[BgT])
        k.op("act", lambda h: h.activation(betaT[:, :], betaT[:, :], AF.Sigmoid), reads=[Bbeta], writes=[Bbeta])
        k.op("act", lambda h: h.activation(gT[:, :], gT[:, :], AF.Exp, bias=hv[:, 1:2]), reads=[BgT, Bhv], writes=[BgT])
        k.op("act", lambda h: h.activation(gT[:, :], gT[:, :], AF.Ln, bias=1.0), reads=[BgT], writes=[BgT])
        k.op("dve", lambda h: h.tensor_scalar(gT[:, :], gT[:, :], negA[:, 0:1], None, ALU.mult), reads=[BgT, BnegA], writes=[BgT])
        k.op("dve", lambda h: h.memset(rm[:, :], 1.0), writes=[Brm])
        k.op("dve", lambda h: h.memset(rm[:, :].rearrange("p (n c) -> p n c", c=c)[:, :, 0:1], 0.0), writes=[Brm])
        k.op("dve", lambda h: h.tensor_tensor_scan(gcT[:, :], rm[:, :], gT[:, :], 0.0, ALU.mult, ALU.add), reads=[Brm, BgT], writes=[BgcT])
        colB = k.sbuf("gd_colB", [128, NCHK, HC], F32, stack=st)
        colG = k.sbuf("gd_colG", [128, NCHK, HC], F32, stack=st)
        colBE = k.sbuf("gd_colBE", [128, NCHK, HC], F32, stack=st)
        colKD = k.sbuf("gd_colKD", [128, NCHK, HC], F32, stack=st)
        Bcol = Buf("gd_col")
        for n in range(NCHK):
            for (src, Bsrc, dst) in ((betaT, Bbeta, colB), (gcT, BgcT, colG)):
                bank, Bb = cx.banks[n % 4], cx.Bbank[n % 4]
                k.op("pe", lambda h: h.transpose(bank[0:c, 0:HC], src[:, n * c:(n + 1) * c], cx.ident[0:HC, 0:HC]), reads=[Bsrc, cx.Bident], writes=[Bb])
                k.op("dve", lambda h: h.tensor_copy(dst[0:c, n, :], bank[0:c, 0:HC]), reads=[Bb], writes=[Bcol])
        NH = NCHK * HC
        cg2 = colG[0:c, :, :].rearrange("p n h -> p (n h)")
        for (o0, on_) in tiles_of(NH, 512):
            bank, Bb = cx.banks[0], cx.Bbank[0]
            k.op("pe", lambda h: h.matmul(bank[0:c, 0:on_], lrow[0:c, 0:c], cg2[:, o0:o0 + on_], start=True, stop=True), reads=[Bgconst, Bcol], writes=[Bb])
            k.op("dve", lambda h: h.tensor_tensor(colKD[0:c, :, :].rearrange("p n h -> p (n h)")[:, o0:o0 + on_], bank[0:c, 0:on_], cg2[:, o0:o0 + on_], ALU.subtract),
                 reads=[Bb, Bcol], writes=[Bcol])
        k.op("act", lambda h: h.activation(colKD[0:c, :, :], colKD[0:c, :, :], AF.Exp), reads=[Bcol], writes=[Bcol])
        k.op("act", lambda h: h.activation(colBE[0:c, :, :], colG[0:c, :, :], AF.Exp), reads=[Bcol], writes=[Bcol])
        k.op("dve", lambda h: h.tensor_tensor(colBE[0:c, :, :], colBE[0:c, :, :], colB[0:c, :, :], ALU.mult), reads=[Bcol], writes=[Bcol])
        ext = [k.sbuf(f"gd_ext{i}", [128, 3 + T], F32, stack=st) for i in range(2)]
        Bext = [Buf(f"gd_ext{i}") for i in range(2)]
        rs = k.sbuf("gd_rs", [128, T], F32, stack=st)
        Brs = Buf("gd_rs")
        sqt = k.sbuf("gd_sq", [128, min(T, 512)], F32, stack=st)
        Bsq = Buf("gd_sq")
        T5 = tiles_of(T)

        class HS:
            pass
        hs = []
        for i in range(HG):
            o = HS()
            o.q = k.sbuf(f"gd_q{i}", [128, T], F32, stack=st); o.Bq = Buf(f"gd_q{i}")
            o.kk = k.sbuf(f"gd_k{i}", [128, T], F32, stack=st); o.Bk = Buf(f"gd_k{i}")
            o.v = k.sbuf(f"gd_v{i}", [128, T], F32, stack=st); o.Bv = Buf(f"gd_v{i}")
            o.sg = k.sbuf(f"gd_sg{i}", [128, T], F32, stack=st); o.Bsg = Buf(f"gd_sg{i}")
            o.yc = k.sbuf(f"gd_yc{i}", [128, T], BF16, stack=st); o.Byc = Buf(f"gd_yc{i}")
            for nm, shp, dt in (("gcb", [128, c], F32), ("bb", [128, c], F32), ("egcb", [128, c], F32), ("ET", [128, c], F32),
                                ("tmp", [128, c], F32), ("P", [128, c], F32), ("Bf", [128, c], F32), ("Af", [128, c], F32),
                                ("Q", [128, c], F32), ("W", [128, c], F32), ("Am", [128, c], F32), ("Pb", [128, c], F32),
                                ("Bb0", [128, c], F32), ("Bb1", [128, c], F32), ("Ab0", [128, c], F32), ("Ab1", [128, c], F32),
                                ("aqk", [128, c], F32), ("rhsu", [128, 128], F32), ("rhsw", [128, 128], F32), ("kdec", [128, 128], F32),
                                ("qd", [128, c], F32), ("wT", [128, c], F32), ("u", [128, 128], F32), ("vn", [128, 128], F32),
                                ("S", [128, 128], F32), ("Sb", [128, 128], F32), ("oT", [128, c], F32), ("t1", [128, c], F32)):
                setattr(o, nm, k.sbuf(f"gd_{nm}{i}", shp, dt, stack=st))
                setattr(o, "B_" + nm, Buf(f"gd_{nm}{i}"))
            hs.append(o)
        bk = [0]

        def nb():
            bk[0] += 1
            b = bk[0] % 8
            return cx.banks[b], cx.Bbank[b]

        for hg in range(HC // HG):
            heads = [hg * HG + i for i in range(HG)]
            for o, h in zip(hs, heads):
                for ci, (dst, Bdst) in enumerate(((o.q, o.Bq), (o.kk, o.Bk), (o.v, o.Bv))):
                    ch = ci * HC + h
                    e = ci % 2
                    k.dma("sp", ext[e][:, 3:3 + T], projT.ap()[base + ch * 128:base + (ch + 1) * 128, 0:T], writes=[Bext[e]])
                    k.op("dve", lambda hh: hh.tensor_copy(ext[e][:, 0:3], cst[:, :, ch]), reads=[Bcst], writes=[Bext[e]])
                    k.op("dve", lambda hh: hh.tensor_copy(tail[:, :, ch], ext[e][:, T:T + 3]), reads=[Bext[e]], writes=[Btail])
                    k.op("act", lambda hh: hh.activation(dst[:, :], ext[e][:, 0:T], AF.Identity, scale=cwc[:, 0, ch:ch + 1]), reads=[Bext[e], Bcwc], writes=[Bdst])
                    for i in range(1, 4):
                        k.op("dve", lambda hh: hh.scalar_tensor_tensor(dst[:, :], ext[e][:, i:i + T], cwc[:, i, ch:ch + 1], dst[:, :], ALU.mult, ALU.add),
                             reads=[Bext[e], Bcwc, Bdst], writes=[Bdst])
                    k.op("act", lambda hh: hh.activation(dst[:, :], dst[:, :], AF.Silu), reads=[Bdst], writes=[Bdst])
                head_norm(cx, o.q, o.Bq, T, T5, None, None, rs, Brs, sqt, Bsq, 6, 1.0)
                k.op("dve", lambda hh: hh.scalar_tensor_tensor(o.q[:, :], o.q[:, :], 128.0 ** -0.5, rs[:, :], ALU.mult, ALU.mult), reads=[o.Bq, Brs], writes=[o.Bq])
                head_norm(cx, o.kk, o.Bk, T, T5, None, None, rs, Brs, sqt, Bsq, 6, 1.0)
                k.op("dve", lambda hh: hh.tensor_tensor(o.kk[:, :], o.kk[:, :], rs[:, :], ALU.mult), reads=[o.Bk, Brs], writes=[o.Bk])
                k.dma("sp", o.sg[:, :], projT.ap()[base + 3 * DC + h * 128:base + 3 * DC + (h + 1) * 128, 0:T], writes=[o.Bsg])
                k.op("act", lambda hh: hh.activation(o.sg[:, :], o.sg[:, :], AF.Silu), reads=[o.Bsg], writes=[o.Bsg])
                if s0_ap is None:
                    k.op("dve", lambda hh: hh.memset(o.S[:, :], 0.0), writes=[o.B_S])
                else:
                    k.dma("sp", o.S[:, :], s0_ap[h, :, :], writes=[o.B_S])
                k.op("act", lambda hh: hh.activation(o.Sb[:, :], o.S[:, :], AF.Copy), reads=[o.B_S], writes=[o.B_Sb])
            for n in range(NCHK):
                t0 = n * c
                sl = slice(t0, t0 + c)
                for o, h in zip(hs, heads):
                    bank, Bb = nb()
                    k.op("pe", lambda hh: hh.matmul(bank[:, 0:c], sel[0:HC, h * 128:(h + 1) * 128], gcT[:, sl], start=True, stop=True), reads=[Bgconst, BgcT], writes=[Bb])
                    k.op("pe", lambda hh: hh.matmul(bank[:, 128:128 + c], sel[0:HC, h * 128:(h + 1) * 128], betaT[:, sl], start=True, stop=True), reads=[Bgconst, Bbeta], writes=[Bb])
                    k.op("dve", lambda hh: hh.tensor_copy(o.gcb[:, :], bank[:, 0:c]), reads=[Bb], writes=[o.B_gcb])
                    k.op("act", lambda hh: hh.activation(o.egcb[:, :], bank[:, 0:c], AF.Exp), reads=[Bb], writes=[o.B_egcb])
                    k.op("dve", lambda hh: hh.tensor_copy(o.bb[:, :], bank[:, 128:128 + c]), reads=[Bb], writes=[o.B_bb])
                for o, h in zip(hs, heads):
                    k.op("dve", lambda hh: hh.tensor_scalar(o.ET[0:c, :], o.gcb[0:c, :], colG[0:c, n, h:h + 1], 0.0, ALU.subtract, ALU.min),
                         reads=[o.B_gcb, Bcol], writes=[o.B_ET])
                    k.op("act", lambda hh: hh.activation(o.ET[0:c, :], o.ET[0:c, :], AF.Exp), reads=[o.B_ET], writes=[o.B_ET])
                    k.op("dve", lambda hh: hh.tensor_tensor(o.tmp[0:c, :], o.ET[0:c, :], mSU[0:c, 0:c], ALU.mult), reads=[o.B_ET, Bgconst], writes=[o.B_tmp])
                    k.op("dve", lambda hh: hh.tensor_tensor(o.tmp[0:c, :], o.tmp[0:c, :], o.bb[0:c, :], ALU.mult), reads=[o.B_tmp, o.B_bb], writes=[o.B_tmp])
                    k.op("dve", lambda hh: hh.tensor_tensor(o.ET[0:c, :], o.ET[0:c, :], mU[0:c, 0:c], ALU.mult), reads=[o.B_ET, Bgconst], writes=[o.B_ET])
                for o, h in zip(hs, heads):
                    bank, Bb = nb()
                    k.op("pe", lambda hh: hh.matmul(bank[0:c, 0:c], o.kk[:, sl], o.kk[:, sl], start=True, stop=True), reads=[o.Bk], writes=[Bb])
                    k.op("pe", lambda hh: hh.matmul(bank[0:c, 128:128 + c], o.kk[:, sl], o.q[:, sl], start=True, stop=True), reads=[o.Bk, o.Bq], writes=[Bb])
                    k.op("dve", lambda hh: hh.scalar_tensor_tensor(o.Bf[0:c, :], bank[0:c, 0:c], -1.0, o.tmp[0:c, :], ALU.mult, ALU.mult), reads=[Bb, o.B_tmp], writes=[o.B_Bf])
                    k.op("dve", lambda hh: hh.tensor_tensor(o.aqk[0:c, :], bank[0:c, 128:128 + c], o.ET[0:c, :], ALU.mult), reads=[Bb, o.B_ET], writes=[o.B_aqk])
                    k.op("dve", lambda hh: hh.tensor_tensor(o.Bb0[0:c, :], o.Bf[0:c, :], bd8[0:c, 0:c], ALU.mult), reads=[o.B_Bf, Bgconst], writes=[o.B_Bb0])
                    k.op("dve", lambda hh: hh.tensor_tensor(o.P[0:c, :], o.Bb0[0:c, :], cx.ident[0:c, 0:c], ALU.add), reads=[o.B_Bb0, cx.Bident], writes=[o.B_P])
                for o, h in zip(hs, heads):
                    bank, Bb = nb()
                    k.op("pe", lambda hh: hh.matmul(bank[0:c, 0:c], o.Bb0[0:c, :], cx.ident[0:c, 0:c], start=True, stop=True), reads=[o.B_Bb0, cx.Bident], writes=[Bb])
                    k.op("act", lambda hh: hh.activation(o.Ab0[0:c, :], bank[0:c, 0:c], AF.Copy), reads=[Bb], writes=[o.B_Ab0])
                for l in range(1, L + 1):
                    for o, h in zip(hs, heads):
                        Bc, Ac = (o.Bb0, o.Ab0) if l % 2 == 1 else (o.Bb1, o.Ab1)
                        Bn, An = (o.Bb1, o.Ab1) if l % 2 == 1 else (o.Bb0, o.Ab0)
                        BBc, BAc = (o.B_Bb0, o.B_Ab0) if l % 2 == 1 else (o.B_Bb1, o.B_Ab1)
                        BBn, BAn = (o.B_Bb1, o.B_Ab1) if l % 2 == 1 else (o.B_Bb0, o.B_Ab0)
                        bank, Bb = nb()
                        k.op("pe", lambda hh: hh.matmul(bank[0:c, 0:c], Ac[0:c, :], Bc[0:c, :], start=True, stop=True), reads=[BAc, BBc], writes=[Bb])
                        k.op("pe", lambda hh: hh.matmul(bank[0:c, 128:128 + c], Bc[0:c, :], Ac[0:c, :], start=True, stop=True), reads=[BAc, BBc], writes=[Bb])
                        k.op("act", lambda hh: hh.activation(Bn[0:c, :], bank[0:c, 0:c], AF.Copy), reads=[Bb], writes=[BBn])
                        k.op("act", lambda hh: hh.activation(An[0:c, :], bank[0:c, 128:128 + c], AF.Copy), reads=[Bb], writes=[BAn])
                        k.op("dve", lambda hh: hh.tensor_copy(o.Pb[0:c, :], o.P[0:c, :]), reads=[o.B_P], writes=[o.B_Pb])
                    for o, h in zip(hs, heads):
                        An = o.Ab1 if l % 2 == 1 else o.Ab0
                        BAn = o.B_Ab1 if l % 2 == 1 else o.B_Ab0
                        bank, Bb = nb()
                        k.op("pe", lambda hh: hh.matmul(bank[0:c, 0:c], An[0:c, :], o.Pb[0:c, :], start=True, stop=True), reads=[BAn, o.B_Pb], writes=[Bb])
                        k.op("dve", lambda hh: hh.tensor_tensor(o.P[0:c, :], o.P[0:c, :], bank[0:c, 0:c], ALU.add), reads=[o.B_P, Bb], writes=[o.B_P])
                if c > 8:
                    for o, h in zip(hs, heads):
                        bank, Bb = nb()
                        k.op("pe", lambda hh: hh.transpose(bank[0:c, 0:c], o.Bf[0:c, :], cx.ident[0:c, 0:c]), reads=[o.B_Bf, cx.Bident], writes=[Bb])
                        k.op("act", lambda hh: hh.activation(o.Af[0:c, :], bank[0:c, 0:c], AF.Copy), reads=[Bb], writes=[o.B_Af])
                    bsz = 8
                    li = 0
                    while bsz < c:
                        for o, h in zip(hs, heads):
                            k.op("dve", lambda hh: hh.tensor_tensor(o.Am[0:c, :], o.Af[0:c, :], LLm[li][0:c, 0:c], ALU.mult), reads=[o.B_Af, Bgconst], writes=[o.B_Am])
                            bank, Bb = nb()
                            k.op("pe", lambda hh: hh.transpose(bank[0:c, 0:c], o.P[0:c, :], cx.ident[0:c, 0:c]), reads=[o.B_P, cx.Bident], writes=[Bb])
                            k.op("pe", lambda hh: hh.matmul(bank[0:c, 128:128 + c], o.Am[0:c, :], o.P[0:c, :], start=True, stop=True), reads=[o.B_Am, o.B_P], writes=[Bb])
                            k.op("act", lambda hh: hh.activation(o.Q[0:c, :], bank[0:c, 0:c], AF.Copy), reads=[Bb], writes=[o.B_Q])
                            k.op("act", lambda hh: hh.activation(o.W[0:c, :], bank[0:c, 128:128 + c], AF.Copy), reads=[Bb], writes=[o.B_W])
                        for o, h in zip(hs, heads):
                            bank, Bb = nb()
                            k.op("pe", lambda hh: hh.matmul(bank[0:c, 0:c], o.Q[0:c, :], o.W[0:c, :], start=True, stop=True), reads=[o.B_Q, o.B_W], writes=[Bb])
                            k.op("dve", lambda hh: hh.tensor_tensor(o.P[0:c, :], o.P[0:c, :], bank[0:c, 0:c], ALU.add), reads=[o.B_P, Bb], writes=[o.B_P])
                        bsz *= 2
                        li += 1
                for o, h in zip(hs, heads):
                    k.op("act", lambda hh: hh.activation(o.Pb[0:c, :], o.P[0:c, :], AF.Copy), reads=[o.B_P], writes=[o.B_Pb])
                    bank, Bb = nb()
                    k.op("pe", lambda hh: hh.transpose(bank[0:c, 0:128], o.kk[:, sl], cx.ident[:, :]), reads=[o.Bk, cx.Bident], writes=[Bb])
                    k.op("pe", lambda hh: hh.transpose(bank[0:c, 128:256], o.v[:, sl], cx.ident[:, :]), reads=[o.Bv, cx.Bident], writes=[Bb])
                    k.op("dve", lambda hh: hh.tensor_scalar(o.rhsw[0:c, :], bank[0:c, 0:128], colBE[0:c, n, h:h + 1], None, ALU.mult), reads=[Bb, Bcol], writes=[o.B_rhsw])
                    k.op("dve", lambda hh: hh.tensor_scalar(o.kdec[0:c, :], bank[0:c, 0:128], colKD[0:c, n, h:h + 1], None, ALU.mult), reads=[Bb, Bcol], writes=[o.B_kdec])
                    k.op("dve", lambda hh: hh.tensor_scalar(o.rhsu[0:c, :], bank[0:c, 128:256], colB[0:c, n, h:h + 1], None, ALU.mult), reads=[Bb, Bcol], writes=[o.B_rhsu])
                    k.op("dve", lambda hh: hh.tensor_tensor(o.qd[:, :], o.q[:, sl], o.egcb[:, :], ALU.mult), reads=[o.Bq, o.B_egcb], writes=[o.B_qd])
                for o, h in zip(hs, heads):
                    bank, Bb = nb()
                    k.op("pe", lambda hh: hh.matmul(bank[0:c, 0:128], o.Pb[0:c, :], o.rhsu[0:c, :], start=True, stop=True), reads=[o.B_Pb, o.B_rhsu], writes=[Bb])
                    k.op("pe", lambda hh: hh.matmul(bank[:, 128:128 + c], o.rhsw[0:c, :], o.Pb[0:c, :], start=True, stop=True), reads=[o.B_Pb, o.B_rhsw], writes=[Bb])
                    k.op("act", lambda hh: hh.activation(o.u[0:c, :], bank[0:c, 0:128], AF.Copy), reads=[Bb], writes=[o.B_u])
                    k.op("act", lambda hh: hh.activation(o.wT[:, :], bank[:, 128:128 + c], AF.Copy), reads=[Bb], writes=[o.B_wT])
                for o, h in zip(hs, heads):
                    bank, Bb = nb()
                    k.op("pe", lambda hh: hh.matmul(bank[0:c, 0:128], o.wT[:, :], o.Sb[:, :], start=True, stop=True), reads=[o.B_wT, o.B_Sb], writes=[Bb])
                    k.op("dve", lambda hh: hh.tensor_tensor(o.vn[0:c, :], o.u[0:c, :], bank[0:c, 0:128], ALU.subtract), reads=[o.B_u, Bb], writes=[o.B_vn])
                for o, h in zip(hs, heads):
                    bank, Bb = nb()
                    k.op("pe", lambda hh: hh.matmul(bank[:, 0:c], o.Sb[:, :], o.qd[:, :], start=True, stop=False), reads=[o.B_Sb, o.B_qd], writes=[Bb])
                    k.op("pe", lambda hh: hh.matmul(bank[:, 0:c], o.vn[0:c, :], o.aqk[0:c, :], start=False, stop=True), reads=[o.B_vn, o.B_aqk], writes=[Bb])
                    k.op("act", lambda hh: hh.activation(o.oT[:, :], bank[:, 0:c], AF.Copy), reads=[Bb], writes=[o.B_oT])
                    bank2, Bb2 = nb()
                    k.op("pe", lambda hh: hh.matmul(bank2[:, 0:128], o.kdec[0:c, :], o.vn[0:c, :], start=True, stop=True), reads=[o.B_kdec, o.B_vn], writes=[Bb2])
                    k.op("dve", lambda hh: hh.scalar_tensor_tensor(o.S[:, :], o.S[:, :], o.egcb[:, c - 1:c], bank2[:, 0:128], ALU.mult, ALU.add),
                         reads=[o.B_S, o.B_egcb, Bb2], writes=[o.B_S])
                    k.op("act", lambda hh: hh.activation(o.Sb[:, :], o.S[:, :], AF.Copy), reads=[o.B_S], writes=[o.B_Sb])
                for o, h in zip(hs, heads):
                    bank, Bb = nb()
                    k.op("act", lambda hh: hh.activation(o.t1[:, :], o.oT[:, :], AF.Square), reads=[o.B_oT], writes=[o.B_t1])
                    k.op("pe", lambda hh: hh.matmul(bank[:, 0:c], cx.ones32[:, :], o.t1[:, :], start=True, stop=True), reads=[o.B_t1, cx.Bconst], writes=[Bb])
                    k.op("act", lambda hh: hh.activation(o.t1[:, :], bank[:, 0:c], AF.Sqrt, bias=EPS, scale=1.0 / 128), reads=[Bb], writes=[o.B_t1])
                    k.op("dve", lambda hh: hh.reciprocal(o.t1[:, :], o.t1[:, :]), reads=[o.B_t1], writes=[o.B_t1])
                    k.op("dve", lambda hh: hh.scalar_tensor_tensor(o.t1[:, :], o.oT[:, :], onc[:, 0:1], o.t1[:, :], ALU.mult, ALU.mult), reads=[o.B_oT, Bonc, o.B_t1], writes=[o.B_t1])
                    k.op("dve", lambda hh: hh.tensor_tensor(o.yc[:, sl], o.t1[:, :], o.sg[:, sl], ALU.mult), reads=[o.B_t1, o.Bsg], writes=[o.Byc])
            for o, h in zip(hs, heads):
                k.dma("sp", mixT.ap()[DA + DB + h * 128:DA + DB + (h + 1) * 128, 0:T], o.yc[:, :], reads=[o.Byc])
                k.dma("sp", s_out_ap[h, :, :], o.S[:, :], reads=[o.B_S])
        so = k.sbuf("gd_so", [128, 128], F32, stack=st)
        Bso = Buf("gd_so")
        for r in range(3):
            n3 = 3 * HC
            bank, Bb = cx.banks[7], cx.Bbank[7]
            k.op("pe", lambda hh: hh.transpose(bank[0:n3, 0:128], tail[:, r, :], cx.ident[:, :]), reads=[Btail, cx.Bident], writes=[Bb])
            k.op("dve", lambda hh: hh.tensor_copy(so[0:n3, :], bank[0:n3, 0:128]), reads=[Bb], writes=[Bso])
            k.dma("sp", cstate_out_ap[r, :].rearrange("(c p) -> c p", p=128), so[0:n3, :], reads=[Bso])
    k.barrier()

from contextlib import ExitStack

CFG_FULL = dict(D=4096, S=2048, T=8, P=2048, HA=12, DB=1024, HC=12, DFF=11008, DEPTH=4, TT=512, HG=2)

WEIGHTS = ("rel_bias", "norm_mix", "w_in", "a_q_norm", "a_k_norm", "a_out_norm", "pool_w", "pool_scale", "gdn_conv_w",
           "gdn_a_log", "gdn_dt_bias", "gdn_out_norm", "w_out", "norm_ffn", "ffn_up", "ffn_conv_w", "ffn_conv_b", "ffn_down")


def dims(cfg):
    D, HA, DB, HC, DFF = cfg["D"], cfg["HA"], cfg["DB"], cfg["HC"], cfg["DFF"]
    DA, DC = HA * 128, HC * 128
    NIN = 3 * DA + DB + 4 * DC + 2 * HC
    DMIX = DA + DB + DC
    return DA, DC, NIN, DMIX


def weight_shapes(cfg):
    D, HA, DB, HC, DFF, L = cfg["D"], cfg["HA"], cfg["DB"], cfg["HC"], cfg["DFF"], cfg["DEPTH"]
    DA, DC, NIN, DMIX = dims(cfg)
    return dict(rel_bias=[32, HA], norm_mix=[L, D], w_in=[L, D, NIN], a_q_norm=[L, 128], a_k_norm=[L, 128], a_out_norm=[L, DA],
                pool_w=[L, 4, 256, 256], pool_scale=[L, DB], gdn_conv_w=[L, 4, 3 * DC], gdn_a_log=[L, HC], gdn_dt_bias=[L, HC],
                gdn_out_norm=[L, 128], w_out=[L, DMIX, D], norm_ffn=[L, D], ffn_up=[L, D, 2 * DFF], ffn_conv_w=[L, 3, 2 * DFF],
                ffn_conv_b=[L, 2 * DFF], ffn_down=[L, DFF, D])


def io_shapes(cfg):
    D, S, T, P, HA, DB, HC, DFF, L = (cfg[x] for x in ("D", "S", "T", "P", "HA", "DB", "HC", "DFF", "DEPTH"))
    DA, DC, NIN, DMIX = dims(cfg)
    ins = dict(x_p=[S, D], x_s=[T, D], cache_kv=[L, P, 2, HA, 128], st_pool=[L, 15, DB], st_gconv=[L, 3, 3 * DC],
               st_gdn=[L, HC, 128, 128], st_fconv=[L, 2, 2 * DFF])
    outs = dict(y_p=[S, D], y_s=[T, D], p_kv=[L, S, 2, HA, 128], s_kv=[L, T, 2, HA, 128], p_pool=[L, 15, DB], s_pool=[L, 15, DB],
                p_gconv=[L, 3, 3 * DC], s_gconv=[L, 3, 3 * DC], p_gdn=[L, HC, 128, 128], s_gdn=[L, HC, 128, 128],
                p_fconv=[L, 2, 2 * DFF], s_fconv=[L, 2, 2 * DFF])
    return ins, outs


def host_consts(cfg):
    ohp, ohs = attn_onehots()
    return dict(consts=np.eye(128, dtype=np.float32), gconst=gdn_consts(cfg["HC"]), ohp=ohp, ohs=ohs)


def build(cfg):
    D, S, T, P, HA, DB, HC, DFF, L, TT, HG = (cfg[x] for x in ("D", "S", "T", "P", "HA", "DB", "HC", "DFF", "DEPTH", "TT", "HG"))
    DA, DC, NIN, DMIX = dims(cfg)
    nc = bass.Bass("TRN2", target_bir_lowering=False)
    ins, outs = io_shapes(cfg)
    I = {n: nc.dram_tensor(n, s, F32, kind="ExternalInput") for n, s in ins.items()}
    W = {n: nc.dram_tensor(n, s, F32, kind="ExternalInput") for n, s in weight_shapes(cfg).items()}
    hc = host_consts(cfg)
    C = {n: nc.dram_tensor(n, list(a.shape), F32, kind="ExternalInput") for n, a in hc.items()}
    O = {n: nc.dram_tensor(n, s, F32, kind="ExternalOutput") for n, s in outs.items()}
    G = {}
    for g, Tg in (("p", S), ("s", T)):
        G[g] = dict(T=Tg, xT=nc.dram_tensor(f"xT_{g}", [D, Tg], F32, kind="Internal"),
                    hT=nc.dram_tensor(f"hT_{g}", [D, Tg], F32, kind="Internal"),
                    projT=nc.dram_tensor(f"projT_{g}", [NIN, Tg], F32, kind="Internal"),
                    mixT=nc.dram_tensor(f"mixT_{g}", [DMIX, Tg], BF16, kind="Internal"))
    Hp = nc.dram_tensor("Hp", [HA, 3, 128, TOE_ML], F32, kind="Internal")
    Hs = nc.dram_tensor("Hs", [HA, 128, SMP_ML], F32, kind="Internal")
    with ExitStack() as st:
        k = KB(nc, st)
        cx = Ctx(k, cfg, C["consts"])
        gconst = k.sbuf("gconst", [128, 512 + HC * 128 + 5 * 128], F32)
        Bgconst = Buf("gconst")
        k.dma("sp", gconst[:, :], C["gconst"].ap(), writes=[Bgconst])
        setup_attn_tables(cx, W["rel_bias"], C["ohp"], C["ohs"], Hp, Hs, HA)
        phase_transpose_in(cx, I["x_p"], G["p"]["xT"], S, D)
        phase_transpose_in(cx, I["x_s"], G["s"]["xT"], T, D)
        for l in range(L):
            for g in ("p", "s"):
                gg = G[g]
                Tg = gg["T"]
                tt = min(TT, Tg)
                phase_inproj(cx, gg["xT"], gg["projT"], Tg, tt, D, NIN, W["w_in"].ap()[l], W["norm_mix"].ap()[l])
                if g == "p":
                    phase_attn_prompt(cx, gg["projT"], gg["mixT"], O["p_kv"].ap()[l], S, HA, Hp,
                                      W["a_q_norm"].ap()[l], W["a_k_norm"].ap()[l], W["a_out_norm"].ap()[l])
                    phase_pool(cx, gg["projT"], gg["mixT"], S, DA, DB, W["pool_w"].ap()[l], W["pool_scale"].ap()[l], None, O["p_pool"].ap()[l], 0)
                    phase_gdn(cx, gg["projT"], gg["mixT"], S, 128, DA, DB, HC, gconst, Bgconst, W["gdn_conv_w"].ap()[l], W["gdn_a_log"].ap()[l],
                              W["gdn_dt_bias"].ap()[l], W["gdn_out_norm"].ap()[l], None, O["p_gconv"].ap()[l], None, O["p_gdn"].ap()[l], HG)
                else:
                    phase_attn_sample(cx, gg["projT"], gg["mixT"], O["s_kv"].ap()[l], I["cache_kv"].ap()[l], T, P, HA, Hs,
                                      W["a_q_norm"].ap()[l], W["a_k_norm"].ap()[l], W["a_out_norm"].ap()[l])
                    phase_pool(cx, gg["projT"], gg["mixT"], T, DA, DB, W["pool_w"].ap()[l], W["pool_scale"].ap()[l], I["st_pool"].ap()[l], O["s_pool"].ap()[l], 15)
                    phase_gdn(cx, gg["projT"], gg["mixT"], T, T, DA, DB, HC, gconst, Bgconst, W["gdn_conv_w"].ap()[l], W["gdn_a_log"].ap()[l],
                              W["gdn_dt_bias"].ap()[l], W["gdn_out_norm"].ap()[l], I["st_gconv"].ap()[l], O["s_gconv"].ap()[l],
                              I["st_gdn"].ap()[l], O["s_gdn"].ap()[l], HG)
                phase_outproj(cx, gg["xT"], gg["mixT"], gg["hT"], Tg, tt, D, DMIX, W["w_out"].ap()[l])
                phase_ffn(cx, gg["hT"], gg["xT"], Tg, tt, D, DFF, W["norm_ffn"].ap()[l], W["ffn_up"].ap()[l], W["ffn_conv_w"].ap()[l],
                          W["ffn_conv_b"].ap()[l], W["ffn_down"].ap()[l], (I["st_fconv"].ap()[l] if g == "s" else None),
                          O["p_fconv" if g == "p" else "s_fconv"].ap()[l])
        phase_transpose_out(cx, G["p"]["xT"], O["y_p"], S, D)
        phase_transpose_out(cx, G["s"]["xT"], O["y_s"], T, D)
        k.finish()
        build.stats = (k.n_inst, k.n_sem)
    return nc


_NC_CACHE = {}


def kernel(x_prompt, x_sample, cache_attn_kv, state_pool, state_gdn_conv, state_gdn, state_ffn_conv,
           rel_bias, norm_mix, w_in, a_q_norm, a_k_norm, a_out_norm, pool_w, pool_scale,
           gdn_conv_w, gdn_a_log, gdn_dt_bias, gdn_out_norm, w_out, norm_ffn,
           ffn_up, ffn_conv_w, ffn_conv_b, ffn_down):
    cfg = CFG_FULL
    n = 8
    if "nc" not in _NC_CACHE:
        _NC_CACHE["nc"] = build(cfg)
    nc = _NC_CACHE["nc"]
    f = lambda a: np.ascontiguousarray(np.asarray(a), dtype=np.float32)
    wts = dict(rel_bias=rel_bias, norm_mix=norm_mix, w_in=w_in, a_q_norm=a_q_norm, a_k_norm=a_k_norm, a_out_norm=a_out_norm,
               pool_w=pool_w, pool_scale=pool_scale, gdn_conv_w=gdn_conv_w, gdn_a_log=gdn_a_log, gdn_dt_bias=gdn_dt_bias,
               gdn_out_norm=gdn_out_norm, w_out=w_out, norm_ffn=norm_ffn, ffn_up=ffn_up, ffn_conv_w=ffn_conv_w,
               ffn_conv_b=ffn_conv_b, ffn_down=ffn_down)
    wts = {k_: f(v) for k_, v in wts.items()}
    hc = host_consts(cfg)
    x_prompt, x_sample = np.asarray(x_prompt), np.asarray(x_sample)
    cache_attn_kv, state_pool, state_gdn_conv = np.asarray(cache_attn_kv), np.asarray(state_pool), np.asarray(state_gdn_conv)
    state_gdn, state_ffn_conv = np.asarray(state_gdn), np.asarray(state_ffn_conv)
    in_maps = []
    for c in range(n):
        m = dict(x_p=f(x_prompt[c % 4]), x_s=f(x_sample[c]), cache_kv=f(cache_attn_kv[:, c]), st_pool=f(state_pool[:, c]),
                 st_gconv=f(state_gdn_conv[:, c]), st_gdn=f(state_gdn[:, c]), st_fconv=f(state_ffn_conv[:, c]))
        m.update(wts)
        m.update(hc)
        in_maps.append(m)
    res = run_bass_kernel_spmd(nc, in_maps, core_ids=list(range(n))).results
    P = lambda name: np.stack([res[c][name] for c in range(4)], axis=0)
    Sg = lambda name: np.stack([res[c][name] for c in range(8)], axis=0)
    mv = lambda a: np.ascontiguousarray(np.moveaxis(a, 0, 1))
    return (P("y_p"), Sg("y_s"), mv(P("p_kv")), mv(Sg("s_kv")), mv(P("p_pool")), mv(Sg("s_pool")),
            mv(P("p_gconv")), mv(Sg("s_gconv")), mv(P("p_gdn")), mv(Sg("s_gdn")), mv(P("p_fconv")), mv(Sg("s_fconv")))
```

```python
import math
from concourse.bass_utils import run_bass_kernel_spmd
import numpy as np
import concourse.bass as bass
import concourse.mybir as mybir

F32 = mybir.dt.float32
BF16 = mybir.dt.bfloat16
I32 = mybir.dt.int32
ALU = mybir.AluOpType
AF = mybir.ActivationFunctionType
AX = mybir.AxisListType

EPOCH = 30000
NSLOT = 20


class Buf:
    __slots__ = ("name", "w", "r", "psum")

    def __init__(self, name, psum=False):
        self.name = name
        self.psum = psum
        self.w = None
        self.r = {}


class Eng:
    def __init__(self, k, name, h, is_compute=True):
        self.k = k
        self.name = name
        self.h = h
        self.sems = []
        self.count = 0
        self.waited = {}
        self.is_compute = is_compute
        self.slots = []
        self.slot_val = []
        self.ndma = 0


class KB:
    def __init__(self, nc, stack):
        self.nc = nc
        self.stack = stack
        self.E = {}
        for name, h in (("pe", nc.tensor), ("act", nc.scalar), ("dve", nc.vector),
                        ("pool", nc.gpsimd), ("sp", nc.sync)):
            self.E[name] = Eng(self, name, h)
        self.n_sem = 0
        self.n_inst = 0
        for e in self.E.values():
            if e.name in ("sp", "act", "pool"):
                for i in range(NSLOT):
                    e.slots.append(self._sem(f"d_{e.name}_{i}"))
                    e.slot_val.append(0)

    def _sem(self, name):
        self.n_sem += 1
        return self.stack.enter_context(self.nc.semaphore(name))

    def sbuf(self, name, shape, dtype, stack=None):
        self.n_alloc = getattr(self, "n_alloc", 0) + 1
        return (stack or self.stack).enter_context(self.nc.sbuf_tensor(f"{name}_{self.n_alloc}", list(shape), dtype))

    def psum(self, name, shape, dtype, stack=None):
        self.n_alloc = getattr(self, "n_alloc", 0) + 1
        return (stack or self.stack).enter_context(self.nc.psum_tensor(f"{name}_{self.n_alloc}", list(shape), dtype))

    def dram(self, name, shape, dtype, kind="Internal"):
        return self.nc.dram_tensor(name, list(shape), dtype, kind=kind)

    def _eng_sem(self, e, seq):
        idx = seq // EPOCH
        while len(e.sems) <= idx:
            e.sems.append(self._sem(f"c_{e.name}_{len(e.sems)}"))
        return e.sems[idx], seq % EPOCH + 1

    def _wait_tok(self, x, tok):
        if tok is None:
            return
        if tok[0] == "E":
            _, en, seq = tok
            key = ("E", en)
            if x.waited.get(key, -1) >= seq:
                return
            e = self.E[en]
            sem, val = self._eng_sem(e, seq)
            x.h.wait_ge(sem, val)
            x.waited[key] = seq
        else:
            _, qn, slot, val = tok
            key = ("D", qn, slot)
            if x.waited.get(key, 0) >= val:
                return
            q = self.E[qn]
            x.h.wait_ge(q.slots[slot], val)
            x.waited[key] = val

    def _deps(self, x, reads, writes, same_eng_waw=True):
        toks = []
        for b in reads:
            if b.w is not None:
                toks.append(b.w)
            if b.psum:
                for t in b.r.values():
                    if not (t[0] == "E" and t[1] == x.name):
                        toks.append(t)
        for b in writes:
            if b.w is not None:
                if b.w[0] == "E" and b.w[1] == x.name and not same_eng_waw:
                    pass
                else:
                    toks.append(b.w)
            for t in b.r.values():
                if t[0] == "E" and t[1] == x.name:
                    continue
                toks.append(t)
        for t in toks:
            self._wait_tok(x, t)

    def _record(self, tok, reads, writes):
        for b in reads:
            if tok[0] == "E":
                b.r[("E", tok[1])] = tok
            else:
                b.r[("D", tok[1], tok[2])] = tok
        for b in writes:
            b.w = tok
            b.r = {}

    def op(self, eng, fn, reads=(), writes=()):
        x = self.E[eng]
        self._deps(x, reads, writes, same_eng_waw=(eng != "pe"))
        inst = fn(x.h)
        seq = x.count
        x.count += 1
        sem, val = self._eng_sem(x, seq)
        inst.then_inc(sem, 1)
        self._record(("E", eng, seq), reads, writes)
        self.n_inst += 1
        return inst

    def dma(self, q, out, in_, reads=(), writes=(), **kw):
        x = self.E[q]
        self._deps(x, reads, writes)
        slot = x.ndma % NSLOT
        x.ndma += 1
        if x.slot_val[slot] > 0:
            self._wait_tok(x, ("D", q, slot, x.slot_val[slot]))
        x.slot_val[slot] += 16
        inst = x.h.dma_start(out=out, in_=in_, **kw)
        inst.then_inc(x.slots[slot], 16)
        self._record(("D", q, slot, x.slot_val[slot]), reads, writes)
        self.n_inst += 1
        return inst

    def barrier(self):
        sp = self.E["sp"]
        for e in self.E.values():
            if e.slots:
                for s in range(NSLOT):
                    if e.slot_val[s] > 0:
                        self._wait_tok(sp, ("D", e.name, s, e.slot_val[s]))
        for e in self.E.values():
            if e.name != "sp" and e.count > 0:
                self._wait_tok(sp, ("E", e.name, e.count - 1))
        seq = sp.count
        sp.count += 1
        sem, val = self._eng_sem(sp, seq)
        sp.h.nop().then_inc(sem, 1)
        for e in self.E.values():
            if e.name != "sp":
                self._wait_tok(e, ("E", "sp", seq))
                for o in self.E.values():
                    if o.count > 0 and o.name != "sp":
                        e.waited[("E", o.name)] = max(e.waited.get(("E", o.name), -1),
                                                      o.count - 1 if o.name != e.name else -1)
                    for s in range(len(o.slots)):
                        e.waited[("D", o.name, s)] = o.slot_val[s]

    def finish(self):
        self.barrier()

from contextlib import ExitStack

EPS = 1e-6


class Ctx:
    def __init__(self, k, cfg, consts_dram):
        self.k = k
        self.cfg = cfg
        nc = k.nc
        self.ident = k.sbuf("ident", [128, 128], F32)
        self.Bident = Buf("ident")
        self.ones32 = k.sbuf("ones32", [128, 128], F32)
        self.onesb = k.sbuf("onesb", [128, 128], BF16)
        self.identb = k.sbuf("identb", [128, 128], BF16)
        self.Bconst = Buf("const")
        k.dma("sp", self.ident[:], consts_dram.ap()[:, 0:128], writes=[self.Bident])
        k.op("dve", lambda h: h.memset(self.ones32[:], 1.0), writes=[self.Bconst])
        k.op("dve", lambda h: h.memset(self.onesb[:], 1.0), writes=[self.Bconst])
        k.op("dve", lambda h: h.tensor_copy(self.identb[:], self.ident[:]), reads=[self.Bident], writes=[self.Bconst])
        self.banks = []
        self.Bbank = []
        for i in range(8):
            self.banks.append(k.psum(f"bank{i}", [128, 512], F32))
            self.Bbank.append(Buf(f"bank{i}", psum=True))
        self.rr = 0
        self.lc_tmp = k.sbuf("lc_tmp", [128, 128], F32)
        self.Blc_tmp = Buf("lc_tmp")

    def evac_eng(self):
        self.rr += 1
        return "act" if self.rr % 2 else "dve"


def copy_on(k, eng, out, in_, reads, writes):
    if eng == "act":
        return k.op("act", lambda h: h.activation(out, in_, AF.Copy), reads=reads, writes=writes)
    return k.op(eng, lambda h: h.tensor_copy(out, in_), reads=reads, writes=writes)


def load_cols(cx, st, vec_ap_rows, R, dst, Bdst, dst_cols=None):
    k = cx.k
    tmp, Bt = cx.lc_tmp, cx.Blc_tmp
    k.dma("sp", tmp[0:R, :], vec_ap_rows, writes=[Bt])
    bank, Bb = cx.banks[7], cx.Bbank[7]
    k.op("pe", lambda h: h.transpose(bank[:, 0:R], tmp[0:R, :], cx.ident[0:R, 0:R]), reads=[Bt, cx.Bident], writes=[Bb])
    d = dst if dst_cols is None else dst_cols
    k.op("dve", lambda h: h.tensor_copy(d, bank[:, 0:R]), reads=[Bb], writes=[Bdst])


def load_vec_cols(cx, st, vec_dram_ap_1d, n, dst, Bdst, col0=0):
    R = n // 128
    rows = vec_dram_ap_1d.rearrange("(r p) -> r p", p=128)
    r0 = 0
    while r0 < R:
        rr = min(128, R - r0)
        load_cols(cx, st, rows[r0:r0 + rr, :], rr, dst, Bdst, dst_cols=dst[:, col0 + r0:col0 + r0 + rr])
        r0 += rr


def phase_transpose_in(cx, x_dram, xT_dram, Tg, D):
    k = cx.k
    DC = D // 128
    with ExitStack() as st:
        xr = [k.sbuf(f"ti_x{i}", [128, D], F32, stack=st) for i in range(2)]
        Bxr = [Buf(f"ti_x{i}") for i in range(2)]
        stg = [k.sbuf(f"ti_s{i}", [128, 4, 128], F32, stack=st) for i in range(3)]
        Bstg = [Buf(f"ti_s{i}") for i in range(3)]
        xTv = xT_dram.ap().rearrange("(c p) t -> p c t", p=128)
        nb = (Tg + 127) // 128
        si = 0
        for tb in range(nb):
            nt = min(128, Tg - tb * 128)
            s = tb % 2
            k.dma("sp", xr[s][0:nt, :], x_dram.ap()[tb * 128:tb * 128 + nt, :], writes=[Bxr[s]])
            for c4 in range(DC // 4):
                b = (tb * (DC // 4) + c4) % 8
                bank, Bb = cx.banks[b], cx.Bbank[b]
                for j in range(4):
                    c = c4 * 4 + j
                    k.op("pe", lambda h: h.transpose(bank[:, j * 128:j * 128 + nt], xr[s][0:nt, c * 128:(c + 1) * 128],
                                                     cx.ident[0:nt, 0:nt]), reads=[Bxr[s], cx.Bident], writes=[Bb])
                g = si % 3
                si += 1
                copy_on(k, cx.evac_eng(), stg[g][:, :, 0:nt], bank[:, :].rearrange("p (j t) -> p j t", j=4)[:, :, 0:nt],
                        [Bb], [Bstg[g]])
                k.dma("sp", xTv[:, c4 * 4:(c4 + 1) * 4, tb * 128:tb * 128 + nt], stg[g][:, :, 0:nt], reads=[Bstg[g]])
    k.barrier()


def phase_transpose_out(cx, xT_dram, y_dram, Tg, D):
    k = cx.k
    DC = D // 128
    with ExitStack() as st:
        xin = [k.sbuf(f"to_x{i}", [128, 4, 128], F32, stack=st) for i in range(3)]
        Bxin = [Buf(f"to_x{i}") for i in range(3)]
        stg = [k.sbuf(f"to_s{i}", [128, 512], F32, stack=st) for i in range(3)]
        Bstg = [Buf(f"to_s{i}") for i in range(3)]
        xTv = xT_dram.ap().rearrange("(c p) t -> p c t", p=128)
        nb = (Tg + 127) // 128
        it = 0
        for tb in range(nb):
            nt = min(128, Tg - tb * 128)
            for c4 in range(DC // 4):
                g = it % 3
                b = it % 8
                it += 1
                bank, Bb = cx.banks[b], cx.Bbank[b]
                k.dma("sp", xin[g][:, :, 0:nt], xTv[:, c4 * 4:(c4 + 1) * 4, tb * 128:tb * 128 + nt], writes=[Bxin[g]])
                for j in range(4):
                    k.op("pe", lambda h: h.transpose(bank[0:nt, j * 128:(j + 1) * 128], xin[g][:, j, 0:nt], cx.ident[:, :]),
                         reads=[Bxin[g], cx.Bident], writes=[Bb])
                copy_on(k, cx.evac_eng(), stg[g][0:nt, :], bank[0:nt, :], [Bb], [Bstg[g]])
                k.dma("sp", y_dram.ap()[tb * 128:tb * 128 + nt, c4 * 512:(c4 + 1) * 512], stg[g][0:nt, :], reads=[Bstg[g]])
    k.barrier()


class NormScratch:
    def __init__(self, cx, st, TT, tag):
        k = cx.k
        self.xs = [k.sbuf(f"{tag}_xs{i}", [128, 4, TT], F32, stack=st) for i in range(2)]
        self.Bxs = [Buf(f"{tag}_xs{i}") for i in range(2)]
        self.sq = [k.sbuf(f"{tag}_sq{i}", [128, TT], F32, stack=st) for i in range(2)]
        self.Bsq = [Buf(f"{tag}_sq{i}") for i in range(2)]
        self.rstd = k.sbuf(f"{tag}_rstd", [128, TT], F32, stack=st)
        self.Brstd = Buf(f"{tag}_rstd")


def norm_tile(cx, ns, xT_dram, t0, TT, D, gain_cols, Bgain, out_bf, Bout, tag):
    k = cx.k
    DC = D // 128
    G = 4
    xs, Bxs, sq, Bsq, rstd, Brstd = ns.xs, ns.Bxs, ns.sq, ns.Bsq, ns.rstd, ns.Brstd
    xTv = xT_dram.ap().rearrange("(c p) t -> p c t", p=128)
    bank, Bb = cx.banks[6], cx.Bbank[6]
    it = 0
    for c4 in range(DC // G):
        g = it % 2
        it += 1
        k.dma("sp", xs[g][:, :, :], xTv[:, c4 * G:(c4 + 1) * G, t0:t0 + TT], writes=[Bxs[g]])
        for j in range(G):
            c = c4 * G + j
            q = c % 2
            k.op("act", lambda h: h.activation(sq[q][:, :], xs[g][:, j, :], AF.Square), reads=[Bxs[g]], writes=[Bsq[q]])
            k.op("pe", lambda h: h.matmul(bank[:, 0:TT], cx.ones32[:, :], sq[q][:, :], start=(c == 0), stop=(c == DC - 1)),
                 reads=[Bsq[q], cx.Bconst], writes=[Bb])
    k.op("act", lambda h: h.activation(rstd[:, :], bank[:, 0:TT], AF.Sqrt, bias=EPS, scale=1.0 / D), reads=[Bb], writes=[Brstd])
    k.op("dve", lambda h: h.reciprocal(rstd[:, :], rstd[:, :]), reads=[Brstd], writes=[Brstd])
    for c4 in range(DC // G):
        g = it % 2
        it += 1
        k.dma("sp", xs[g][:, :, :], xTv[:, c4 * G:(c4 + 1) * G, t0:t0 + TT], writes=[Bxs[g]])
        for j in range(G):
            c = c4 * G + j
            k.op("dve", lambda h: h.scalar_tensor_tensor(out_bf[:, c, 0:TT], xs[g][:, j, :], gain_cols[:, c:c + 1], rstd[:, :],
                                                         ALU.mult, ALU.mult), reads=[Bxs[g], Bgain, Brstd], writes=[Bout])


class WStream:
    def __init__(self, cx, st, KG=4, NW=3, tag="ws"):
        self.cx = cx
        k = cx.k
        self.KG = KG
        self.NW = NW
        self.w = [k.sbuf(f"{tag}_w{i}", [128, KG, 512], BF16, stack=st) for i in range(NW)]
        self.Bw = [Buf(f"{tag}_w{i}") for i in range(NW)]
        self.wi = 0
        self.bi = 0

    def run(self, W_ap2d, K, blocks, subs):
        cx, k = self.cx, self.cx.k
        KC = K // 128
        KG = self.KG
        assert KC % KG == 0 and len(subs) in (1, 2)
        Wv = W_ap2d.rearrange("(kc p) n -> p kc n", p=128)
        for blk in blocks:
            chunks = []
            off = 0
            for (c0, ncol) in blk:
                o = 0
                while o < ncol:
                    r = min(128, ncol - o)
                    chunks.append((off + o, r, c0 + o))
                    o += r
                off += ncol
            assert off <= 512 and len(chunks) <= 4
            half = self.bi % 2
            self.bi += 1
            base = [half * 4] if len(subs) == 1 else [0, 4]
            for kg in range(KC // KG):
                s = self.wi % self.NW
                self.wi += 1
                off = 0
                for (c0, ncol) in blk:
                    k.dma("pool", self.w[s][:, :, off:off + ncol], Wv[:, kg * KG:(kg + 1) * KG, c0:c0 + ncol], writes=[self.Bw[s]])
                    off += ncol
                for si, (rhs, Brhs, TT, _) in enumerate(subs):
                    for kl in range(KG):
                        kc = kg * KG + kl
                        for j, (woff, rows, _c) in enumerate(chunks):
                            b = base[si] + j
                            k.op("pe", lambda h: h.matmul(cx.banks[b][0:rows, 0:TT], self.w[s][:, kl, woff:woff + rows], rhs[:, kc, 0:TT],
                                                          start=(kc == 0), stop=(kc == KC - 1)),
                                 reads=[self.Bw[s], Brhs], writes=[cx.Bbank[b]])
            for si, (rhs, Brhs, TT, evac) in enumerate(subs):
                for j, (woff, rows, col0) in enumerate(chunks):
                    b = base[si] + j
                    evac(j, cx.banks[b][0:rows, 0:TT], cx.Bbank[b], rows, col0)


def std_blocks(N):
    out = []
    c = 0
    while c < N:
        n = min(512, N - c)
        out.append([(c, n)])
        c += n
    return out


def phase_inproj(cx, xT_dram, projT_dram, Tg, TT, D, NIN, w_in_ap, gain_vec_ap):
    k = cx.k
    DC = D // 128
    NS = 2 if Tg >= 2 * TT else 1
    with ExitStack() as st:
        gain = k.sbuf("ip_gain", [128, DC], F32, stack=st)
        Bgain = Buf("ip_gain")
        load_vec_cols(cx, st, gain_vec_ap, D, gain, Bgain)
        xnb = [k.sbuf(f"ip_xnb{i}", [128, DC, TT], BF16, stack=st) for i in range(NS)]
        Bxnb = [Buf(f"ip_xnb{i}") for i in range(NS)]
        stg = [k.sbuf(f"ip_stg{i}", [128, TT], F32, stack=st) for i in range(4)]
        Bstg = [Buf(f"ip_stg{i}") for i in range(4)]
        ws = WStream(cx, st, NW=6, tag="ip")
        ns = NormScratch(cx, st, TT, "ipn")
        cnt = [0]
        for t0 in range(0, Tg, NS * TT):
            subs = []
            for si in range(NS):
                ts = t0 + si * TT
                norm_tile(cx, ns, xT_dram, ts, TT, D, gain, Bgain, xnb[si], Bxnb[si], "ipn")

                def evac(j, bank_ap, Bb, rows, col0, ts=ts):
                    g = cnt[0] % 4
                    cnt[0] += 1
                    copy_on(k, cx.evac_eng(), stg[g][0:rows, :], bank_ap, [Bb], [Bstg[g]])
                    k.dma("sp", projT_dram.ap()[col0:col0 + rows, ts:ts + TT], stg[g][0:rows, :], reads=[Bstg[g]])
                subs.append((xnb[si], Bxnb[si], TT, evac))
            ws.run(w_in_ap, D, std_blocks(NIN), subs)
    k.barrier()


def phase_outproj(cx, xT_dram, mixT_dram, hT_dram, Tg, TT, D, DMIX, w_out_ap):
    k = cx.k
    MC = DMIX // 128
    NS = 2 if Tg >= 2 * TT else 1
    with ExitStack() as st:
        mixb = [k.sbuf(f"op_mixb{i}", [128, MC, TT], BF16, stack=st) for i in range(NS)]
        Bmixb = [Buf(f"op_mixb{i}") for i in range(NS)]
        xres = [k.sbuf(f"op_x{i}", [128, TT], F32, stack=st) for i in range(4)]
        Bxres = [Buf(f"op_x{i}") for i in range(4)]
        ws = WStream(cx, st, NW=6, tag="op")
        cnt = [0]
        mv = mixT_dram.ap().rearrange("(c p) t -> p c t", p=128)
        for t0 in range(0, Tg, NS * TT):
            subs = []
            for si in range(NS):
                ts = t0 + si * TT
                for m0 in range(0, MC, 8):
                    m1 = min(MC, m0 + 8)
                    k.dma("sp", mixb[si][:, m0:m1, 0:TT], mv[:, m0:m1, ts:ts + TT], writes=[Bmixb[si]])

                def evac(j, bank_ap, Bb, rows, col0, ts=ts):
                    g = cnt[0] % 4
                    cnt[0] += 1
                    k.dma("sp", xres[g][0:rows, :], xT_dram.ap()[col0:col0 + rows, ts:ts + TT], writes=[Bxres[g]])
                    k.op("dve", lambda h: h.tensor_tensor(xres[g][0:rows, :], bank_ap, xres[g][0:rows, :], ALU.add),
                         reads=[Bb, Bxres[g]], writes=[Bxres[g]])
                    k.dma("sp", hT_dram.ap()[col0:col0 + rows, ts:ts + TT], xres[g][0:rows, :], reads=[Bxres[g]])
                subs.append((mixb[si], Bmixb[si], TT, evac))
            ws.run(w_out_ap, DMIX, std_blocks(D), subs)
    k.barrier()


def phase_ffn(cx, hT_dram, xT_dram, Tg, TT, D, DFF, gain_vec_ap, up_ap, convw_ap, convb_ap, down_ap,
              state_ap, out_state_ap):
    k = cx.k
    DC = D // 128
    FC = DFF // 128
    with ExitStack() as st:
        gain = k.sbuf("ff_gain", [128, DC], F32, stack=st)
        Bgain = Buf("ff_gain")
        load_vec_cols(cx, st, gain_vec_ap, D, gain, Bgain)
        cw = k.sbuf("ff_cw", [128, 3, 2 * FC], F32, stack=st)
        cb = k.sbuf("ff_cb", [128, 2 * FC], F32, stack=st)
        tails = k.sbuf("ff_tails", [128, 2 * FC, 2], F32, stack=st)
        tl2 = k.sbuf("ff_tl2", [128, 2, 2 * FC], F32, stack=st)
        Bcw, Bcb, Btails = Buf("ff_cw"), Buf("ff_cb"), Buf("ff_tails")
        for i in range(3):
            load_vec_cols(cx, st, convw_ap[i, :], 2 * DFF, cw[:, i, :], Bcw)
        load_vec_cols(cx, st, convb_ap, 2 * DFF, cb, Bcb)
        if state_ap is None:
            k.op("dve", lambda h: h.memset(tails[:, :, :], 0.0), writes=[Btails])
        else:
            for r in range(2):
                load_vec_cols(cx, st, state_ap[r, :], 2 * DFF, tl2[:, r, :], Btails)
            k.op("dve", lambda h: h.tensor_copy(tails[:, :, :], tl2[:, :, :].rearrange("p r c -> p c r")), reads=[Btails], writes=[Btails])
        hnb = k.sbuf("ff_hnb", [128, DC, TT], BF16, stack=st)
        Bhnb = Buf("ff_hnb")
        actb = k.sbuf("ff_actb", [128, FC, TT], BF16, stack=st)
        Bactb = Buf("ff_actb")
        NE = 4
        ext = [k.sbuf(f"ff_ext{i}", [128, TT + 2], F32, stack=st) for i in range(NE)]
        Bext = [Buf(f"ff_ext{i}") for i in range(NE)]
        acc = [k.sbuf(f"ff_acc{i}", [128, TT], F32, stack=st) for i in range(NE)]
        Bacc = [Buf(f"ff_acc{i}") for i in range(NE)]
        hres = [k.sbuf(f"ff_h{i}", [128, TT], F32, stack=st) for i in range(2)]
        Bhres = [Buf(f"ff_h{i}") for i in range(2)]
        ns = NormScratch(cx, st, TT, "ffn")
        ws = WStream(cx, st, KG=(4 if DC % 4 == 0 else 2), tag="ffu")
        wsd = WStream(cx, st, KG=(4 if FC % 4 == 0 else (2 if FC % 2 == 0 else 1)), tag="ffd")
        cnt = [0]
        ublocks = []
        f = 0
        while f < FC:
            n = min(2, FC - f)
            ublocks.append([(f * 128, n * 128), (DFF + f * 128, n * 128)])
            f += n
        for t0 in range(0, Tg, TT):
            norm_tile(cx, ns, hT_dram, t0, TT, D, gain, Bgain, hnb, Bhnb, "ffn")
            pend = {}

            def evac_up(j, bank_ap, Bb, rows, col0):
                ch = col0 // 128
                e = cnt[0] % NE
                cnt[0] += 1
                k.op("act", lambda h: h.activation(ext[e][:, 2:2 + TT], bank_ap, AF.Copy), reads=[Bb], writes=[Bext[e]])
                k.op("dve", lambda h: h.tensor_copy(ext[e][:, 0:2], tails[:, ch, :]), reads=[Btails], writes=[Bext[e]])
                k.op("dve", lambda h: h.tensor_copy(tails[:, ch, :], ext[e][:, TT:TT + 2]), reads=[Bext[e]], writes=[Btails])
                k.op("act", lambda h: h.activation(acc[e][:, :], ext[e][:, 2:2 + TT], AF.Identity, bias=cb[:, ch:ch + 1], scale=cw[:, 2, ch:ch + 1]),
                     reads=[Bext[e], Bcw, Bcb], writes=[Bacc[e]])
                k.op("dve", lambda h: h.scalar_tensor_tensor(acc[e][:, :], ext[e][:, 1:1 + TT], cw[:, 1, ch:ch + 1], acc[e][:, :], ALU.mult, ALU.add),
                     reads=[Bext[e], Bcw, Bacc[e]], writes=[Bacc[e]])
                k.op("dve", lambda h: h.scalar_tensor_tensor(acc[e][:, :], ext[e][:, 0:TT], cw[:, 0, ch:ch + 1], acc[e][:, :], ALU.mult, ALU.add),
                     reads=[Bext[e], Bcw, Bacc[e]], writes=[Bacc[e]])
                if ch < FC:
                    k.op("act", lambda h: h.activation(acc[e][:, :], acc[e][:, :], AF.Silu), reads=[Bacc[e]], writes=[Bacc[e]])
                    pend[ch] = e
                else:
                    ge = pend.pop(ch - FC)
                    k.op("dve", lambda h: h.tensor_tensor(actb[:, ch - FC, 0:TT], acc[ge][:, :], acc[e][:, :], ALU.mult),
                         reads=[Bacc[ge], Bacc[e]], writes=[Bactb])
            ws.run(up_ap, D, ublocks, [(hnb, Bhnb, TT, evac_up)])

            def evac_dn(j, bank_ap, Bb, rows, col0):
                g = cnt[0] % 2
                cnt[0] += 1
                k.dma("sp", hres[g][0:rows, :], hT_dram.ap()[col0:col0 + rows, t0:t0 + TT], writes=[Bhres[g]])
                k.op("dve", lambda h: h.tensor_tensor(hres[g][0:rows, :], bank_ap, hres[g][0:rows, :], ALU.add),
                     reads=[Bb, Bhres[g]], writes=[Bhres[g]])
                k.dma("sp", xT_dram.ap()[col0:col0 + rows, t0:t0 + TT], hres[g][0:rows, :], reads=[Bhres[g]])
            wsd.run(down_ap, DFF, std_blocks(D), [(actb, Bactb, TT, evac_dn)])
        k.op("dve", lambda h: h.tensor_copy(tl2[:, :, :], tails[:, :, :].rearrange("p c r -> p r c")), reads=[Btails], writes=[Btails])
        so = k.sbuf("ff_so", [128, 128], F32, stack=st)
        Bso = Buf("ff_so")
        for r in range(2):
            c0 = 0
            while c0 < 2 * FC:
                n = min(128, 2 * FC - c0)
                bank, Bb = cx.banks[7], cx.Bbank[7]
                k.op("pe", lambda h: h.transpose(bank[0:n, 0:128], tl2[:, r, c0:c0 + n], cx.ident[:, :]), reads=[Btails, cx.Bident], writes=[Bb])
                k.op("dve", lambda h: h.tensor_copy(so[0:n, :], bank[0:n, 0:128]), reads=[Bb], writes=[Bso])
                k.dma("sp", out_state_ap[r, c0 * 128:(c0 + n) * 128].rearrange("(c p) -> c p", p=128), so[0:n, :], reads=[Bso])
                c0 += n
    k.barrier()

import math
from contextlib import ExitStack

BRANCHES = ((128, 1), (512, 4), (2048, 16))
TOE_ML = 384
SMP_ML = 2064


def t5_bucket_np(dist):
    dist = np.asarray(dist, np.int64)
    d = np.maximum(dist, 1).astype(np.float32)
    large = 16 + (np.log(d / np.float32(16)) / np.float32(math.log(2048 / 16)) * np.float32(16)).astype(np.int32)
    large = np.minimum(large, 31)
    return np.where(dist < 16, dist, large)


def attn_onehots():
    ohp = np.zeros((3, 32, TOE_ML), np.float32)
    for bi, (w, d) in enumerate(BRANCHES):
        for m in range(TOE_ML):
            j = m - 127
            if 0 <= j <= w // d:
                ohp[bi, t5_bucket_np(j * d), m] = 1.0
    ohs = np.zeros((32, SMP_ML), np.float32)
    for m in range(SMP_ML):
        rel = m - 8
        if rel < 0:
            continue
        cnt = sum(1 for (w, d) in BRANCHES if rel % d == 0 and rel <= w)
        ohs[t5_bucket_np(rel), m] = cnt
    return ohp, ohs


def setup_attn_tables(cx, rel_bias_dram, ohp_dram, ohs_dram, Hp_dram, Hs_dram, HA, do_prompt=True, do_sample=True):
    k = cx.k
    with ExitStack() as st:
        rb = k.sbuf("at_rb", [32, HA], F32, stack=st)
        Brb = Buf("at_rb")
        k.dma("sp", rb[:, :], rel_bias_dram.ap(), writes=[Brb])
        k.op("act", lambda h: h.activation(rb[:, :], rb[:, :], AF.Exp), reads=[Brb], writes=[Brb])
        ohp = k.sbuf("at_ohp", [32, 3, TOE_ML], F32, stack=st)
        ohs = k.sbuf("at_ohs", [32, SMP_ML], F32, stack=st)
        Boh = Buf("at_oh")
        k.dma("sp", ohp[:, :, :], ohp_dram.ap().rearrange("b k m -> k b m"), writes=[Boh])
        k.dma("sp", ohs[:, :], ohs_dram.ap(), writes=[Boh])
        erb = [k.sbuf(f"at_erb{i}", [32, 128], F32, stack=st) for i in range(2)]
        Berb = [Buf(f"at_erb{i}") for i in range(2)]
        stg = [k.sbuf(f"at_stg{i}", [128, 512], F32, stack=st) for i in range(3)]
        Bstg = [Buf(f"at_stg{i}") for i in range(3)]
        it = 0
        for h in range(HA):
            e = h % 2
            k.op("dve", lambda hh: hh.tensor_scalar(erb[e][:, :], cx.ones32[0:32, :], rb[:, h:h + 1], None, ALU.mult),
                 reads=[Brb, cx.Bconst], writes=[Berb[e]])
            jobs = []
            if do_prompt:
                for bi in range(3):
                    jobs.append((ohp[:, bi, :], TOE_ML, Hp_dram.ap()[h, bi, :, :]))
            if do_sample:
                c0 = 0
                while c0 < SMP_ML:
                    n = min(512, SMP_ML - c0)
                    jobs.append((ohs[:, c0:c0 + n], n, Hs_dram.ap()[h, :, c0:c0 + n]))
                    c0 += n
            for (rhs, n, dst) in jobs:
                b = it % 8
                g = it % 3
                it += 1
                k.op("pe", lambda hh: hh.matmul(cx.banks[b][:, 0:n], erb[e][:, :], rhs, start=True, stop=True),
                     reads=[Berb[e], Boh], writes=[cx.Bbank[b]])
                copy_on(k, cx.evac_eng(), stg[g][:, 0:n], cx.banks[b][:, 0:n], [cx.Bbank[b]], [Bstg[g]])
                k.dma("sp", dst, stg[g][:, 0:n], reads=[Bstg[g]])
    k.barrier()


def head_norm(cx, src, Bsrc, n, TT_list, gcol, Bg, rs, Brs, sqt, Bsq, bank_i, eps_scale):
    k = cx.k
    for (c0, cn) in TT_list:
        k.op("act", lambda h: h.activation(sqt[:, 0:cn], src[:, c0:c0 + cn], AF.Square), reads=[Bsrc], writes=[Bsq])
        k.op("pe", lambda h: h.matmul(cx.banks[bank_i][:, 0:cn], cx.ones32[:, :], sqt[:, 0:cn], start=True, stop=True),
             reads=[Bsq, cx.Bconst], writes=[cx.Bbank[bank_i]])
        k.op("act", lambda h: h.activation(rs[:, c0:c0 + cn], cx.banks[bank_i][:, 0:cn], AF.Sqrt, bias=EPS, scale=eps_scale),
             reads=[cx.Bbank[bank_i]], writes=[Brs])
    k.op("dve", lambda h: h.reciprocal(rs[:, 0:n], rs[:, 0:n]), reads=[Brs], writes=[Brs])


def tiles_of(n, t=512):
    return [(c, min(t, n - c)) for c in range(0, n, t)]


def phase_attn_prompt(cx, projT, mixT, kv_out_ap, S, HA, Hp_dram, qn_ap, kn_ap, on_ap):
    k = cx.k
    DA = HA * 128
    NB = S // 128
    assert S % 2048 == 0 or S in (256, 512, 1024, 2048)
    with ExitStack() as st:
        gq = k.sbuf("ap_gq", [128, 1], F32, stack=st)
        gk = k.sbuf("ap_gk", [128, 1], F32, stack=st)
        go = k.sbuf("ap_go", [128, HA], F32, stack=st)
        Bg = Buf("ap_g")
        load_vec_cols(cx, st, qn_ap, 128, gq, Bg)
        load_vec_cols(cx, st, kn_ap, 128, gk, Bg)
        load_vec_cols(cx, st, on_ap, DA, go, Bg)
        k.op("dve", lambda h: h.tensor_scalar(gq[:, :], gq[:, :], 128.0 ** -0.5, None, ALU.mult), reads=[Bg], writes=[Bg])
        raw = [k.sbuf(f"ap_raw{i}", [128, S], F32, stack=st) for i in range(3)]
        Braw = [Buf(f"ap_raw{i}") for i in range(3)]
        rs = k.sbuf("ap_rs", [128, S], F32, stack=st)
        Brs = Buf("ap_rs")
        sqt = k.sbuf("ap_sq", [128, 512], F32, stack=st)
        Bsq = Buf("ap_sq")
        knf = k.sbuf("ap_knf", [128, S], F32, stack=st)
        Bknf = Buf("ap_knf")
        qb = [k.sbuf(f"ap_qb{i}", [128, S], BF16, stack=st) for i in range(3)]
        kb = [k.sbuf(f"ap_kb{i}", [128, S], BF16, stack=st) for i in range(3)]
        Bqb = [Buf(f"ap_qb{i}") for i in range(3)]
        Bkb = [Buf(f"ap_kb{i}") for i in range(3)]
        vperm = k.sbuf("ap_vperm", [128, S], F32, stack=st)
        Bvperm = Buf("ap_vperm")
        vtok = k.sbuf("ap_vtok", [128, 3, NB, 128], BF16, stack=st)
        Bvtok = Buf("ap_vtok")
        kvst = [k.sbuf(f"ap_kvst{i}", [128, 2, 128], F32, stack=st) for i in range(3)]
        Bkvst = [Buf(f"ap_kvst{i}") for i in range(3)]
        eb = k.sbuf("ap_eb", [128, 3, 2, 128], F32, stack=st)
        Beb = Buf("ap_eb")
        pe_ = [k.sbuf(f"ap_pe{i}", [128, 128], F32, stack=st) for i in range(4)]
        Bpe = [Buf(f"ap_pe{i}") for i in range(4)]
        pt = [k.sbuf(f"ap_pt{i}", [128, 128], BF16, stack=st) for i in range(4)]
        Bpt = [Buf(f"ap_pt{i}") for i in range(4)]
        oacc = k.sbuf("ap_oacc", [128, S], F32, stack=st)
        dacc = k.sbuf("ap_dacc", [128, S], F32, stack=st)
        Boacc, Bdacc = Buf("ap_oacc"), Buf("ap_dacc")
        yab = k.sbuf("ap_yab", [128, S], BF16, stack=st)
        Byab = Buf("ap_yab")
        T5 = tiles_of(S)
        it = 0
        for h in range(HA):
            for i in range(3):
                k.dma("sp", raw[i][:, :], projT.ap()[i * DA + h * 128:i * DA + (h + 1) * 128, 0:S], writes=[Braw[i]])
            for bi in range(3):
                for vi, off in enumerate((127, 255)):
                    src = bass.AP(Hp_dram, (h * 3 + bi) * 128 * TOE_ML + off, [[TOE_ML - 1, 128], [1, 128]])
                    k.dma("sp", eb[:, bi, vi, :], src, writes=[Beb])
            head_norm(cx, raw[0], Braw[0], S, T5, None, None, rs, Brs, sqt, Bsq, 6, 1.0 / 128)
            k.op("dve", lambda hh: hh.scalar_tensor_tensor(qb[0][:, :], raw[0][:, :], gq[:, 0:1], rs[:, :], ALU.mult, ALU.mult),
                 reads=[Braw[0], Bg, Brs], writes=[Bqb[0]])
            head_norm(cx, raw[1], Braw[1], S, T5, None, None, rs, Brs, sqt, Bsq, 6, 1.0 / 128)
            k.op("dve", lambda hh: hh.scalar_tensor_tensor(knf[:, :], raw[1][:, :], gk[:, 0:1], rs[:, :], ALU.mult, ALU.mult),
                 reads=[Braw[1], Bg, Brs], writes=[Bknf])
            k.op("act", lambda hh: hh.activation(kb[0][:, :], knf[:, :], AF.Copy), reads=[Bknf], writes=[Bkb[0]])
            for bi, d in ((1, 4), (2, 16)):
                k.op("dve", lambda hh: hh.tensor_copy(qb[bi][:, :].rearrange("p (r u) -> p r u", r=d),
                                                      qb[0][:, :].rearrange("p (u r) -> p r u", r=d)), reads=[Bqb[0]], writes=[Bqb[bi]])
                k.op("dve", lambda hh: hh.tensor_copy(kb[bi][:, :].rearrange("p (r u) -> p r u", r=d),
                                                      kb[0][:, :].rearrange("p (u r) -> p r u", r=d)), reads=[Bkb[0]], writes=[Bkb[bi]])
            for tb in range(NB):
                b = it % 4
                g = it % 3
                it += 1
                bank, Bb = cx.banks[b], cx.Bbank[b]
                k.op("pe", lambda hh: hh.transpose(bank[:, 0:128], knf[:, tb * 128:(tb + 1) * 128], cx.ident[:, :]),
                     reads=[Bknf, cx.Bident], writes=[Bb])
                k.op("pe", lambda hh: hh.transpose(bank[:, 128:256], raw[2][:, tb * 128:(tb + 1) * 128], cx.ident[:, :]),
                     reads=[Braw[2], cx.Bident], writes=[Bb])
                k.op("act", lambda hh: hh.activation(kvst[g][:, :, :], bank[:, 0:256].rearrange("p (a d) -> p a d", a=2), AF.Copy),
                     reads=[Bb], writes=[Bkvst[g]])
                k.op("dve", lambda hh: hh.tensor_copy(vtok[:, 0, tb, :], bank[:, 128:256]), reads=[Bb], writes=[Bvtok])
                k.dma("sp", kv_out_ap[tb * 128:(tb + 1) * 128, :, h, :], kvst[g][:, :, :], reads=[Bkvst[g]])
            for bi, d in ((1, 4), (2, 16)):
                k.op("dve", lambda hh: hh.tensor_copy(vperm[:, :].rearrange("p (r u) -> p r u", r=d),
                                                      raw[2][:, :].rearrange("p (u r) -> p r u", r=d)), reads=[Braw[2]], writes=[Bvperm])
                for tb in range(NB):
                    b = it % 4
                    it += 1
                    bank, Bb = cx.banks[b], cx.Bbank[b]
                    k.op("pe", lambda hh: hh.transpose(bank[:, 0:128], vperm[:, tb * 128:(tb + 1) * 128], cx.ident[:, :]),
                         reads=[Bvperm, cx.Bident], writes=[Bb])
                    copy_on(k, cx.evac_eng(), vtok[:, bi, tb, :], bank[:, 0:128], [Bb], [Bvtok])
            for bi, (w, d) in enumerate(BRANCHES):
                L = S // d
                nbc = max(1, L // 128)
                jobs = []
                for Q in range(NB // 4):
                    for jj in range(4):
                        B = Q * 4 + jj
                        n = B % nbc
                        kbs = ([B - 1] if n >= 1 else []) + [B]
                        for ki, KB_ in enumerate(kbs):
                            jobs.append(dict(Q=Q, jj=jj, B=B, KB=KB_, vi=(0 if KB_ == B else 1), first=(ki == 0), last=(ki == len(kbs) - 1),
                                             qend=(jj == 3 and ki == len(kbs) - 1)))

                def stage1(jb):
                    nonlocal it
                    sb = it % 4
                    g = it % 4
                    it += 1
                    jb["g"] = g
                    k.op("pe", lambda hh: hh.matmul(cx.banks[sb][:, 0:128], kb[bi][:, jb["KB"] * 128:(jb["KB"] + 1) * 128],
                                                    qb[bi][:, jb["B"] * 128:(jb["B"] + 1) * 128], start=True, stop=True),
                         reads=[Bkb[bi], Bqb[bi]], writes=[cx.Bbank[sb]])
                    k.op("act", lambda hh: hh.activation(pe_[g][:, :], cx.banks[sb][:, 0:128], AF.Exp),
                         reads=[cx.Bbank[sb]], writes=[Bpe[g]])
                    k.op("dve", lambda hh: hh.tensor_tensor(pt[g][:, :], pe_[g][:, :], eb[:, bi, jb["vi"], :], ALU.mult),
                         reads=[Bpe[g], Beb], writes=[Bpt[g]])

                def stage2(jb):
                    Q, jj, g = jb["Q"], jb["jj"], jb["g"]
                    ob, db = 4 + (Q % 2) * 2, 5 + (Q % 2) * 2
                    k.op("pe", lambda hh: hh.matmul(cx.banks[ob][:, jj * 128:(jj + 1) * 128], vtok[:, bi, jb["KB"], :], pt[g][:, :],
                                                    start=jb["first"], stop=jb["last"]),
                         reads=[Bvtok, Bpt[g]], writes=[cx.Bbank[ob]])
                    k.op("pe", lambda hh: hh.matmul(cx.banks[db][:, jj * 128:(jj + 1) * 128], cx.onesb[:, :], pt[g][:, :],
                                                    start=jb["first"], stop=jb["last"]),
                         reads=[cx.Bconst, Bpt[g]], writes=[cx.Bbank[db]])
                    if not jb["qend"]:
                        return
                    if d == 1:
                        ov = oacc[:, Q * 512:(Q + 1) * 512]
                        dv = dacc[:, Q * 512:(Q + 1) * 512]
                        k.op("act", lambda hh: hh.activation(ov, cx.banks[ob][:, :], AF.Copy), reads=[cx.Bbank[ob]], writes=[Boacc])
                        k.op("dve", lambda hh: hh.tensor_copy(dv, cx.banks[db][:, :]), reads=[cx.Bbank[db]], writes=[Bdacc])
                    else:
                        cpq = 512 // L if L < 512 else 1
                        if L >= 512:
                            r = Q // (L // 512)
                            u0 = (Q % (L // 512)) * 512
                            ov = oacc[:, :].rearrange("p (u r) -> p r u", r=d)[:, r, u0:u0 + 512]
                            dv = dacc[:, :].rearrange("p (u r) -> p r u", r=d)[:, r, u0:u0 + 512]
                            oi = cx.banks[ob][:, :]
                            di = cx.banks[db][:, :]
                        else:
                            ov = oacc[:, :].rearrange("p (u r) -> p r u", r=d)[:, Q * cpq:(Q + 1) * cpq, :]
                            dv = dacc[:, :].rearrange("p (u r) -> p r u", r=d)[:, Q * cpq:(Q + 1) * cpq, :]
                            oi = cx.banks[ob][:, :].rearrange("p (c u) -> p c u", c=cpq)
                            di = cx.banks[db][:, :].rearrange("p (c u) -> p c u", c=cpq)
                        k.op("dve", lambda hh: hh.tensor_tensor(ov, oi, ov, ALU.add), reads=[cx.Bbank[ob], Boacc], writes=[Boacc])
                        k.op("dve", lambda hh: hh.tensor_tensor(dv, di, dv, ALU.add), reads=[cx.Bbank[db], Bdacc], writes=[Bdacc])

                LA = 2
                for i in range(len(jobs) + LA):
                    if i < len(jobs):
                        stage1(jobs[i])
                    if i >= LA:
                        stage2(jobs[i - LA])
            k.op("dve", lambda hh: hh.reciprocal(dacc[:, :], dacc[:, :]), reads=[Bdacc], writes=[Bdacc])
            k.op("dve", lambda hh: hh.tensor_tensor(oacc[:, :], oacc[:, :], dacc[:, :], ALU.mult), reads=[Boacc, Bdacc], writes=[Boacc])
            head_norm(cx, oacc, Boacc, S, T5, None, None, rs, Brs, sqt, Bsq, 6, 1.0 / 128)
            k.op("dve", lambda hh: hh.scalar_tensor_tensor(yab[:, :], oacc[:, :], go[:, h:h + 1], rs[:, :], ALU.mult, ALU.mult),
                 reads=[Boacc, Bg, Brs], writes=[Byab])
            k.dma("sp", mixT.ap()[h * 128:(h + 1) * 128, 0:S], yab[:, :], reads=[Byab])
    k.barrier()


def phase_attn_sample(cx, projT, mixT, kv_out_ap, cache_ap, T, P, HA, Hs_dram, qn_ap, kn_ap, on_ap):
    k = cx.k
    DA = HA * 128
    PB = P // 128
    with ExitStack() as st:
        gq = k.sbuf("as_gq", [128, 1], F32, stack=st)
        gk = k.sbuf("as_gk", [128, 1], F32, stack=st)
        go = k.sbuf("as_go", [128, HA], F32, stack=st)
        Bg = Buf("as_g")
        load_vec_cols(cx, st, qn_ap, 128, gq, Bg)
        load_vec_cols(cx, st, kn_ap, 128, gk, Bg)
        load_vec_cols(cx, st, on_ap, DA, go, Bg)
        k.op("dve", lambda h: h.tensor_scalar(gq[:, :], gq[:, :], 128.0 ** -0.5, None, ALU.mult), reads=[Bg], writes=[Bg])
        raw = [k.sbuf(f"as_raw{i}", [128, T], F32, stack=st) for i in range(3)]
        Braw = [Buf(f"as_raw{i}") for i in range(3)]
        rs = k.sbuf("as_rs", [128, T], F32, stack=st)
        Brs = Buf("as_rs")
        sqt = k.sbuf("as_sq", [128, T], F32, stack=st)
        Bsq = Buf("as_sq")
        knf = k.sbuf("as_knf", [128, T], F32, stack=st)
        Bknf = Buf("as_knf")
        qb = k.sbuf("as_qb", [128, T], BF16, stack=st)
        kbn = k.sbuf("as_kbn", [128, T], BF16, stack=st)
        Bqb, Bkbn = Buf("as_qb"), Buf("as_kbn")
        kvst = k.sbuf("as_kvst", [128, 2, 128], F32, stack=st)
        Bkvst = Buf("as_kvst")
        vnb = k.sbuf("as_vnb", [128, 128], BF16, stack=st)
        Bvnb = Buf("as_vnb")
        kc = [k.sbuf(f"as_kc{i}", [128, PB, 128], F32, stack=st) for i in range(2)]
        vc = [k.sbuf(f"as_vc{i}", [128, PB, 128], F32, stack=st) for i in range(2)]
        Bkc = [Buf(f"as_kc{i}") for i in range(2)]
        Bvc = [Buf(f"as_vc{i}") for i in range(2)]
        ktb = k.sbuf("as_ktb", [128, PB, 128], BF16, stack=st)
        vcb = k.sbuf("as_vcb", [128, PB, 128], BF16, stack=st)
        Bktb, Bvcb = Buf("as_ktb"), Buf("as_vcb")
        cs = k.sbuf("as_cs", [128, PB, T], F32, stack=st)
        cn = k.sbuf("as_cn", [128, T], F32, stack=st)
        Bcs = Buf("as_cs")
        pe_ = k.sbuf("as_pe", [128, PB, T], F32, stack=st)
        pn_ = k.sbuf("as_pn", [128, T], F32, stack=st)
        ptb = k.sbuf("as_ptb", [128, PB, T], BF16, stack=st)
        pnb = k.sbuf("as_pnb", [128, T], BF16, stack=st)
        Bpe, Bptb = Buf("as_pe"), Buf("as_ptb")
        oacc = k.sbuf("as_oacc", [128, T], F32, stack=st)
        dacc = k.sbuf("as_dacc", [128, T], F32, stack=st)
        Boacc = Buf("as_oacc")
        yab = k.sbuf("as_yab", [128, T], BF16, stack=st)
        Byab = Buf("as_yab")
        TL = [(0, T)]
        for h in range(HA):
            s = h % 2
            for i in range(3):
                k.dma("sp", raw[i][:, :], projT.ap()[i * DA + h * 128:i * DA + (h + 1) * 128, 0:T], writes=[Braw[i]])
            k.dma("sp", kc[s][:, :, :], cache_ap[:, 0, h, :].rearrange("(b p) d -> p b d", p=128), writes=[Bkc[s]])
            k.dma("sp", vc[s][:, :, :], cache_ap[:, 1, h, :].rearrange("(b p) d -> p b d", p=128), writes=[Bvc[s]])
            src = bass.AP(Hs_dram, h * 128 * SMP_ML + 8 + 128, [[SMP_ML - 1, 128], [128, PB], [1, T]])
            k.dma("sp", cs[:, :, :], src, writes=[Bcs])
            srcn = bass.AP(Hs_dram, h * 128 * SMP_ML + 8, [[SMP_ML - 1, T], [1, T]])
            k.dma("sp", cn[0:T, :], srcn, writes=[Bcs])
            head_norm(cx, raw[0], Braw[0], T, TL, None, None, rs, Brs, sqt, Bsq, 6, 1.0 / 128)
            k.op("dve", lambda hh: hh.scalar_tensor_tensor(qb[:, :], raw[0][:, :], gq[:, 0:1], rs[:, :], ALU.mult, ALU.mult),
                 reads=[Braw[0], Bg, Brs], writes=[Bqb])
            head_norm(cx, raw[1], Braw[1], T, TL, None, None, rs, Brs, sqt, Bsq, 6, 1.0 / 128)
            k.op("dve", lambda hh: hh.scalar_tensor_tensor(knf[:, :], raw[1][:, :], gk[:, 0:1], rs[:, :], ALU.mult, ALU.mult),
                 reads=[Braw[1], Bg, Brs], writes=[Bknf])
            k.op("act", lambda hh: hh.activation(kbn[:, :], knf[:, :], AF.Copy), reads=[Bknf], writes=[Bkbn])
            bank, Bb = cx.banks[0], cx.Bbank[0]
            k.op("pe", lambda hh: hh.transpose(bank[0:T, 0:128], knf[:, 0:T], cx.ident[:, :]), reads=[Bknf, cx.Bident], writes=[Bb])
            k.op("pe", lambda hh: hh.transpose(bank[0:T, 128:256], raw[2][:, 0:T], cx.ident[:, :]), reads=[Braw[2], cx.Bident], writes=[Bb])
            k.op("act", lambda hh: hh.activation(kvst[0:T, :, :], bank[0:T, 0:256].rearrange("p (a d) -> p a d", a=2), AF.Copy),
                 reads=[Bb], writes=[Bkvst])
            k.op("dve", lambda hh: hh.tensor_copy(vnb[0:T, :], bank[0:T, 128:256]), reads=[Bb], writes=[Bvnb])
            k.dma("sp", kv_out_ap[0:T, :, h, :], kvst[0:T, :, :], reads=[Bkvst])
            for b4 in range(PB // 4):
                bi_ = 1 + (b4 % 3)
                bank, Bb = cx.banks[bi_], cx.Bbank[bi_]
                for j in range(4):
                    blk = b4 * 4 + j
                    k.op("pe", lambda hh: hh.transpose(bank[:, j * 128:(j + 1) * 128], kc[s][:, blk, :], cx.ident[:, :]),
                         reads=[Bkc[s], cx.Bident], writes=[Bb])
                copy_on(k, cx.evac_eng(), ktb[:, b4 * 4:(b4 + 1) * 4, :], bank[:, :].rearrange("p (j t) -> p j t", j=4), [Bb], [Bktb])
            k.op("dve", lambda hh: hh.tensor_copy(vcb[:, :, :], vc[s][:, :, :]), reads=[Bvc[s]], writes=[Bvcb])
            sbank, Bsb = cx.banks[4], cx.Bbank[4]
            for blk in range(PB):
                k.op("pe", lambda hh: hh.matmul(sbank[:, blk * T:(blk + 1) * T], ktb[:, blk, :], qb[:, :], start=True, stop=True),
                     reads=[Bktb, Bqb], writes=[Bsb])
            k.op("pe", lambda hh: hh.matmul(sbank[0:T, PB * T:(PB + 1) * T], kbn[:, 0:T], qb[:, :], start=True, stop=True),
                 reads=[Bkbn, Bqb], writes=[Bsb])
            k.op("act", lambda hh: hh.activation(pe_[:, :, :], sbank[:, 0:PB * T].rearrange("p (b t) -> p b t", t=T), AF.Exp),
                 reads=[Bsb], writes=[Bpe])
            k.op("act", lambda hh: hh.activation(pn_[0:T, :], sbank[0:T, PB * T:(PB + 1) * T], AF.Exp), reads=[Bsb], writes=[Bpe])
            for blk in range(PB):
                k.op("dve", lambda hh: hh.tensor_tensor(ptb[:, blk, :], pe_[:, blk, :], cs[:, PB - 1 - blk, :], ALU.mult),
                     reads=[Bpe, Bcs], writes=[Bptb])
            k.op("dve", lambda hh: hh.tensor_tensor(pnb[0:T, :], pn_[0:T, :], cn[0:T, :], ALU.mult), reads=[Bpe, Bcs], writes=[Bptb])
            obank, Bob = cx.banks[5], cx.Bbank[5]
            for blk in range(PB):
                k.op("pe", lambda hh: hh.matmul(obank[:, 0:T], vcb[:, blk, :], ptb[:, blk, :], start=(blk == 0), stop=False),
                     reads=[Bvcb, Bptb], writes=[Bob])
            k.op("pe", lambda hh: hh.matmul(obank[:, 0:T], vnb[0:T, :], pnb[0:T, :], start=False, stop=True), reads=[Bvnb, Bptb], writes=[Bob])
            for blk in range(PB):
                k.op("pe", lambda hh: hh.matmul(obank[:, 128:128 + T], cx.onesb[:, :], ptb[:, blk, :], start=(blk == 0), stop=False),
                     reads=[cx.Bconst, Bptb], writes=[Bob])
            k.op("pe", lambda hh: hh.matmul(obank[:, 128:128 + T], cx.onesb[0:T, :], pnb[0:T, :], start=False, stop=True),
                 reads=[cx.Bconst, Bptb], writes=[Bob])
            k.op("dve", lambda hh: hh.reciprocal(dacc[:, :], obank[:, 128:128 + T]), reads=[Bob], writes=[Boacc])
            k.op("dve", lambda hh: hh.tensor_tensor(oacc[:, :], obank[:, 0:T], dacc[:, :], ALU.mult), reads=[Bob, Boacc], writes=[Boacc])
            head_norm(cx, oacc, Boacc, T, TL, None, None, rs, Brs, sqt, Bsq, 6, 1.0 / 128)
            k.op("dve", lambda hh: hh.scalar_tensor_tensor(yab[:, :], oacc[:, :], go[:, h:h + 1], rs[:, :], ALU.mult, ALU.mult),
                 reads=[Boacc, Bg, Brs], writes=[Byab])
            k.dma("sp", mixT.ap()[h * 128:(h + 1) * 128, 0:T], yab[:, :], reads=[Byab])
    k.barrier()


def phase_pool(cx, projT, mixT, T, DA, DB, pw_ap, pscale_ap, state_ap, out_state_ap, n_valid):
    k = cx.k
    WINS = (2, 4, 8, 16)
    NCH = DB // 128
    TT = min(512, T)
    with ExitStack() as st:
        psc = k.sbuf("pl_psc", [128, NCH], F32, stack=st)
        Bpsc = Buf("pl_psc")
        load_vec_cols(cx, st, pscale_ap, DB, psc, Bpsc)
        pwb = k.sbuf("pl_pwb", [128, 4, 2, 256], BF16, stack=st)
        Bpwb = Buf("pl_pwb")
        k.dma("pool", pwb[:, :, :, :], pw_ap.rearrange("g (cc p) e -> p g cc e", p=128), writes=[Bpwb])
        icnt = k.sbuf("pl_icnt", [128, 4, 16], F32, stack=st)
        Bicnt = Buf("pl_icnt")
        for gi, w in enumerate(WINS):
            for t in range(16):
                cnt = min(w, n_valid + t + 1)
                k.op("dve", lambda h: h.memset(icnt[:, gi, t:t + 1], 1.0 / cnt), writes=[Bicnt])
        ext = [k.sbuf(f"pl_ext{i}", [128, 15 + T], F32, stack=st) for i in range(2)]
        sA = [k.sbuf(f"pl_sA{i}", [128, 15 + T], F32, stack=st) for i in range(2)]
        sB = [k.sbuf(f"pl_sB{i}", [128, 15 + T], F32, stack=st) for i in range(2)]
        Bext = [Buf(f"pl_ext{i}") for i in range(2)]
        BsA = [Buf(f"pl_sA{i}") for i in range(2)]
        BsB = [Buf(f"pl_sB{i}") for i in range(2)]
        db_ = k.sbuf("pl_db", [128, 2, T], BF16, stack=st)
        Bdb = Buf("pl_db")
        tmp16 = k.sbuf("pl_t16", [128, 16], F32, stack=st)
        Bt16 = Buf("pl_t16")
        stt = k.sbuf("pl_stt", [15, DB], F32, stack=st)
        Bstt = Buf("pl_stt")
        sto = k.sbuf("pl_sto", [15, DB], F32, stack=st)
        Bsto = Buf("pl_sto")
        yf = [k.sbuf(f"pl_yf{i}", [128, TT], F32, stack=st) for i in range(2)]
        Byf = [Buf(f"pl_yf{i}") for i in range(2)]
        sqt = k.sbuf("pl_sq", [128, TT], F32, stack=st)
        Bsq = Buf("pl_sq")
        rs = k.sbuf("pl_rs", [128, TT], F32, stack=st)
        Brs = Buf("pl_rs")
        yb = [k.sbuf(f"pl_yb{i}", [128, TT], BF16, stack=st) for i in range(2)]
        Byb = [Buf(f"pl_yb{i}") for i in range(2)]
        if state_ap is not None:
            k.dma("sp", stt[:, :], state_ap, writes=[Bstt])
        it = 0
        for gi, w in enumerate(WINS):
            for cc in range(2):
                ch = gi * 2 + cc
                e = ch % 2
                k.dma("sp", ext[e][:, 15:15 + T], projT.ap()[3 * DA + ch * 128:3 * DA + (ch + 1) * 128, 0:T], writes=[Bext[e]])
                if state_ap is None:
                    k.op("dve", lambda h: h.memset(ext[e][:, 0:15], 0.0), writes=[Bext[e]])
                else:
                    bank, Bb = cx.banks[7], cx.Bbank[7]
                    k.op("pe", lambda h: h.transpose(bank[:, 0:15], stt[0:15, ch * 128:(ch + 1) * 128], cx.ident[0:15, 0:15]),
                         reads=[Bstt, cx.Bident], writes=[Bb])
                    k.op("dve", lambda h: h.tensor_copy(ext[e][:, 0:15], bank[:, 0:15]), reads=[Bb], writes=[Bext[e]])
                bank, Bb = cx.banks[7], cx.Bbank[7]
                k.op("pe", lambda h: h.transpose(bank[0:15, 0:128], ext[e][:, T:T + 15], cx.ident[:, :]), reads=[Bext[e], cx.Bident], writes=[Bb])
                k.op("dve", lambda h: h.tensor_copy(sto[0:15, ch * 128:(ch + 1) * 128], bank[0:15, 0:128]), reads=[Bb], writes=[Bsto])
                n = 15 + T
                cur, Bcur = ext[e], Bext[e]
                nxt = [(sA[e], BsA[e]), (sB[e], BsB[e])]
                sh = 1
                li = 0
                lo = 0
                while sh < w:
                    dst, Bdst = nxt[li % 2]
                    li += 1
                    k.op("dve", lambda h: h.tensor_tensor(dst[:, lo + sh:n], cur[:, lo + sh:n], cur[:, lo:n - sh], ALU.add), reads=[Bcur], writes=[Bdst])
                    cur, Bcur = dst, Bdst
                    lo += sh
                    sh *= 2
                k.op("dve", lambda h: h.scalar_tensor_tensor(db_[:, cc, :], cur[:, 15:15 + T], 1.0 / w, ext[e][:, 15:15 + T], ALU.mult, ALU.subtract),
                     reads=[Bcur, Bext[e]], writes=[Bdb])
                nf = min(16, T)
                k.op("dve", lambda h: h.tensor_tensor(tmp16[:, 0:nf], cur[:, 15:15 + nf], icnt[:, gi, 0:nf], ALU.mult), reads=[Bcur, Bicnt], writes=[Bt16])
                k.op("dve", lambda h: h.tensor_tensor(db_[:, cc, 0:nf], tmp16[:, 0:nf], ext[e][:, 15:15 + nf], ALU.subtract),
                     reads=[Bt16, Bext[e]], writes=[Bdb])
            for (t0, tn) in tiles_of(T, TT):
                for ec in range(2):
                    b = ec
                    for cc in range(2):
                        k.op("pe", lambda h: h.matmul(cx.banks[b][:, 0:tn], pwb[:, gi, cc, ec * 128:(ec + 1) * 128], db_[:, cc, t0:t0 + tn],
                                                      start=(cc == 0), stop=(cc == 1)), reads=[Bpwb, Bdb], writes=[cx.Bbank[b]])
                    k.op("act", lambda h: h.activation(yf[ec][:, 0:tn], cx.banks[b][:, 0:tn], AF.Copy), reads=[cx.Bbank[b]], writes=[Byf[ec]])
                    k.op("act", lambda h: h.activation(sqt[:, 0:tn], yf[ec][:, 0:tn], AF.Square), reads=[Byf[ec]], writes=[Bsq])
                    k.op("pe", lambda h: h.matmul(cx.banks[2][:, 0:tn], cx.ones32[:, :], sqt[:, 0:tn], start=(ec == 0), stop=(ec == 1)),
                         reads=[Bsq, cx.Bconst], writes=[cx.Bbank[2]])
                k.op("act", lambda h: h.activation(rs[:, 0:tn], cx.banks[2][:, 0:tn], AF.Sqrt, bias=EPS, scale=1.0 / 256), reads=[cx.Bbank[2]], writes=[Brs])
                k.op("dve", lambda h: h.reciprocal(rs[:, 0:tn], rs[:, 0:tn]), reads=[Brs], writes=[Brs])
                for ec in range(2):
                    ch = gi * 2 + ec
                    k.op("dve", lambda h: h.scalar_tensor_tensor(yb[ec][:, 0:tn], yf[ec][:, 0:tn], psc[:, ch:ch + 1], rs[:, 0:tn], ALU.mult, ALU.mult),
                         reads=[Byf[ec], Bpsc, Brs], writes=[Byb[ec]])
                    k.dma("sp", mixT.ap()[DA + ch * 128:DA + (ch + 1) * 128, t0:t0 + tn], yb[ec][:, 0:tn], reads=[Byb[ec]])
        k.dma("sp", out_state_ap, sto[0:15, :], reads=[Bsto])
    k.barrier()


def gdn_consts(HC):
    mU = np.triu(np.ones((128, 128), np.float32))
    mSU = np.triu(np.ones((128, 128), np.float32), 1)
    l128 = np.zeros((128, 128), np.float32); l128[127, :] = 1.0
    l8 = np.zeros((128, 128), np.float32); l8[7, 0:8] = 1.0
    sel = np.zeros((128, HC * 128), np.float32)
    for h in range(HC):
        sel[h, h * 128:(h + 1) * 128] = 1.0
    idx = np.arange(128)
    bd8 = (idx[:, None] // 8 == idx[None, :] // 8).astype(np.float32)
    lls = []
    for b in (8, 16, 32, 64):
        same = idx[:, None] // (2 * b) == idx[None, :] // (2 * b)
        ll = same & ((idx[:, None] % (2 * b)) >= b) & ((idx[None, :] % (2 * b)) < b)
        lls.append(ll.astype(np.float32))
    return np.concatenate([mU, mSU, l128, l8, sel, bd8] + lls, axis=1)


def phase_gdn(cx, projT, mixT, T, c, DA, DB, HC, gconst, Bgconst, convw_ap, alog_ap, dtb_ap, onorm_ap,
              cstate_ap, cstate_out_ap, s0_ap, s_out_ap, HG):
    k = cx.k
    DC = HC * 128
    base = 3 * DA + DB
    NCHK = T // c
    L = 2
    mU, mSU = gconst[:, 0:128], gconst[:, 128:256]
    lrow = gconst[:, 256:384] if c == 128 else gconst[:, 384:512]
    sel = gconst[:, 512:512 + HC * 128]
    o_ = 512 + HC * 128
    bd8 = gconst[:, o_:o_ + 128]
    LLm = [gconst[:, o_ + 128 * (i + 1):o_ + 128 * (i + 2)] for i in range(4)]
    with ExitStack() as st:
        cwc = k.sbuf("gd_cwc", [128, 4, 3 * HC], F32, stack=st)
        Bcwc = Buf("gd_cwc")
        for i in range(4):
            load_vec_cols(cx, st, convw_ap[i, :], 3 * DC, cwc[:, i, :], Bcwc)
        cst = k.sbuf("gd_cst", [128, 3, 3 * HC], F32, stack=st)
        Bcst = Buf("gd_cst")
        if cstate_ap is not None:
            for r in range(3):
                load_vec_cols(cx, st, cstate_ap[r, :], 3 * DC, cst[:, r, :], Bcst)
        else:
            k.op("dve", lambda h: h.memset(cst[:, :, :], 0.0), writes=[Bcst])
        tail = k.sbuf("gd_tail", [128, 3, 3 * HC], F32, stack=st)
        Btail = Buf("gd_tail")
        onc = k.sbuf("gd_onc", [128, 1], F32, stack=st)
        Bonc = Buf("gd_onc")
        load_vec_cols(cx, st, onorm_ap, 128, onc, Bonc)
        hv = k.sbuf("gd_hv", [HC, 2], F32, stack=st)
        Bhv = Buf("gd_hv")
        k.dma("sp", hv[:, 0:1], alog_ap.rearrange("(h o) -> h o", o=1), writes=[Bhv])
        k.dma("sp", hv[:, 1:2], dtb_ap.rearrange("(h o) -> h o", o=1), writes=[Bhv])
        negA = k.sbuf("gd_negA", [HC, 1], F32, stack=st)
        BnegA = Buf("gd_negA")
        k.op("act", lambda h: h.activation(negA[:, :], hv[:, 0:1], AF.Exp), reads=[Bhv], writes=[BnegA])
        k.op("dve", lambda h: h.tensor_scalar(negA[:, :], negA[:, :], -1.0, None, ALU.mult), reads=[BnegA], writes=[BnegA])
        betaT = k.sbuf("gd_betaT", [HC, T], F32, stack=st)
        gT = k.sbuf("gd_gT", [HC, T], F32, stack=st)
        gcT = k.sbuf("gd_gcT", [HC, T], F32, stack=st)
        rm = k.sbuf("gd_rm", [HC, T], F32, stack=st)
        Bbeta, BgT, BgcT, Brm = Buf("gd_betaT"), Buf("gd_gT"), Buf("gd_gcT"), Buf("gd_rm")
        k.dma("sp", betaT[:, :], projT.ap()[base + 4 * DC:base + 4 * DC + HC, 0:T], writes=[Bbeta])
        k.dma("sp", gT[:, :], projT.ap()[base + 4 * DC + HC:base + 4 * DC + 2 * HC, 0:T], writes=[BgT])
        k.op("act", lambda h: h.activation(betaT[:, :], betaT[:, :], AF.Sigmoid), reads=[Bbeta], writes=[Bbeta])
        k.op("act", lambda h: h.activation(gT[:, :], gT[:, :], AF.Exp, bias=hv[:, 1:2]), reads=[BgT, Bhv], writes=[BgT])
        k.op("act", lambda h: h.activation(gT[:, :], gT[:, :], AF.Ln, bias=1.0), reads=[BgT], writes=[BgT])
        k.op("dve", lambda h: h.tensor_scalar(gT[:, :], gT[:, :], negA[:, 0:1], None, ALU.mult), reads=[BgT, BnegA], writes=[BgT])
        k.op("dve", lambda h: h.memset(rm[:, :], 1.0), writes=[Brm])
        k.op("dve", lambda h: h.memset(rm[:, :].rearrange("p (n c) -> p n c", c=c)[:, :, 0:1], 0.0), writes=[Brm])
        k.op("dve", lambda h: h.tensor_tensor_scan(gcT[:, :], rm[:, :], gT[:, :], 0.0, ALU.mult, ALU.add), reads=[Brm, BgT], writes=[BgcT])
        colB = k.sbuf("gd_colB", [128, NCHK, HC], F32, stack=st)
        colG = k.sbuf("gd_colG", [128, NCHK, HC], F32, stack=st)
        colBE = k.sbuf("gd_colBE", [128, NCHK, HC], F32, stack=st)
        colKD = k.sbuf("gd_colKD", [128, NCHK, HC], F32, stack=st)
        Bcol = Buf("gd_col")
        for n in range(NCHK):
            for (src, Bsrc, dst) in ((betaT, Bbeta, colB), (gcT, BgcT, colG)):
                bank, Bb = cx.banks[n % 4], cx.Bbank[n % 4]
                k.op("pe", lambda h: h.transpose(bank[0:c, 0:HC], src[:, n * c:(n + 1) * c], cx.ident[0:HC, 0:HC]), reads=[Bsrc, cx.Bident], writes=[Bb])
                k.op("dve", lambda h: h.tensor_copy(dst[0:c, n, :], bank[0:c, 0:HC]), reads=[Bb], writes=[Bcol])
        NH = NCHK * HC
        cg2 = colG[0:c, :, :].rearrange("p n h -> p (n h)")
        for (o0, on_) in tiles_of(NH, 512):
            bank, Bb = cx.banks[0], cx.Bbank[0]
            k.op("pe", lambda h: h.matmul(bank[0:c, 0:on_], lrow[0:c, 0:c], cg2[:, o0:o0 + on_], start=True, stop=True), reads=[Bgconst, Bcol], writes=[Bb])
            k.op("dve", lambda h: h.tensor_tensor(colKD[0:c, :, :].rearrange("p n h -> p (n h)")[:, o0:o0 + on_], bank[0:c, 0:on_], cg2[:, o0:o0 + on_], ALU.subtract),
                 reads=[Bb, Bcol], writes=[Bcol])
        k.op("act", lambda h: h.activation(colKD[0:c, :, :], colKD[0:c, :, :], AF.Exp), reads=[Bcol], writes=[Bcol])
        k.op("act", lambda h: h.activation(colBE[0:c, :, :], colG[0:c, :, :], AF.Exp), reads=[Bcol], writes=[Bcol])
        k.op("dve", lambda h: h.tensor_tensor(colBE[0:c, :, :], colBE[0:c, :, :], colB[0:c, :, :], ALU.mult), reads=[Bcol], writes=[Bcol])
        ext = [k.sbuf(f"gd_ext{i}", [128, 3 + T], F32, stack=st) for i in range(2)]
        Bext = [Buf(f"gd_ext{i}") for i in range(2)]
        rs = k.sbuf("gd_rs", [128, T], F32, stack=st)
        Brs = Buf("gd_rs")
        sqt = k.sbuf("gd_sq", [128, min(T, 512)], F32, stack=st)
        Bsq = Buf("gd_sq")
        T5 = tiles_of(T)

        class HS:
            pass
        hs = []
        for i in range(HG):
            o = HS()
            o.q = k.sbuf(f"gd_q{i}", [128, T], F32, stack=st); o.Bq = Buf(f"gd_q{i}")
            o.kk = k.sbuf(f"gd_k{i}", [128, T], F32, stack=st); o.Bk = Buf(f"gd_k{i}")
            o.v = k.sbuf(f"gd_v{i}", [128, T], F32, stack=st); o.Bv = Buf(f"gd_v{i}")
            o.sg = k.sbuf(f"gd_sg{i}", [128, T], F32, stack=st); o.Bsg = Buf(f"gd_sg{i}")
            o.yc = k.sbuf(f"gd_yc{i}", [128, T], BF16, stack=st); o.Byc = Buf(f"gd_yc{i}")
            for nm, shp, dt in (("gcb", [128, c], F32), ("bb", [128, c], F32), ("egcb", [128, c], F32), ("ET", [128, c], F32),
                                ("tmp", [128, c], F32), ("P", [128, c], F32), ("Bf", [128, c], F32), ("Af", [128, c], F32),
                                ("Q", [128, c], F32), ("W", [128, c], F32), ("Am", [128, c], F32), ("Pb", [128, c], F32),
                                ("Bb0", [128, c], F32), ("Bb1", [128, c], F32), ("Ab0", [128, c], F32), ("Ab1", [128, c], F32),
                                ("aqk", [128, c], F32), ("rhsu", [128, 128], F32), ("rhsw", [128, 128], F32), ("kdec", [128, 128], F32),
                                ("qd", [128, c], F32), ("wT", [128, c], F32), ("u", [128, 128], F32), ("vn", [128, 128], F32),
                                ("S", [128, 128], F32), ("Sb", [128, 128], F32), ("oT", [128, c], F32), ("t1", [128, c], F32)):
                setattr(o, nm, k.sbuf(f"gd_{nm}{i}", shp, dt, stack=st))
                setattr(o, "B_" + nm, Buf(f"gd_{nm}{i}"))
            hs.append(o)
        bk = [0]

        def nb():
            bk[0] += 1
            b = bk[0] % 8
            return cx.banks[b], cx.Bbank[b]

        for hg in range(HC // HG):
            heads = [hg * HG + i for i in range(HG)]
            for o, h in zip(hs, heads):
                for ci, (dst, Bdst) in enumerate(((o.q, o.Bq), (o.kk, o.Bk), (o.v, o.Bv))):
                    ch = ci * HC + h
                    e = ci % 2
                    k.dma("sp", ext[e][:, 3:3 + T], projT.ap()[base + ch * 128:base + (ch + 1) * 128, 0:T], writes=[Bext[e]])
                    k.op("dve", lambda hh: hh.tensor_copy(ext[e][:, 0:3], cst[:, :, ch]), reads=[Bcst], writes=[Bext[e]])
                    k.op("dve", lambda hh: hh.tensor_copy(tail[:, :, ch], ext[e][:, T:T + 3]), reads=[Bext[e]], writes=[Btail])
                    k.op("act", lambda hh: hh.activation(dst[:, :], ext[e][:, 0:T], AF.Identity, scale=cwc[:, 0, ch:ch + 1]), reads=[Bext[e], Bcwc], writes=[Bdst])
                    for i in range(1, 4):
                        k.op("dve", lambda hh: hh.scalar_tensor_tensor(dst[:, :], ext[e][:, i:i + T], cwc[:, i, ch:ch + 1], dst[:, :], ALU.mult, ALU.add),
                             reads=[Bext[e], Bcwc, Bdst], writes=[Bdst])
                    k.op("act", lambda hh: hh.activation(dst[:, :], dst[:, :], AF.Silu), reads=[Bdst], writes=[Bdst])
                head_norm(cx, o.q, o.Bq, T, T5, None, None, rs, Brs, sqt, Bsq, 6, 1.0)
                k.op("dve", lambda hh: hh.scalar_tensor_tensor(o.q[:, :], o.q[:, :], 128.0 ** -0.5, rs[:, :], ALU.mult, ALU.mult), reads=[o.Bq, Brs], writes=[o.Bq])
                head_norm(cx, o.kk, o.Bk, T, T5, None, None, rs, Brs, sqt, Bsq, 6, 1.0)
                k.op("dve", lambda hh: hh.tensor_tensor(o.kk[:, :], o.kk[:, :], rs[:, :], ALU.mult), reads=[o.Bk, Brs], writes=[o.Bk])
                k.dma("sp", o.sg[:, :], projT.ap()[base + 3 * DC + h * 128:base + 3 * DC + (h + 1) * 128, 0:T], writes=[o.Bsg])
                k.op("act", lambda hh: hh.activation(o.sg[:, :], o.sg[:, :], AF.Silu), reads=[o.Bsg], writes=[o.Bsg])
                if s0_ap is None:
                    k.op("dve", lambda hh: hh.memset(o.S[:, :], 0.0), writes=[o.B_S])
                else:
                    k.dma("sp", o.S[:, :], s0_ap[h, :, :], writes=[o.B_S])
                k.op("act", lambda hh: hh.activation(o.Sb[:, :], o.S[:, :], AF.Copy), reads=[o.B_S], writes=[o.B_Sb])
            for n in range(NCHK):
                t0 = n * c
                sl = slice(t0, t0 + c)
                for o, h in zip(hs, heads):
                    bank, Bb = nb()
                    k.op("pe", lambda hh: hh.matmul(bank[:, 0:c], sel[0:HC, h * 128:(h + 1) * 128], gcT[:, sl], start=True, stop=True), reads=[Bgconst, BgcT], writes=[Bb])
                    k.op("pe", lambda hh: hh.matmul(bank[:, 128:128 + c], sel[0:HC, h * 128:(h + 1) * 128], betaT[:, sl], start=True, stop=True), reads=[Bgconst, Bbeta], writes=[Bb])
                    k.op("dve", lambda hh: hh.tensor_copy(o.gcb[:, :], bank[:, 0:c]), reads=[Bb], writes=[o.B_gcb])
                    k.op("act", lambda hh: hh.activation(o.egcb[:, :], bank[:, 0:c], AF.Exp), reads=[Bb], writes=[o.B_egcb])
                    k.op("dve", lambda hh: hh.tensor_copy(o.bb[:, :], bank[:, 128:128 + c]), reads=[Bb], writes=[o.B_bb])
                for o, h in zip(hs, heads):
                    k.op("dve", lambda hh: hh.tensor_scalar(o.ET[0:c, :], o.gcb[0:c, :], colG[0:c, n, h:h + 1], 0.0, ALU.subtract, ALU.min),
                         reads=[o.B_gcb, Bcol], writes=[o.B_ET])
                    k.op("act", lambda hh: hh.activation(o.ET[0:c, :], o.ET[0:c, :], AF.Exp), reads=[o.B_ET], writes=[o.B_ET])
                    k.op("dve", lambda hh: hh.tensor_tensor(o.tmp[0:c, :], o.ET[0:c, :], mSU[0:c, 0:c], ALU.mult), reads=[o.B_ET, Bgconst], writes=[o.B_tmp])
                    k.op("dve", lambda hh: hh.tensor_tensor(o.tmp[0:c, :], o.tmp[0:c, :], o.bb[0:c, :], ALU.mult), reads=[o.B_tmp, o.B_bb], writes=[o.B_tmp])
                    k.op("dve", lambda hh: hh.tensor_tensor(o.ET[0:c, :], o.ET[0:c, :], mU[0:c, 0:c], ALU.mult), reads=[o.B_ET, Bgconst], writes=[o.B_ET])
                for o, h in zip(hs, heads):
                    bank, Bb = nb()
                    k.op("pe", lambda hh: hh.matmul(bank[0:c, 0:c], o.kk[:, sl], o.kk[:, sl], start=True, stop=True), reads=[o.Bk], writes=[Bb])
                    k.op("pe", lambda hh: hh.matmul(bank[0:c, 128:128 + c], o.kk[:, sl], o.q[:, sl], start=True, stop=True), reads=[o.Bk, o.Bq], writes=[Bb])
                    k.op("dve", lambda hh: hh.scalar_tensor_tensor(o.Bf[0:c, :], bank[0:c, 0:c], -1.0, o.tmp[0:c, :], ALU.mult, ALU.mult), reads=[Bb, o.B_tmp], writes=[o.B_Bf])
                    k.op("dve", lambda hh: hh.tensor_tensor(o.aqk[0:c, :], bank[0:c, 128:128 + c], o.ET[0:c, :], ALU.mult), reads=[Bb, o.B_ET], writes=[o.B_aqk])
                    k.op("dve", lambda hh: hh.tensor_tensor(o.Bb0[0:c, :], o.Bf[0:c, :], bd8[0:c, 0:c], ALU.mult), reads=[o.B_Bf, Bgconst], writes=[o.B_Bb0])
                    k.op("dve", lambda hh: hh.tensor_tensor(o.P[0:c, :], o.Bb0[0:c, :], cx.ident[0:c, 0:c], ALU.add), reads=[o.B_Bb0, cx.Bident], writes=[o.B_P])
                for o, h in zip(hs, heads):
                    bank, Bb = nb()
                    k.op("pe", lambda hh: hh.matmul(bank[0:c, 0:c], o.Bb0[0:c, :], cx.ident[0:c, 0:c], start=True, stop=True), reads=[o.B_Bb0, cx.Bident], writes=[Bb])
                    k.op("act", lambda hh: hh.activation(o.Ab0[0:c, :], bank[0:c, 0:c], AF.Copy), reads=[Bb], writes=[o.B_Ab0])
                for l in range(1, L + 1):
                    for o, h in zip(hs, heads):
                        Bc, Ac = (o.Bb0, o.Ab0) if l % 2 == 1 else (o.Bb1, o.Ab1)
                        Bn, An = (o.Bb1, o.Ab1) if l % 2 == 1 else (o.Bb0, o.Ab0)
                        BBc, BAc = (o.B_Bb0, o.B_Ab0) if l % 2 == 1 else (o.B_Bb1, o.B_Ab1)
                        BBn, BAn = (o.B_Bb1, o.B_Ab1) if l % 2 == 1 else (o.B_Bb0, o.B_Ab0)
                        bank, Bb = nb()
                        k.op("pe", lambda hh: hh.matmul(bank[0:c, 0:c], Ac[0:c, :], Bc[0:c, :], start=True, stop=True), reads=[BAc, BBc], writes=[Bb])
                        k.op("pe", lambda hh: hh.matmul(bank[0:c, 128:128 + c], Bc[0:c, :], Ac[0:c, :], start=True, stop=True), reads=[BAc, BBc], writes=[Bb])
                        k.op("act", lambda hh: hh.activation(Bn[0:c, :], bank[0:c, 0:c], AF.Copy), reads=[Bb], writes=[BBn])
                        k.op("act", lambda hh: hh.activation(An[0:c, :], bank[0:c, 128:128 + c], AF.Copy), reads=[Bb], writes=[BAn])
                        k.op("dve", lambda hh: hh.tensor_copy(o.Pb[0:c, :], o.P[0:c, :]), reads=[o.B_P], writes=[o.B_Pb])
                    for o, h in zip(hs, heads):
                        An = o.Ab1 if l % 2 == 1 else o.Ab0
                        BAn = o.B_Ab1 if l % 2 == 1 else o.B_Ab0
                        bank, Bb = nb()
                        k.op("pe", lambda hh: hh.matmul(bank[0:c, 0:c], An[0:c, :], o.Pb[0:c, :], start=True, stop=True), reads=[BAn, o.B_Pb], writes=[Bb])
                        k.op("dve", lambda hh: hh.tensor_tensor(o.P[0:c, :], o.P[0:c, :], bank[0:c, 0:c], ALU.add), reads=[o.B_P, Bb], writes=[o.B_P])
                if c > 8:
                    for o, h in zip(hs, heads):
                        bank, Bb = nb()
                        k.op("pe", lambda hh: hh.transpose(bank[0:c, 0:c], o.Bf[0:c, :], cx.ident[0:c, 0:c]), reads=[o.B_Bf, cx.Bident], writes=[Bb])
                        k.op("act", lambda hh: hh.activation(o.Af[0:c, :], bank[0:c, 0:c], AF.Copy), reads=[Bb], writes=[o.B_Af])
                    bsz = 8
                    li = 0
                    while bsz < c:
                        for o, h in zip(hs, heads):
                            k.op("dve", lambda hh: hh.tensor_tensor(o.Am[0:c, :], o.Af[0:c, :], LLm[li][0:c, 0:c], ALU.mult), reads=[o.B_Af, Bgconst], writes=[o.B_Am])
                            bank, Bb = nb()
                            k.op("pe", lambda hh: hh.transpose(bank[0:c, 0:c], o.P[0:c, :], cx.ident[0:c, 0:c]), reads=[o.B_P, cx.Bident], writes=[Bb])
                            k.op("pe", lambda hh: hh.matmul(bank[0:c, 128:128 + c], o.Am[0:c, :], o.P[0:c, :], start=True, stop=True), reads=[o.B_Am, o.B_P], writes=[Bb])
                            k.op("act", lambda hh: hh.activation(o.Q[0:c, :], bank[0:c, 0:c], AF.Copy), reads=[Bb], writes=[o.B_Q])
                            k.op("act", lambda hh: hh.activation(o.W[0:c, :], bank[0:c, 128:128 + c], AF.Copy), reads=[Bb], writes=[o.B_W])
                        for o, h in zip(hs, heads):
                            bank, Bb = nb()
                            k.op("pe", lambda hh: hh.matmul(bank[0:c, 0:c], o.Q[0:c, :], o.W[0:c, :], start=True, stop=True), reads=[o.B_Q, o.B_W], writes=[Bb])
                            k.op("dve", lambda hh: hh.tensor_tensor(o.P[0:c, :], o.P[0:c, :], bank[0:c, 0:c], ALU.add), reads=[o.B_P, Bb], writes=[o.B_P])
                        bsz *= 2
                        li += 1
                for o, h in zip(hs, heads):
                    k.op("act", lambda hh: hh.activation(o.Pb[0:c, :], o.P[0:c, :], AF.Copy), reads=[o.B_P], writes=[o.B_Pb])
                    bank, Bb = nb()
                    k.op("pe", lambda hh: hh.transpose(bank[0:c, 0:128], o.kk[:, sl], cx.ident[:, :]), reads=[o.Bk, cx.Bident], writes=[Bb])
                    k.op("pe", lambda hh: hh.transpose(bank[0:c, 128:256], o.v[:, sl], cx.ident[:, :]), reads=[o.Bv, cx.Bident], writes=[Bb])
                    k.op("dve", lambda hh: hh.tensor_scalar(o.rhsw[0:c, :], bank[0:c, 0:128], colBE[0:c, n, h:h + 1], None, ALU.mult), reads=[Bb, Bcol], writes=[o.B_rhsw])
                    k.op("dve", lambda hh: hh.tensor_scalar(o.kdec[0:c, :], bank[0:c, 0:128], colKD[0:c, n, h:h + 1], None, ALU.mult), reads=[Bb, Bcol], writes=[o.B_kdec])
                    k.op("dve", lambda hh: hh.tensor_scalar(o.rhsu[0:c, :], bank[0:c, 128:256], colB[0:c, n, h:h + 1], None, ALU.mult), reads=[Bb, Bcol], writes=[o.B_rhsu])
                    k.op("dve", lambda hh: hh.tensor_tensor(o.qd[:, :], o.q[:, sl], o.egcb[:, :], ALU.mult), reads=[o.Bq, o.B_egcb], writes=[o.B_qd])
                for o, h in zip(hs, heads):
                    bank, Bb = nb()
                    k.op("pe", lambda hh: hh.matmul(bank[0:c, 0:128], o.Pb[0:c, :], o.rhsu[0:c, :], start=True, stop=True), reads=[o.B_Pb, o.B_rhsu], writes=[Bb])
                    k.op("pe", lambda hh: hh.matmul(bank[:, 128:128 + c], o.rhsw[0:c, :], o.Pb[0:c, :], start=True, stop=True), reads=[o.B_Pb, o.B_rhsw], writes=[Bb])
                    k.op("act", lambda hh: hh.activation(o.u[0:c, :], bank[0:c, 0:128], AF.Copy), reads=[Bb], writes=[o.B_u])
                    k.op("act", lambda hh: hh.activation(o.wT[:, :], bank[:, 128:128 + c], AF.Copy), reads=[Bb], writes=[o.B_wT])
                for o, h in zip(hs, heads):
                    bank, Bb = nb()
                    k.op("pe", lambda hh: hh.matmul(bank[0:c, 0:128], o.wT[:, :], o.Sb[:, :], start=True, stop=True), reads=[o.B_wT, o.B_Sb], writes=[Bb])
                    k.op("dve", lambda hh: hh.tensor_tensor(o.vn[0:c, :], o.u[0:c, :], bank[0:c, 0:128], ALU.subtract), reads=[o.B_u, Bb], writes=[o.B_vn])
                for o, h in zip(hs, heads):
                    bank, Bb = nb()
                    k.op("pe", lambda hh: hh.matmul(bank[:, 0:c], o.Sb[:, :], o.qd[:, :], start=True, stop=False), reads=[o.B_Sb, o.B_qd], writes=[Bb])
                    k.op("pe", lambda hh: hh.matmul(bank[:, 0:c], o.vn[0:c, :], o.aqk[0:c, :], start=False, stop=True), reads=[o.B_vn, o.B_aqk], writes=[Bb])
                    k.op("act", lambda hh: hh.activation(o.oT[:, :], bank[:, 0:c], AF.Copy), reads=[Bb], writes=[o.B_oT])
                    bank2, Bb2 = nb()
                    k.op("pe", lambda hh: hh.matmul(bank2[:, 0:128], o.kdec[0:c, :], o.vn[0:c, :], start=True, stop=True), reads=[o.B_kdec, o.B_vn], writes=[Bb2])
                    k.op("dve", lambda hh: hh.scalar_tensor_tensor(o.S[:, :], o.S[:, :], o.egcb[:, c - 1:c], bank2[:, 0:128], ALU.mult, ALU.add),
                         reads=[o.B_S, o.B_egcb, Bb2], writes=[o.B_S])
                    k.op("act", lambda hh: hh.activation(o.Sb[:, :], o.S[:, :], AF.Copy), reads=[o.B_S], writes=[o.B_Sb])
                for o, h in zip(hs, heads):
                    bank, Bb = nb()
                    k.op("act", lambda hh: hh.activation(o.t1[:, :], o.oT[:, :], AF.Square), reads=[o.B_oT], writes=[o.B_t1])
                    k.op("pe", lambda hh: hh.matmul(bank[:, 0:c], cx.ones32[:, :], o.t1[:, :], start=True, stop=True), reads=[o.B_t1, cx.Bconst], writes=[Bb])
                    k.op("act", lambda hh: hh.activation(o.t1[:, :], bank[:, 0:c], AF.Sqrt, bias=EPS, scale=1.0 / 128), reads=[Bb], writes=[o.B_t1])
                    k.op("dve", lambda hh: hh.reciprocal(o.t1[:, :], o.t1[:, :]), reads=[o.B_t1], writes=[o.B_t1])
                    k.op("dve", lambda hh: hh.scalar_tensor_tensor(o.t1[:, :], o.oT[:, :], onc[:, 0:1], o.t1[:, :], ALU.mult, ALU.mult), reads=[o.B_oT, Bonc, o.B_t1], writes=[o.B_t1])
                    k.op("dve", lambda hh: hh.tensor_tensor(o.yc[:, sl], o.t1[:, :], o.sg[:, sl], ALU.mult), reads=[o.B_t1, o.Bsg], writes=[o.Byc])
            for o, h in zip(hs, heads):
                k.dma("sp", mixT.ap()[DA + DB + h * 128:DA + DB + (h + 1) * 128, 0:T], o.yc[:, :], reads=[o.Byc])
                k.dma("sp", s_out_ap[h, :, :], o.S[:, :], reads=[o.B_S])
        so = k.sbuf("gd_so", [128, 128], F32, stack=st)
        Bso = Buf("gd_so")
        for r in range(3):
            n3 = 3 * HC
            bank, Bb = cx.banks[7], cx.Bbank[7]
            k.op("pe", lambda hh: hh.transpose(bank[0:n3, 0:128], tail[:, r, :], cx.ident[:, :]), reads=[Btail, cx.Bident], writes=[Bb])
            k.op("dve", lambda hh: hh.tensor_copy(so[0:n3, :], bank[0:n3, 0:128]), reads=[Bb], writes=[Bso])
            k.dma("sp", cstate_out_ap[r, :].rearrange("(c p) -> c p", p=128), so[0:n3, :], reads=[Bso])
    k.barrier()

from contextlib import ExitStack

CFG_FULL = dict(D=4096, S=2048, T=8, P=2048, HA=12, DB=1024, HC=12, DFF=11008, DEPTH=4, TT=512, HG=2)

WEIGHTS = ("rel_bias", "norm_mix", "w_in", "a_q_norm", "a_k_norm", "a_out_norm", "pool_w", "pool_scale", "gdn_conv_w",
           "gdn_a_log", "gdn_dt_bias", "gdn_out_norm", "w_out", "norm_ffn", "ffn_up", "ffn_conv_w", "ffn_conv_b", "ffn_down")


def dims(cfg):
    D, HA, DB, HC, DFF = cfg["D"], cfg["HA"], cfg["DB"], cfg["HC"], cfg["DFF"]
    DA, DC = HA * 128, HC * 128
    NIN = 3 * DA + DB + 4 * DC + 2 * HC
    DMIX = DA + DB + DC
    return DA, DC, NIN, DMIX


def weight_shapes(cfg):
    D, HA, DB, HC, DFF, L = cfg["D"], cfg["HA"], cfg["DB"], cfg["HC"], cfg["DFF"], cfg["DEPTH"]
    DA, DC, NIN, DMIX = dims(cfg)
    return dict(rel_bias=[32, HA], norm_mix=[L, D], w_in=[L, D, NIN], a_q_norm=[L, 128], a_k_norm=[L, 128], a_out_norm=[L, DA],
                pool_w=[L, 4, 256, 256], pool_scale=[L, DB], gdn_conv_w=[L, 4, 3 * DC], gdn_a_log=[L, HC], gdn_dt_bias=[L, HC],
                gdn_out_norm=[L, 128], w_out=[L, DMIX, D], norm_ffn=[L, D], ffn_up=[L, D, 2 * DFF], ffn_conv_w=[L, 3, 2 * DFF],
                ffn_conv_b=[L, 2 * DFF], ffn_down=[L, DFF, D])


def io_shapes(cfg):
    D, S, T, P, HA, DB, HC, DFF, L = (cfg[x] for x in ("D", "S", "T", "P", "HA", "DB", "HC", "DFF", "DEPTH"))
    DA, DC, NIN, DMIX = dims(cfg)
    ins = dict(x_p=[S, D], x_s=[T, D], cache_kv=[L, P, 2, HA, 128], st_pool=[L, 15, DB], st_gconv=[L, 3, 3 * DC],
               st_gdn=[L, HC, 128, 128], st_fconv=[L, 2, 2 * DFF])
    outs = dict(y_p=[S, D], y_s=[T, D], p_kv=[L, S, 2, HA, 128], s_kv=[L, T, 2, HA, 128], p_pool=[L, 15, DB], s_pool=[L, 15, DB],
                p_gconv=[L, 3, 3 * DC], s_gconv=[L, 3, 3 * DC], p_gdn=[L, HC, 128, 128], s_gdn=[L, HC, 128, 128],
                p_fconv=[L, 2, 2 * DFF], s_fconv=[L, 2, 2 * DFF])
    return ins, outs


def host_consts(cfg):
    ohp, ohs = attn_onehots()
    return dict(consts=np.eye(128, dtype=np.float32), gconst=gdn_consts(cfg["HC"]), ohp=ohp, ohs=ohs)


def build(cfg):
    D, S, T, P, HA, DB, HC, DFF, L, TT, HG = (cfg[x] for x in ("D", "S", "T", "P", "HA", "DB", "HC", "DFF", "DEPTH", "TT", "HG"))
    DA, DC, NIN, DMIX = dims(cfg)
    nc = bass.Bass("TRN2", target_bir_lowering=False)
    ins, outs = io_shapes(cfg)
    I = {n: nc.dram_tensor(n, s, F32, kind="ExternalInput") for n, s in ins.items()}
    W = {n: nc.dram_tensor(n, s, F32, kind="ExternalInput") for n, s in weight_shapes(cfg).items()}
    hc = host_consts(cfg)
    C = {n: nc.dram_tensor(n, list(a.shape), F32, kind="ExternalInput") for n, a in hc.items()}
    O = {n: nc.dram_tensor(n, s, F32, kind="ExternalOutput") for n, s in outs.items()}
    G = {}
    for g, Tg in (("p", S), ("s", T)):
        G[g] = dict(T=Tg, xT=nc.dram_tensor(f"xT_{g}", [D, Tg], F32, kind="Internal"),
                    hT=nc.dram_tensor(f"hT_{g}", [D, Tg], F32, kind="Internal"),
                    projT=nc.dram_tensor(f"projT_{g}", [NIN, Tg], F32, kind="Internal"),
                    mixT=nc.dram_tensor(f"mixT_{g}", [DMIX, Tg], BF16, kind="Internal"))
    Hp = nc.dram_tensor("Hp", [HA, 3, 128, TOE_ML], F32, kind="Internal")
    Hs = nc.dram_tensor("Hs", [HA, 128, SMP_ML], F32, kind="Internal")
    with ExitStack() as st:
        k = KB(nc, st)
        cx = Ctx(k, cfg, C["consts"])
        gconst = k.sbuf("gconst", [128, 512 + HC * 128 + 5 * 128], F32)
        Bgconst = Buf("gconst")
        k.dma("sp", gconst[:, :], C["gconst"].ap(), writes=[Bgconst])
        setup_attn_tables(cx, W["rel_bias"], C["ohp"], C["ohs"], Hp, Hs, HA)
        phase_transpose_in(cx, I["x_p"], G["p"]["xT"], S, D)
        phase_transpose_in(cx, I["x_s"], G["s"]["xT"], T, D)
        for l in range(L):
            for g in ("p", "s"):
                gg = G[g]
                Tg = gg["T"]
                tt = min(TT, Tg)
                phase_inproj(cx, gg["xT"], gg["projT"], Tg, tt, D, NIN, W["w_in"].ap()[l], W["norm_mix"].ap()[l])
                if g == "p":
                    phase_attn_prompt(cx, gg["projT"], gg["mixT"], O["p_kv"].ap()[l], S, HA, Hp,
                                      W["a_q_norm"].ap()[l], W["a_k_norm"].ap()[l], W["a_out_norm"].ap()[l])
                    phase_pool(cx, gg["projT"], gg["mixT"], S, DA, DB, W["pool_w"].ap()[l], W["pool_scale"].ap()[l], None, O["p_pool"].ap()[l], 0)
                    phase_gdn(cx, gg["projT"], gg["mixT"], S, 128, DA, DB, HC, gconst, Bgconst, W["gdn_conv_w"].ap()[l], W["gdn_a_log"].ap()[l],
                              W["gdn_dt_bias"].ap()[l], W["gdn_out_norm"].ap()[l], None, O["p_gconv"].ap()[l], None, O["p_gdn"].ap()[l], HG)
                else:
                    phase_attn_sample(cx, gg["projT"], gg["mixT"], O["s_kv"].ap()[l], I["cache_kv"].ap()[l], T, P, HA, Hs,
                                      W["a_q_norm"].ap()[l], W["a_k_norm"].ap()[l], W["a_out_norm"].ap()[l])
                    phase_pool(cx, gg["projT"], gg["mixT"], T, DA, DB, W["pool_w"].ap()[l], W["pool_scale"].ap()[l], I["st_pool"].ap()[l], O["s_pool"].ap()[l], 15)
                    phase_gdn(cx, gg["projT"], gg["mixT"], T, T, DA, DB, HC, gconst, Bgconst, W["gdn_conv_w"].ap()[l], W["gdn_a_log"].ap()[l],
                              W["gdn_dt_bias"].ap()[l], W["gdn_out_norm"].ap()[l], I["st_gconv"].ap()[l], O["s_gconv"].ap()[l],
                              I["st_gdn"].ap()[l], O["s_gdn"].ap()[l], HG)
                phase_outproj(cx, gg["xT"], gg["mixT"], gg["hT"], Tg, tt, D, DMIX, W["w_out"].ap()[l])
                phase_ffn(cx, gg["hT"], gg["xT"], Tg, tt, D, DFF, W["norm_ffn"].ap()[l], W["ffn_up"].ap()[l], W["ffn_conv_w"].ap()[l],
                          W["ffn_conv_b"].ap()[l], W["ffn_down"].ap()[l], (I["st_fconv"].ap()[l] if g == "s" else None),
                          O["p_fconv" if g == "p" else "s_fconv"].ap()[l])
        phase_transpose_out(cx, G["p"]["xT"], O["y_p"], S, D)
        phase_transpose_out(cx, G["s"]["xT"], O["y_s"], T, D)
        k.finish()
        build.stats = (k.n_inst, k.n_sem)
    return nc


_NC_CACHE = {}


def kernel(x_prompt, x_sample, cache_attn_kv, state_pool, state_gdn_conv, state_gdn, state_ffn_conv,
           rel_bias, norm_mix, w_in, a_q_norm, a_k_norm, a_out_norm, pool_w, pool_scale,
           gdn_conv_w, gdn_a_log, gdn_dt_bias, gdn_out_norm, w_out, norm_ffn,
           ffn_up, ffn_conv_w, ffn_conv_b, ffn_down):
    cfg = CFG_FULL
    n = 8
    if "nc" not in _NC_CACHE:
        _NC_CACHE["nc"] = build(cfg)
    nc = _NC_CACHE["nc"]
    f = lambda a: np.ascontiguousarray(np.asarray(a), dtype=np.float32)
    wts = dict(rel_bias=rel_bias, norm_mix=norm_mix, w_in=w_in, a_q_norm=a_q_norm, a_k_norm=a_k_norm, a_out_norm=a_out_norm,
               pool_w=pool_w, pool_scale=pool_scale, gdn_conv_w=gdn_conv_w, gdn_a_log=gdn_a_log, gdn_dt_bias=gdn_dt_bias,
               gdn_out_norm=gdn_out_norm, w_out=w_out, norm_ffn=norm_ffn, ffn_up=ffn_up, ffn_conv_w=ffn_conv_w,
               ffn_conv_b=ffn_conv_b, ffn_down=ffn_down)
    wts = {k_: f(v) for k_, v in wts.items()}
    hc = host_consts(cfg)
    x_prompt, x_sample = np.asarray(x_prompt), np.asarray(x_sample)
    cache_attn_kv, state_pool, state_gdn_conv = np.asarray(cache_attn_kv), np.asarray(state_pool), np.asarray(state_gdn_conv)
    state_gdn, state_ffn_conv = np.asarray(state_gdn), np.asarray(state_ffn_conv)
    in_maps = []
    for c in range(n):
        m = dict(x_p=f(x_prompt[c % 4]), x_s=f(x_sample[c]), cache_kv=f(cache_attn_kv[:, c]), st_pool=f(state_pool[:, c]),
                 st_gconv=f(state_gdn_conv[:, c]), st_gdn=f(state_gdn[:, c]), st_fconv=f(state_ffn_conv[:, c]))
        m.update(wts)
        m.update(hc)
        in_maps.append(m)
    res = run_bass_kernel_spmd(nc, in_maps, core_ids=list(range(n))).results
    P = lambda name: np.stack([res[c][name] for c in range(4)], axis=0)
    Sg = lambda name: np.stack([res[c][name] for c in range(8)], axis=0)
    mv = lambda a: np.ascontiguousarray(np.moveaxis(a, 0, 1))
    return (P("y_p"), Sg("y_s"), mv(P("p_kv")), mv(Sg("s_kv")), mv(P("p_pool")), mv(Sg("s_pool")),
            mv(P("p_gconv")), mv(Sg("s_gconv")), mv(P("p_gdn")), mv(Sg("s_gdn")), mv(P("p_fconv")), mv(Sg("s_fconv")))
```

```python
import math
from concourse.bass_utils import run_bass_kernel_spmd
import numpy as np
import concourse.bass as bass
import concourse.mybir as mybir

F32 = mybir.dt.float32
BF16 = mybir.dt.bfloat16
I32 = mybir.dt.int32
ALU = mybir.AluOpType
AF = mybir.ActivationFunctionType
AX = mybir.AxisListType

EPOCH = 30000
NSLOT = 20


class Buf:
    __slots__ = ("name", "w", "r", "psum")

    def __init__(self, name, psum=False):
        self.name = name
        self.psum = psum
        self.w = None
        self.r = {}


class Eng:
    def __init__(self, k, name, h, is_compute=True):
        self.k = k
        self.name = name
        self.h = h
        self.sems = []
        self.count = 0
        self.waited = {}
        self.is_compute = is_compute
        self.slots = []
        self.slot_val = []
        self.ndma = 0


class KB:
    def __init__(self, nc, stack):
        self.nc = nc
        self.stack = stack
        self.E = {}
        for name, h in (("pe", nc.tensor), ("act", nc.scalar), ("dve", nc.vector),
                        ("pool", nc.gpsimd), ("sp", nc.sync)):
            self.E[name] = Eng(self, name, h)
        self.n_sem = 0
        self.n_inst = 0
        for e in self.E.values():
            if e.name in ("sp", "act", "pool"):
                for i in range(NSLOT):
                    e.slots.append(self._sem(f"d_{e.name}_{i}"))
                    e.slot_val.append(0)

    def _sem(self, name):
        self.n_sem += 1
        return self.stack.enter_context(self.nc.semaphore(name))

    def sbuf(self, name, shape, dtype, stack=None):
        self.n_alloc = getattr(self, "n_alloc", 0) + 1
        return (stack or self.stack).enter_context(self.nc.sbuf_tensor(f"{name}_{self.n_alloc}", list(shape), dtype))

    def psum(self, name, shape, dtype, stack=None):
        self.n_alloc = getattr(self, "n_alloc", 0) + 1
        return (stack or self.stack).enter_context(self.nc.psum_tensor(f"{name}_{self.n_alloc}", list(shape), dtype))

    def dram(self, name, shape, dtype, kind="Internal"):
        return self.nc.dram_tensor(name, list(shape), dtype, kind=kind)

    def _eng_sem(self, e, seq):
        idx = seq // EPOCH
        while len(e.sems) <= idx:
            e.sems.append(self._sem(f"c_{e.name}_{len(e.sems)}"))
        return e.sems[idx], seq % EPOCH + 1

    def _wait_tok(self, x, tok):
        if tok is None:
            return
        if tok[0] == "E":
            _, en, seq = tok
            key = ("E", en)
            if x.waited.get(key, -1) >= seq:
                return
            e = self.E[en]
            sem, val = self._eng_sem(e, seq)
            x.h.wait_ge(sem, val)
            x.waited[key] = seq
        else:
            _, qn, slot, val = tok
            key = ("D", qn, slot)
            if x.waited.get(key, 0) >= val:
                return
            q = self.E[qn]
            x.h.wait_ge(q.slots[slot], val)
            x.waited[key] = val

    def _deps(self, x, reads, writes, same_eng_waw=True):
        toks = []
        for b in reads:
            if b.w is not None:
                toks.append(b.w)
            if b.psum:
                for t in b.r.values():
                    if not (t[0] == "E" and t[1] == x.name):
                        toks.append(t)
        for b in writes:
            if b.w is not None:
                if b.w[0] == "E" and b.w[1] == x.name and not same_eng_waw:
                    pass
                else:
                    toks.append(b.w)
            for t in b.r.values():
                if t[0] == "E" and t[1] == x.name:
                    continue
                toks.append(t)
        for t in toks:
            self._wait_tok(x, t)

    def _record(self, tok, reads, writes):
        for b in reads:
            if tok[0] == "E":
                b.r[("E", tok[1])] = tok
            else:
                b.r[("D", tok[1], tok[2])] = tok
        for b in writes:
            b.w = tok
            b.r = {}

    def op(self, eng, fn, reads=(), writes=()):
        x = self.E[eng]
        self._deps(x, reads, writes, same_eng_waw=(eng != "pe"))
        inst = fn(x.h)
        seq = x.count
        x.count += 1
        sem, val = self._eng_sem(x, seq)
        inst.then_inc(sem, 1)
        self._record(("E", eng, seq), reads, writes)
        self.n_inst += 1
        return inst

    def dma(self, q, out, in_, reads=(), writes=(), **kw):
        x = self.E[q]
        self._deps(x, reads, writes)
        slot = x.ndma % NSLOT
        x.ndma += 1
        if x.slot_val[slot] > 0:
            self._wait_tok(x, ("D", q, slot, x.slot_val[slot]))
        x.slot_val[slot] += 16
        inst = x.h.dma_start(out=out, in_=in_, **kw)
        inst.then_inc(x.slots[slot], 16)
        self._record(("D", q, slot, x.slot_val[slot]), reads, writes)
        self.n_inst += 1
        return inst

    def barrier(self):
        sp = self.E["sp"]
        for e in self.E.values():
            if e.slots:
                for s in range(NSLOT):
                    if e.slot_val[s] > 0:
                        self._wait_tok(sp, ("D", e.name, s, e.slot_val[s]))
        for e in self.E.values():
            if e.name != "sp" and e.count > 0:
                self._wait_tok(sp, ("E", e.name, e.count - 1))
        seq = sp.count
        sp.count += 1
        sem, val = self._eng_sem(sp, seq)
        sp.h.nop().then_inc(sem, 1)
        for e in self.E.values():
            if e.name != "sp":
                self._wait_tok(e, ("E", "sp", seq))
                for o in self.E.values():
                    if o.count > 0 and o.name != "sp":
                        e.waited[("E", o.name)] = max(e.waited.get(("E", o.name), -1),
                                                      o.count - 1 if o.name != e.name else -1)
                    for s in range(len(o.slots)):
                        e.waited[("D", o.name, s)] = o.slot_val[s]

    def finish(self):
        self.barrier()

from contextlib import ExitStack

EPS = 1e-6


class Ctx:
    def __init__(self, k, cfg, consts_dram):
        self.k = k
        self.cfg = cfg
        nc = k.nc
        self.ident = k.sbuf("ident", [128, 128], F32)
        self.Bident = Buf("ident")
        self.ones32 = k.sbuf("ones32", [128, 128], F32)
        self.onesb = k.sbuf("onesb", [128, 128], BF16)
        self.identb = k.sbuf("identb", [128, 128], BF16)
        self.Bconst = Buf("const")
        k.dma("sp", self.ident[:], consts_dram.ap()[:, 0:128], writes=[self.Bident])
        k.op("dve", lambda h: h.memset(self.ones32[:], 1.0), writes=[self.Bconst])
        k.op("dve", lambda h: h.memset(self.onesb[:], 1.0), writes=[self.Bconst])
        k.op("dve", lambda h: h.tensor_copy(self.identb[:], self.ident[:]), reads=[self.Bident], writes=[self.Bconst])
        self.banks = []
        self.Bbank = []
        for i in range(8):
            self.banks.append(k.psum(f"bank{i}", [128, 512], F32))
            self.Bbank.append(Buf(f"bank{i}", psum=True))
        self.rr = 0
        self.lc_tmp = k.sbuf("lc_tmp", [128, 128], F32)
        self.Blc_tmp = Buf("lc_tmp")

    def evac_eng(self):
        self.rr += 1
        return "act" if self.rr % 2 else "dve"


def copy_on(k, eng, out, in_, reads, writes):
    if eng == "act":
        return k.op("act", lambda h: h.activation(out, in_, AF.Copy), reads=reads, writes=writes)
    return k.op(eng, lambda h: h.tensor_copy(out, in_), reads=reads, writes=writes)


def load_cols(cx, st, vec_ap_rows, R, dst, Bdst, dst_cols=None):
    k = cx.k
    tmp, Bt = cx.lc_tmp, cx.Blc_tmp
    k.dma("sp", tmp[0:R, :], vec_ap_rows, writes=[Bt])
    bank, Bb = cx.banks[7], cx.Bbank[7]
    k.op("pe", lambda h: h.transpose(bank[:, 0:R], tmp[0:R, :], cx.ident[0:R, 0:R]), reads=[Bt, cx.Bident], writes=[Bb])
    d = dst if dst_cols is None else dst_cols
    k.op("dve", lambda h: h.tensor_copy(d, bank[:, 0:R]), reads=[Bb], writes=[Bdst])


def load_vec_cols(cx, st, vec_dram_ap_1d, n, dst, Bdst, col0=0):
    R = n // 128
    rows = vec_dram_ap_1d.rearrange("(r p) -> r p", p=128)
    r0 = 0
    while r0 < R:
        rr = min(128, R - r0)
        load_cols(cx, st, rows[r0:r0 + rr, :], rr, dst, Bdst, dst_cols=dst[:, col0 + r0:col0 + r0 + rr])
        r0 += rr


def phase_transpose_in(cx, x_dram, xT_dram, Tg, D):
    k = cx.k
    DC = D // 128
    with ExitStack() as st:
        xr = [k.sbuf(f"ti_x{i}", [128, D], F32, stack=st) for i in range(2)]
        Bxr = [Buf(f"ti_x{i}") for i in range(2)]
        stg = [k.sbuf(f"ti_s{i}", [128, 4, 128], F32, stack=st) for i in range(3)]
        Bstg = [Buf(f"ti_s{i}") for i in range(3)]
        xTv = xT_dram.ap().rearrange("(c p) t -> p c t", p=128)
        nb = (Tg + 127) // 128
        si = 0
        for tb in range(nb):
            nt = min(128, Tg - tb * 128)
            s = tb % 2
            k.dma("sp", xr[s][0:nt, :], x_dram.ap()[tb * 128:tb * 128 + nt, :], writes=[Bxr[s]])
            for c4 in range(DC // 4):
                b = (tb * (DC // 4) + c4) % 8
                bank, Bb = cx.banks[b], cx.Bbank[b]
                for j in range(4):
                    c = c4 * 4 + j
                    k.op("pe", lambda h: h.transpose(bank[:, j * 128:j * 128 + nt], xr[s][0:nt, c * 128:(c + 1) * 128],
                                                     cx.ident[0:nt, 0:nt]), reads=[Bxr[s], cx.Bident], writes=[Bb])
                g = si % 3
                si += 1
                copy_on(k, cx.evac_eng(), stg[g][:, :, 0:nt], bank[:, :].rearrange("p (j t) -> p j t", j=4)[:, :, 0:nt],
                        [Bb], [Bstg[g]])
                k.dma("sp", xTv[:, c4 * 4:(c4 + 1) * 4, tb * 128:tb * 128 + nt], stg[g][:, :, 0:nt], reads=[Bstg[g]])
    k.barrier()


def phase_transpose_out(cx, xT_dram, y_dram, Tg, D):
    k = cx.k
    DC = D // 128
    with ExitStack() as st:
        xin = [k.sbuf(f"to_x{i}", [128, 4, 128], F32, stack=st) for i in range(3)]
        Bxin = [Buf(f"to_x{i}") for i in range(3)]
        stg = [k.sbuf(f"to_s{i}", [128, 512], F32, stack=st) for i in range(3)]
        Bstg = [Buf(f"to_s{i}") for i in range(3)]
        xTv = xT_dram.ap().rearrange("(c p) t -> p c t", p=128)
        nb = (Tg + 127) // 128
        it = 0
        for tb in range(nb):
            nt = min(128, Tg - tb * 128)
            for c4 in range(DC // 4):
                g = it % 3
                b = it % 8
                it += 1
                bank, Bb = cx.banks[b], cx.Bbank[b]
                k.dma("sp", xin[g][:, :, 0:nt], xTv[:, c4 * 4:(c4 + 1) * 4, tb * 128:tb * 128 + nt], writes=[Bxin[g]])
                for j in range(4):
                    k.op("pe", lambda h: h.transpose(bank[0:nt, j * 128:(j + 1) * 128], xin[g][:, j, 0:nt], cx.ident[:, :]),
                         reads=[Bxin[g], cx.Bident], writes=[Bb])
                copy_on(k, cx.evac_eng(), stg[g][0:nt, :], bank[0:nt, :], [Bb], [Bstg[g]])
                k.dma("sp", y_dram.ap()[tb * 128:tb * 128 + nt, c4 * 512:(c4 + 1) * 512], stg[g][0:nt, :], reads=[Bstg[g]])
    k.barrier()


class NormScratch:
    def __init__(self, cx, st, TT, tag):
        k = cx.k
        self.xs = [k.sbuf(f"{tag}_xs{i}", [128, 2, TT], F32, stack=st) for i in range(2)]
        self.Bxs = [Buf(f"{tag}_xs{i}") for i in range(2)]
        self.sq = [k.sbuf(f"{tag}_sq{i}", [128, TT], F32, stack=st) for i in range(2)]
        self.Bsq = [Buf(f"{tag}_sq{i}") for i in range(2)]
        self.rstd = k.sbuf(f"{tag}_rstd", [128, TT], F32, stack=st)
        self.Brstd = Buf(f"{tag}_rstd")


def norm_tile(cx, ns, xT_dram, t0, TT, D, gain_cols, Bgain, out_bf, Bout, tag):
    k = cx.k
    DC = D // 128
    G = 2
    xs, Bxs, sq, Bsq, rstd, Brstd = ns.xs, ns.Bxs, ns.sq, ns.Bsq, ns.rstd, ns.Brstd
    xTv = xT_dram.ap().rearrange("(c p) t -> p c t", p=128)
    bank, Bb = cx.banks[6], cx.Bbank[6]
    it = 0
    for c4 in range(DC // G):
        g = it % 2
        it += 1
        k.dma("sp", xs[g][:, :, :], xTv[:, c4 * G:(c4 + 1) * G, t0:t0 + TT], writes=[Bxs[g]])
        for j in range(G):
            c = c4 * G + j
            q = c % 2
            k.op("act", lambda h: h.activation(sq[q][:, :], xs[g][:, j, :], AF.Square), reads=[Bxs[g]], writes=[Bsq[q]])
            k.op("pe", lambda h: h.matmul(bank[:, 0:TT], cx.ones32[:, :], sq[q][:, :], start=(c == 0), stop=(c == DC - 1)),
                 reads=[Bsq[q], cx.Bconst], writes=[Bb])
    k.op("act", lambda h: h.activation(rstd[:, :], bank[:, 0:TT], AF.Sqrt, bias=EPS, scale=1.0 / D), reads=[Bb], writes=[Brstd])
    k.op("dve", lambda h: h.reciprocal(rstd[:, :], rstd[:, :]), reads=[Brstd], writes=[Brstd])
    for c4 in range(DC // G):
        g = it % 2
        it += 1
        k.dma("sp", xs[g][:, :, :], xTv[:, c4 * G:(c4 + 1) * G, t0:t0 + TT], writes=[Bxs[g]])
        for j in range(G):
            c = c4 * G + j
            k.op("dve", lambda h: h.scalar_tensor_tensor(out_bf[:, c, 0:TT], xs[g][:, j, :], gain_cols[:, c:c + 1], rstd[:, :],
                                                         ALU.mult, ALU.mult), reads=[Bxs[g], Bgain, Brstd], writes=[Bout])


class WStream:
    def __init__(self, cx, st, KG=4, NW=3, tag="ws"):
        self.cx = cx
        k = cx.k
        self.KG = KG
        self.NW = NW
        self.w = [k.sbuf(f"{tag}_w{i}", [128, KG, 512], BF16, stack=st) for i in range(NW)]
        self.Bw = [Buf(f"{tag}_w{i}") for i in range(NW)]
        self.wi = 0
        self.bi = 0

    def run(self, W_ap2d, K, blocks, subs):
        cx, k = self.cx, self.cx.k
        KC = K // 128
        KG = self.KG
        assert KC % KG == 0 and len(subs) in (1, 2)
        Wv = W_ap2d.rearrange("(kc p) n -> p kc n", p=128)
        for blk in blocks:
            chunks = []
            off = 0
            for (c0, ncol) in blk:
                o = 0
                while o < ncol:
                    r = min(128, ncol - o)
                    chunks.append((off + o, r, c0 + o))
                    o += r
                off += ncol
            assert off <= 512 and len(chunks) <= 4
            half = self.bi % 2
            self.bi += 1
            base = [half * 4] if len(subs) == 1 else [0, 4]
            for kg in range(KC // KG):
                s = self.wi % self.NW
                self.wi += 1
                off = 0
                for (c0, ncol) in blk:
                    k.dma("pool", self.w[s][:, :, off:off + ncol], Wv[:, kg * KG:(kg + 1) * KG, c0:c0 + ncol], writes=[self.Bw[s]])
                    off += ncol
                for si, (rhs, Brhs, TT, _) in enumerate(subs):
                    for kl in range(KG):
                        kc = kg * KG + kl
                        for j, (woff, rows, _c) in enumerate(chunks):
                            b = base[si] + j
                            k.op("pe", lambda h: h.matmul(cx.banks[b][0:rows, 0:TT], self.w[s][:, kl, woff:woff + rows], rhs[:, kc, 0:TT],
                                                          start=(kc == 0), stop=(kc == KC - 1)),
                                 reads=[self.Bw[s], Brhs], writes=[cx.Bbank[b]])
            for si, (rhs, Brhs, TT, evac) in enumerate(subs):
                for j, (woff, rows, col0) in enumerate(chunks):
                    b = base[si] + j
                    evac(j, cx.banks[b][0:rows, 0:TT], cx.Bbank[b], rows, col0)


def std_blocks(N):
    out = []
    c = 0
    while c < N:
        n = min(512, N - c)
        out.append([(c, n)])
        c += n
    return out


def phase_inproj(cx, xT_dram, projT_dram, Tg, TT, D, NIN, w_in_ap, gain_vec_ap):
    k = cx.k
    DC = D // 128
    NS = 2 if Tg >= 2 * TT else 1
    with ExitStack() as st:
        gain = k.sbuf("ip_gain", [128, DC], F32, stack=st)
        Bgain = Buf("ip_gain")
        load_vec_cols(cx, st, gain_vec_ap, D, gain, Bgain)
        xnb = [k.sbuf(f"ip_xnb{i}", [128, DC, TT], BF16, stack=st) for i in range(NS)]
        Bxnb = [Buf(f"ip_xnb{i}") for i in range(NS)]
        stg = [k.sbuf(f"ip_stg{i}", [128, TT], F32, stack=st) for i in range(4)]
        Bstg = [Buf(f"ip_stg{i}") for i in range(4)]
        ws = WStream(cx, st, NW=(8 if TT <= 64 else 6), tag="ip")
        ns = NormScratch(cx, st, TT, "ipn")
        cnt = [0]
        for t0 in range(0, Tg, NS * TT):
            subs = []
            for si in range(NS):
                ts = t0 + si * TT
                norm_tile(cx, ns, xT_dram, ts, TT, D, gain, Bgain, xnb[si], Bxnb[si], "ipn")

                def evac(j, bank_ap, Bb, rows, col0, ts=ts):
                    g = cnt[0] % 4
                    cnt[0] += 1
                    copy_on(k, cx.evac_eng(), stg[g][0:rows, :], bank_ap, [Bb], [Bstg[g]])
                    k.dma("sp", projT_dram.ap()[col0:col0 + rows, ts:ts + TT], stg[g][0:rows, :], reads=[Bstg[g]])
                subs.append((xnb[si], Bxnb[si], TT, evac))
            ws.run(w_in_ap, D, std_blocks(NIN), subs)
    k.barrier()


def phase_outproj(cx, xT_dram, mixT_dram, hT_dram, Tg, TT, D, DMIX, w_out_ap):
    k = cx.k
    MC = DMIX // 128
    NS = 2 if Tg >= 2 * TT else 1
    with ExitStack() as st:
        mixb = [k.sbuf(f"op_mixb{i}", [128, MC, TT], BF16, stack=st) for i in range(NS)]
        Bmixb = [Buf(f"op_mixb{i}") for i in range(NS)]
        xres = [k.sbuf(f"op_x{i}", [128, TT], F32, stack=st) for i in range(4)]
        Bxres = [Buf(f"op_x{i}") for i in range(4)]
        ws = WStream(cx, st, NW=(8 if TT <= 64 else 6), tag="op")
        cnt = [0]
        mv = mixT_dram.ap().rearrange("(c p) t -> p c t", p=128)
        for t0 in range(0, Tg, NS * TT):
            subs = []
            for si in range(NS):
                ts = t0 + si * TT
                for m0 in range(0, MC, 8):
                    m1 = min(MC, m0 + 8)
                    k.dma("sp", mixb[si][:, m0:m1, 0:TT], mv[:, m0:m1, ts:ts + TT], writes=[Bmixb[si]])

                def evac(j, bank_ap, Bb, rows, col0, ts=ts):
                    g = cnt[0] % 4
                    cnt[0] += 1
                    k.dma("sp", xres[g][0:rows, :], xT_dram.ap()[col0:col0 + rows, ts:ts + TT], writes=[Bxres[g]])
                    k.op("dve", lambda h: h.tensor_tensor(xres[g][0:rows, :], bank_ap, xres[g][0:rows, :], ALU.add),
                         reads=[Bb, Bxres[g]], writes=[Bxres[g]])
                    k.dma("sp", hT_dram.ap()[col0:col0 + rows, ts:ts + TT], xres[g][0:rows, :], reads=[Bxres[g]])
                subs.append((mixb[si], Bmixb[si], TT, evac))
            ws.run(w_out_ap, DMIX, std_blocks(D), subs)
    k.barrier()


def phase_ffn(cx, hT_dram, xT_dram, Tg, TT, D, DFF, gain_vec_ap, up_ap, convw_ap, convb_ap, down_ap,
              state_ap, out_state_ap):
    k = cx.k
    DC = D // 128
    FC = DFF // 128
    with ExitStack() as st:
        gain = k.sbuf("ff_gain", [128, DC], F32, stack=st)
        Bgain = Buf("ff_gain")
        load_vec_cols(cx, st, gain_vec_ap, D, gain, Bgain)
        cw = k.sbuf("ff_cw", [128, 3, 2 * FC], F32, stack=st)
        cb = k.sbuf("ff_cb", [128, 2 * FC], F32, stack=st)
        tails = k.sbuf("ff_tails", [128, 2 * FC, 2], F32, stack=st)
        tl2 = k.sbuf("ff_tl2", [128, 2, 2 * FC], F32, stack=st)
        Bcw, Bcb, Btails = Buf("ff_cw"), Buf("ff_cb"), Buf("ff_tails")
        for i in range(3):
            load_vec_cols(cx, st, convw_ap[i, :], 2 * DFF, cw[:, i, :], Bcw)
        load_vec_cols(cx, st, convb_ap, 2 * DFF, cb, Bcb)
        if state_ap is None:
            k.op("dve", lambda h: h.memset(tails[:, :, :], 0.0), writes=[Btails])
        else:
            for r in range(2):
                load_vec_cols(cx, st, state_ap[r, :], 2 * DFF, tl2[:, r, :], Btails)
            k.op("dve", lambda h: h.tensor_copy(tails[:, :, :], tl2[:, :, :].rearrange("p r c -> p c r")), reads=[Btails], writes=[Btails])
        hnb = k.sbuf("ff_hnb", [128, DC, TT], BF16, stack=st)
        Bhnb = Buf("ff_hnb")
        actb = k.sbuf("ff_actb", [128, FC, TT], BF16, stack=st)
        Bactb = Buf("ff_actb")
        NE = 3
        ext = [k.sbuf(f"ff_ext{i}", [128, TT + 2], F32, stack=st) for i in range(NE)]
        Bext = [Buf(f"ff_ext{i}") for i in range(NE)]
        acc = [k.sbuf(f"ff_acc{i}", [128, TT], F32, stack=st) for i in range(NE)]
        Bacc = [Buf(f"ff_acc{i}") for i in range(NE)]
        hres = [k.sbuf(f"ff_h{i}", [128, TT], F32, stack=st) for i in range(2)]
        Bhres = [Buf(f"ff_h{i}") for i in range(2)]
        ns = NormScratch(cx, st, TT, "ffn")
        ws = WStream(cx, st, KG=(4 if DC % 4 == 0 else 2), NW=(8 if TT <= 64 else 6), tag="ffu")
        wsd = WStream(cx, st, KG=(4 if FC % 4 == 0 else (2 if FC % 2 == 0 else 1)), NW=(8 if TT <= 64 else 6), tag="ffd")
        cnt = [0]
        ublocks = []
        f = 0
        while f < FC:
            n = min(2, FC - f)
            ublocks.append([(f * 128, n * 128), (DFF + f * 128, n * 128)])
            f += n
        for t0 in range(0, Tg, TT):
            norm_tile(cx, ns, hT_dram, t0, TT, D, gain, Bgain, hnb, Bhnb, "ffn")
            pend = {}

            def evac_up(j, bank_ap, Bb, rows, col0):
                ch = col0 // 128
                e = cnt[0] % NE
                cnt[0] += 1
                k.op("act", lambda h: h.activation(ext[e][:, 2:2 + TT], bank_ap, AF.Copy), reads=[Bb], writes=[Bext[e]])
                k.op("dve", lambda h: h.tensor_copy(ext[e][:, 0:2], tails[:, ch, :]), reads=[Btails], writes=[Bext[e]])
                k.op("dve", lambda h: h.tensor_copy(tails[:, ch, :], ext[e][:, TT:TT + 2]), reads=[Bext[e]], writes=[Btails])
                k.op("act", lambda h: h.activation(acc[e][:, :], ext[e][:, 2:2 + TT], AF.Identity, bias=cb[:, ch:ch + 1], scale=cw[:, 2, ch:ch + 1]),
                     reads=[Bext[e], Bcw, Bcb], writes=[Bacc[e]])
                k.op("dve", lambda h: h.scalar_tensor_tensor(acc[e][:, :], ext[e][:, 1:1 + TT], cw[:, 1, ch:ch + 1], acc[e][:, :], ALU.mult, ALU.add),
                     reads=[Bext[e], Bcw, Bacc[e]], writes=[Bacc[e]])
                k.op("dve", lambda h: h.scalar_tensor_tensor(acc[e][:, :], ext[e][:, 0:TT], cw[:, 0, ch:ch + 1], acc[e][:, :], ALU.mult, ALU.add),
                     reads=[Bext[e], Bcw, Bacc[e]], writes=[Bacc[e]])
                if ch < FC:
                    k.op("act", lambda h: h.activation(acc[e][:, :], acc[e][:, :], AF.Silu), reads=[Bacc[e]], writes=[Bacc[e]])
                    pend[ch] = e
                else:
                    ge = pend.pop(ch - FC)
                    k.op("dve", lambda h: h.tensor_tensor(actb[:, ch - FC, 0:TT], acc[ge][:, :], acc[e][:, :], ALU.mult),
                         reads=[Bacc[ge], Bacc[e]], writes=[Bactb])
            ws.run(up_ap, D, ublocks, [(hnb, Bhnb, TT, evac_up)])

            def evac_dn(j, bank_ap, Bb, rows, col0):
                g = cnt[0] % 2
                cnt[0] += 1
                k.dma("sp", hres[g][0:rows, :], hT_dram.ap()[col0:col0 + rows, t0:t0 + TT], writes=[Bhres[g]])
                k.op("dve", lambda h: h.tensor_tensor(hres[g][0:rows, :], bank_ap, hres[g][0:rows, :], ALU.add),
                     reads=[Bb, Bhres[g]], writes=[Bhres[g]])
                k.dma("sp", xT_dram.ap()[col0:col0 + rows, t0:t0 + TT], hres[g][0:rows, :], reads=[Bhres[g]])
            wsd.run(down_ap, DFF, std_blocks(D), [(actb, Bactb, TT, evac_dn)])
        k.op("dve", lambda h: h.tensor_copy(tl2[:, :, :], tails[:, :, :].rearrange("p c r -> p r c")), reads=[Btails], writes=[Btails])
        so = k.sbuf("ff_so", [128, 128], F32, stack=st)
        Bso = Buf("ff_so")
        for r in range(2):
            c0 = 0
            while c0 < 2 * FC:
                n = min(128, 2 * FC - c0)
                bank, Bb = cx.banks[7], cx.Bbank[7]
                k.op("pe", lambda h: h.transpose(bank[0:n, 0:128], tl2[:, r, c0:c0 + n], cx.ident[:, :]), reads=[Btails, cx.Bident], writes=[Bb])
                k.op("dve", lambda h: h.tensor_copy(so[0:n, :], bank[0:n, 0:128]), reads=[Bb], writes=[Bso])
                k.dma("sp", out_state_ap[r, c0 * 128:(c0 + n) * 128].rearrange("(c p) -> c p", p=128), so[0:n, :], reads=[Bso])
                c0 += n
    k.barrier()

import math
from contextlib import ExitStack

BRANCHES = ((128, 1), (512, 4), (2048, 16))
TOE_ML = 384
SMP_ML = 2064


def t5_bucket_np(dist):
    dist = np.asarray(dist, np.int64)
    d = np.maximum(dist, 1).astype(np.float32)
    large = 16 + (np.log(d / np.float32(16)) / np.float32(math.log(2048 / 16)) * np.float32(16)).astype(np.int32)
    large = np.minimum(large, 31)
    return np.where(dist < 16, dist, large)


def attn_onehots():
    ohp = np.zeros((3, 32, TOE_ML), np.float32)
    for bi, (w, d) in enumerate(BRANCHES):
        for m in range(TOE_ML):
            j = m - 127
            if 0 <= j <= w // d:
                ohp[bi, t5_bucket_np(j * d), m] = 1.0
    ohs = np.zeros((32, SMP_ML), np.float32)
    for m in range(SMP_ML):
        rel = m - 8
        if rel < 0:
            continue
        cnt = sum(1 for (w, d) in BRANCHES if rel % d == 0 and rel <= w)
        ohs[t5_bucket_np(rel), m] = cnt
    return ohp, ohs


def setup_attn_tables(cx, rel_bias_dram, ohp_dram, ohs_dram, Hp_dram, Hs_dram, HA, do_prompt=True, do_sample=True):
    k = cx.k
    with ExitStack() as st:
        rb = k.sbuf("at_rb", [32, HA], F32, stack=st)
        Brb = Buf("at_rb")
        k.dma("sp", rb[:, :], rel_bias_dram.ap(), writes=[Brb])
        k.op("act", lambda h: h.activation(rb[:, :], rb[:, :], AF.Exp), reads=[Brb], writes=[Brb])
        ohp = k.sbuf("at_ohp", [32, 3, TOE_ML], F32, stack=st)
        ohs = k.sbuf("at_ohs", [32, SMP_ML], F32, stack=st)
        Boh = Buf("at_oh")
        k.dma("sp", ohp[:, :, :], ohp_dram.ap().rearrange("b k m -> k b m"), writes=[Boh])
        k.dma("sp", ohs[:, :], ohs_dram.ap(), writes=[Boh])
        erb = [k.sbuf(f"at_erb{i}", [32, 128], F32, stack=st) for i in range(2)]
        Berb = [Buf(f"at_erb{i}") for i in range(2)]
        stg = [k.sbuf(f"at_stg{i}", [128, 512], F32, stack=st) for i in range(3)]
        Bstg = [Buf(f"at_stg{i}") for i in range(3)]
        it = 0
        for h in range(HA):
            e = h % 2
            k.op("dve", lambda hh: hh.tensor_scalar(erb[e][:, :], cx.ones32[0:32, :], rb[:, h:h + 1], None, ALU.mult),
                 reads=[Brb, cx.Bconst], writes=[Berb[e]])
            jobs = []
            if do_prompt:
                for bi in range(3):
                    jobs.append((ohp[:, bi, :], TOE_ML, Hp_dram.ap()[h, bi, :, :]))
            if do_sample:
                c0 = 0
                while c0 < SMP_ML:
                    n = min(512, SMP_ML - c0)
                    jobs.append((ohs[:, c0:c0 + n], n, Hs_dram.ap()[h, :, c0:c0 + n]))
                    c0 += n
            for (rhs, n, dst) in jobs:
                b = it % 8
                g = it % 3
                it += 1
                k.op("pe", lambda hh: hh.matmul(cx.banks[b][:, 0:n], erb[e][:, :], rhs, start=True, stop=True),
                     reads=[Berb[e], Boh], writes=[cx.Bbank[b]])
                copy_on(k, cx.evac_eng(), stg[g][:, 0:n], cx.banks[b][:, 0:n], [cx.Bbank[b]], [Bstg[g]])
                k.dma("sp", dst, stg[g][:, 0:n], reads=[Bstg[g]])
    k.barrier()


def head_norm(cx, src, Bsrc, n, TT_list, gcol, Bg, rs, Brs, sqt, Bsq, bank_i, eps_scale):
    k = cx.k
    for (c0, cn) in TT_list:
        k.op("act", lambda h: h.activation(sqt[:, 0:cn], src[:, c0:c0 + cn], AF.Square), reads=[Bsrc], writes=[Bsq])
        k.op("pe", lambda h: h.matmul(cx.banks[bank_i][:, 0:cn], cx.ones32[:, :], sqt[:, 0:cn], start=True, stop=True),
             reads=[Bsq, cx.Bconst], writes=[cx.Bbank[bank_i]])
        k.op("act", lambda h: h.activation(rs[:, c0:c0 + cn], cx.banks[bank_i][:, 0:cn], AF.Sqrt, bias=EPS, scale=eps_scale),
             reads=[cx.Bbank[bank_i]], writes=[Brs])
    k.op("dve", lambda h: h.reciprocal(rs[:, 0:n], rs[:, 0:n]), reads=[Brs], writes=[Brs])


def tiles_of(n, t=512):
    return [(c, min(t, n - c)) for c in range(0, n, t)]


def phase_attn_prompt(cx, projT, mixT, kv_out_ap, S, HA, Hp_dram, qn_ap, kn_ap, on_ap):
    k = cx.k
    DA = HA * 128
    NB = S // 128
    assert S % 2048 == 0 or S in (256, 512, 1024, 2048)
    with ExitStack() as st:
        gq = k.sbuf("ap_gq", [128, 1], F32, stack=st)
        gk = k.sbuf("ap_gk", [128, 1], F32, stack=st)
        go = k.sbuf("ap_go", [128, HA], F32, stack=st)
        Bg = Buf("ap_g")
        load_vec_cols(cx, st, qn_ap, 128, gq, Bg)
        load_vec_cols(cx, st, kn_ap, 128, gk, Bg)
        load_vec_cols(cx, st, on_ap, DA, go, Bg)
        k.op("dve", lambda h: h.tensor_scalar(gq[:, :], gq[:, :], 128.0 ** -0.5, None, ALU.mult), reads=[Bg], writes=[Bg])
        raw = [k.sbuf(f"ap_raw{i}", [128, S], F32, stack=st) for i in range(3)]
        Braw = [Buf(f"ap_raw{i}") for i in range(3)]
        rs = k.sbuf("ap_rs", [128, S], F32, stack=st)
        Brs = Buf("ap_rs")
        sqt = k.sbuf("ap_sq", [128, 512], F32, stack=st)
        Bsq = Buf("ap_sq")
        knf = k.sbuf("ap_knf", [128, S], F32, stack=st)
        Bknf = Buf("ap_knf")
        qb = [k.sbuf(f"ap_qb{i}", [128, S], BF16, stack=st) for i in range(3)]
        kb = [k.sbuf(f"ap_kb{i}", [128, S], BF16, stack=st) for i in range(3)]
        Bqb = [Buf(f"ap_qb{i}") for i in range(3)]
        Bkb = [Buf(f"ap_kb{i}") for i in range(3)]
        vperm = k.sbuf("ap_vperm", [128, S], F32, stack=st)
        Bvperm = Buf("ap_vperm")
        vtok = k.sbuf("ap_vtok", [128, 3, NB, 128], BF16, stack=st)
        Bvtok = Buf("ap_vtok")
        kvst = [k.sbuf(f"ap_kvst{i}", [128, 2, 128], F32, stack=st) for i in range(3)]
        Bkvst = [Buf(f"ap_kvst{i}") for i in range(3)]
        eb = k.sbuf("ap_eb", [128, 3, 2, 128], F32, stack=st)
        Beb = Buf("ap_eb")
        pe_ = [k.sbuf(f"ap_pe{i}", [128, 128], F32, stack=st) for i in range(4)]
        Bpe = [Buf(f"ap_pe{i}") for i in range(4)]
        pt = [k.sbuf(f"ap_pt{i}", [128, 128], BF16, stack=st) for i in range(4)]
        Bpt = [Buf(f"ap_pt{i}") for i in range(4)]
        oacc = k.sbuf("ap_oacc", [128, S], F32, stack=st)
        dacc = k.sbuf("ap_dacc", [128, S], F32, stack=st)
        Boacc, Bdacc = Buf("ap_oacc"), Buf("ap_dacc")
        yab = k.sbuf("ap_yab", [128, S], BF16, stack=st)
        Byab = Buf("ap_yab")
        T5 = tiles_of(S)
        it = 0
        for h in range(HA):
            for i in range(3):
                k.dma("sp", raw[i][:, :], projT.ap()[i * DA + h * 128:i * DA + (h + 1) * 128, 0:S], writes=[Braw[i]])
            for bi in range(3):
                for vi, off in enumerate((127, 255)):
                    src = bass.AP(Hp_dram, (h * 3 + bi) * 128 * TOE_ML + off, [[TOE_ML - 1, 128], [1, 128]])
                    k.dma("sp", eb[:, bi, vi, :], src, writes=[Beb])
            head_norm(cx, raw[0], Braw[0], S, T5, None, None, rs, Brs, sqt, Bsq, 6, 1.0 / 128)
            k.op("dve", lambda hh: hh.scalar_tensor_tensor(qb[0][:, :], raw[0][:, :], gq[:, 0:1], rs[:, :], ALU.mult, ALU.mult),
                 reads=[Braw[0], Bg, Brs], writes=[Bqb[0]])
            head_norm(cx, raw[1], Braw[1], S, T5, None, None, rs, Brs, sqt, Bsq, 6, 1.0 / 128)
            k.op("dve", lambda hh: hh.scalar_tensor_tensor(knf[:, :], raw[1][:, :], gk[:, 0:1], rs[:, :], ALU.mult, ALU.mult),
                 reads=[Braw[1], Bg, Brs], writes=[Bknf])
            k.op("act", lambda hh: hh.activation(kb[0][:, :], knf[:, :], AF.Copy), reads=[Bknf], writes=[Bkb[0]])
            for bi, d in ((1, 4), (2, 16)):
                k.op("dve", lambda hh: hh.tensor_copy(qb[bi][:, :].rearrange("p (r u) -> p r u", r=d),
                                                      qb[0][:, :].rearrange("p (u r) -> p r u", r=d)), reads=[Bqb[0]], writes=[Bqb[bi]])
                k.op("dve", lambda hh: hh.tensor_copy(kb[bi][:, :].rearrange("p (r u) -> p r u", r=d),
                                                      kb[0][:, :].rearrange("p (u r) -> p r u", r=d)), reads=[Bkb[0]], writes=[Bkb[bi]])
            for tb in range(NB):
                b = it % 4
                g = it % 3
                it += 1
                bank, Bb = cx.banks[b], cx.Bbank[b]
                k.op("pe", lambda hh: hh.transpose(bank[:, 0:128], knf[:, tb * 128:(tb + 1) * 128], cx.ident[:, :]),
                     reads=[Bknf, cx.Bident], writes=[Bb])
                k.op("pe", lambda hh: hh.transpose(bank[:, 128:256], raw[2][:, tb * 128:(tb + 1) * 128], cx.ident[:, :]),
                     reads=[Braw[2], cx.Bident], writes=[Bb])
                k.op("act", lambda hh: hh.activation(kvst[g][:, :, :], bank[:, 0:256].rearrange("p (a d) -> p a d", a=2), AF.Copy),
                     reads=[Bb], writes=[Bkvst[g]])
                k.op("dve", lambda hh: hh.tensor_copy(vtok[:, 0, tb, :], bank[:, 128:256]), reads=[Bb], writes=[Bvtok])
                k.dma("sp", kv_out_ap[tb * 128:(tb + 1) * 128, :, h, :], kvst[g][:, :, :], reads=[Bkvst[g]])
            for bi, d in ((1, 4), (2, 16)):
                k.op("dve", lambda hh: hh.tensor_copy(vperm[:, :].rearrange("p (r u) -> p r u", r=d),
                                                      raw[2][:, :].rearrange("p (u r) -> p r u", r=d)), reads=[Braw[2]], writes=[Bvperm])
                for tb in range(NB):
                    b = it % 4
                    it += 1
                    bank, Bb = cx.banks[b], cx.Bbank[b]
                    k.op("pe", lambda hh: hh.transpose(bank[:, 0:128], vperm[:, tb * 128:(tb + 1) * 128], cx.ident[:, :]),
                         reads=[Bvperm, cx.Bident], writes=[Bb])
                    copy_on(k, cx.evac_eng(), vtok[:, bi, tb, :], bank[:, 0:128], [Bb], [Bvtok])
            for bi, (w, d) in enumerate(BRANCHES):
                L = S // d
                nbc = max(1, L // 128)
                jobs = []
                for Q in range(NB // 4):
                    for jj in range(4):
                        B = Q * 4 + jj
                        n = B % nbc
                        kbs = ([B - 1] if n >= 1 else []) + [B]
                        for ki, KB_ in enumerate(kbs):
                            jobs.append(dict(Q=Q, jj=jj, B=B, KB=KB_, vi=(0 if KB_ == B else 1), first=(ki == 0), last=(ki == len(kbs) - 1),
                                             qend=(jj == 3 and ki == len(kbs) - 1)))

                def stage1(jb):
                    nonlocal it
                    sb = it % 4
                    g = it % 4
                    it += 1
                    jb["g"] = g
                    k.op("pe", lambda hh: hh.matmul(cx.banks[sb][:, 0:128], kb[bi][:, jb["KB"] * 128:(jb["KB"] + 1) * 128],
                                                    qb[bi][:, jb["B"] * 128:(jb["B"] + 1) * 128], start=True, stop=True),
                         reads=[Bkb[bi], Bqb[bi]], writes=[cx.Bbank[sb]])
                    k.op("act", lambda hh: hh.activation(pe_[g][:, :], cx.banks[sb][:, 0:128], AF.Exp),
                         reads=[cx.Bbank[sb]], writes=[Bpe[g]])
                    k.op("dve", lambda hh: hh.tensor_tensor(pt[g][:, :], pe_[g][:, :], eb[:, bi, jb["vi"], :], ALU.mult),
                         reads=[Bpe[g], Beb], writes=[Bpt[g]])

                def stage2(jb):
                    Q, jj, g = jb["Q"], jb["jj"], jb["g"]
                    ob, db = 4 + (Q % 2) * 2, 5 + (Q % 2) * 2
                    k.op("pe", lambda hh: hh.matmul(cx.banks[ob][:, jj * 128:(jj + 1) * 128], vtok[:, bi, jb["KB"], :], pt[g][:, :],
                                                    start=jb["first"], stop=jb["last"]),
                         reads=[Bvtok, Bpt[g]], writes=[cx.Bbank[ob]])
                    k.op("pe", lambda hh: hh.matmul(cx.banks[db][:, jj * 128:(jj + 1) * 128], cx.onesb[:, :], pt[g][:, :],
                                                    start=jb["first"], stop=jb["last"]),
                         reads=[cx.Bconst, Bpt[g]], writes=[cx.Bbank[db]])
                    if not jb["qend"]:
                        return
                    if d == 1:
                        ov = oacc[:, Q * 512:(Q + 1) * 512]
                        dv = dacc[:, Q * 512:(Q + 1) * 512]
                        k.op("act", lambda hh: hh.activation(ov, cx.banks[ob][:, :], AF.Copy), reads=[cx.Bbank[ob]], writes=[Boacc])
                        k.op("dve", lambda hh: hh.tensor_copy(dv, cx.banks[db][:, :]), reads=[cx.Bbank[db]], writes=[Bdacc])
                    else:
                        cpq = 512 // L if L < 512 else 1
                        if L >= 512:
                            r = Q // (L // 512)
                            u0 = (Q % (L // 512)) * 512
                            ov = oacc[:, :].rearrange("p (u r) -> p r u", r=d)[:, r, u0:u0 + 512]
                            dv = dacc[:, :].rearrange("p (u r) -> p r u", r=d)[:, r, u0:u0 + 512]
                            oi = cx.banks[ob][:, :]
                            di = cx.banks[db][:, :]
                        else:
                            ov = oacc[:, :].rearrange("p (u r) -> p r u", r=d)[:, Q * cpq:(Q + 1) * cpq, :]
                            dv = dacc[:, :].rearrange("p (u r) -> p r u", r=d)[:, Q * cpq:(Q + 1) * cpq, :]
                            oi = cx.banks[ob][:, :].rearrange("p (c u) -> p c u", c=cpq)
                            di = cx.banks[db][:, :].rearrange("p (c u) -> p c u", c=cpq)
                        k.op("dve", lambda hh: hh.tensor_tensor(ov, oi, ov, ALU.add), reads=[cx.Bbank[ob], Boacc], writes=[Boacc])
                        k.op("dve", lambda hh: hh.tensor_tensor(dv, di, dv, ALU.add), reads=[cx.Bbank[db], Bdacc], writes=[Bdacc])

                LA = 2
                for i in range(len(jobs) + LA):
                    if i < len(jobs):
                        stage1(jobs[i])
                    if i >= LA:
                        stage2(jobs[i - LA])
            k.op("dve", lambda hh: hh.reciprocal(dacc[:, :], dacc[:, :]), reads=[Bdacc], writes=[Bdacc])
            k.op("dve", lambda hh: hh.tensor_tensor(oacc[:, :], oacc[:, :], dacc[:, :], ALU.mult), reads=[Boacc, Bdacc], writes=[Boacc])
            head_norm(cx, oacc, Boacc, S, T5, None, None, rs, Brs, sqt, Bsq, 6, 1.0 / 128)
            k.op("dve", lambda hh: hh.scalar_tensor_tensor(yab[:, :], oacc[:, :], go[:, h:h + 1], rs[:, :], ALU.mult, ALU.mult),
                 reads=[Boacc, Bg, Brs], writes=[Byab])
            k.dma("sp", mixT.ap()[h * 128:(h + 1) * 128, 0:S], yab[:, :], reads=[Byab])
    k.barrier()


def phase_attn_sample(cx, projT, mixT, kv_out_ap, cache_ap, T, P, HA, Hs_dram, qn_ap, kn_ap, on_ap):
    k = cx.k
    DA = HA * 128
    PB = P // 128
    with ExitStack() as st:
        gq = k.sbuf("as_gq", [128, 1], F32, stack=st)
        gk = k.sbuf("as_gk", [128, 1], F32, stack=st)
        go = k.sbuf("as_go", [128, HA], F32, stack=st)
        Bg = Buf("as_g")
        load_vec_cols(cx, st, qn_ap, 128, gq, Bg)
        load_vec_cols(cx, st, kn_ap, 128, gk, Bg)
        load_vec_cols(cx, st, on_ap, DA, go, Bg)
        k.op("dve", lambda h: h.tensor_scalar(gq[:, :], gq[:, :], 128.0 ** -0.5, None, ALU.mult), reads=[Bg], writes=[Bg])
        raw = [k.sbuf(f"as_raw{i}", [128, T], F32, stack=st) for i in range(3)]
        Braw = [Buf(f"as_raw{i}") for i in range(3)]
        rs = k.sbuf("as_rs", [128, T], F32, stack=st)
        Brs = Buf("as_rs")
        sqt = k.sbuf("as_sq", [128, T], F32, stack=st)
        Bsq = Buf("as_sq")
        knf = k.sbuf("as_knf", [128, T], F32, stack=st)
        Bknf = Buf("as_knf")
        qb = k.sbuf("as_qb", [128, T], BF16, stack=st)
        kbn = k.sbuf("as_kbn", [128, T], BF16, stack=st)
        Bqb, Bkbn = Buf("as_qb"), Buf("as_kbn")
        kvst = k.sbuf("as_kvst", [128, 2, 128], F32, stack=st)
        Bkvst = Buf("as_kvst")
        vnb = k.sbuf("as_vnb", [128, 128], BF16, stack=st)
        Bvnb = Buf("as_vnb")
        kc = [k.sbuf(f"as_kc{i}", [128, PB, 128], F32, stack=st) for i in range(2)]
        vc = [k.sbuf(f"as_vc{i}", [128, PB, 128], F32, stack=st) for i in range(2)]
        Bkc = [Buf(f"as_kc{i}") for i in range(2)]
        Bvc = [Buf(f"as_vc{i}") for i in range(2)]
        ktb = k.sbuf("as_ktb", [128, PB, 128], BF16, stack=st)
        vcb = k.sbuf("as_vcb", [128, PB, 128], BF16, stack=st)
        Bktb, Bvcb = Buf("as_ktb"), Buf("as_vcb")
        cs = k.sbuf("as_cs", [128, PB, T], F32, stack=st)
        cn = k.sbuf("as_cn", [128, T], F32, stack=st)
        Bcs = Buf("as_cs")
        pe_ = k.sbuf("as_pe", [128, PB, T], F32, stack=st)
        pn_ = k.sbuf("as_pn", [128, T], F32, stack=st)
        ptb = k.sbuf("as_ptb", [128, PB, T], BF16, stack=st)
        pnb = k.sbuf("as_pnb", [128, T], BF16, stack=st)
        Bpe, Bptb = Buf("as_pe"), Buf("as_ptb")
        oacc = k.sbuf("as_oacc", [128, T], F32, stack=st)
        dacc = k.sbuf("as_dacc", [128, T], F32, stack=st)
        Boacc = Buf("as_oacc")
        yab = k.sbuf("as_yab", [128, T], BF16, stack=st)
        Byab = Buf("as_yab")
        TL = [(0, T)]
        for h in range(HA):
            s = h % 2
            for i in range(3):
                k.dma("sp", raw[i][:, :], projT.ap()[i * DA + h * 128:i * DA + (h + 1) * 128, 0:T], writes=[Braw[i]])
            k.dma("sp", kc[s][:, :, :], cache_ap[:, 0, h, :].rearrange("(b p) d -> p b d", p=128), writes=[Bkc[s]])
            k.dma("sp", vc[s][:, :, :], cache_ap[:, 1, h, :].rearrange("(b p) d -> p b d", p=128), writes=[Bvc[s]])
            src = bass.AP(Hs_dram, h * 128 * SMP_ML + 8 + 128, [[SMP_ML - 1, 128], [128, PB], [1, T]])
            k.dma("sp", cs[:, :, :], src, writes=[Bcs])
            srcn = bass.AP(Hs_dram, h * 128 * SMP_ML + 8, [[SMP_ML - 1, T], [1, T]])
            k.dma("sp", cn[0:T, :], srcn, writes=[Bcs])
            head_norm(cx, raw[0], Braw[0], T, TL, None, None, rs, Brs, sqt, Bsq, 6, 1.0 / 128)
            k.op("dve", lambda hh: hh.scalar_tensor_tensor(qb[:, :], raw[0][:, :], gq[:, 0:1], rs[:, :], ALU.mult, ALU.mult),
                 reads=[Braw[0], Bg, Brs], writes=[Bqb])
            head_norm(cx, raw[1], Braw[1], T, TL, None, None, rs, Brs, sqt, Bsq, 6, 1.0 / 128)
            k.op("dve", lambda hh: hh.scalar_tensor_tensor(knf[:, :], raw[1][:, :], gk[:, 0:1], rs[:, :], ALU.mult, ALU.mult),
                 reads=[Braw[1], Bg, Brs], writes=[Bknf])
            k.op("act", lambda hh: hh.activation(kbn[:, :], knf[:, :], AF.Copy), reads=[Bknf], writes=[Bkbn])
            bank, Bb = cx.banks[0], cx.Bbank[0]
            k.op("pe", lambda hh: hh.transpose(bank[0:T, 0:128], knf[:, 0:T], cx.ident[:, :]), reads=[Bknf, cx.Bident], writes=[Bb])
            k.op("pe", lambda hh: hh.transpose(bank[0:T, 128:256], raw[2][:, 0:T], cx.ident[:, :]), reads=[Braw[2], cx.Bident], writes=[Bb])
            k.op("act", lambda hh: hh.activation(kvst[0:T, :, :], bank[0:T, 0:256].rearrange("p (a d) -> p a d", a=2), AF.Copy),
                 reads=[Bb], writes=[Bkvst])
            k.op("dve", lambda hh: hh.tensor_copy(vnb[0:T, :], bank[0:T, 128:256]), reads=[Bb], writes=[Bvnb])
            k.dma("sp", kv_out_ap[0:T, :, h, :], kvst[0:T, :, :], reads=[Bkvst])
            for b4 in range(PB // 4):
                bi_ = 1 + (b4 % 3)
                bank, Bb = cx.banks[bi_], cx.Bbank[bi_]
                for j in range(4):
                    blk = b4 * 4 + j
                    k.op("pe", lambda hh: hh.transpose(bank[:, j * 128:(j + 1) * 128], kc[s][:, blk, :], cx.ident[:, :]),
                         reads=[Bkc[s], cx.Bident], writes=[Bb])
                copy_on(k, cx.evac_eng(), ktb[:, b4 * 4:(b4 + 1) * 4, :], bank[:, :].rearrange("p (j t) -> p j t", j=4), [Bb], [Bktb])
            k.op("dve", lambda hh: hh.tensor_copy(vcb[:, :, :], vc[s][:, :, :]), reads=[Bvc[s]], writes=[Bvcb])
            sbank, Bsb = cx.banks[4], cx.Bbank[4]
            for blk in range(PB):
                k.op("pe", lambda hh: hh.matmul(sbank[:, blk * T:(blk + 1) * T], ktb[:, blk, :], qb[:, :], start=True, stop=True),
                     reads=[Bktb, Bqb], writes=[Bsb])
            k.op("pe", lambda hh: hh.matmul(sbank[0:T, PB * T:(PB + 1) * T], kbn[:, 0:T], qb[:, :], start=True, stop=True),
                 reads=[Bkbn, Bqb], writes=[Bsb])
            k.op("act", lambda hh: hh.activation(pe_[:, :, :], sbank[:, 0:PB * T].rearrange("p (b t) -> p b t", t=T), AF.Exp),
                 reads=[Bsb], writes=[Bpe])
            k.op("act", lambda hh: hh.activation(pn_[0:T, :], sbank[0:T, PB * T:(PB + 1) * T], AF.Exp), reads=[Bsb], writes=[Bpe])
            for blk in range(PB):
                k.op("dve", lambda hh: hh.tensor_tensor(ptb[:, blk, :], pe_[:, blk, :], cs[:, PB - 1 - blk, :], ALU.mult),
                     reads=[Bpe, Bcs], writes=[Bptb])
            k.op("dve", lambda hh: hh.tensor_tensor(pnb[0:T, :], pn_[0:T, :], cn[0:T, :], ALU.mult), reads=[Bpe, Bcs], writes=[Bptb])
            obank, Bob = cx.banks[5], cx.Bbank[5]
            for blk in range(PB):
                k.op("pe", lambda hh: hh.matmul(obank[:, 0:T], vcb[:, blk, :], ptb[:, blk, :], start=(blk == 0), stop=False),
                     reads=[Bvcb, Bptb], writes=[Bob])
            k.op("pe", lambda hh: hh.matmul(obank[:, 0:T], vnb[0:T, :], pnb[0:T, :], start=False, stop=True), reads=[Bvnb, Bptb], writes=[Bob])
            for blk in range(PB):
                k.op("pe", lambda hh: hh.matmul(obank[:, 128:128 + T], cx.onesb[:, :], ptb[:, blk, :], start=(blk == 0), stop=False),
                     reads=[cx.Bconst, Bptb], writes=[Bob])
            k.op("pe", lambda hh: hh.matmul(obank[:, 128:128 + T], cx.onesb[0:T, :], pnb[0:T, :], start=False, stop=True),
                 reads=[cx.Bconst, Bptb], writes=[Bob])
            k.op("dve", lambda hh: hh.reciprocal(dacc[:, :], obank[:, 128:128 + T]), reads=[Bob], writes=[Boacc])
            k.op("dve", lambda hh: hh.tensor_tensor(oacc[:, :], obank[:, 0:T], dacc[:, :], ALU.mult), reads=[Bob, Boacc], writes=[Boacc])
            head_norm(cx, oacc, Boacc, T, TL, None, None, rs, Brs, sqt, Bsq, 6, 1.0 / 128)
            k.op("dve", lambda hh: hh.scalar_tensor_tensor(yab[:, :], oacc[:, :], go[:, h:h + 1], rs[:, :], ALU.mult, ALU.mult),
                 reads=[Boacc, Bg, Brs], writes=[Byab])
            k.dma("sp", mixT.ap()[h * 128:(h + 1) * 128, 0:T], yab[:, :], reads=[Byab])
    k.barrier()


def phase_pool(cx, projT, mixT, T, DA, DB, pw_ap, pscale_ap, state_ap, out_state_ap, n_valid):
    k = cx.k
    WINS = (2, 4, 8, 16)
    NCH = DB // 128
    TT = min(512, T)
    with ExitStack() as st:
        psc = k.sbuf("pl_psc", [128, NCH], F32, stack=st)
        Bpsc = Buf("pl_psc")
        load_vec_cols(cx, st, pscale_ap, DB, psc, Bpsc)
        pwb = k.sbuf("pl_pwb", [128, 4, 2, 256], BF16, stack=st)
        Bpwb = Buf("pl_pwb")
        k.dma("pool", pwb[:, :, :, :], pw_ap.rearrange("g (cc p) e -> p g cc e", p=128), writes=[Bpwb])
        icnt = k.sbuf("pl_icnt", [128, 4, 16], F32, stack=st)
        Bicnt = Buf("pl_icnt")
        for gi, w in enumerate(WINS):
            for t in range(16):
                cnt = min(w, n_valid + t + 1)
                k.op("dve", lambda h: h.memset(icnt[:, gi, t:t + 1], 1.0 / cnt), writes=[Bicnt])
        ext = [k.sbuf(f"pl_ext{i}", [128, 15 + T], F32, stack=st) for i in range(2)]
        sA = [k.sbuf(f"pl_sA{i}", [128, 15 + T], F32, stack=st) for i in range(2)]
        sB = [k.sbuf(f"pl_sB{i}", [128, 15 + T], F32, stack=st) for i in range(2)]
        Bext = [Buf(f"pl_ext{i}") for i in range(2)]
        BsA = [Buf(f"pl_sA{i}") for i in range(2)]
        BsB = [Buf(f"pl_sB{i}") for i in range(2)]
        db_ = k.sbuf("pl_db", [128, 2, T], BF16, stack=st)
        Bdb = Buf("pl_db")
        tmp16 = k.sbuf("pl_t16", [128, 16], F32, stack=st)
        Bt16 = Buf("pl_t16")
        stt = k.sbuf("pl_stt", [15, DB], F32, stack=st)
        Bstt = Buf("pl_stt")
        sto = k.sbuf("pl_sto", [15, DB], F32, stack=st)
        Bsto = Buf("pl_sto")
        yf = [k.sbuf(f"pl_yf{i}", [128, TT], F32, stack=st) for i in range(2)]
        Byf = [Buf(f"pl_yf{i}") for i in range(2)]
        sqt = k.sbuf("pl_sq", [128, TT], F32, stack=st)
        Bsq = Buf("pl_sq")
        rs = k.sbuf("pl_rs", [128, TT], F32, stack=st)
        Brs = Buf("pl_rs")
        yb = [k.sbuf(f"pl_yb{i}", [128, TT], BF16, stack=st) for i in range(2)]
        Byb = [Buf(f"pl_yb{i}") for i in range(2)]
        if state_ap is not None:
            k.dma("sp", stt[:, :], state_ap, writes=[Bstt])
        it = 0
        for gi, w in enumerate(WINS):
            for cc in range(2):
                ch = gi * 2 + cc
                e = ch % 2
                k.dma("sp", ext[e][:, 15:15 + T], projT.ap()[3 * DA + ch * 128:3 * DA + (ch + 1) * 128, 0:T], writes=[Bext[e]])
                if state_ap is None:
                    k.op("dve", lambda h: h.memset(ext[e][:, 0:15], 0.0), writes=[Bext[e]])
                else:
                    bank, Bb = cx.banks[7], cx.Bbank[7]
                    k.op("pe", lambda h: h.transpose(bank[:, 0:15], stt[0:15, ch * 128:(ch + 1) * 128], cx.ident[0:15, 0:15]),
                         reads=[Bstt, cx.Bident], writes=[Bb])
                    k.op("dve", lambda h: h.tensor_copy(ext[e][:, 0:15], bank[:, 0:15]), reads=[Bb], writes=[Bext[e]])
                bank, Bb = cx.banks[7], cx.Bbank[7]
                k.op("pe", lambda h: h.transpose(bank[0:15, 0:128], ext[e][:, T:T + 15], cx.ident[:, :]), reads=[Bext[e], cx.Bident], writes=[Bb])
                k.op("dve", lambda h: h.tensor_copy(sto[0:15, ch * 128:(ch + 1) * 128], bank[0:15, 0:128]), reads=[Bb], writes=[Bsto])
                n = 15 + T
                cur, Bcur = ext[e], Bext[e]
                nxt = [(sA[e], BsA[e]), (sB[e], BsB[e])]
                sh = 1
                li = 0
                lo = 0
                while sh < w:
                    dst, Bdst = nxt[li % 2]
                    li += 1
                    k.op("dve", lambda h: h.tensor_tensor(dst[:, lo + sh:n], cur[:, lo + sh:n], cur[:, lo:n - sh], ALU.add), reads=[Bcur], writes=[Bdst])
                    cur, Bcur = dst, Bdst
                    lo += sh
                    sh *= 2
                k.op("dve", lambda h: h.scalar_tensor_tensor(db_[:, cc, :], cur[:, 15:15 + T], 1.0 / w, ext[e][:, 15:15 + T], ALU.mult, ALU.subtract),
                     reads=[Bcur, Bext[e]], writes=[Bdb])
                nf = min(16, T)
                k.op("dve", lambda h: h.tensor_tensor(tmp16[:, 0:nf], cur[:, 15:15 + nf], icnt[:, gi, 0:nf], ALU.mult), reads=[Bcur, Bicnt], writes=[Bt16])
                k.op("dve", lambda h: h.tensor_tensor(db_[:, cc, 0:nf], tmp16[:, 0:nf], ext[e][:, 15:15 + nf], ALU.subtract),
                     reads=[Bt16, Bext[e]], writes=[Bdb])
            for (t0, tn) in tiles_of(T, TT):
                for ec in range(2):
                    b = ec
                    for cc in range(2):
                        k.op("pe", lambda h: h.matmul(cx.banks[b][:, 0:tn], pwb[:, gi, cc, ec * 128:(ec + 1) * 128], db_[:, cc, t0:t0 + tn],
                                                      start=(cc == 0), stop=(cc == 1)), reads=[Bpwb, Bdb], writes=[cx.Bbank[b]])
                    k.op("act", lambda h: h.activation(yf[ec][:, 0:tn], cx.banks[b][:, 0:tn], AF.Copy), reads=[cx.Bbank[b]], writes=[Byf[ec]])
                    k.op("act", lambda h: h.activation(sqt[:, 0:tn], yf[ec][:, 0:tn], AF.Square), reads=[Byf[ec]], writes=[Bsq])
                    k.op("pe", lambda h: h.matmul(cx.banks[2][:, 0:tn], cx.ones32[:, :], sqt[:, 0:tn], start=(ec == 0), stop=(ec == 1)),
                         reads=[Bsq, cx.Bconst], writes=[cx.Bbank[2]])
                k.op("act", lambda h: h.activation(rs[:, 0:tn], cx.banks[2][:, 0:tn], AF.Sqrt, bias=EPS, scale=1.0 / 256), reads=[cx.Bbank[2]], writes=[Brs])
                k.op("dve", lambda h: h.reciprocal(rs[:, 0:tn], rs[:, 0:tn]), reads=[Brs], writes=[Brs])
                for ec in range(2):
                    ch = gi * 2 + ec
                    k.op("dve", lambda h: h.scalar_tensor_tensor(yb[ec][:, 0:tn], yf[ec][:, 0:tn], psc[:, ch:ch + 1], rs[:, 0:tn], ALU.mult, ALU.mult),
                         reads=[Byf[ec], Bpsc, Brs], writes=[Byb[ec]])
                    k.dma("sp", mixT.ap()[DA + ch * 128:DA + (ch + 1) * 128, t0:t0 + tn], yb[ec][:, 0:tn], reads=[Byb[ec]])
        k.dma("sp", out_state_ap, sto[0:15, :], reads=[Bsto])
    k.barrier()


def gdn_consts(HC):
    mU = np.triu(np.ones((128, 128), np.float32))
    mSU = np.triu(np.ones((128, 128), np.float32), 1)
    l128 = np.zeros((128, 128), np.float32); l128[127, :] = 1.0
    l8 = np.zeros((128, 128), np.float32); l8[7, 0:8] = 1.0
    sel = np.zeros((128, HC * 128), np.float32)
    for h in range(HC):
        sel[h, h * 128:(h + 1) * 128] = 1.0
    idx = np.arange(128)
    bd8 = (idx[:, None] // 8 == idx[None, :] // 8).astype(np.float32)
    lls = []
    for b in (8, 16, 32, 64):
        same = idx[:, None] // (2 * b) == idx[None, :] // (2 * b)
        ll = same & ((idx[:, None] % (2 * b)) >= b) & ((idx[None, :] % (2 * b)) < b)
        lls.append(ll.astype(np.float32))
    return np.concatenate([mU, mSU, l128, l8, sel, bd8] + lls, axis=1)


def phase_gdn(cx, projT, mixT, T, c, DA, DB, HC, gconst, Bgconst, convw_ap, alog_ap, dtb_ap, onorm_ap,
              cstate_ap, cstate_out_ap, s0_ap, s_out_ap, HG):
    k = cx.k
    DC = HC * 128
    base = 3 * DA + DB
    NCHK = T // c
    L = 2
    mU, mSU = gconst[:, 0:128], gconst[:, 128:256]
    lrow = gconst[:, 256:384] if c == 128 else gconst[:, 384:512]
    sel = gconst[:, 512:512 + HC * 128]
    o_ = 512 + HC * 128
    bd8 = gconst[:, o_:o_ + 128]
    LLm = [gconst[:, o_ + 128 * (i + 1):o_ + 128 * (i + 2)] for i in range(4)]
    with ExitStack() as st:
        cwc = k.sbuf("gd_cwc", [128, 4, 3 * HC], F32, stack=st)
        Bcwc = Buf("gd_cwc")
        for i in range(4):
            load_vec_cols(cx, st, convw_ap[i, :], 3 * DC, cwc[:, i, :], Bcwc)
        cst = k.sbuf("gd_cst", [128, 3, 3 * HC], F32, stack=st)
        Bcst = Buf("gd_cst")
        if cstate_ap is not None:
            for r in range(3):
                load_vec_cols(cx, st, cstate_ap[r, :], 3 * DC, cst[:, r, :], Bcst)
        else:
            k.op("dve", lambda h: h.memset(cst[:, :, :], 0.0), writes=[Bcst])
        tail = k.sbuf("gd_tail", [128, 3, 3 * HC], F32, stack=st)
        Btail = Buf("gd_tail")
        onc = k.sbuf("gd_onc", [128, 1], F32, stack=st)
        Bonc = Buf("gd_onc")
        load_vec_cols(cx, st, onorm_ap, 128, onc, Bonc)
        hv = k.sbuf("gd_hv", [HC, 2], F32, stack=st)
        Bhv = Buf("gd_hv")
        k.dma("sp", hv[:, 0:1], alog_ap.rearrange("(h o) -> h o", o=1), writes=[Bhv])
        k.dma("sp", hv[:, 1:2], dtb_ap.rearrange("(h o) -> h o", o=1), writes=[Bhv])
        negA = k.sbuf("gd_negA", [HC, 1], F32, stack=st)
        BnegA = Buf("gd_negA")
        k.op("act", lambda h: h.activation(negA[:, :], hv[:, 0:1], AF.Exp), reads=[Bhv], writes=[BnegA])
        k.op("dve", lambda h: h.tensor_scalar(negA[:, :], negA[:, :], -1.0, None, ALU.mult), reads=[BnegA], writes=[BnegA])
        betaT = k.sbuf("gd_betaT", [HC, T], F32, stack=st)
        gT = k.sbuf("gd_gT", [HC, T], F32, stack=st)
        gcT = k.sbuf("gd_gcT", [HC, T], F32, stack=st)
        rm = k.sbuf("gd_rm", [HC, T], F32, stack=st)
        Bbeta, BgT, BgcT, Brm = Buf("gd_betaT"), Buf("gd_gT"), Buf("gd_gcT"), Buf("gd_rm")
        k.dma("sp", betaT[:, :], projT.ap()[base + 4 * DC:base + 4 * DC + HC, 0:T], writes=[Bbeta])
        k.dma("sp", gT[:, :], projT.ap()[base + 4 * DC + HC:base + 4 * DC + 2 * HC, 0:T], writes=[BgT])
        k.op("act", lambda h: h.activation(betaT[:, :], betaT[:, :], AF.Sigmoid), reads=[Bbeta], writes=[Bbeta])
        k.op("act", lambda h: h.activation(gT[:, :], gT[:, :], AF.Exp, bias=hv[:, 1:2]), reads=[BgT, Bhv], writes=[BgT])
        k.op("act", lambda h: h.activation(gT[:, :], gT[:, :], AF.Ln, bias=1.0), reads=[BgT], writes=[BgT])
        k.op("dve", lambda h: h.tensor_scalar(gT[:, :], gT[:, :], negA[:, 0:1], None, ALU.mult), reads=[BgT, BnegA], writes=[BgT])
        k.op("dve", lambda h: h.memset(rm[:, :], 1.0), writes=[Brm])
        k.op("dve", lambda h: h.memset(rm[:, :].rearrange("p (n c) -> p n c", c=c)[:, :, 0:1], 0.0), writes=[Brm])
        k.op("dve", lambda h: h.tensor_tensor_scan(gcT[:, :], rm[:, :], gT[:, :], 0.0, ALU.mult, ALU.add), reads=[Brm, BgT], writes=[BgcT])
        colB = k.sbuf("gd_colB", [128, NCHK, HC], F32, stack=st)
        colG = k.sbuf("gd_colG", [128, NCHK, HC], F32, stack=st)
        colBE = k.sbuf("gd_colBE", [128, NCHK, HC], F32, stack=st)
        colKD = k.sbuf("gd_colKD", [128, NCHK, HC], F32, stack=st)
        Bcol = Buf("gd_col")
        for n in range(NCHK):
            for (src, Bsrc, dst) in ((betaT, Bbeta, colB), (gcT, BgcT, colG)):
                bank, Bb = cx.banks[n % 4], cx.Bbank[n % 4]
                k.op("pe", lambda h: h.transpose(bank[0:c, 0:HC], src[:, n * c:(n + 1) * c], cx.ident[0:HC, 0:HC]), reads=[Bsrc, cx.Bident], writes=[Bb])
                k.op("dve", lambda h: h.tensor_copy(dst[0:c, n, :], bank[0:c, 0:HC]), reads=[Bb], writes=[Bcol])
        NH = NCHK * HC
        cg2 = colG[0:c, :, :].rearrange("p n h -> p (n h)")
        for (o0, on_) in tiles_of(NH, 512):
            bank, Bb = cx.banks[0], cx.Bbank[0]
            k.op("pe", lambda h: h.matmul(bank[0:c, 0:on_], lrow[0:c, 0:c], cg2[:, o0:o0 + on_], start=True, stop=True), reads=[Bgconst, Bcol], writes=[Bb])
            k.op("dve", lambda h: h.tensor_tensor(colKD[0:c, :, :].rearrange("p n h -> p (n h)")[:, o0:o0 + on_], bank[0:c, 0:on_], cg2[:, o0:o0 + on_], ALU.subtract),
                 reads=[Bb, Bcol], writes=[Bcol])
        k.op("act", lambda h: h.activation(colKD[0:c, :, :], colKD[0:c, :, :], AF.Exp), reads=[Bcol], writes=[Bcol])
        k.op("act", lambda h: h.activation(colBE[0:c, :, :], colG[0:c, :, :], AF.Exp), reads=[Bcol], writes=[Bcol])
        k.op("dve", lambda h: h.tensor_tensor(colBE[0:c, :, :], colBE[0:c, :, :], colB[0:c, :, :], ALU.mult), reads=[Bcol], writes=[Bcol])
        ext = [k.sbuf(f"gd_ext{i}", [128, 3 + T], F32, stack=st) for i in range(2)]
        Bext = [Buf(f"gd_ext{i}") for i in range(2)]
        rs = k.sbuf("gd_rs", [128, T], F32, stack=st)
        Brs = Buf("gd_rs")
        sqt = k.sbuf("gd_sq", [128, min(T, 512)], F32, stack=st)
        Bsq = Buf("gd_sq")
        T5 = tiles_of(T)

        class HS:
            pass
        hs = []
        for i in range(HG):
            o = HS()
            o.q = k.sbuf(f"gd_q{i}", [128, T], F32, stack=st); o.Bq = Buf(f"gd_q{i}")
            o.kk = k.sbuf(f"gd_k{i}", [128, T], F32, stack=st); o.Bk = Buf(f"gd_k{i}")
            o.v = k.sbuf(f"gd_v{i}", [128, T], F32, stack=st); o.Bv = Buf(f"gd_v{i}")
            o.sg = k.sbuf(f"gd_sg{i}", [128, T], F32, stack=st); o.Bsg = Buf(f"gd_sg{i}")
            o.yc = k.sbuf(f"gd_yc{i}", [128, T], BF16, stack=st); o.Byc = Buf(f"gd_yc{i}")
            for nm, shp, dt in (("gcb", [128, c], F32), ("bb", [128, c], F32), ("egcb", [128, c], F32), ("ET", [128, c], F32),
                                ("tmp", [128, c], F32), ("P", [128, c], F32), ("Bf", [128, c], F32), ("Af", [128, c], F32),
                                ("Q", [128, c], F32), ("W", [128, c], F32), ("Am", [128, c], F32), ("Pb", [128, c], F32),
                                ("Bb0", [128, c], F32), ("Bb1", [128, c], F32), ("Ab0", [128, c], F32), ("Ab1", [128, c], F32),
                                ("aqk", [128, c], F32), ("rhsu", [128, 128], F32), ("rhsw", [128, 128], F32), ("kdec", [128, 128], F32),
                                ("qd", [128, c], F32), ("wT", [128, c], F32), ("u", [128, 128], F32), ("vn", [128, 128], F32),
                                ("S", [128, 128], F32), ("Sb", [128, 128], F32), ("oT", [128, c], F32), ("t1", [128, c], F32)):
                setattr(o, nm, k.sbuf(f"gd_{nm}{i}", shp, dt, stack=st))
                setattr(o, "B_" + nm, Buf(f"gd_{nm}{i}"))
            hs.append(o)
        bk = [0]

        def nb():
            bk[0] += 1
            b = bk[0] % 8
            return cx.banks[b], cx.Bbank[b]

        for hg in range(HC // HG):
            heads = [hg * HG + i for i in range(HG)]
            for o, h in zip(hs, heads):
                for ci, (dst, Bdst) in enumerate(((o.q, o.Bq), (o.kk, o.Bk), (o.v, o.Bv))):
                    ch = ci * HC + h
                    e = ci % 2
                    k.dma("sp", ext[e][:, 3:3 + T], projT.ap()[base + ch * 128:base + (ch + 1) * 128, 0:T], writes=[Bext[e]])
                    k.op("dve", lambda hh: hh.tensor_copy(ext[e][:, 0:3], cst[:, :, ch]), reads=[Bcst], writes=[Bext[e]])
                    k.op("dve", lambda hh: hh.tensor_copy(tail[:, :, ch], ext[e][:, T:T + 3]), reads=[Bext[e]], writes=[Btail])
                    k.op("act", lambda hh: hh.activation(dst[:, :], ext[e][:, 0:T], AF.Identity, scale=cwc[:, 0, ch:ch + 1]), reads=[Bext[e], Bcwc], writes=[Bdst])
                    for i in range(1, 4):
                        k.op("dve", lambda hh: hh.scalar_tensor_tensor(dst[:, :], ext[e][:, i:i + T], cwc[:, i, ch:ch + 1], dst[:, :], ALU.mult, ALU.add),
                             reads=[Bext[e], Bcwc, Bdst], writes=[Bdst])
                    k.op("act", lambda hh: hh.activation(dst[:, :], dst[:, :], AF.Silu), reads=[Bdst], writes=[Bdst])
                head_norm(cx, o.q, o.Bq, T, T5, None, None, rs, Brs, sqt, Bsq, 6, 1.0)
                k.op("dve", lambda hh: hh.scalar_tensor_tensor(o.q[:, :], o.q[:, :], 128.0 ** -0.5, rs[:, :], ALU.mult, ALU.mult), reads=[o.Bq, Brs], writes=[o.Bq])
                head_norm(cx, o.kk, o.Bk, T, T5, None, None, rs, Brs, sqt, Bsq, 6, 1.0)
                k.op("dve", lambda hh: hh.tensor_tensor(o.kk[:, :], o.kk[:, :], rs[:, :], ALU.mult), reads=[o.Bk, Brs], writes=[o.Bk])
                k.dma("sp", o.sg[:, :], projT.ap()[base + 3 * DC + h * 128:base + 3 * DC + (h + 1) * 128, 0:T], writes=[o.Bsg])
                k.op("act", lambda hh: hh.activation(o.sg[:, :], o.sg[:, :], AF.Silu), reads=[o.Bsg], writes=[o.Bsg])
                if s0_ap is None:
                    k.op("dve", lambda hh: hh.memset(o.S[:, :], 0.0), writes=[o.B_S])
                else:
                    k.dma("sp", o.S[:, :], s0_ap[h, :, :], writes=[o.B_S])
                k.op("act", lambda hh: hh.activation(o.Sb[:, :], o.S[:, :], AF.Copy), reads=[o.B_S], writes=[o.B_Sb])
            for n in range(NCHK):
                t0 = n * c
                sl = slice(t0, t0 + c)
                for o, h in zip(hs, heads):
                    bank, Bb = nb()
                    k.op("pe", lambda hh: hh.matmul(bank[:, 0:c], sel[0:HC, h * 128:(h + 1) * 128], gcT[:, sl], start=True, stop=True), reads=[Bgconst, BgcT], writes=[Bb])
                    k.op("pe", lambda hh: hh.matmul(bank[:, 128:128 + c], sel[0:HC, h * 128:(h + 1) * 128], betaT[:, sl], start=True, stop=True), reads=[Bgconst, Bbeta], writes=[Bb])
                    k.op("dve", lambda hh: hh.tensor_copy(o.gcb[:, :], bank[:, 0:c]), reads=[Bb], writes=[o.B_gcb])
                    k.op("act", lambda hh: hh.activation(o.egcb[:, :], bank[:, 0:c], AF.Exp), reads=[Bb], writes=[o.B_egcb])
                    k.op("dve", lambda hh: hh.tensor_copy(o.bb[:, :], bank[:, 128:128 + c]), reads=[Bb], writes=[o.B_bb])
                for o, h in zip(hs, heads):
                    k.op("dve", lambda hh: hh.tensor_scalar(o.ET[0:c, :], o.gcb[0:c, :], colG[0:c, n, h:h + 1], 0.0, ALU.subtract, ALU.min),
                         reads=[o.B_gcb, Bcol], writes=[o.B_ET])
                    k.op("act", lambda hh: hh.activation(o.ET[0:c, :], o.ET[0:c, :], AF.Exp), reads=[o.B_ET], writes=[o.B_ET])
                    k.op("dve", lambda hh: hh.tensor_tensor(o.tmp[0:c, :], o.ET[0:c, :], mSU[0:c, 0:c], ALU.mult), reads=[o.B_ET, Bgconst], writes=[o.B_tmp])
                    k.op("dve", lambda hh: hh.tensor_tensor(o.tmp[0:c, :], o.tmp[0:c, :], o.bb[0:c, :], ALU.mult), reads=[o.B_tmp, o.B_bb], writes=[o.B_tmp])
                    k.op("dve", lambda hh: hh.tensor_tensor(o.ET[0:c, :], o.ET[0:c, :], mU[0:c, 0:c], ALU.mult), reads=[o.B_ET, Bgconst], writes=[o.B_ET])
                for o, h in zip(hs, heads):
                    bank, Bb = nb()
                    k.op("pe", lambda hh: hh.matmul(bank[0:c, 0:c], o.kk[:, sl], o.kk[:, sl], start=True, stop=True), reads=[o.Bk], writes=[Bb])
                    k.op("pe", lambda hh: hh.matmul(bank[0:c, 128:128 + c], o.kk[:, sl], o.q[:, sl], start=True, stop=True), reads=[o.Bk, o.Bq], writes=[Bb])
                    k.op("dve", lambda hh: hh.scalar_tensor_tensor(o.Bf[0:c, :], bank[0:c, 0:c], -1.0, o.tmp[0:c, :], ALU.mult, ALU.mult), reads=[Bb, o.B_tmp], writes=[o.B_Bf])
                    k.op("dve", lambda hh: hh.tensor_tensor(o.aqk[0:c, :], bank[0:c, 128:128 + c], o.ET[0:c, :], ALU.mult), reads=[Bb, o.B_ET], writes=[o.B_aqk])
                    k.op("dve", lambda hh: hh.tensor_tensor(o.Bb0[0:c, :], o.Bf[0:c, :], bd8[0:c, 0:c], ALU.mult), reads=[o.B_Bf, Bgconst], writes=[o.B_Bb0])
                    k.op("dve", lambda hh: hh.tensor_tensor(o.P[0:c, :], o.Bb0[0:c, :], cx.ident[0:c, 0:c], ALU.add), reads=[o.B_Bb0, cx.Bident], writes=[o.B_P])
                for o, h in zip(hs, heads):
                    bank, Bb = nb()
                    k.op("pe", lambda hh: hh.matmul(bank[0:c, 0:c], o.Bb0[0:c, :], cx.ident[0:c, 0:c], start=True, stop=True), reads=[o.B_Bb0, cx.Bident], writes=[Bb])
                    k.op("act", lambda hh: hh.activation(o.Ab0[0:c, :], bank[0:c, 0:c], AF.Copy), reads=[Bb], writes=[o.B_Ab0])
                for l in range(1, L + 1):
                    for o, h in zip(hs, heads):
                        Bc, Ac = (o.Bb0, o.Ab0) if l % 2 == 1 else (o.Bb1, o.Ab1)
                        Bn, An = (o.Bb1, o.Ab1) if l % 2 == 1 else (o.Bb0, o.Ab0)
                        BBc, BAc = (o.B_Bb0, o.B_Ab0) if l % 2 == 1 else (o.B_Bb1, o.B_Ab1)
                        BBn, BAn = (o.B_Bb1, o.B_Ab1) if l % 2 == 1 else (o.B_Bb0, o.B_Ab0)
                        bank, Bb = nb()
                        k.op("pe", lambda hh: hh.matmul(bank[0:c, 0:c], Ac[0:c, :], Bc[0:c, :], start=True, stop=True), reads=[BAc, BBc], writes=[Bb])
                        k.op("pe", lambda hh: hh.matmul(bank[0:c, 128:128 + c], Bc[0:c, :], Ac[0:c, :], start=True, stop=True), reads=[BAc, BBc], writes=[Bb])
                        k.op("act", lambda hh: hh.activation(Bn[0:c, :], bank[0:c, 0:c], AF.Copy), reads=[Bb], writes=[BBn])
                        k.op("act", lambda hh: hh.activation(An[0:c, :], bank[0:c, 128:128 + c], AF.Copy), reads=[Bb], writes=[BAn])
                        k.op("dve", lambda hh: hh.tensor_copy(o.Pb[0:c, :], o.P[0:c, :]), reads=[o.B_P], writes=[o.B_Pb])
                    for o, h in zip(hs, heads):
                        An = o.Ab1 if l % 2 == 1 else o.Ab0
                        BAn = o.B_Ab1 if l % 2 == 1 else o.B_Ab0
                        bank, Bb = nb()
                        k.op("pe", lambda hh: hh.matmul(bank[0:c, 0:c], An[0:c, :], o.Pb[0:c, :], start=True, stop=True), reads=[BAn, o.B_Pb], writes=[Bb])
                        k.op("dve", lambda hh: hh.tensor_tensor(o.P[0:c, :], o.P[0:c, :], bank[0:c, 0:c], ALU.add), reads=[o.B_P, Bb], writes=[o.B_P])
                if c > 8:
                    for o, h in zip(hs, heads):
                        bank, Bb = nb()
                        k.op("pe", lambda hh: hh.transpose(bank[0:c, 0:c], o.Bf[0:c, :], cx.ident[0:c, 0:c]), reads=[o.B_Bf, cx.Bident], writes=[Bb])
                        k.op("act", lambda hh: hh.activation(o.Af[0:c, :], bank[0:c, 0:c], AF.Copy), reads=[Bb], writes=[o.B_Af])
                    bsz = 8
                    li = 0
                    while bsz < c:
                        for o, h in zip(hs, heads):
                            k.op("dve", lambda hh: hh.tensor_tensor(o.Am[0:c, :], o.Af[0:c, :], LLm[li][0:c, 0:c], ALU.mult), reads=[o.B_Af, Bgconst], writes=[o.B_Am])
                            bank, Bb = nb()
                            k.op("pe", lambda hh: hh.transpose(bank[0:c, 0:c], o.P[0:c, :], cx.ident[0:c, 0:c]), reads=[o.B_P, cx.Bident], writes=[Bb])
                            k.op("pe", lambda hh: hh.matmul(bank[0:c, 128:128 + c], o.Am[0:c, :], o.P[0:c, :], start=True, stop=True), reads=[o.B_Am, o.B_P], writes=[Bb])
                            k.op("act", lambda hh: hh.activation(o.Q[0:c, :], bank[0:c, 0:c], AF.Copy), reads=[Bb], writes=[o.B_Q])
                            k.op("act", lambda hh: hh.activation(o.W[0:c, :], bank[0:c, 128:128 + c], AF.Copy), reads=[Bb], writes=[o.B_W])
                        for o, h in zip(hs, heads):
                            bank, Bb = nb()
                            k.op("pe", lambda hh: hh.matmul(bank[0:c, 0:c], o.Q[0:c, :], o.W[0:c, :], start=True, stop=True), reads=[o.B_Q, o.B_W], writes=[Bb])
                            k.op("dve", lambda hh: hh.tensor_tensor(o.P[0:c, :], o.P[0:c, :], bank[0:c, 0:c], ALU.add), reads=[o.B_P, Bb], writes=[o.B_P])
                        bsz *= 2
                        li += 1
                for o, h in zip(hs, heads):
                    k.op("act", lambda hh: hh.activation(o.Pb[0:c, :], o.P[0:c, :], AF.Copy), reads=[o.B_P], writes=[o.B_Pb])
                    bank, Bb = nb()
                    k.op("pe", lambda hh: hh.transpose(bank[0:c, 0:128], o.kk[:, sl], cx.ident[:, :]), reads=[o.Bk, cx.Bident], writes=[Bb])
                    k.op("pe", lambda hh: hh.transpose(bank[0:c, 128:256], o.v[:, sl], cx.ident[:, :]), reads=[o.Bv, cx.Bident], writes=[Bb])
                    k.op("dve", lambda hh: hh.tensor_scalar(o.rhsw[0:c, :], bank[0:c, 0:128], colBE[0:c, n, h:h + 1], None, ALU.mult), reads=[Bb, Bcol], writes=[o.B_rhsw])
                    k.op("dve", lambda hh: hh.tensor_scalar(o.kdec[0:c, :], bank[0:c, 0:128], colKD[0:c, n, h:h + 1], None, ALU.mult), reads=[Bb, Bcol], writes=[o.B_kdec])
                    k.op("dve", lambda hh: hh.tensor_scalar(o.rhsu[0:c, :], bank[0:c, 128:256], colB[0:c, n, h:h + 1], None, ALU.mult), reads=[Bb, Bcol], writes=[o.B_rhsu])
                    k.op("dve", lambda hh: hh.tensor_tensor(o.qd[:, :], o.q[:, sl], o.egcb[:, :], ALU.mult), reads=[o.Bq, o.B_egcb], writes=[o.B_qd])
                for o, h in zip(hs, heads):
                    bank, Bb = nb()
                    k.op("pe", lambda hh: hh.matmul(bank[0:c, 0:128], o.Pb[0:c, :], o.rhsu[0:c, :], start=True, stop=True), reads=[o.B_Pb, o.B_rhsu], writes=[Bb])
                    k.op("pe", lambda hh: hh.matmul(bank[:, 128:128 + c], o.rhsw[0:c, :], o.Pb[0:c, :], start=True, stop=True), reads=[o.B_Pb, o.B_rhsw], writes=[Bb])
                    k.op("act", lambda hh: hh.activation(o.u[0:c, :], bank[0:c, 0:128], AF.Copy), reads=[Bb], writes=[o.B_u])
                    k.op("act", lambda hh: hh.activation(o.wT[:, :], bank[:, 128:128 + c], AF.Copy), reads=[Bb], writes=[o.B_wT])
                for o, h in zip(hs, heads):
                    bank, Bb = nb()
                    k.op("pe", lambda hh: hh.matmul(bank[0:c, 0:128], o.wT[:, :], o.Sb[:, :], start=True, stop=True), reads=[o.B_wT, o.B_Sb], writes=[Bb])
                    k.op("dve", lambda hh: hh.tensor_tensor(o.vn[0:c, :], o.u[0:c, :], bank[0:c, 0:128], ALU.subtract), reads=[o.B_u, Bb], writes=[o.B_vn])
                for o, h in zip(hs, heads):
                    bank, Bb = nb()
                    k.op("pe", lambda hh: hh.matmul(bank[:, 0:c], o.Sb[:, :], o.qd[:, :], start=True, stop=False), reads=[o.B_Sb, o.B_qd], writes=[Bb])
                    k.op("pe", lambda hh: hh.matmul(bank[:, 0:c], o.vn[0:c, :], o.aqk[0:c, :], start=False, stop=True), reads=[o.B_vn, o.B_aqk], writes=[Bb])
                    k.op("act", lambda hh: hh.activation(o.oT[:, :], bank[:, 0:c], AF.Copy), reads=[Bb], writes=[o.B_oT])
                    bank2, Bb2 = nb()
                    k.op("pe", lambda hh: hh.matmul(bank2[:, 0:128], o.kdec[0:c, :], o.vn[0:c, :], start=True, stop=True), reads=[o.B_kdec, o.B_vn], writes=[Bb2])
                    k.op("dve", lambda hh: hh.scalar_tensor_tensor(o.S[:, :], o.S[:, :], o.egcb[:, c - 1:c], bank2[:, 0:128], ALU.mult, ALU.add),
                         reads=[o.B_S, o.B_egcb, Bb2], writes=[o.B_S])
                    k.op("act", lambda hh: hh.activation(o.Sb[:, :], o.S[:, :], AF.Copy), reads=[o.B_S], writes=[o.B_Sb])
                for o, h in zip(hs, heads):
                    bank, Bb = nb()
                    k.op("act", lambda hh: hh.activation(o.t1[:, :], o.oT[:, :], AF.Square), reads=[o.B_oT], writes=[o.B_t1])
                    k.op("pe", lambda hh: hh.matmul(bank[:, 0:c], cx.ones32[:, :], o.t1[:, :], start=True, stop=True), reads=[o.B_t1, cx.Bconst], writes=[Bb])
                    k.op("act", lambda hh: hh.activation(o.t1[:, :], bank[:, 0:c], AF.Sqrt, bias=EPS, scale=1.0 / 128), reads=[Bb], writes=[o.B_t1])
                    k.op("dve", lambda hh: hh.reciprocal(o.t1[:, :], o.t1[:, :]), reads=[o.B_t1], writes=[o.B_t1])
                    k.op("dve", lambda hh: hh.scalar_tensor_tensor(o.t1[:, :], o.oT[:, :], onc[:, 0:1], o.t1[:, :], ALU.mult, ALU.mult), reads=[o.B_oT, Bonc, o.B_t1], writes=[o.B_t1])
                    k.op("dve", lambda hh: hh.tensor_tensor(o.yc[:, sl], o.t1[:, :], o.sg[:, sl], ALU.mult), reads=[o.B_t1, o.Bsg], writes=[o.Byc])
            for o, h in zip(hs, heads):
                k.dma("sp", mixT.ap()[DA + DB + h * 128:DA + DB + (h + 1) * 128, 0:T], o.yc[:, :], reads=[o.Byc])
                k.dma("sp", s_out_ap[h, :, :], o.S[:, :], reads=[o.B_S])
        so = k.sbuf("gd_so", [128, 128], F32, stack=st)
        Bso = Buf("gd_so")
        for r in range(3):
            n3 = 3 * HC
            bank, Bb = cx.banks[7], cx.Bbank[7]
            k.op("pe", lambda hh: hh.transpose(bank[0:n3, 0:128], tail[:, r, :], cx.ident[:, :]), reads=[Btail, cx.Bident], writes=[Bb])
            k.op("dve", lambda hh: hh.tensor_copy(so[0:n3, :], bank[0:n3, 0:128]), reads=[Bb], writes=[Bso])
            k.dma("sp", cstate_out_ap[r, :].rearrange("(c p) -> c p", p=128), so[0:n3, :], reads=[Bso])
    k.barrier()

from contextlib import ExitStack

CFG_FULL = dict(D=4096, S=2048, T=8, P=2048, HA=12, DB=1024, HC=12, DFF=11008, DEPTH=4, TT=512, HG=2)

WEIGHTS = ("rel_bias", "norm_mix", "w_in", "a_q_norm", "a_k_norm", "a_out_norm", "pool_w", "pool_scale", "gdn_conv_w",
           "gdn_a_log", "gdn_dt_bias", "gdn_out_norm", "w_out", "norm_ffn", "ffn_up", "ffn_conv_w", "ffn_conv_b", "ffn_down")


def dims(cfg):
    D, HA, DB, HC, DFF = cfg["D"], cfg["HA"], cfg["DB"], cfg["HC"], cfg["DFF"]
    DA, DC = HA * 128, HC * 128
    NIN = 3 * DA + DB + 4 * DC + 2 * HC
    DMIX = DA + DB + DC
    return DA, DC, NIN, DMIX


def weight_shapes(cfg):
    D, HA, DB, HC, DFF, L = cfg["D"], cfg["HA"], cfg["DB"], cfg["HC"], cfg["DFF"], cfg["DEPTH"]
    DA, DC, NIN, DMIX = dims(cfg)
    return dict(rel_bias=[32, HA], norm_mix=[L, D], w_in=[L, D, NIN], a_q_norm=[L, 128], a_k_norm=[L, 128], a_out_norm=[L, DA],
                pool_w=[L, 4, 256, 256], pool_scale=[L, DB], gdn_conv_w=[L, 4, 3 * DC], gdn_a_log=[L, HC], gdn_dt_bias=[L, HC],
                gdn_out_norm=[L, 128], w_out=[L, DMIX, D], norm_ffn=[L, D], ffn_up=[L, D, 2 * DFF], ffn_conv_w=[L, 3, 2 * DFF],
                ffn_conv_b=[L, 2 * DFF], ffn_down=[L, DFF, D])


def io_shapes(cfg):
    D, S, T, P, HA, DB, HC, DFF, L = (cfg[x] for x in ("D", "S", "T", "P", "HA", "DB", "HC", "DFF", "DEPTH"))
    DA, DC, NIN, DMIX = dims(cfg)
    ins = dict(x_p=[S, D], x_s=[T, D], cache_kv=[L, P, 2, HA, 128], st_pool=[L, 15, DB], st_gconv=[L, 3, 3 * DC],
               st_gdn=[L, HC, 128, 128], st_fconv=[L, 2, 2 * DFF])
    outs = dict(y_p=[S, D], y_s=[T, D], p_kv=[L, S, 2, HA, 128], s_kv=[L, T, 2, HA, 128], p_pool=[L, 15, DB], s_pool=[L, 15, DB],
                p_gconv=[L, 3, 3 * DC], s_gconv=[L, 3, 3 * DC], p_gdn=[L, HC, 128, 128], s_gdn=[L, HC, 128, 128],
                p_fconv=[L, 2, 2 * DFF], s_fconv=[L, 2, 2 * DFF])
    return ins, outs


def host_consts(cfg):
    ohp, ohs = attn_onehots()
    return dict(consts=np.eye(128, dtype=np.float32), gconst=gdn_consts(cfg["HC"]), ohp=ohp, ohs=ohs)


def build(cfg):
    D, S, T, P, HA, DB, HC, DFF, L, TT, HG = (cfg[x] for x in ("D", "S", "T", "P", "HA", "DB", "HC", "DFF", "DEPTH", "TT", "HG"))
    DA, DC, NIN, DMIX = dims(cfg)
    nc = bass.Bass("TRN2", target_bir_lowering=False)
    ins, outs = io_shapes(cfg)
    I = {n: nc.dram_tensor(n, s, F32, kind="ExternalInput") for n, s in ins.items()}
    W = {n: nc.dram_tensor(n, s, F32, kind="ExternalInput") for n, s in weight_shapes(cfg).items()}
    hc = host_consts(cfg)
    C = {n: nc.dram_tensor(n, list(a.shape), F32, kind="ExternalInput") for n, a in hc.items()}
    O = {n: nc.dram_tensor(n, s, F32, kind="ExternalOutput") for n, s in outs.items()}
    G = {}
    for g, Tg in (("p", S), ("s", T)):
        G[g] = dict(T=Tg, xT=nc.dram_tensor(f"xT_{g}", [D, Tg], F32, kind="Internal"),
                    hT=nc.dram_tensor(f"hT_{g}", [D, Tg], F32, kind="Internal"),
                    projT=nc.dram_tensor(f"projT_{g}", [NIN, Tg], F32, kind="Internal"),
                    mixT=nc.dram_tensor(f"mixT_{g}", [DMIX, Tg], BF16, kind="Internal"))
    Hp = nc.dram_tensor("Hp", [HA, 3, 128, TOE_ML], F32, kind="Internal")
    Hs = nc.dram_tensor("Hs", [HA, 128, SMP_ML], F32, kind="Internal")
    with ExitStack() as st:
        k = KB(nc, st)
        cx = Ctx(k, cfg, C["consts"])
        gconst = k.sbuf("gconst", [128, 512 + HC * 128 + 5 * 128], F32)
        Bgconst = Buf("gconst")
        k.dma("sp", gconst[:, :], C["gconst"].ap(), writes=[Bgconst])
        setup_attn_tables(cx, W["rel_bias"], C["ohp"], C["ohs"], Hp, Hs, HA)
        phase_transpose_in(cx, I["x_p"], G["p"]["xT"], S, D)
        phase_transpose_in(cx, I["x_s"], G["s"]["xT"], T, D)
        for l in range(L):
            for g in ("p", "s"):
                gg = G[g]
                Tg = gg["T"]
                tt = min(TT, Tg)
                phase_inproj(cx, gg["xT"], gg["projT"], Tg, tt, D, NIN, W["w_in"].ap()[l], W["norm_mix"].ap()[l])
                if g == "p":
                    phase_attn_prompt(cx, gg["projT"], gg["mixT"], O["p_kv"].ap()[l], S, HA, Hp,
                                      W["a_q_norm"].ap()[l], W["a_k_norm"].ap()[l], W["a_out_norm"].ap()[l])
                    phase_pool(cx, gg["projT"], gg["mixT"], S, DA, DB, W["pool_w"].ap()[l], W["pool_scale"].ap()[l], None, O["p_pool"].ap()[l], 0)
                    phase_gdn(cx, gg["projT"], gg["mixT"], S, 128, DA, DB, HC, gconst, Bgconst, W["gdn_conv_w"].ap()[l], W["gdn_a_log"].ap()[l],
                              W["gdn_dt_bias"].ap()[l], W["gdn_out_norm"].ap()[l], None, O["p_gconv"].ap()[l], None, O["p_gdn"].ap()[l], HG)
                else:
                    phase_attn_sample(cx, gg["projT"], gg["mixT"], O["s_kv"].ap()[l], I["cache_kv"].ap()[l], T, P, HA, Hs,
                                      W["a_q_norm"].ap()[l], W["a_k_norm"].ap()[l], W["a_out_norm"].ap()[l])
                    phase_pool(cx, gg["projT"], gg["mixT"], T, DA, DB, W["pool_w"].ap()[l], W["pool_scale"].ap()[l], I["st_pool"].ap()[l], O["s_pool"].ap()[l], 15)
                    phase_gdn(cx, gg["projT"], gg["mixT"], T, T, DA, DB, HC, gconst, Bgconst, W["gdn_conv_w"].ap()[l], W["gdn_a_log"].ap()[l],
                              W["gdn_dt_bias"].ap()[l], W["gdn_out_norm"].ap()[l], I["st_gconv"].ap()[l], O["s_gconv"].ap()[l],
                              I["st_gdn"].ap()[l], O["s_gdn"].ap()[l], HG)
                phase_outproj(cx, gg["xT"], gg["mixT"], gg["hT"], Tg, tt, D, DMIX, W["w_out"].ap()[l])
                phase_ffn(cx, gg["hT"], gg["xT"], Tg, tt, D, DFF, W["norm_ffn"].ap()[l], W["ffn_up"].ap()[l], W["ffn_conv_w"].ap()[l],
                          W["ffn_conv_b"].ap()[l], W["ffn_down"].ap()[l], (I["st_fconv"].ap()[l] if g == "s" else None),
                          O["p_fconv" if g == "p" else "s_fconv"].ap()[l])
        phase_transpose_out(cx, G["p"]["xT"], O["y_p"], S, D)
        phase_transpose_out(cx, G["s"]["xT"], O["y_s"], T, D)
        k.finish()
        build.stats = (k.n_inst, k.n_sem)
    return nc


_NC_CACHE = {}


def kernel(x_prompt, x_sample, cache_attn_kv, state_pool, state_gdn_conv, state_gdn, state_ffn_conv,
           rel_bias, norm_mix, w_in, a_q_norm, a_k_norm, a_out_norm, pool_w, pool_scale,
           gdn_conv_w, gdn_a_log, gdn_dt_bias, gdn_out_norm, w_out, norm_ffn,
           ffn_up, ffn_conv_w, ffn_conv_b, ffn_down):
    cfg = CFG_FULL
    n = 8
    if "nc" not in _NC_CACHE:
        _NC_CACHE["nc"] = build(cfg)
    nc = _NC_CACHE["nc"]
    f = lambda a: np.ascontiguousarray(np.asarray(a), dtype=np.float32)
    wts = dict(rel_bias=rel_bias, norm_mix=norm_mix, w_in=w_in, a_q_norm=a_q_norm, a_k_norm=a_k_norm, a_out_norm=a_out_norm,
               pool_w=pool_w, pool_scale=pool_scale, gdn_conv_w=gdn_conv_w, gdn_a_log=gdn_a_log, gdn_dt_bias=gdn_dt_bias,
               gdn_out_norm=gdn_out_norm, w_out=w_out, norm_ffn=norm_ffn, ffn_up=ffn_up, ffn_conv_w=ffn_conv_w,
               ffn_conv_b=ffn_conv_b, ffn_down=ffn_down)
    wts = {k_: f(v) for k_, v in wts.items()}
    hc = host_consts(cfg)
    x_prompt, x_sample = np.asarray(x_prompt), np.asarray(x_sample)
    cache_attn_kv, state_pool, state_gdn_conv = np.asarray(cache_attn_kv), np.asarray(state_pool), np.asarray(state_gdn_conv)
    state_gdn, state_ffn_conv = np.asarray(state_gdn), np.asarray(state_ffn_conv)
    in_maps = []
    for c in range(n):
        m = dict(x_p=f(x_prompt[c % 4]), x_s=f(x_sample[c]), cache_kv=f(cache_attn_kv[:, c]), st_pool=f(state_pool[:, c]),
                 st_gconv=f(state_gdn_conv[:, c]), st_gdn=f(state_gdn[:, c]), st_fconv=f(state_ffn_conv[:, c]))
        m.update(wts)
        m.update(hc)
        in_maps.append(m)
    res = run_bass_kernel_spmd(nc, in_maps, core_ids=list(range(n))).results
    P = lambda name: np.stack([res[c][name] for c in range(4)], axis=0)
    Sg = lambda name: np.stack([res[c][name] for c in range(8)], axis=0)
    mv = lambda a: np.ascontiguousarray(np.moveaxis(a, 0, 1))
    return (P("y_p"), Sg("y_s"), mv(P("p_kv")), mv(Sg("s_kv")), mv(P("p_pool")), mv(Sg("s_pool")),
            mv(P("p_gconv")), mv(Sg("s_gconv")), mv(P("p_gdn")), mv(Sg("s_gdn")), mv(P("p_fconv")), mv(Sg("s_fconv")))
```

```python
import math
from concourse.bass_utils import run_bass_kernel_spmd
import numpy as np
import concourse.bass as bass
import concourse.mybir as mybir

F32 = mybir.dt.float32
BF16 = mybir.dt.bfloat16
I32 = mybir.dt.int32
ALU = mybir.AluOpType
AF = mybir.ActivationFunctionType
AX = mybir.AxisListType

EPOCH = 30000
NSLOT = 20


class Buf:
    __slots__ = ("name", "w", "r", "psum")

    def __init__(self, name, psum=False):
        self.name = name
        self.psum = psum
        self.w = None
        self.r = {}


class Eng:
    def __init__(self, k, name, h, is_compute=True):
        self.k = k
        self.name = name
        self.h = h
        self.sems = []
        self.count = 0
        self.waited = {}
        self.is_compute = is_compute
        self.slots = []
        self.slot_val = []
        self.ndma = 0


class KB:
    def __init__(self, nc, stack):
        self.nc = nc
        self.stack = stack
        self.E = {}
        for name, h in (("pe", nc.tensor), ("act", nc.scalar), ("dve", nc.vector),
                        ("pool", nc.gpsimd), ("sp", nc.sync)):
            self.E[name] = Eng(self, name, h)
        self.n_sem = 0
        self.n_inst = 0
        for e in self.E.values():
            if e.name in ("sp", "act", "pool"):
                for i in range(NSLOT):
                    e.slots.append(self._sem(f"d_{e.name}_{i}"))
                    e.slot_val.append(0)

    def _sem(self, name):
        self.n_sem += 1
        return self.stack.enter_context(self.nc.semaphore(name))

    def sbuf(self, name, shape, dtype, stack=None):
        self.n_alloc = getattr(self, "n_alloc", 0) + 1
        return (stack or self.stack).enter_context(self.nc.sbuf_tensor(f"{name}_{self.n_alloc}", list(shape), dtype))

    def psum(self, name, shape, dtype, stack=None):
        self.n_alloc = getattr(self, "n_alloc", 0) + 1
        return (stack or self.stack).enter_context(self.nc.psum_tensor(f"{name}_{self.n_alloc}", list(shape), dtype))

    def dram(self, name, shape, dtype, kind="Internal"):
        return self.nc.dram_tensor(name, list(shape), dtype, kind=kind)

    def _eng_sem(self, e, seq):
        idx = seq // EPOCH
        while len(e.sems) <= idx:
            e.sems.append(self._sem(f"c_{e.name}_{len(e.sems)}"))
        return e.sems[idx], seq % EPOCH + 1

    def _wait_tok(self, x, tok):
        if tok is None:
            return
        if tok[0] == "E":
            _, en, seq = tok
            key = ("E", en)
            if x.waited.get(key, -1) >= seq:
                return
            e = self.E[en]
            sem, val = self._eng_sem(e, seq)
            x.h.wait_ge(sem, val)
            x.waited[key] = seq
        else:
            _, qn, slot, val = tok
            key = ("D", qn, slot)
            if x.waited.get(key, 0) >= val:
                return
            q = self.E[qn]
            x.h.wait_ge(q.slots[slot], val)
            x.waited[key] = val

    def _deps(self, x, reads, writes, same_eng_waw=True):
        toks = []
        for b in reads:
            if b.w is not None:
                toks.append(b.w)
            if b.psum:
                for t in b.r.values():
                    if not (t[0] == "E" and t[1] == x.name):
                        toks.append(t)
        for b in writes:
            if b.w is not None:
                if b.w[0] == "E" and b.w[1] == x.name and not same_eng_waw:
                    pass
                else:
                    toks.append(b.w)
            for t in b.r.values():
                if t[0] == "E" and t[1] == x.name:
                    continue
                toks.append(t)
        for t in toks:
            self._wait_tok(x, t)

    def _record(self, tok, reads, writes):
        for b in reads:
            if tok[0] == "E":
                b.r[("E", tok[1])] = tok
            else:
                b.r[("D", tok[1], tok[2])] = tok
        for b in writes:
            b.w = tok
            b.r = {}

    def op(self, eng, fn, reads=(), writes=()):
        x = self.E[eng]
        self._deps(x, reads, writes, same_eng_waw=(eng != "pe"))
        inst = fn(x.h)
        seq = x.count
        x.count += 1
        sem, val = self._eng_sem(x, seq)
        inst.then_inc(sem, 1)
        self._record(("E", eng, seq), reads, writes)
        self.n_inst += 1
        return inst

    def dma(self, q, out, in_, reads=(), writes=(), **kw):
        x = self.E[q]
        self._deps(x, reads, writes)
        slot = x.ndma % NSLOT
        x.ndma += 1
        if x.slot_val[slot] > 0:
            self._wait_tok(x, ("D", q, slot, x.slot_val[slot]))
        x.slot_val[slot] += 16
        inst = x.h.dma_start(out=out, in_=in_, **kw)
        inst.then_inc(x.slots[slot], 16)
        self._record(("D", q, slot, x.slot_val[slot]), reads, writes)
        self.n_inst += 1
        return inst

    def barrier(self):
        sp = self.E["sp"]
        for e in self.E.values():
            if e.slots:
                for s in range(NSLOT):
                    if e.slot_val[s] > 0:
                        self._wait_tok(sp, ("D", e.name, s, e.slot_val[s]))
        for e in self.E.values():
            if e.name != "sp" and e.count > 0:
                self._wait_tok(sp, ("E", e.name, e.count - 1))
        seq = sp.count
        sp.count += 1
        sem, val = self._eng_sem(sp, seq)
        sp.h.nop().then_inc(sem, 1)
        for e in self.E.values():
            if e.name != "sp":
                self._wait_tok(e, ("E", "sp", seq))
                for o in self.E.values():
                    if o.count > 0 and o.name != "sp":
                        e.waited[("E", o.name)] = max(e.waited.get(("E", o.name), -1),
                                                      o.count - 1 if o.name != e.name else -1)
                    for s in range(len(o.slots)):
                        e.waited[("D", o.name, s)] = o.slot_val[s]

    def finish(self):
        self.barrier()

from contextlib import ExitStack

EPS = 1e-6


class Ctx:
    def __init__(self, k, cfg, consts_dram):
        self.k = k
        self.cfg = cfg
        nc = k.nc
        self.ident = k.sbuf("ident", [128, 128], F32)
        self.Bident = Buf("ident")
        self.ones32 = k.sbuf("ones32", [128, 128], F32)
        self.onesb = k.sbuf("onesb", [128, 128], BF16)
        self.identb = k.sbuf("identb", [128, 128], BF16)
        self.Bconst = Buf("const")
        k.dma("sp", self.ident[:], consts_dram.ap()[:, 0:128], writes=[self.Bident])
        k.op("dve", lambda h: h.memset(self.ones32[:], 1.0), writes=[self.Bconst])
        k.op("dve", lambda h: h.memset(self.onesb[:], 1.0), writes=[self.Bconst])
        k.op("dve", lambda h: h.tensor_copy(self.identb[:], self.ident[:]), reads=[self.Bident], writes=[self.Bconst])
        self.banks = []
        self.Bbank = []
        for i in range(8):
            self.banks.append(k.psum(f"bank{i}", [128, 512], F32))
            self.Bbank.append(Buf(f"bank{i}", psum=True))
        self.rr = 0
        self.lc_tmp = k.sbuf("lc_tmp", [128, 128], F32)
        self.Blc_tmp = Buf("lc_tmp")

    def evac_eng(self):
        self.rr += 1
        return "act" if self.rr % 2 else "dve"


def copy_on(k, eng, out, in_, reads, writes):
    if eng == "act":
        return k.op("act", lambda h: h.activation(out, in_, AF.Copy), reads=reads, writes=writes)
    return k.op(eng, lambda h: h.tensor_copy(out, in_), reads=reads, writes=writes)


def load_cols(cx, st, vec_ap_rows, R, dst, Bdst, dst_cols=None):
    k = cx.k
    tmp, Bt = cx.lc_tmp, cx.Blc_tmp
    k.dma("sp", tmp[0:R, :], vec_ap_rows, writes=[Bt])
    bank, Bb = cx.banks[7], cx.Bbank[7]
    k.op("pe", lambda h: h.transpose(bank[:, 0:R], tmp[0:R, :], cx.ident[0:R, 0:R]), reads=[Bt, cx.Bident], writes=[Bb])
    d = dst if dst_cols is None else dst_cols
    k.op("dve", lambda h: h.tensor_copy(d, bank[:, 0:R]), reads=[Bb], writes=[Bdst])


def load_vec_cols(cx, st, vec_dram_ap_1d, n, dst, Bdst, col0=0):
    R = n // 128
    rows = vec_dram_ap_1d.rearrange("(r p) -> r p", p=128)
    r0 = 0
    while r0 < R:
        rr = min(128, R - r0)
        load_cols(cx, st, rows[r0:r0 + rr, :], rr, dst, Bdst, dst_cols=dst[:, col0 + r0:col0 + r0 + rr])
        r0 += rr


def phase_transpose_in(cx, x_dram, xT_dram, Tg, D):
    k = cx.k
    DC = D // 128
    with ExitStack() as st:
        xr = [k.sbuf(f"ti_x{i}", [128, D], F32, stack=st) for i in range(2)]
        Bxr = [Buf(f"ti_x{i}") for i in range(2)]
        stg = [k.sbuf(f"ti_s{i}", [128, 4, 128], F32, stack=st) for i in range(3)]
        Bstg = [Buf(f"ti_s{i}") for i in range(3)]
        xTv = xT_dram.ap().rearrange("(c p) t -> p c t", p=128)
        nb = (Tg + 127) // 128
        si = 0
        for tb in range(nb):
            nt = min(128, Tg - tb * 128)
            s = tb % 2
            k.dma("sp", xr[s][0:nt, :], x_dram.ap()[tb * 128:tb * 128 + nt, :], writes=[Bxr[s]])
            for c4 in range(DC // 4):
                b = (tb * (DC // 4) + c4) % 8
                bank, Bb = cx.banks[b], cx.Bbank[b]
                for j in range(4):
                    c = c4 * 4 + j
                    k.op("pe", lambda h: h.transpose(bank[:, j * 128:j * 128 + nt], xr[s][0:nt, c * 128:(c + 1) * 128],
                                                     cx.ident[0:nt, 0:nt]), reads=[Bxr[s], cx.Bident], writes=[Bb])
                g = si % 3
                si += 1
                copy_on(k, cx.evac_eng(), stg[g][:, :, 0:nt], bank[:, :].rearrange("p (j t) -> p j t", j=4)[:, :, 0:nt],
                        [Bb], [Bstg[g]])
                k.dma("sp", xTv[:, c4 * 4:(c4 + 1) * 4, tb * 128:tb * 128 + nt], stg[g][:, :, 0:nt], reads=[Bstg[g]])
    k.barrier()


def phase_transpose_out(cx, xT_dram, y_dram, Tg, D):
    k = cx.k
    DC = D // 128
    with ExitStack() as st:
        xin = [k.sbuf(f"to_x{i}", [128, 4, 128], F32, stack=st) for i in range(3)]
        Bxin = [Buf(f"to_x{i}") for i in range(3)]
        stg = [k.sbuf(f"to_s{i}", [128, 512], F32, stack=st) for i in range(3)]
        Bstg = [Buf(f"to_s{i}") for i in range(3)]
        xTv = xT_dram.ap().rearrange("(c p) t -> p c t", p=128)
        nb = (Tg + 127) // 128
        it = 0
        for tb in range(nb):
            nt = min(128, Tg - tb * 128)
            for c4 in range(DC // 4):
                g = it % 3
                b = it % 8
                it += 1
                bank, Bb = cx.banks[b], cx.Bbank[b]
                k.dma("sp", xin[g][:, :, 0:nt], xTv[:, c4 * 4:(c4 + 1) * 4, tb * 128:tb * 128 + nt], writes=[Bxin[g]])
                for j in range(4):
                    k.op("pe", lambda h: h.transpose(bank[0:nt, j * 128:(j + 1) * 128], xin[g][:, j, 0:nt], cx.ident[:, :]),
                         reads=[Bxin[g], cx.Bident], writes=[Bb])
                copy_on(k, cx.evac_eng(), stg[g][0:nt, :], bank[0:nt, :], [Bb], [Bstg[g]])
                k.dma("sp", y_dram.ap()[tb * 128:tb * 128 + nt, c4 * 512:(c4 + 1) * 512], stg[g][0:nt, :], reads=[Bstg[g]])
    k.barrier()


class NormScratch:
    def __init__(self, cx, st, TT, tag):
        k = cx.k
        self.xs = [k.sbuf(f"{tag}_xs{i}", [128, 2, TT], F32, stack=st) for i in range(2)]
        self.Bxs = [Buf(f"{tag}_xs{i}") for i in range(2)]
        self.sq = [k.sbuf(f"{tag}_sq{i}", [128, TT], F32, stack=st) for i in range(2)]
        self.Bsq = [Buf(f"{tag}_sq{i}") for i in range(2)]
        self.rstd = k.sbuf(f"{tag}_rstd", [128, TT], F32, stack=st)
        self.Brstd = Buf(f"{tag}_rstd")


def norm_tile(cx, ns, xT_dram, t0, TT, D, gain_cols, Bgain, out_bf, Bout, tag):
    k = cx.k
    DC = D // 128
    G = 2
    xs, Bxs, sq, Bsq, rstd, Brstd = ns.xs, ns.Bxs, ns.sq, ns.Bsq, ns.rstd, ns.Brstd
    xTv = xT_dram.ap().rearrange("(c p) t -> p c t", p=128)
    bank, Bb = cx.banks[6], cx.Bbank[6]
    it = 0
    for c4 in range(DC // G):
        g = it % 2
        it += 1
        k.dma("sp", xs[g][:, :, :], xTv[:, c4 * G:(c4 + 1) * G, t0:t0 + TT], writes=[Bxs[g]])
        for j in range(G):
            c = c4 * G + j
            q = c % 2
            k.op("act", lambda h: h.activation(sq[q][:, :], xs[g][:, j, :], AF.Square), reads=[Bxs[g]], writes=[Bsq[q]])
            k.op("pe", lambda h: h.matmul(bank[:, 0:TT], cx.ones32[:, :], sq[q][:, :], start=(c == 0), stop=(c == DC - 1)),
                 reads=[Bsq[q], cx.Bconst], writes=[Bb])
    k.op("act", lambda h: h.activation(rstd[:, :], bank[:, 0:TT], AF.Sqrt, bias=EPS, scale=1.0 / D), reads=[Bb], writes=[Brstd])
    k.op("dve", lambda h: h.reciprocal(rstd[:, :], rstd[:, :]), reads=[Brstd], writes=[Brstd])
    for c4 in range(DC // G):
        g = it % 2
        it += 1
        k.dma("sp", xs[g][:, :, :], xTv[:, c4 * G:(c4 + 1) * G, t0:t0 + TT], writes=[Bxs[g]])
        for j in range(G):
            c = c4 * G + j
            k.op("dve", lambda h: h.scalar_tensor_tensor(out_bf[:, c, 0:TT], xs[g][:, j, :], gain_cols[:, c:c + 1], rstd[:, :],
                                                         ALU.mult, ALU.mult), reads=[Bxs[g], Bgain, Brstd], writes=[Bout])


class WStream:
    def __init__(self, cx, st, KG=4, NW=3, tag="ws"):
        self.cx = cx
        k = cx.k
        self.KG = KG
        self.NW = NW
        self.w = [k.sbuf(f"{tag}_w{i}", [128, KG, 512], BF16, stack=st) for i in range(NW)]
        self.Bw = [Buf(f"{tag}_w{i}") for i in range(NW)]
        self.wi = 0
        self.bi = 0

    def run(self, W_ap2d, K, blocks, subs):
        cx, k = self.cx, self.cx.k
        KC = K // 128
        KG = self.KG
        assert KC % KG == 0 and len(subs) in (1, 2)
        Wv = W_ap2d.rearrange("(kc p) n -> p kc n", p=128)
        for blk in blocks:
            chunks = []
            off = 0
            for (c0, ncol) in blk:
                o = 0
                while o < ncol:
                    r = min(128, ncol - o)
                    chunks.append((off + o, r, c0 + o))
                    o += r
                off += ncol
            assert off <= 512 and len(chunks) <= 4
            half = self.bi % 2
            self.bi += 1
            base = [half * 4] if len(subs) == 1 else [0, 4]
            for kg in range(KC // KG):
                s = self.wi % self.NW
                self.wi += 1
                off = 0
                for (c0, ncol) in blk:
                    k.dma("pool", self.w[s][:, :, off:off + ncol], Wv[:, kg * KG:(kg + 1) * KG, c0:c0 + ncol], writes=[self.Bw[s]])
                    off += ncol
                for si, (rhs, Brhs, TT, _) in enumerate(subs):
                    for kl in range(KG):
                        kc = kg * KG + kl
                        for j, (woff, rows, _c) in enumerate(chunks):
                            b = base[si] + j
                            k.op("pe", lambda h: h.matmul(cx.banks[b][0:rows, 0:TT], self.w[s][:, kl, woff:woff + rows], rhs[:, kc, 0:TT],
                                                          start=(kc == 0), stop=(kc == KC - 1)),
                                 reads=[self.Bw[s], Brhs], writes=[cx.Bbank[b]])
            for si, (rhs, Brhs, TT, evac) in enumerate(subs):
                for j, (woff, rows, col0) in enumerate(chunks):
                    b = base[si] + j
                    evac(j, cx.banks[b][0:rows, 0:TT], cx.Bbank[b], rows, col0)


def std_blocks(N):
    out = []
    c = 0
    while c < N:
        n = min(512, N - c)
        out.append([(c, n)])
        c += n
    return out


def phase_inproj(cx, xT_dram, projT_dram, Tg, TT, D, NIN, w_in_ap, gain_vec_ap):
    k = cx.k
    DC = D // 128
    NS = 2 if Tg >= 2 * TT else 1
    with ExitStack() as st:
        gain = k.sbuf("ip_gain", [128, DC], F32, stack=st)
        Bgain = Buf("ip_gain")
        load_vec_cols(cx, st, gain_vec_ap, D, gain, Bgain)
        xnb = [k.sbuf(f"ip_xnb{i}", [128, DC, TT], BF16, stack=st) for i in range(NS)]
        Bxnb = [Buf(f"ip_xnb{i}") for i in range(NS)]
        stg = [k.sbuf(f"ip_stg{i}", [128, TT], F32, stack=st) for i in range(4)]
        Bstg = [Buf(f"ip_stg{i}") for i in range(4)]
        ws = WStream(cx, st, NW=(8 if TT <= 64 else 6), tag="ip")
        ns = NormScratch(cx, st, TT, "ipn")
        cnt = [0]
        for t0 in range(0, Tg, NS * TT):
            subs = []
            for si in range(NS):
                ts = t0 + si * TT
                norm_tile(cx, ns, xT_dram, ts, TT, D, gain, Bgain, xnb[si], Bxnb[si], "ipn")

                def evac(j, bank_ap, Bb, rows, col0, ts=ts):
                    g = cnt[0] % 4
                    cnt[0] += 1
                    copy_on(k, cx.evac_eng(), stg[g][0:rows, :], bank_ap, [Bb], [Bstg[g]])
                    k.dma("sp", projT_dram.ap()[col0:col0 + rows, ts:ts + TT], stg[g][0:rows, :], reads=[Bstg[g]])
                subs.append((xnb[si], Bxnb[si], TT, evac))
            ws.run(w_in_ap, D, std_blocks(NIN), subs)
    k.barrier()


def phase_outproj(cx, xT_dram, mixT_dram, hT_dram, Tg, TT, D, DMIX, w_out_ap):
    k = cx.k
    MC = DMIX // 128
    NS = 2 if Tg >= 2 * TT else 1
    with ExitStack() as st:
        mixb = [k.sbuf(f"op_mixb{i}", [128, MC, TT], BF16, stack=st) for i in range(NS)]
        Bmixb = [Buf(f"op_mixb{i}") for i in range(NS)]
        xres = [k.sbuf(f"op_x{i}", [128, TT], F32, stack=st) for i in range(4)]
        Bxres = [Buf(f"op_x{i}") for i in range(4)]
        ws = WStream(cx, st, NW=(8 if TT <= 64 else 6), tag="op")
        cnt = [0]
        mv = mixT_dram.ap().rearrange("(c p) t -> p c t", p=128)
        for t0 in range(0, Tg, NS * TT):
            subs = []
            for si in range(NS):
                ts = t0 + si * TT
                for m0 in range(0, MC, 8):
                    m1 = min(MC, m0 + 8)
                    k.dma("sp", mixb[si][:, m0:m1, 0:TT], mv[:, m0:m1, ts:ts + TT], writes=[Bmixb[si]])

                def evac(j, bank_ap, Bb, rows, col0, ts=ts):
                    g = cnt[0] % 4
                    cnt[0] += 1
                    k.dma("sp", xres[g][0:rows, :], xT_dram.ap()[col0:col0 + rows, ts:ts + TT], writes=[Bxres[g]])
                    k.op("dve", lambda h: h.tensor_tensor(xres[g][0:rows, :], bank_ap, xres[g][0:rows, :], ALU.add),
                         reads=[Bb, Bxres[g]], writes=[Bxres[g]])
                    k.dma("sp", hT_dram.ap()[col0:col0 + rows, ts:ts + TT], xres[g][0:rows, :], reads=[Bxres[g]])
                subs.append((mixb[si], Bmixb[si], TT, evac))
            ws.run(w_out_ap, DMIX, std_blocks(D), subs)
    k.barrier()


def phase_ffn(cx, hT_dram, xT_dram, Tg, TT, D, DFF, gain_vec_ap, up_ap, convw_ap, convb_ap, down_ap,
              state_ap, out_state_ap):
    k = cx.k
    DC = D // 128
    FC = DFF // 128
    with ExitStack() as st:
        gain = k.sbuf("ff_gain", [128, DC], F32, stack=st)
        Bgain = Buf("ff_gain")
        load_vec_cols(cx, st, gain_vec_ap, D, gain, Bgain)
        cw = k.sbuf("ff_cw", [128, 3, 2 * FC], F32, stack=st)
        cb = k.sbuf("ff_cb", [128, 2 * FC], F32, stack=st)
        tails = k.sbuf("ff_tails", [128, 2 * FC, 2], F32, stack=st)
        tl2 = k.sbuf("ff_tl2", [128, 2, 2 * FC], F32, stack=st)
        Bcw, Bcb, Btails = Buf("ff_cw"), Buf("ff_cb"), Buf("ff_tails")
        for i in range(3):
            load_vec_cols(cx, st, convw_ap[i, :], 2 * DFF, cw[:, i, :], Bcw)
        load_vec_cols(cx, st, convb_ap, 2 * DFF, cb, Bcb)
        if state_ap is None:
            k.op("dve", lambda h: h.memset(tails[:, :, :], 0.0), writes=[Btails])
        else:
            for r in range(2):
                load_vec_cols(cx, st, state_ap[r, :], 2 * DFF, tl2[:, r, :], Btails)
            k.op("dve", lambda h: h.tensor_copy(tails[:, :, :], tl2[:, :, :].rearrange("p r c -> p c r")), reads=[Btails], writes=[Btails])
        hnb = k.sbuf("ff_hnb", [128, DC, TT], BF16, stack=st)
        Bhnb = Buf("ff_hnb")
        actb = k.sbuf("ff_actb", [128, FC, TT], BF16, stack=st)
        Bactb = Buf("ff_actb")
        NE = 3
        ext = [k.sbuf(f"ff_ext{i}", [128, TT + 2], F32, stack=st) for i in range(NE)]
        Bext = [Buf(f"ff_ext{i}") for i in range(NE)]
        gacc = [k.sbuf(f"ff_gacc{i}", [128, TT], F32, stack=st) for i in range(4)]
        Bgacc = [Buf(f"ff_gacc{i}") for i in range(4)]
        vacc = [k.sbuf(f"ff_vacc{i}", [128, TT], F32, stack=st) for i in range(2)]
        Bvacc = [Buf(f"ff_vacc{i}") for i in range(2)]
        hres = [k.sbuf(f"ff_h{i}", [128, TT], F32, stack=st) for i in range(2)]
        Bhres = [Buf(f"ff_h{i}") for i in range(2)]
        ns = NormScratch(cx, st, TT, "ffn")
        ws = WStream(cx, st, KG=(4 if DC % 4 == 0 else 2), NW=(8 if TT <= 64 else 6), tag="ffu")
        wsd = WStream(cx, st, KG=(4 if FC % 4 == 0 else (2 if FC % 2 == 0 else 1)), NW=(8 if TT <= 64 else 5), tag="ffd")
        cnt = [0]
        ublocks = []
        f = 0
        while f < FC:
            n = min(4, FC - f)
            ublocks.append([(f * 128, n * 128)])
            ublocks.append([(DFF + f * 128, n * 128)])
            f += n
        for t0 in range(0, Tg, TT):
            norm_tile(cx, ns, hT_dram, t0, TT, D, gain, Bgain, hnb, Bhnb, "ffn")

            def evac_up(j, bank_ap, Bb, rows, col0):
                ch = col0 // 128
                e = cnt[0] % NE
                cnt[0] += 1
                isg = ch < FC
                if isg:
                    ac, Bac = gacc[j], Bgacc[j]
                else:
                    ac, Bac = vacc[cnt[0] % 2], Bvacc[cnt[0] % 2]
                k.op("act", lambda h: h.activation(ext[e][:, 2:2 + TT], bank_ap, AF.Copy), reads=[Bb], writes=[Bext[e]])
                k.op("dve", lambda h: h.tensor_copy(ext[e][:, 0:2], tails[:, ch, :]), reads=[Btails], writes=[Bext[e]])
                k.op("dve", lambda h: h.tensor_copy(tails[:, ch, :], ext[e][:, TT:TT + 2]), reads=[Bext[e]], writes=[Btails])
                k.op("act", lambda h: h.activation(ac[:, :], ext[e][:, 2:2 + TT], AF.Identity, bias=cb[:, ch:ch + 1], scale=cw[:, 2, ch:ch + 1]),
                     reads=[Bext[e], Bcw, Bcb], writes=[Bac])
                k.op("dve", lambda h: h.scalar_tensor_tensor(ac[:, :], ext[e][:, 1:1 + TT], cw[:, 1, ch:ch + 1], ac[:, :], ALU.mult, ALU.add),
                     reads=[Bext[e], Bcw, Bac], writes=[Bac])
                k.op("dve", lambda h: h.scalar_tensor_tensor(ac[:, :], ext[e][:, 0:TT], cw[:, 0, ch:ch + 1], ac[:, :], ALU.mult, ALU.add),
                     reads=[Bext[e], Bcw, Bac], writes=[Bac])
                if isg:
                    k.op("act", lambda h: h.activation(ac[:, :], ac[:, :], AF.Silu), reads=[Bac], writes=[Bac])
                else:
                    k.op("dve", lambda h: h.tensor_tensor(actb[:, ch - FC, 0:TT], gacc[j][:, :], ac[:, :], ALU.mult),
                         reads=[Bgacc[j], Bac], writes=[Bactb])
            ws.run(up_ap, D, ublocks, [(hnb, Bhnb, TT, evac_up)])

            def evac_dn(j, bank_ap, Bb, rows, col0):
                g = cnt[0] % 2
                cnt[0] += 1
                k.dma("sp", hres[g][0:rows, :], hT_dram.ap()[col0:col0 + rows, t0:t0 + TT], writes=[Bhres[g]])
                k.op("dve", lambda h: h.tensor_tensor(hres[g][0:rows, :], bank_ap, hres[g][0:rows, :], ALU.add),
                     reads=[Bb, Bhres[g]], writes=[Bhres[g]])
                k.dma("sp", xT_dram.ap()[col0:col0 + rows, t0:t0 + TT], hres[g][0:rows, :], reads=[Bhres[g]])
            wsd.run(down_ap, DFF, std_blocks(D), [(actb, Bactb, TT, evac_dn)])
        k.op("dve", lambda h: h.tensor_copy(tl2[:, :, :], tails[:, :, :].rearrange("p c r -> p r c")), reads=[Btails], writes=[Btails])
        so = k.sbuf("ff_so", [128, 128], F32, stack=st)
        Bso = Buf("ff_so")
        for r in range(2):
            c0 = 0
            while c0 < 2 * FC:
                n = min(128, 2 * FC - c0)
                bank, Bb = cx.banks[7], cx.Bbank[7]
                k.op("pe", lambda h: h.transpose(bank[0:n, 0:128], tl2[:, r, c0:c0 + n], cx.ident[:, :]), reads=[Btails, cx.Bident], writes=[Bb])
                k.op("dve", lambda h: h.tensor_copy(so[0:n, :], bank[0:n, 0:128]), reads=[Bb], writes=[Bso])
                k.dma("sp", out_state_ap[r, c0 * 128:(c0 + n) * 128].rearrange("(c p) -> c p", p=128), so[0:n, :], reads=[Bso])
                c0 += n
    k.barrier()

import math
from contextlib import ExitStack

BRANCHES = ((128, 1), (512, 4), (2048, 16))
TOE_ML = 384
SMP_ML = 2064


def t5_bucket_np(dist):
    dist = np.asarray(dist, np.int64)
    d = np.maximum(dist, 1).astype(np.float32)
    large = 16 + (np.log(d / np.float32(16)) / np.float32(math.log(2048 / 16)) * np.float32(16)).astype(np.int32)
    large = np.minimum(large, 31)
    return np.where(dist < 16, dist, large)


def attn_onehots():
    ohp = np.zeros((3, 32, TOE_ML), np.float32)
    for bi, (w, d) in enumerate(BRANCHES):
        for m in range(TOE_ML):
            j = m - 127
            if 0 <= j <= w // d:
                ohp[bi, t5_bucket_np(j * d), m] = 1.0
    ohs = np.zeros((32, SMP_ML), np.float32)
    for m in range(SMP_ML):
        rel = m - 8
        if rel < 0:
            continue
        cnt = sum(1 for (w, d) in BRANCHES if rel % d == 0 and rel <= w)
        ohs[t5_bucket_np(rel), m] = cnt
    return ohp, ohs


def setup_attn_tables(cx, rel_bias_dram, ohp_dram, ohs_dram, Hp_dram, Hs_dram, HA, do_prompt=True, do_sample=True):
    k = cx.k
    with ExitStack() as st:
        rb = k.sbuf("at_rb", [32, HA], F32, stack=st)
        Brb = Buf("at_rb")
        k.dma("sp", rb[:, :], rel_bias_dram.ap(), writes=[Brb])
        k.op("act", lambda h: h.activation(rb[:, :], rb[:, :], AF.Exp), reads=[Brb], writes=[Brb])
        ohp = k.sbuf("at_ohp", [32, 3, TOE_ML], F32, stack=st)
        ohs = k.sbuf("at_ohs", [32, SMP_ML], F32, stack=st)
        Boh = Buf("at_oh")
        k.dma("sp", ohp[:, :, :], ohp_dram.ap().rearrange("b k m -> k b m"), writes=[Boh])
        k.dma("sp", ohs[:, :], ohs_dram.ap(), writes=[Boh])
        erb = [k.sbuf(f"at_erb{i}", [32, 128], F32, stack=st) for i in range(2)]
        Berb = [Buf(f"at_erb{i}") for i in range(2)]
        stg = [k.sbuf(f"at_stg{i}", [128, 512], F32, stack=st) for i in range(3)]
        Bstg = [Buf(f"at_stg{i}") for i in range(3)]
        it = 0
        for h in range(HA):
            e = h % 2
            k.op("dve", lambda hh: hh.tensor_scalar(erb[e][:, :], cx.ones32[0:32, :], rb[:, h:h + 1], None, ALU.mult),
                 reads=[Brb, cx.Bconst], writes=[Berb[e]])
            jobs = []
            if do_prompt:
                for bi in range(3):
                    jobs.append((ohp[:, bi, :], TOE_ML, Hp_dram.ap()[h, bi, :, :]))
            if do_sample:
                c0 = 0
                while c0 < SMP_ML:
                    n = min(512, SMP_ML - c0)
                    jobs.append((ohs[:, c0:c0 + n], n, Hs_dram.ap()[h, :, c0:c0 + n]))
                    c0 += n
            for (rhs, n, dst) in jobs:
                b = it % 8
                g = it % 3
                it += 1
                k.op("pe", lambda hh: hh.matmul(cx.banks[b][:, 0:n], erb[e][:, :], rhs, start=True, stop=True),
                     reads=[Berb[e], Boh], writes=[cx.Bbank[b]])
                copy_on(k, cx.evac_eng(), stg[g][:, 0:n], cx.banks[b][:, 0:n], [cx.Bbank[b]], [Bstg[g]])
                k.dma("sp", dst, stg[g][:, 0:n], reads=[Bstg[g]])
    k.barrier()


def head_norm(cx, src, Bsrc, n, TT_list, gcol, Bg, rs, Brs, sqt, Bsq, bank_i, eps_scale):
    k = cx.k
    for (c0, cn) in TT_list:
        k.op("act", lambda h: h.activation(sqt[:, 0:cn], src[:, c0:c0 + cn], AF.Square), reads=[Bsrc], writes=[Bsq])
        k.op("pe", lambda h: h.matmul(cx.banks[bank_i][:, 0:cn], cx.ones32[:, :], sqt[:, 0:cn], start=True, stop=True),
             reads=[Bsq, cx.Bconst], writes=[cx.Bbank[bank_i]])
        k.op("act", lambda h: h.activation(rs[:, c0:c0 + cn], cx.banks[bank_i][:, 0:cn], AF.Sqrt, bias=EPS, scale=eps_scale),
             reads=[cx.Bbank[bank_i]], writes=[Brs])
    k.op("dve", lambda h: h.reciprocal(rs[:, 0:n], rs[:, 0:n]), reads=[Brs], writes=[Brs])


def tiles_of(n, t=512):
    return [(c, min(t, n - c)) for c in range(0, n, t)]


def phase_attn_prompt(cx, projT, mixT, kv_out_ap, S, HA, Hp_dram, qn_ap, kn_ap, on_ap):
    k = cx.k
    DA = HA * 128
    NB = S // 128
    assert S % 2048 == 0 or S in (256, 512, 1024, 2048)
    with ExitStack() as st:
        gq = k.sbuf("ap_gq", [128, 1], F32, stack=st)
        gk = k.sbuf("ap_gk", [128, 1], F32, stack=st)
        go = k.sbuf("ap_go", [128, HA], F32, stack=st)
        Bg = Buf("ap_g")
        load_vec_cols(cx, st, qn_ap, 128, gq, Bg)
        load_vec_cols(cx, st, kn_ap, 128, gk, Bg)
        load_vec_cols(cx, st, on_ap, DA, go, Bg)
        k.op("dve", lambda h: h.tensor_scalar(gq[:, :], gq[:, :], 128.0 ** -0.5, None, ALU.mult), reads=[Bg], writes=[Bg])
        raw = [k.sbuf(f"ap_raw{i}", [128, S], F32, stack=st) for i in range(3)]
        Braw = [Buf(f"ap_raw{i}") for i in range(3)]
        rs = k.sbuf("ap_rs", [128, S], F32, stack=st)
        Brs = Buf("ap_rs")
        sqt = k.sbuf("ap_sq", [128, 512], F32, stack=st)
        Bsq = Buf("ap_sq")
        knf = k.sbuf("ap_knf", [128, S], F32, stack=st)
        Bknf = Buf("ap_knf")
        qb = [k.sbuf(f"ap_qb{i}", [128, S], BF16, stack=st) for i in range(3)]
        kb = [k.sbuf(f"ap_kb{i}", [128, S], BF16, stack=st) for i in range(3)]
        Bqb = [Buf(f"ap_qb{i}") for i in range(3)]
        Bkb = [Buf(f"ap_kb{i}") for i in range(3)]
        vperm = k.sbuf("ap_vperm", [128, S], F32, stack=st)
        Bvperm = Buf("ap_vperm")
        vtok = k.sbuf("ap_vtok", [128, 3, NB, 128], BF16, stack=st)
        Bvtok = Buf("ap_vtok")
        kvst = [k.sbuf(f"ap_kvst{i}", [128, 2, 128], F32, stack=st) for i in range(3)]
        Bkvst = [Buf(f"ap_kvst{i}") for i in range(3)]
        eb = k.sbuf("ap_eb", [128, 3, 2, 128], F32, stack=st)
        Beb = Buf("ap_eb")
        pe_ = [k.sbuf(f"ap_pe{i}", [128, 128], F32, stack=st) for i in range(4)]
        Bpe = [Buf(f"ap_pe{i}") for i in range(4)]
        pt = [k.sbuf(f"ap_pt{i}", [128, 128], BF16, stack=st) for i in range(4)]
        Bpt = [Buf(f"ap_pt{i}") for i in range(4)]
        oacc = k.sbuf("ap_oacc", [128, S], F32, stack=st)
        dacc = k.sbuf("ap_dacc", [128, S], F32, stack=st)
        Boacc, Bdacc = Buf("ap_oacc"), Buf("ap_dacc")
        yab = k.sbuf("ap_yab", [128, S], BF16, stack=st)
        Byab = Buf("ap_yab")
        T5 = tiles_of(S)
        it = 0
        for h in range(HA):
            for i in range(3):
                k.dma("sp", raw[i][:, :], projT.ap()[i * DA + h * 128:i * DA + (h + 1) * 128, 0:S], writes=[Braw[i]])
            for bi in range(3):
                for vi, off in enumerate((127, 255)):
                    src = bass.AP(Hp_dram, (h * 3 + bi) * 128 * TOE_ML + off, [[TOE_ML - 1, 128], [1, 128]])
                    k.dma("sp", eb[:, bi, vi, :], src, writes=[Beb])
            head_norm(cx, raw[0], Braw[0], S, T5, None, None, rs, Brs, sqt, Bsq, 6, 1.0 / 128)
            k.op("dve", lambda hh: hh.scalar_tensor_tensor(qb[0][:, :], raw[0][:, :], gq[:, 0:1], rs[:, :], ALU.mult, ALU.mult),
                 reads=[Braw[0], Bg, Brs], writes=[Bqb[0]])
            head_norm(cx, raw[1], Braw[1], S, T5, None, None, rs, Brs, sqt, Bsq, 6, 1.0 / 128)
            k.op("dve", lambda hh: hh.scalar_tensor_tensor(knf[:, :], raw[1][:, :], gk[:, 0:1], rs[:, :], ALU.mult, ALU.mult),
                 reads=[Braw[1], Bg, Brs], writes=[Bknf])
            k.op("act", lambda hh: hh.activation(kb[0][:, :], knf[:, :], AF.Copy), reads=[Bknf], writes=[Bkb[0]])
            for bi, d in ((1, 4), (2, 16)):
                k.op("dve", lambda hh: hh.tensor_copy(qb[bi][:, :].rearrange("p (r u) -> p r u", r=d),
                                                      qb[0][:, :].rearrange("p (u r) -> p r u", r=d)), reads=[Bqb[0]], writes=[Bqb[bi]])
                k.op("dve", lambda hh: hh.tensor_copy(kb[bi][:, :].rearrange("p (r u) -> p r u", r=d),
                                                      kb[0][:, :].rearrange("p (u r) -> p r u", r=d)), reads=[Bkb[0]], writes=[Bkb[bi]])
            for tb in range(NB):
                b = it % 4
                g = it % 3
                it += 1
                bank, Bb = cx.banks[b], cx.Bbank[b]
                k.op("pe", lambda hh: hh.transpose(bank[:, 0:128], knf[:, tb * 128:(tb + 1) * 128], cx.ident[:, :]),
                     reads=[Bknf, cx.Bident], writes=[Bb])
                k.op("pe", lambda hh: hh.transpose(bank[:, 128:256], raw[2][:, tb * 128:(tb + 1) * 128], cx.ident[:, :]),
                     reads=[Braw[2], cx.Bident], writes=[Bb])
                k.op("act", lambda hh: hh.activation(kvst[g][:, :, :], bank[:, 0:256].rearrange("p (a d) -> p a d", a=2), AF.Copy),
                     reads=[Bb], writes=[Bkvst[g]])
                k.op("dve", lambda hh: hh.tensor_copy(vtok[:, 0, tb, :], bank[:, 128:256]), reads=[Bb], writes=[Bvtok])
                k.dma("sp", kv_out_ap[tb * 128:(tb + 1) * 128, :, h, :], kvst[g][:, :, :], reads=[Bkvst[g]])
            for bi, d in ((1, 4), (2, 16)):
                k.op("dve", lambda hh: hh.tensor_copy(vperm[:, :].rearrange("p (r u) -> p r u", r=d),
                                                      raw[2][:, :].rearrange("p (u r) -> p r u", r=d)), reads=[Braw[2]], writes=[Bvperm])
                for tb in range(NB):
                    b = it % 4
                    it += 1
                    bank, Bb = cx.banks[b], cx.Bbank[b]
                    k.op("pe", lambda hh: hh.transpose(bank[:, 0:128], vperm[:, tb * 128:(tb + 1) * 128], cx.ident[:, :]),
                         reads=[Bvperm, cx.Bident], writes=[Bb])
                    copy_on(k, cx.evac_eng(), vtok[:, bi, tb, :], bank[:, 0:128], [Bb], [Bvtok])
            for bi, (w, d) in enumerate(BRANCHES):
                L = S // d
                nbc = max(1, L // 128)
                jobs = []
                for Q in range(NB // 4):
                    for jj in range(4):
                        B = Q * 4 + jj
                        n = B % nbc
                        kbs = ([B - 1] if n >= 1 else []) + [B]
                        for ki, KB_ in enumerate(kbs):
                            jobs.append(dict(Q=Q, jj=jj, B=B, KB=KB_, vi=(0 if KB_ == B else 1), first=(ki == 0), last=(ki == len(kbs) - 1),
                                             qend=(jj == 3 and ki == len(kbs) - 1)))

                def stage1(jb):
                    nonlocal it
                    sb = it % 4
                    g = it % 4
                    it += 1
                    jb["g"] = g
                    k.op("pe", lambda hh: hh.matmul(cx.banks[sb][:, 0:128], kb[bi][:, jb["KB"] * 128:(jb["KB"] + 1) * 128],
                                                    qb[bi][:, jb["B"] * 128:(jb["B"] + 1) * 128], start=True, stop=True),
                         reads=[Bkb[bi], Bqb[bi]], writes=[cx.Bbank[sb]])
                    k.op("act", lambda hh: hh.activation(pe_[g][:, :], cx.banks[sb][:, 0:128], AF.Exp),
                         reads=[cx.Bbank[sb]], writes=[Bpe[g]])
                    k.op("dve", lambda hh: hh.tensor_tensor(pt[g][:, :], pe_[g][:, :], eb[:, bi, jb["vi"], :], ALU.mult),
                         reads=[Bpe[g], Beb], writes=[Bpt[g]])

                def stage2(jb):
                    Q, jj, g = jb["Q"], jb["jj"], jb["g"]
                    ob, db = 4 + (Q % 2) * 2, 5 + (Q % 2) * 2
                    k.op("pe", lambda hh: hh.matmul(cx.banks[ob][:, jj * 128:(jj + 1) * 128], vtok[:, bi, jb["KB"], :], pt[g][:, :],
                                                    start=jb["first"], stop=jb["last"]),
                         reads=[Bvtok, Bpt[g]], writes=[cx.Bbank[ob]])
                    k.op("pe", lambda hh: hh.matmul(cx.banks[db][:, jj * 128:(jj + 1) * 128], cx.onesb[:, :], pt[g][:, :],
                                                    start=jb["first"], stop=jb["last"]),
                         reads=[cx.Bconst, Bpt[g]], writes=[cx.Bbank[db]])
                    if not jb["qend"]:
                        return
                    if d == 1:
                        ov = oacc[:, Q * 512:(Q + 1) * 512]
                        dv = dacc[:, Q * 512:(Q + 1) * 512]
                        k.op("act", lambda hh: hh.activation(ov, cx.banks[ob][:, :], AF.Copy), reads=[cx.Bbank[ob]], writes=[Boacc])
                        k.op("dve", lambda hh: hh.tensor_copy(dv, cx.banks[db][:, :]), reads=[cx.Bbank[db]], writes=[Bdacc])
                    else:
                        cpq = 512 // L if L < 512 else 1
                        if L >= 512:
                            r = Q // (L // 512)
                            u0 = (Q % (L // 512)) * 512
                            ov = oacc[:, :].rearrange("p (u r) -> p r u", r=d)[:, r, u0:u0 + 512]
                            dv = dacc[:, :].rearrange("p (u r) -> p r u", r=d)[:, r, u0:u0 + 512]
                            oi = cx.banks[ob][:, :]
                            di = cx.banks[db][:, :]
                        else:
                            ov = oacc[:, :].rearrange("p (u r) -> p r u", r=d)[:, Q * cpq:(Q + 1) * cpq, :]
                            dv = dacc[:, :].rearrange("p (u r) -> p r u", r=d)[:, Q * cpq:(Q + 1) * cpq, :]
                            oi = cx.banks[ob][:, :].rearrange("p (c u) -> p c u", c=cpq)
                            di = cx.banks[db][:, :].rearrange("p (c u) -> p c u", c=cpq)
                        k.op("dve", lambda hh: hh.tensor_tensor(ov, oi, ov, ALU.add), reads=[cx.Bbank[ob], Boacc], writes=[Boacc])
                        k.op("dve", lambda hh: hh.tensor_tensor(dv, di, dv, ALU.add), reads=[cx.Bbank[db], Bdacc], writes=[Bdacc])

                LA = 2
                for i in range(len(jobs) + LA):
                    if i < len(jobs):
                        stage1(jobs[i])
                    if i >= LA:
                        stage2(jobs[i - LA])
            k.op("dve", lambda hh: hh.reciprocal(dacc[:, :], dacc[:, :]), reads=[Bdacc], writes=[Bdacc])
            k.op("dve", lambda hh: hh.tensor_tensor(oacc[:, :], oacc[:, :], dacc[:, :], ALU.mult), reads=[Boacc, Bdacc], writes=[Boacc])
            head_norm(cx, oacc, Boacc, S, T5, None, None, rs, Brs, sqt, Bsq, 6, 1.0 / 128)
            k.op("dve", lambda hh: hh.scalar_tensor_tensor(yab[:, :], oacc[:, :], go[:, h:h + 1], rs[:, :], ALU.mult, ALU.mult),
                 reads=[Boacc, Bg, Brs], writes=[Byab])
            k.dma("sp", mixT.ap()[h * 128:(h + 1) * 128, 0:S], yab[:, :], reads=[Byab])
    k.barrier()


def phase_attn_sample(cx, projT, mixT, kv_out_ap, cache_ap, T, P, HA, Hs_dram, qn_ap, kn_ap, on_ap):
    k = cx.k
    DA = HA * 128
    PB = P // 128
    with ExitStack() as st:
        gq = k.sbuf("as_gq", [128, 1], F32, stack=st)
        gk = k.sbuf("as_gk", [128, 1], F32, stack=st)
        go = k.sbuf("as_go", [128, HA], F32, stack=st)
        Bg = Buf("as_g")
        load_vec_cols(cx, st, qn_ap, 128, gq, Bg)
        load_vec_cols(cx, st, kn_ap, 128, gk, Bg)
        load_vec_cols(cx, st, on_ap, DA, go, Bg)
        k.op("dve", lambda h: h.tensor_scalar(gq[:, :], gq[:, :], 128.0 ** -0.5, None, ALU.mult), reads=[Bg], writes=[Bg])
        raw = [k.sbuf(f"as_raw{i}", [128, T], F32, stack=st) for i in range(3)]
        Braw = [Buf(f"as_raw{i}") for i in range(3)]
        rs = k.sbuf("as_rs", [128, T], F32, stack=st)
        Brs = Buf("as_rs")
        sqt = k.sbuf("as_sq", [128, T], F32, stack=st)
        Bsq = Buf("as_sq")
        knf = k.sbuf("as_knf", [128, T], F32, stack=st)
        Bknf = Buf("as_knf")
        qb = k.sbuf("as_qb", [128, T], BF16, stack=st)
        kbn = k.sbuf("as_kbn", [128, T], BF16, stack=st)
        Bqb, Bkbn = Buf("as_qb"), Buf("as_kbn")
        kvst = k.sbuf("as_kvst", [128, 2, 128], F32, stack=st)
        Bkvst = Buf("as_kvst")
        vnb = k.sbuf("as_vnb", [128, 128], BF16, stack=st)
        Bvnb = Buf("as_vnb")
        kc = [k.sbuf(f"as_kc{i}", [128, PB, 128], F32, stack=st) for i in range(2)]
        vc = [k.sbuf(f"as_vc{i}", [128, PB, 128], F32, stack=st) for i in range(2)]
        Bkc = [Buf(f"as_kc{i}") for i in range(2)]
        Bvc = [Buf(f"as_vc{i}") for i in range(2)]
        ktb = k.sbuf("as_ktb", [128, PB, 128], BF16, stack=st)
        vcb = k.sbuf("as_vcb", [128, PB, 128], BF16, stack=st)
        Bktb, Bvcb = Buf("as_ktb"), Buf("as_vcb")
        cs = k.sbuf("as_cs", [128, PB, T], F32, stack=st)
        cn = k.sbuf("as_cn", [128, T], F32, stack=st)
        Bcs = Buf("as_cs")
        pe_ = k.sbuf("as_pe", [128, PB, T], F32, stack=st)
        pn_ = k.sbuf("as_pn", [128, T], F32, stack=st)
        ptb = k.sbuf("as_ptb", [128, PB, T], BF16, stack=st)
        pnb = k.sbuf("as_pnb", [128, T], BF16, stack=st)
        Bpe, Bptb = Buf("as_pe"), Buf("as_ptb")
        oacc = k.sbuf("as_oacc", [128, T], F32, stack=st)
        dacc = k.sbuf("as_dacc", [128, T], F32, stack=st)
        Boacc = Buf("as_oacc")
        yab = k.sbuf("as_yab", [128, T], BF16, stack=st)
        Byab = Buf("as_yab")
        TL = [(0, T)]
        for h in range(HA):
            s = h % 2
            for i in range(3):
                k.dma("sp", raw[i][:, :], projT.ap()[i * DA + h * 128:i * DA + (h + 1) * 128, 0:T], writes=[Braw[i]])
            k.dma("sp", kc[s][:, :, :], cache_ap[:, 0, h, :].rearrange("(b p) d -> p b d", p=128), writes=[Bkc[s]])
            k.dma("sp", vc[s][:, :, :], cache_ap[:, 1, h, :].rearrange("(b p) d -> p b d", p=128), writes=[Bvc[s]])
            src = bass.AP(Hs_dram, h * 128 * SMP_ML + 8 + 128, [[SMP_ML - 1, 128], [128, PB], [1, T]])
            k.dma("sp", cs[:, :, :], src, writes=[Bcs])
            srcn = bass.AP(Hs_dram, h * 128 * SMP_ML + 8, [[SMP_ML - 1, T], [1, T]])
            k.dma("sp", cn[0:T, :], srcn, writes=[Bcs])
            head_norm(cx, raw[0], Braw[0], T, TL, None, None, rs, Brs, sqt, Bsq, 6, 1.0 / 128)
            k.op("dve", lambda hh: hh.scalar_tensor_tensor(qb[:, :], raw[0][:, :], gq[:, 0:1], rs[:, :], ALU.mult, ALU.mult),
                 reads=[Braw[0], Bg, Brs], writes=[Bqb])
            head_norm(cx, raw[1], Braw[1], T, TL, None, None, rs, Brs, sqt, Bsq, 6, 1.0 / 128)
            k.op("dve", lambda hh: hh.scalar_tensor_tensor(knf[:, :], raw[1][:, :], gk[:, 0:1], rs[:, :], ALU.mult, ALU.mult),
                 reads=[Braw[1], Bg, Brs], writes=[Bknf])
            k.op("act", lambda hh: hh.activation(kbn[:, :], knf[:, :], AF.Copy), reads=[Bknf], writes=[Bkbn])
            bank, Bb = cx.banks[0], cx.Bbank[0]
            k.op("pe", lambda hh: hh.transpose(bank[0:T, 0:128], knf[:, 0:T], cx.ident[:, :]), reads=[Bknf, cx.Bident], writes=[Bb])
            k.op("pe", lambda hh: hh.transpose(bank[0:T, 128:256], raw[2][:, 0:T], cx.ident[:, :]), reads=[Braw[2], cx.Bident], writes=[Bb])
            k.op("act", lambda hh: hh.activation(kvst[0:T, :, :], bank[0:T, 0:256].rearrange("p (a d) -> p a d", a=2), AF.Copy),
                 reads=[Bb], writes=[Bkvst])
            k.op("dve", lambda hh: hh.tensor_copy(vnb[0:T, :], bank[0:T, 128:256]), reads=[Bb], writes=[Bvnb])
            k.dma("sp", kv_out_ap[0:T, :, h, :], kvst[0:T, :, :], reads=[Bkvst])
            for b4 in range(PB // 4):
                bi_ = 1 + (b4 % 3)
                bank, Bb = cx.banks[bi_], cx.Bbank[bi_]
                for j in range(4):
                    blk = b4 * 4 + j
                    k.op("pe", lambda hh: hh.transpose(bank[:, j * 128:(j + 1) * 128], kc[s][:, blk, :], cx.ident[:, :]),
                         reads=[Bkc[s], cx.Bident], writes=[Bb])
                copy_on(k, cx.evac_eng(), ktb[:, b4 * 4:(b4 + 1) * 4, :], bank[:, :].rearrange("p (j t) -> p j t", j=4), [Bb], [Bktb])
            k.op("dve", lambda hh: hh.tensor_copy(vcb[:, :, :], vc[s][:, :, :]), reads=[Bvc[s]], writes=[Bvcb])
            sbank, Bsb = cx.banks[4], cx.Bbank[4]
            for blk in range(PB):
                k.op("pe", lambda hh: hh.matmul(sbank[:, blk * T:(blk + 1) * T], ktb[:, blk, :], qb[:, :], start=True, stop=True),
                     reads=[Bktb, Bqb], writes=[Bsb])
            k.op("pe", lambda hh: hh.matmul(sbank[0:T, PB * T:(PB + 1) * T], kbn[:, 0:T], qb[:, :], start=True, stop=True),
                 reads=[Bkbn, Bqb], writes=[Bsb])
            k.op("act", lambda hh: hh.activation(pe_[:, :, :], sbank[:, 0:PB * T].rearrange("p (b t) -> p b t", t=T), AF.Exp),
                 reads=[Bsb], writes=[Bpe])
            k.op("act", lambda hh: hh.activation(pn_[0:T, :], sbank[0:T, PB * T:(PB + 1) * T], AF.Exp), reads=[Bsb], writes=[Bpe])
            for blk in range(PB):
                k.op("dve", lambda hh: hh.tensor_tensor(ptb[:, blk, :], pe_[:, blk, :], cs[:, PB - 1 - blk, :], ALU.mult),
                     reads=[Bpe, Bcs], writes=[Bptb])
            k.op("dve", lambda hh: hh.tensor_tensor(pnb[0:T, :], pn_[0:T, :], cn[0:T, :], ALU.mult), reads=[Bpe, Bcs], writes=[Bptb])
            obank, Bob = cx.banks[5], cx.Bbank[5]
            for blk in range(PB):
                k.op("pe", lambda hh: hh.matmul(obank[:, 0:T], vcb[:, blk, :], ptb[:, blk, :], start=(blk == 0), stop=False),
                     reads=[Bvcb, Bptb], writes=[Bob])
            k.op("pe", lambda hh: hh.matmul(obank[:, 0:T], vnb[0:T, :], pnb[0:T, :], start=False, stop=True), reads=[Bvnb, Bptb], writes=[Bob])
            for blk in range(PB):
                k.op("pe", lambda hh: hh.matmul(obank[:, 128:128 + T], cx.onesb[:, :], ptb[:, blk, :], start=(blk == 0), stop=False),
                     reads=[cx.Bconst, Bptb], writes=[Bob])
            k.op("pe", lambda hh: hh.matmul(obank[:, 128:128 + T], cx.onesb[0:T, :], pnb[0:T, :], start=False, stop=True),
                 reads=[cx.Bconst, Bptb], writes=[Bob])
            k.op("dve", lambda hh: hh.reciprocal(dacc[:, :], obank[:, 128:128 + T]), reads=[Bob], writes=[Boacc])
            k.op("dve", lambda hh: hh.tensor_tensor(oacc[:, :], obank[:, 0:T], dacc[:, :], ALU.mult), reads=[Bob, Boacc], writes=[Boacc])
            head_norm(cx, oacc, Boacc, T, TL, None, None, rs, Brs, sqt, Bsq, 6, 1.0 / 128)
            k.op("dve", lambda hh: hh.scalar_tensor_tensor(yab[:, :], oacc[:, :], go[:, h:h + 1], rs[:, :], ALU.mult, ALU.mult),
                 reads=[Boacc, Bg, Brs], writes=[Byab])
            k.dma("sp", mixT.ap()[h * 128:(h + 1) * 128, 0:T], yab[:, :], reads=[Byab])
    k.barrier()


def phase_pool(cx, projT, mixT, T, DA, DB, pw_ap, pscale_ap, state_ap, out_state_ap, n_valid):
    k = cx.k
    WINS = (2, 4, 8, 16)
    NCH = DB // 128
    TT = min(512, T)
    with ExitStack() as st:
        psc = k.sbuf("pl_psc", [128, NCH], F32, stack=st)
        Bpsc = Buf("pl_psc")
        load_vec_cols(cx, st, pscale_ap, DB, psc, Bpsc)
        pwb = k.sbuf("pl_pwb", [128, 4, 2, 256], BF16, stack=st)
        Bpwb = Buf("pl_pwb")
        k.dma("pool", pwb[:, :, :, :], pw_ap.rearrange("g (cc p) e -> p g cc e", p=128), writes=[Bpwb])
        icnt = k.sbuf("pl_icnt", [128, 4, 16], F32, stack=st)
        Bicnt = Buf("pl_icnt")
        for gi, w in enumerate(WINS):
            for t in range(16):
                cnt = min(w, n_valid + t + 1)
                k.op("dve", lambda h: h.memset(icnt[:, gi, t:t + 1], 1.0 / cnt), writes=[Bicnt])
        ext = [k.sbuf(f"pl_ext{i}", [128, 15 + T], F32, stack=st) for i in range(2)]
        sA = [k.sbuf(f"pl_sA{i}", [128, 15 + T], F32, stack=st) for i in range(2)]
        sB = [k.sbuf(f"pl_sB{i}", [128, 15 + T], F32, stack=st) for i in range(2)]
        Bext = [Buf(f"pl_ext{i}") for i in range(2)]
        BsA = [Buf(f"pl_sA{i}") for i in range(2)]
        BsB = [Buf(f"pl_sB{i}") for i in range(2)]
        db_ = k.sbuf("pl_db", [128, 2, T], BF16, stack=st)
        Bdb = Buf("pl_db")
        tmp16 = k.sbuf("pl_t16", [128, 16], F32, stack=st)
        Bt16 = Buf("pl_t16")
        stt = k.sbuf("pl_stt", [15, DB], F32, stack=st)
        Bstt = Buf("pl_stt")
        sto = k.sbuf("pl_sto", [15, DB], F32, stack=st)
        Bsto = Buf("pl_sto")
        yf = [k.sbuf(f"pl_yf{i}", [128, TT], F32, stack=st) for i in range(2)]
        Byf = [Buf(f"pl_yf{i}") for i in range(2)]
        sqt = k.sbuf("pl_sq", [128, TT], F32, stack=st)
        Bsq = Buf("pl_sq")
        rs = k.sbuf("pl_rs", [128, TT], F32, stack=st)
        Brs = Buf("pl_rs")
        yb = [k.sbuf(f"pl_yb{i}", [128, TT], BF16, stack=st) for i in range(2)]
        Byb = [Buf(f"pl_yb{i}") for i in range(2)]
        if state_ap is not None:
            k.dma("sp", stt[:, :], state_ap, writes=[Bstt])
        it = 0
        for gi, w in enumerate(WINS):
            for cc in range(2):
                ch = gi * 2 + cc
                e = ch % 2
                k.dma("sp", ext[e][:, 15:15 + T], projT.ap()[3 * DA + ch * 128:3 * DA + (ch + 1) * 128, 0:T], writes=[Bext[e]])
                if state_ap is None:
                    k.op("dve", lambda h: h.memset(ext[e][:, 0:15], 0.0), writes=[Bext[e]])
                else:
                    bank, Bb = cx.banks[7], cx.Bbank[7]
                    k.op("pe", lambda h: h.transpose(bank[:, 0:15], stt[0:15, ch * 128:(ch + 1) * 128], cx.ident[0:15, 0:15]),
                         reads=[Bstt, cx.Bident], writes=[Bb])
                    k.op("dve", lambda h: h.tensor_copy(ext[e][:, 0:15], bank[:, 0:15]), reads=[Bb], writes=[Bext[e]])
                bank, Bb = cx.banks[7], cx.Bbank[7]
                k.op("pe", lambda h: h.transpose(bank[0:15, 0:128], ext[e][:, T:T + 15], cx.ident[:, :]), reads=[Bext[e], cx.Bident], writes=[Bb])
                k.op("dve", lambda h: h.tensor_copy(sto[0:15, ch * 128:(ch + 1) * 128], bank[0:15, 0:128]), reads=[Bb], writes=[Bsto])
                n = 15 + T
                cur, Bcur = ext[e], Bext[e]
                nxt = [(sA[e], BsA[e]), (sB[e], BsB[e])]
                sh = 1
                li = 0
                lo = 0
                while sh < w:
                    dst, Bdst = nxt[li % 2]
                    li += 1
                    k.op("dve", lambda h: h.tensor_tensor(dst[:, lo + sh:n], cur[:, lo + sh:n], cur[:, lo:n - sh], ALU.add), reads=[Bcur], writes=[Bdst])
                    cur, Bcur = dst, Bdst
                    lo += sh
                    sh *= 2
                k.op("dve", lambda h: h.scalar_tensor_tensor(db_[:, cc, :], cur[:, 15:15 + T], 1.0 / w, ext[e][:, 15:15 + T], ALU.mult, ALU.subtract),
                     reads=[Bcur, Bext[e]], writes=[Bdb])
                nf = min(16, T)
                k.op("dve", lambda h: h.tensor_tensor(tmp16[:, 0:nf], cur[:, 15:15 + nf], icnt[:, gi, 0:nf], ALU.mult), reads=[Bcur, Bicnt], writes=[Bt16])
                k.op("dve", lambda h: h.tensor_tensor(db_[:, cc, 0:nf], tmp16[:, 0:nf], ext[e][:, 15:15 + nf], ALU.subtract),
                     reads=[Bt16, Bext[e]], writes=[Bdb])
            for (t0, tn) in tiles_of(T, TT):
                for ec in range(2):
                    b = ec
                    for cc in range(2):
                        k.op("pe", lambda h: h.matmul(cx.banks[b][:, 0:tn], pwb[:, gi, cc, ec * 128:(ec + 1) * 128], db_[:, cc, t0:t0 + tn],
                                                      start=(cc == 0), stop=(cc == 1)), reads=[Bpwb, Bdb], writes=[cx.Bbank[b]])
                    k.op("act", lambda h: h.activation(yf[ec][:, 0:tn], cx.banks[b][:, 0:tn], AF.Copy), reads=[cx.Bbank[b]], writes=[Byf[ec]])
                    k.op("act", lambda h: h.activation(sqt[:, 0:tn], yf[ec][:, 0:tn], AF.Square), reads=[Byf[ec]], writes=[Bsq])
                    k.op("pe", lambda h: h.matmul(cx.banks[2][:, 0:tn], cx.ones32[:, :], sqt[:, 0:tn], start=(ec == 0), stop=(ec == 1)),
                         reads=[Bsq, cx.Bconst], writes=[cx.Bbank[2]])
                k.op("act", lambda h: h.activation(rs[:, 0:tn], cx.banks[2][:, 0:tn], AF.Sqrt, bias=EPS, scale=1.0 / 256), reads=[cx.Bbank[2]], writes=[Brs])
                k.op("dve", lambda h: h.reciprocal(rs[:, 0:tn], rs[:, 0:tn]), reads=[Brs], writes=[Brs])
                for ec in range(2):
                    ch = gi * 2 + ec
                    k.op("dve", lambda h: h.scalar_tensor_tensor(yb[ec][:, 0:tn], yf[ec][:, 0:tn], psc[:, ch:ch + 1], rs[:, 0:tn], ALU.mult, ALU.mult),
                         reads=[Byf[ec], Bpsc, Brs], writes=[Byb[ec]])
                    k.dma("sp", mixT.ap()[DA + ch * 128:DA + (ch + 1) * 128, t0:t0 + tn], yb[ec][:, 0:tn], reads=[Byb[ec]])
        k.dma("sp", out_state_ap, sto[0:15, :], reads=[Bsto])
    k.barrier()


def gdn_consts(HC):
    mU = np.triu(np.ones((128, 128), np.float32))
    mSU = np.triu(np.ones((128, 128), np.float32), 1)
    l128 = np.zeros((128, 128), np.float32); l128[127, :] = 1.0
    l8 = np.zeros((128, 128), np.float32); l8[7, 0:8] = 1.0
    sel = np.zeros((128, HC * 128), np.float32)
    for h in range(HC):
        sel[h, h * 128:(h + 1) * 128] = 1.0
    idx = np.arange(128)
    bd8 = (idx[:, None] // 8 == idx[None, :] // 8).astype(np.float32)
    lls = []
    for b in (8, 16, 32, 64):
        same = idx[:, None] // (2 * b) == idx[None, :] // (2 * b)
        ll = same & ((idx[:, None] % (2 * b)) >= b) & ((idx[None, :] % (2 * b)) < b)
        lls.append(ll.astype(np.float32))
    return np.concatenate([mU, mSU, l128, l8, sel, bd8] + lls, axis=1)


def phase_gdn(cx, projT, mixT, T, c, DA, DB, HC, gconst, Bgconst, convw_ap, alog_ap, dtb_ap, onorm_ap,
              cstate_ap, cstate_out_ap, s0_ap, s_out_ap, HG):
    k = cx.k
    DC = HC * 128
    base = 3 * DA + DB
    NCHK = T // c
    L = 2
    mU, mSU = gconst[:, 0:128], gconst[:, 128:256]
    lrow = gconst[:, 256:384] if c == 128 else gconst[:, 384:512]
    sel = gconst[:, 512:512 + HC * 128]
    o_ = 512 + HC * 128
    bd8 = gconst[:, o_:o_ + 128]
    LLm = [gconst[:, o_ + 128 * (i + 1):o_ + 128 * (i + 2)] for i in range(4)]
    with ExitStack() as st:
        cwc = k.sbuf("gd_cwc", [128, 4, 3 * HC], F32, stack=st)
        Bcwc = Buf("gd_cwc")
        for i in range(4):
            load_vec_cols(cx, st, convw_ap[i, :], 3 * DC, cwc[:, i, :], Bcwc)
        cst = k.sbuf("gd_cst", [128, 3, 3 * HC], F32, stack=st)
        Bcst = Buf("gd_cst")
        if cstate_ap is not None:
            for r in range(3):
                load_vec_cols(cx, st, cstate_ap[r, :], 3 * DC, cst[:, r, :], Bcst)
        else:
            k.op("dve", lambda h: h.memset(cst[:, :, :], 0.0), writes=[Bcst])
        tail = k.sbuf("gd_tail", [128, 3, 3 * HC], F32, stack=st)
        Btail = Buf("gd_tail")
        onc = k.sbuf("gd_onc", [128, 1], F32, stack=st)
        Bonc = Buf("gd_onc")
        load_vec_cols(cx, st, onorm_ap, 128, onc, Bonc)
        hv = k.sbuf("gd_hv", [HC, 2], F32, stack=st)
        Bhv = Buf("gd_hv")
        k.dma("sp", hv[:, 0:1], alog_ap.rearrange("(h o) -> h o", o=1), writes=[Bhv])
        k.dma("sp", hv[:, 1:2], dtb_ap.rearrange("(h o) -> h o", o=1), writes=[Bhv])
        negA = k.sbuf("gd_negA", [HC, 1], F32, stack=st)
        BnegA = Buf("gd_negA")
        k.op("act", lambda h: h.activation(negA[:, :], hv[:, 0:1], AF.Exp), reads=[Bhv], writes=[BnegA])
        k.op("dve", lambda h: h.tensor_scalar(negA[:, :], negA[:, :], -1.0, None, ALU.mult), reads=[BnegA], writes=[BnegA])
        betaT = k.sbuf("gd_betaT", [HC, T], F32, stack=st)
        gT = k.sbuf("gd_gT", [HC, T], F32, stack=st)
        gcT = k.sbuf("gd_gcT", [HC, T], F32, stack=st)
        rm = k.sbuf("gd_rm", [HC, T], F32, stack=st)
        Bbeta, BgT, BgcT, Brm = Buf("gd_betaT"), Buf("gd_gT"), Buf("gd_gcT"), Buf("gd_rm")
        k.dma("sp", betaT[:, :], projT.ap()[base + 4 * DC:base + 4 * DC + HC, 0:T], writes=[Bbeta])
        k.dma("sp", gT[:, :], projT.ap()[base + 4 * DC + HC:base + 4 * DC + 2 * HC, 0:T], writes=[BgT])
        k.op("act", lambda h: h.activation(betaT[:, :], betaT[:, :], AF.Sigmoid), reads=[Bbeta], writes=[Bbeta])
        k.op("act", lambda h: h.activation(gT[:, :], gT[:, :], AF.Exp, bias=hv[:, 1:2]), reads=[BgT, Bhv], writes=[BgT])
        k.op("act", lambda h: h.activation(gT[:, :], gT[:, :], AF.Ln, bias=1.0), reads=[BgT], writes=[BgT])
        k.op("dve", lambda h: h.tensor_scalar(gT[:, :], gT[:, :], negA[:, 0:1], None, ALU.mult), reads=[BgT, BnegA], writes=[BgT])
        k.op("dve", lambda h: h.memset(rm[:, :], 1.0), writes=[Brm])
        k.op("dve", lambda h: h.memset(rm[:, :].rearrange("p (n c) -> p n c", c=c)[:, :, 0:1], 0.0), writes=[Brm])
        k.op("dve", lambda h: h.tensor_tensor_scan(gcT[:, :], rm[:, :], gT[:, :], 0.0, ALU.mult, ALU.add), reads=[Brm, BgT], writes=[BgcT])
        colB = k.sbuf("gd_colB", [128, NCHK, HC], F32, stack=st)
        colG = k.sbuf("gd_colG", [128, NCHK, HC], F32, stack=st)
        colBE = k.sbuf("gd_colBE", [128, NCHK, HC], F32, stack=st)
        colKD = k.sbuf("gd_colKD", [128, NCHK, HC], F32, stack=st)
        Bcol = Buf("gd_col")
        for n in range(NCHK):
            for (src, Bsrc, dst) in ((betaT, Bbeta, colB), (gcT, BgcT, colG)):
                bank, Bb = cx.banks[n % 4], cx.Bbank[n % 4]
                k.op("pe", lambda h: h.transpose(bank[0:c, 0:HC], src[:, n * c:(n + 1) * c], cx.ident[0:HC, 0:HC]), reads=[Bsrc, cx.Bident], writes=[Bb])
                k.op("dve", lambda h: h.tensor_copy(dst[0:c, n, :], bank[0:c, 0:HC]), reads=[Bb], writes=[Bcol])
        NH = NCHK * HC
        cg2 = colG[0:c, :, :].rearrange("p n h -> p (n h)")
        for (o0, on_) in tiles_of(NH, 512):
            bank, Bb = cx.banks[0], cx.Bbank[0]
            k.op("pe", lambda h: h.matmul(bank[0:c, 0:on_], lrow[0:c, 0:c], cg2[:, o0:o0 + on_], start=True, stop=True), reads=[Bgconst, Bcol], writes=[Bb])
            k.op("dve", lambda h: h.tensor_tensor(colKD[0:c, :, :].rearrange("p n h -> p (n h)")[:, o0:o0 + on_], bank[0:c, 0:on_], cg2[:, o0:o0 + on_], ALU.subtract),
                 reads=[Bb, Bcol], writes=[Bcol])
        k.op("act", lambda h: h.activation(colKD[0:c, :, :], colKD[0:c, :, :], AF.Exp), reads=[Bcol], writes=[Bcol])
        k.op("act", lambda h: h.activation(colBE[0:c, :, :], colG[0:c, :, :], AF.Exp), reads=[Bcol], writes=[Bcol])
        k.op("dve", lambda h: h.tensor_tensor(colBE[0:c, :, :], colBE[0:c, :, :], colB[0:c, :, :], ALU.mult), reads=[Bcol], writes=[Bcol])
        ext = [k.sbuf(f"gd_ext{i}", [128, 3 + T], F32, stack=st) for i in range(2)]
        Bext = [Buf(f"gd_ext{i}") for i in range(2)]
        rs = k.sbuf("gd_rs", [128, T], F32, stack=st)
        Brs = Buf("gd_rs")
        sqt = k.sbuf("gd_sq", [128, min(T, 512)], F32, stack=st)
        Bsq = Buf("gd_sq")
        T5 = tiles_of(T)

        class HS:
            pass
        hs = []
        for i in range(HG):
            o = HS()
            o.q = k.sbuf(f"gd_q{i}", [128, T], F32, stack=st); o.Bq = Buf(f"gd_q{i}")
            o.kk = k.sbuf(f"gd_k{i}", [128, T], F32, stack=st); o.Bk = Buf(f"gd_k{i}")
            o.v = k.sbuf(f"gd_v{i}", [128, T], F32, stack=st); o.Bv = Buf(f"gd_v{i}")
            o.sg = k.sbuf(f"gd_sg{i}", [128, T], F32, stack=st); o.Bsg = Buf(f"gd_sg{i}")
            o.yc = k.sbuf(f"gd_yc{i}", [128, T], BF16, stack=st); o.Byc = Buf(f"gd_yc{i}")
            for nm, shp, dt in (("gcb", [128, c], F32), ("bb", [128, c], F32), ("egcb", [128, c], F32), ("ET", [128, c], F32),
                                ("tmp", [128, c], F32), ("P", [128, c], F32), ("Bf", [128, c], F32), ("Af", [128, c], F32),
                                ("Q", [128, c], F32), ("W", [128, c], F32), ("Am", [128, c], F32), ("Pb", [128, c], F32),
                                ("Bb0", [128, c], F32), ("Bb1", [128, c], F32), ("Ab0", [128, c], F32), ("Ab1", [128, c], F32),
                                ("aqk", [128, c], F32), ("rhsu", [128, 128], F32), ("rhsw", [128, 128], F32), ("kdec", [128, 128], F32),
                                ("qd", [128, c], F32), ("wT", [128, c], F32), ("u", [128, 128], F32), ("vn", [128, 128], F32),
                                ("S", [128, 128], F32), ("Sb", [128, 128], F32), ("oT", [128, c], F32), ("t1", [128, c], F32)):
                setattr(o, nm, k.sbuf(f"gd_{nm}{i}", shp, dt, stack=st))
                setattr(o, "B_" + nm, Buf(f"gd_{nm}{i}"))
            hs.append(o)
        bk = [0]

        def nb():
            bk[0] += 1
            b = bk[0] % 8
            return cx.banks[b], cx.Bbank[b]

        for hg in range(HC // HG):
            heads = [hg * HG + i for i in range(HG)]
            for o, h in zip(hs, heads):
                for ci, (dst, Bdst) in enumerate(((o.q, o.Bq), (o.kk, o.Bk), (o.v, o.Bv))):
                    ch = ci * HC + h
                    e = ci % 2
                    k.dma("sp", ext[e][:, 3:3 + T], projT.ap()[base + ch * 128:base + (ch + 1) * 128, 0:T], writes=[Bext[e]])
                    k.op("dve", lambda hh: hh.tensor_copy(ext[e][:, 0:3], cst[:, :, ch]), reads=[Bcst], writes=[Bext[e]])
                    k.op("dve", lambda hh: hh.tensor_copy(tail[:, :, ch], ext[e][:, T:T + 3]), reads=[Bext[e]], writes=[Btail])
                    k.op("act", lambda hh: hh.activation(dst[:, :], ext[e][:, 0:T], AF.Identity, scale=cwc[:, 0, ch:ch + 1]), reads=[Bext[e], Bcwc], writes=[Bdst])
                    for i in range(1, 4):
                        k.op("dve", lambda hh: hh.scalar_tensor_tensor(dst[:, :], ext[e][:, i:i + T], cwc[:, i, ch:ch + 1], dst[:, :], ALU.mult, ALU.add),
                             reads=[Bext[e], Bcwc, Bdst], writes=[Bdst])
                    k.op("act", lambda hh: hh.activation(dst[:, :], dst[:, :], AF.Silu), reads=[Bdst], writes=[Bdst])
                head_norm(cx, o.q, o.Bq, T, T5, None, None, rs, Brs, sqt, Bsq, 6, 1.0)
                k.op("dve", lambda hh: hh.scalar_tensor_tensor(o.q[:, :], o.q[:, :], 128.0 ** -0.5, rs[:, :], ALU.mult, ALU.mult), reads=[o.Bq, Brs], writes=[o.Bq])
                head_norm(cx, o.kk, o.Bk, T, T5, None, None, rs, Brs, sqt, Bsq, 6, 1.0)
                k.op("dve", lambda hh: hh.tensor_tensor(o.kk[:, :], o.kk[:, :], rs[:, :], ALU.mult), reads=[o.Bk, Brs], writes=[o.Bk])
                k.dma("sp", o.sg[:, :], projT.ap()[base + 3 * DC + h * 128:base + 3 * DC + (h + 1) * 128, 0:T], writes=[o.Bsg])
                k.op("act", lambda hh: hh.activation(o.sg[:, :], o.sg[:, :], AF.Silu), reads=[o.Bsg], writes=[o.Bsg])
                if s0_ap is None:
                    k.op("dve", lambda hh: hh.memset(o.S[:, :], 0.0), writes=[o.B_S])
                else:
                    k.dma("sp", o.S[:, :], s0_ap[h, :, :], writes=[o.B_S])
                k.op("act", lambda hh: hh.activation(o.Sb[:, :], o.S[:, :], AF.Copy), reads=[o.B_S], writes=[o.B_Sb])
            for n in range(NCHK):
                t0 = n * c
                sl = slice(t0, t0 + c)
                for o, h in zip(hs, heads):
                    bank, Bb = nb()
                    k.op("pe", lambda hh: hh.matmul(bank[:, 0:c], sel[0:HC, h * 128:(h + 1) * 128], gcT[:, sl], start=True, stop=True), reads=[Bgconst, BgcT], writes=[Bb])
                    k.op("pe", lambda hh: hh.matmul(bank[:, 128:128 + c], sel[0:HC, h * 128:(h + 1) * 128], betaT[:, sl], start=True, stop=True), reads=[Bgconst, Bbeta], writes=[Bb])
                    k.op("dve", lambda hh: hh.tensor_copy(o.gcb[:, :], bank[:, 0:c]), reads=[Bb], writes=[o.B_gcb])
                    k.op("act", lambda hh: hh.activation(o.egcb[:, :], bank[:, 0:c], AF.Exp), reads=[Bb], writes=[o.B_egcb])
                    k.op("dve", lambda hh: hh.tensor_copy(o.bb[:, :], bank[:, 128:128 + c]), reads=[Bb], writes=[o.B_bb])
                for o, h in zip(hs, heads):
                    k.op("dve", lambda hh: hh.tensor_scalar(o.ET[0:c, :], o.gcb[0:c, :], colG[0:c, n, h:h + 1], 0.0, ALU.subtract, ALU.min),
                         reads=[o.B_gcb, Bcol], writes=[o.B_ET])
                    k.op("act", lambda hh: hh.activation(o.ET[0:c, :], o.ET[0:c, :], AF.Exp), reads=[o.B_ET], writes=[o.B_ET])
                    k.op("dve", lambda hh: hh.tensor_tensor(o.tmp[0:c, :], o.ET[0:c, :], mSU[0:c, 0:c], ALU.mult), reads=[o.B_ET, Bgconst], writes=[o.B_tmp])
                    k.op("dve", lambda hh: hh.tensor_tensor(o.tmp[0:c, :], o.tmp[0:c, :], o.bb[0:c, :], ALU.mult), reads=[o.B_tmp, o.B_bb], writes=[o.B_tmp])
                    k.op("dve", lambda hh: hh.tensor_tensor(o.ET[0:c, :], o.ET[0:c, :], mU[0:c, 0:c], ALU.mult), reads=[o.B_ET, Bgconst], writes=[o.B_ET])
                for o, h in zip(hs, heads):
                    bank, Bb = nb()
                    k.op("pe", lambda hh: hh.matmul(bank[0:c, 0:c], o.kk[:, sl], o.kk[:, sl], start=True, stop=True), reads=[o.Bk], writes=[Bb])
                    k.op("pe", lambda hh: hh.matmul(bank[0:c, 128:128 + c], o.kk[:, sl], o.q[:, sl], start=True, stop=True), reads=[o.Bk, o.Bq], writes=[Bb])
                    k.op("dve", lambda hh: hh.scalar_tensor_tensor(o.Bf[0:c, :], bank[0:c, 0:c], -1.0, o.tmp[0:c, :], ALU.mult, ALU.mult), reads=[Bb, o.B_tmp], writes=[o.B_Bf])
                    k.op("dve", lambda hh: hh.tensor_tensor(o.aqk[0:c, :], bank[0:c, 128:128 + c], o.ET[0:c, :], ALU.mult), reads=[Bb, o.B_ET], writes=[o.B_aqk])
                    k.op("dve", lambda hh: hh.tensor_tensor(o.Bb0[0:c, :], o.Bf[0:c, :], bd8[0:c, 0:c], ALU.mult), reads=[o.B_Bf, Bgconst], writes=[o.B_Bb0])
                    k.op("dve", lambda hh: hh.tensor_tensor(o.P[0:c, :], o.Bb0[0:c, :], cx.ident[0:c, 0:c], ALU.add), reads=[o.B_Bb0, cx.Bident], writes=[o.B_P])
                for o, h in zip(hs, heads):
                    bank, Bb = nb()
                    k.op("pe", lambda hh: hh.matmul(bank[0:c, 0:c], o.Bb0[0:c, :], cx.ident[0:c, 0:c], start=True, stop=True), reads=[o.B_Bb0, cx.Bident], writes=[Bb])
                    k.op("act", lambda hh: hh.activation(o.Ab0[0:c, :], bank[0:c, 0:c], AF.Copy), reads=[Bb], writes=[o.B_Ab0])
                for l in range(1, L + 1):
                    for o, h in zip(hs, heads):
                        Bc, Ac = (o.Bb0, o.Ab0) if l % 2 == 1 else (o.Bb1, o.Ab1)
                        Bn, An = (o.Bb1, o.Ab1) if l % 2 == 1 else (o.Bb0, o.Ab0)
                        BBc, BAc = (o.B_Bb0, o.B_Ab0) if l % 2 == 1 else (o.B_Bb1, o.B_Ab1)
                        BBn, BAn = (o.B_Bb1, o.B_Ab1) if l % 2 == 1 else (o.B_Bb0, o.B_Ab0)
                        bank, Bb = nb()
                        k.op("pe", lambda hh: hh.matmul(bank[0:c, 0:c], Ac[0:c, :], Bc[0:c, :], start=True, stop=True), reads=[BAc, BBc], writes=[Bb])
                        k.op("pe", lambda hh: hh.matmul(bank[0:c, 128:128 + c], Bc[0:c, :], Ac[0:c, :], start=True, stop=True), reads=[BAc, BBc], writes=[Bb])
                        k.op("act", lambda hh: hh.activation(Bn[0:c, :], bank[0:c, 0:c], AF.Copy), reads=[Bb], writes=[BBn])
                        k.op("act", lambda hh: hh.activation(An[0:c, :], bank[0:c, 128:128 + c], AF.Copy), reads=[Bb], writes=[BAn])
                        k.op("dve", lambda hh: hh.tensor_copy(o.Pb[0:c, :], o.P[0:c, :]), reads=[o.B_P], writes=[o.B_Pb])
                    for o, h in zip(hs, heads):
                        An = o.Ab1 if l % 2 == 1 else o.Ab0
                        BAn = o.B_Ab1 if l % 2 == 1 else o.B_Ab0
                        bank, Bb = nb()
                        k.op("pe", lambda hh: hh.matmul(bank[0:c, 0:c], An[0:c, :], o.Pb[0:c, :], start=True, stop=True), reads=[BAn, o.B_Pb], writes=[Bb])
                        k.op("dve", lambda hh: hh.tensor_tensor(o.P[0:c, :], o.P[0:c, :], bank[0:c, 0:c], ALU.add), reads=[o.B_P, Bb], writes=[o.B_P])
                if c > 8:
                    for o, h in zip(hs, heads):
                        bank, Bb = nb()
                        k.op("pe", lambda hh: hh.transpose(bank[0:c, 0:c], o.Bf[0:c, :], cx.ident[0:c, 0:c]), reads=[o.B_Bf, cx.Bident], writes=[Bb])
                        k.op("act", lambda hh: hh.activation(o.Af[0:c, :], bank[0:c, 0:c], AF.Copy), reads=[Bb], writes=[o.B_Af])
                    bsz = 8
                    li = 0
                    while bsz < c:
                        for o, h in zip(hs, heads):
                            k.op("dve", lambda hh: hh.tensor_tensor(o.Am[0:c, :], o.Af[0:c, :], LLm[li][0:c, 0:c], ALU.mult), reads=[o.B_Af, Bgconst], writes=[o.B_Am])
                            bank, Bb = nb()
                            k.op("pe", lambda hh: hh.transpose(bank[0:c, 0:c], o.P[0:c, :], cx.ident[0:c, 0:c]), reads=[o.B_P, cx.Bident], writes=[Bb])
                            k.op("pe", lambda hh: hh.matmul(bank[0:c, 128:128 + c], o.Am[0:c, :], o.P[0:c, :], start=True, stop=True), reads=[o.B_Am, o.B_P], writes=[Bb])
                            k.op("act", lambda hh: hh.activation(o.Q[0:c, :], bank[0:c, 0:c], AF.Copy), reads=[Bb], writes=[o.B_Q])
                            k.op("act", lambda hh: hh.activation(o.W[0:c, :], bank[0:c, 128:128 + c], AF.Copy), reads=[Bb], writes=[o.B_W])
                        for o, h in zip(hs, heads):
                            bank, Bb = nb()
                            k.op("pe", lambda hh: hh.matmul(bank[0:c, 0:c], o.Q[0:c, :], o.W[0:c, :], start=True, stop=True), reads=[o.B_Q, o.B_W], writes=[Bb])
                            k.op("dve", lambda hh: hh.tensor_tensor(o.P[0:c, :], o.P[0:c, :], bank[0:c, 0:c], ALU.add), reads=[o.B_P, Bb], writes=[o.B_P])
                        bsz *= 2
                        li += 1
                for o, h in zip(hs, heads):
                    k.op("act", lambda hh: hh.activation(o.Pb[0:c, :], o.P[0:c, :], AF.Copy), reads=[o.B_P], writes=[o.B_Pb])
                    bank, Bb = nb()
                    k.op("pe", lambda hh: hh.transpose(bank[0:c, 0:128], o.kk[:, sl], cx.ident[:, :]), reads=[o.Bk, cx.Bident], writes=[Bb])
                    k.op("pe", lambda hh: hh.transpose(bank[0:c, 128:256], o.v[:, sl], cx.ident[:, :]), reads=[o.Bv, cx.Bident], writes=[Bb])
                    k.op("dve", lambda hh: hh.tensor_scalar(o.rhsw[0:c, :], bank[0:c, 0:128], colBE[0:c, n, h:h + 1], None, ALU.mult), reads=[Bb, Bcol], writes=[o.B_rhsw])
                    k.op("dve", lambda hh: hh.tensor_scalar(o.kdec[0:c, :], bank[0:c, 0:128], colKD[0:c, n, h:h + 1], None, ALU.mult), reads=[Bb, Bcol], writes=[o.B_kdec])
                    k.op("dve", lambda hh: hh.tensor_scalar(o.rhsu[0:c, :], bank[0:c, 128:256], colB[0:c, n, h:h + 1], None, ALU.mult), reads=[Bb, Bcol], writes=[o.B_rhsu])
                    k.op("dve", lambda hh: hh.tensor_tensor(o.qd[:, :], o.q[:, sl], o.egcb[:, :], ALU.mult), reads=[o.Bq, o.B_egcb], writes=[o.B_qd])
                for o, h in zip(hs, heads):
                    bank, Bb = nb()
                    k.op("pe", lambda hh: hh.matmul(bank[0:c, 0:128], o.Pb[0:c, :], o.rhsu[0:c, :], start=True, stop=True), reads=[o.B_Pb, o.B_rhsu], writes=[Bb])
                    k.op("pe", lambda hh: hh.matmul(bank[:, 128:128 + c], o.rhsw[0:c, :], o.Pb[0:c, :], start=True, stop=True), reads=[o.B_Pb, o.B_rhsw], writes=[Bb])
                    k.op("act", lambda hh: hh.activation(o.u[0:c, :], bank[0:c, 0:128], AF.Copy), reads=[Bb], writes=[o.B_u])
                    k.op("act", lambda hh: hh.activation(o.wT[:, :], bank[:, 128:128 + c], AF.Copy), reads=[Bb], writes=[o.B_wT])
                for o, h in zip(hs, heads):
                    bank, Bb = nb()
                    k.op("pe", lambda hh: hh.matmul(bank[0:c, 0:128], o.wT[:, :], o.Sb[:, :], start=True, stop=True), reads=[o.B_wT, o.B_Sb], writes=[Bb])
                    k.op("dve", lambda hh: hh.tensor_tensor(o.vn[0:c, :], o.u[0:c, :], bank[0:c, 0:128], ALU.subtract), reads=[o.B_u, Bb], writes=[o.B_vn])
                for o, h in zip(hs, heads):
                    bank, Bb = nb()
                    k.op("pe", lambda hh: hh.matmul(bank[:, 0:c], o.Sb[:, :], o.qd[:, :], start=True, stop=False), reads=[o.B_Sb, o.B_qd], writes=[Bb])
                    k.op("pe", lambda hh: hh.matmul(bank[:, 0:c], o.vn[0:c, :], o.aqk[0:c, :], start=False, stop=True), reads=[o.B_vn, o.B_aqk], writes=[Bb])
                    k.op("act", lambda hh: hh.activation(o.oT[:, :], bank[:, 0:c], AF.Copy), reads=[Bb], writes=[o.B_oT])
                    bank2, Bb2 = nb()
                    k.op("pe", lambda hh: hh.matmul(bank2[:, 0:128], o.kdec[0:c, :], o.vn[0:c, :], start=True, stop=True), reads=[o.B_kdec, o.B_vn], writes=[Bb2])
                    k.op("dve", lambda hh: hh.scalar_tensor_tensor(o.S[:, :], o.S[:, :], o.egcb[:, c - 1:c], bank2[:, 0:128], ALU.mult, ALU.add),
                         reads=[o.B_S, o.B_egcb, Bb2], writes=[o.B_S])
                    k.op("act", lambda hh: hh.activation(o.Sb[:, :], o.S[:, :], AF.Copy), reads=[o.B_S], writes=[o.B_Sb])
                for o, h in zip(hs, heads):
                    bank, Bb = nb()
                    k.op("act", lambda hh: hh.activation(o.t1[:, :], o.oT[:, :], AF.Square), reads=[o.B_oT], writes=[o.B_t1])
                    k.op("pe", lambda hh: hh.matmul(bank[:, 0:c], cx.ones32[:, :], o.t1[:, :], start=True, stop=True), reads=[o.B_t1, cx.Bconst], writes=[Bb])
                    k.op("act", lambda hh: hh.activation(o.t1[:, :], bank[:, 0:c], AF.Sqrt, bias=EPS, scale=1.0 / 128), reads=[Bb], writes=[o.B_t1])
                    k.op("dve", lambda hh: hh.reciprocal(o.t1[:, :], o.t1[:, :]), reads=[o.B_t1], writes=[o.B_t1])
                    k.op("dve", lambda hh: hh.scalar_tensor_tensor(o.t1[:, :], o.oT[:, :], onc[:, 0:1], o.t1[:, :], ALU.mult, ALU.mult), reads=[o.B_oT, Bonc, o.B_t1], writes=[o.B_t1])
                    k.op("dve", lambda hh: hh.tensor_tensor(o.yc[:, sl], o.t1[:, :], o.sg[:, sl], ALU.mult), reads=[o.B_t1, o.Bsg], writes=[o.Byc])
            for o, h in zip(hs, heads):
                k.dma("sp", mixT.ap()[DA + DB + h * 128:DA + DB + (h + 1) * 128, 0:T], o.yc[:, :], reads=[o.Byc])
                k.dma("sp", s_out_ap[h, :, :], o.S[:, :], reads=[o.B_S])
        so = k.sbuf("gd_so", [128, 128], F32, stack=st)
        Bso = Buf("gd_so")
        for r in range(3):
            n3 = 3 * HC
            bank, Bb = cx.banks[7], cx.Bbank[7]
            k.op("pe", lambda hh: hh.transpose(bank[0:n3, 0:128], tail[:, r, :], cx.ident[:, :]), reads=[Btail, cx.Bident], writes=[Bb])
            k.op("dve", lambda hh: hh.tensor_copy(so[0:n3, :], bank[0:n3, 0:128]), reads=[Bb], writes=[Bso])
            k.dma("sp", cstate_out_ap[r, :].rearrange("(c p) -> c p", p=128), so[0:n3, :], reads=[Bso])
    k.barrier()

from contextlib import ExitStack

CFG_FULL = dict(D=4096, S=2048, T=8, P=2048, HA=12, DB=1024, HC=12, DFF=11008, DEPTH=4, TT=512, HG=2)

WEIGHTS = ("rel_bias", "norm_mix", "w_in", "a_q_norm", "a_k_norm", "a_out_norm", "pool_w", "pool_scale", "gdn_conv_w",
           "gdn_a_log", "gdn_dt_bias", "gdn_out_norm", "w_out", "norm_ffn", "ffn_up", "ffn_conv_w", "ffn_conv_b", "ffn_down")


def dims(cfg):
    D, HA, DB, HC, DFF = cfg["D"], cfg["HA"], cfg["DB"], cfg["HC"], cfg["DFF"]
    DA, DC = HA * 128, HC * 128
    NIN = 3 * DA + DB + 4 * DC + 2 * HC
    DMIX = DA + DB + DC
    return DA, DC, NIN, DMIX


def weight_shapes(cfg):
    D, HA, DB, HC, DFF, L = cfg["D"], cfg["HA"], cfg["DB"], cfg["HC"], cfg["DFF"], cfg["DEPTH"]
    DA, DC, NIN, DMIX = dims(cfg)
    return dict(rel_bias=[32, HA], norm_mix=[L, D], w_in=[L, D, NIN], a_q_norm=[L, 128], a_k_norm=[L, 128], a_out_norm=[L, DA],
                pool_w=[L, 4, 256, 256], pool_scale=[L, DB], gdn_conv_w=[L, 4, 3 * DC], gdn_a_log=[L, HC], gdn_dt_bias=[L, HC],
                gdn_out_norm=[L, 128], w_out=[L, DMIX, D], norm_ffn=[L, D], ffn_up=[L, D, 2 * DFF], ffn_conv_w=[L, 3, 2 * DFF],
                ffn_conv_b=[L, 2 * DFF], ffn_down=[L, DFF, D])


def io_shapes(cfg):
    D, S, T, P, HA, DB, HC, DFF, L = (cfg[x] for x in ("D", "S", "T", "P", "HA", "DB", "HC", "DFF", "DEPTH"))
    DA, DC, NIN, DMIX = dims(cfg)
    ins = dict(x_p=[S, D], x_s=[T, D], cache_kv=[L, P, 2, HA, 128], st_pool=[L, 15, DB], st_gconv=[L, 3, 3 * DC],
               st_gdn=[L, HC, 128, 128], st_fconv=[L, 2, 2 * DFF])
    outs = dict(y_p=[S, D], y_s=[T, D], p_kv=[L, S, 2, HA, 128], s_kv=[L, T, 2, HA, 128], p_pool=[L, 15, DB], s_pool=[L, 15, DB],
                p_gconv=[L, 3, 3 * DC], s_gconv=[L, 3, 3 * DC], p_gdn=[L, HC, 128, 128], s_gdn=[L, HC, 128, 128],
                p_fconv=[L, 2, 2 * DFF], s_fconv=[L, 2, 2 * DFF])
    return ins, outs


def host_consts(cfg):
    ohp, ohs = attn_onehots()
    return dict(consts=np.eye(128, dtype=np.float32), gconst=gdn_consts(cfg["HC"]), ohp=ohp, ohs=ohs)


def build(cfg):
    D, S, T, P, HA, DB, HC, DFF, L, TT, HG = (cfg[x] for x in ("D", "S", "T", "P", "HA", "DB", "HC", "DFF", "DEPTH", "TT", "HG"))
    DA, DC, NIN, DMIX = dims(cfg)
    nc = bass.Bass("TRN2", target_bir_lowering=False)
    ins, outs = io_shapes(cfg)
    I = {n: nc.dram_tensor(n, s, F32, kind="ExternalInput") for n, s in ins.items()}
    W = {n: nc.dram_tensor(n, s, F32, kind="ExternalInput") for n, s in weight_shapes(cfg).items()}
    hc = host_consts(cfg)
    C = {n: nc.dram_tensor(n, list(a.shape), F32, kind="ExternalInput") for n, a in hc.items()}
    O = {n: nc.dram_tensor(n, s, F32, kind="ExternalOutput") for n, s in outs.items()}
    G = {}
    for g, Tg in (("p", S), ("s", T)):
        G[g] = dict(T=Tg, xT=nc.dram_tensor(f"xT_{g}", [D, Tg], F32, kind="Internal"),
                    hT=nc.dram_tensor(f"hT_{g}", [D, Tg], F32, kind="Internal"),
                    projT=nc.dram_tensor(f"projT_{g}", [NIN, Tg], F32, kind="Internal"),
                    mixT=nc.dram_tensor(f"mixT_{g}", [DMIX, Tg], BF16, kind="Internal"))
    Hp = nc.dram_tensor("Hp", [HA, 3, 128, TOE_ML], F32, kind="Internal")
    Hs = nc.dram_tensor("Hs", [HA, 128, SMP_ML], F32, kind="Internal")
    with ExitStack() as st:
        k = KB(nc, st)
        cx = Ctx(k, cfg, C["consts"])
        gconst = k.sbuf("gconst", [128, 512 + HC * 128 + 5 * 128], F32)
        Bgconst = Buf("gconst")
        k.dma("sp", gconst[:, :], C["gconst"].ap(), writes=[Bgconst])
        setup_attn_tables(cx, W["rel_bias"], C["ohp"], C["ohs"], Hp, Hs, HA)
        phase_transpose_in(cx, I["x_p"], G["p"]["xT"], S, D)
        phase_transpose_in(cx, I["x_s"], G["s"]["xT"], T, D)
        for l in range(L):
            for g in ("p", "s"):
                gg = G[g]
                Tg = gg["T"]
                tt = min(TT, Tg)
                phase_inproj(cx, gg["xT"], gg["projT"], Tg, tt, D, NIN, W["w_in"].ap()[l], W["norm_mix"].ap()[l])
                if g == "p":
                    phase_attn_prompt(cx, gg["projT"], gg["mixT"], O["p_kv"].ap()[l], S, HA, Hp,
                                      W["a_q_norm"].ap()[l], W["a_k_norm"].ap()[l], W["a_out_norm"].ap()[l])
                    phase_pool(cx, gg["projT"], gg["mixT"], S, DA, DB, W["pool_w"].ap()[l], W["pool_scale"].ap()[l], None, O["p_pool"].ap()[l], 0)
                    phase_gdn(cx, gg["projT"], gg["mixT"], S, 128, DA, DB, HC, gconst, Bgconst, W["gdn_conv_w"].ap()[l], W["gdn_a_log"].ap()[l],
                              W["gdn_dt_bias"].ap()[l], W["gdn_out_norm"].ap()[l], None, O["p_gconv"].ap()[l], None, O["p_gdn"].ap()[l], HG)
                else:
                    phase_attn_sample(cx, gg["projT"], gg["mixT"], O["s_kv"].ap()[l], I["cache_kv"].ap()[l], T, P, HA, Hs,
                                      W["a_q_norm"].ap()[l], W["a_k_norm"].ap()[l], W["a_out_norm"].ap()[l])
                    phase_pool(cx, gg["projT"], gg["mixT"], T, DA, DB, W["pool_w"].ap()[l], W["pool_scale"].ap()[l], I["st_pool"].ap()[l], O["s_pool"].ap()[l], 15)
                    phase_gdn(cx, gg["projT"], gg["mixT"], T, T, DA, DB, HC, gconst, Bgconst, W["gdn_conv_w"].ap()[l], W["gdn_a_log"].ap()[l],
                              W["gdn_dt_bias"].ap()[l], W["gdn_out_norm"].ap()[l], I["st_gconv"].ap()[l], O["s_gconv"].ap()[l],
                              I["st_gdn"].ap()[l], O["s_gdn"].ap()[l], HG)
                phase_outproj(cx, gg["xT"], gg["mixT"], gg["hT"], Tg, tt, D, DMIX, W["w_out"].ap()[l])
                phase_ffn(cx, gg["hT"], gg["xT"], Tg, tt, D, DFF, W["norm_ffn"].ap()[l], W["ffn_up"].ap()[l], W["ffn_conv_w"].ap()[l],
                          W["ffn_conv_b"].ap()[l], W["ffn_down"].ap()[l], (I["st_fconv"].ap()[l] if g == "s" else None),
                          O["p_fconv" if g == "p" else "s_fconv"].ap()[l])
        phase_transpose_out(cx, G["p"]["xT"], O["y_p"], S, D)
        phase_transpose_out(cx, G["s"]["xT"], O["y_s"], T, D)
        k.finish()
        build.stats = (k.n_inst, k.n_sem)
    return nc


_NC_CACHE = {}


def kernel(x_prompt, x_sample, cache_attn_kv, state_pool, state_gdn_conv, state_gdn, state_ffn_conv,
           rel_bias, norm_mix, w_in, a_q_norm, a_k_norm, a_out_norm, pool_w, pool_scale,
           gdn_conv_w, gdn_a_log, gdn_dt_bias, gdn_out_norm, w_out, norm_ffn,
           ffn_up, ffn_conv_w, ffn_conv_b, ffn_down):
    cfg = CFG_FULL
    n = 8
    if "nc" not in _NC_CACHE:
        _NC_CACHE["nc"] = build(cfg)
    nc = _NC_CACHE["nc"]
    f = lambda a: np.ascontiguousarray(np.asarray(a), dtype=np.float32)
    wts = dict(rel_bias=rel_bias, norm_mix=norm_mix, w_in=w_in, a_q_norm=a_q_norm, a_k_norm=a_k_norm, a_out_norm=a_out_norm,
               pool_w=pool_w, pool_scale=pool_scale, gdn_conv_w=gdn_conv_w, gdn_a_log=gdn_a_log, gdn_dt_bias=gdn_dt_bias,
               gdn_out_norm=gdn_out_norm, w_out=w_out, norm_ffn=norm_ffn, ffn_up=ffn_up, ffn_conv_w=ffn_conv_w,
               ffn_conv_b=ffn_conv_b, ffn_down=ffn_down)
    wts = {k_: f(v) for k_, v in wts.items()}
    hc = host_consts(cfg)
    x_prompt, x_sample = np.asarray(x_prompt), np.asarray(x_sample)
    cache_attn_kv, state_pool, state_gdn_conv = np.asarray(cache_attn_kv), np.asarray(state_pool), np.asarray(state_gdn_conv)
    state_gdn, state_ffn_conv = np.asarray(state_gdn), np.asarray(state_ffn_conv)
    in_maps = []
    for c in range(n):
        m = dict(x_p=f(x_prompt[c % 4]), x_s=f(x_sample[c]), cache_kv=f(cache_attn_kv[:, c]), st_pool=f(state_pool[:, c]),
                 st_gconv=f(state_gdn_conv[:, c]), st_gdn=f(state_gdn[:, c]), st_fconv=f(state_ffn_conv[:, c]))
        m.update(wts)
        m.update(hc)
        in_maps.append(m)
    res = run_bass_kernel_spmd(nc, in_maps, core_ids=list(range(n))).results
    P = lambda name: np.stack([res[c][name] for c in range(4)], axis=0)
    Sg = lambda name: np.stack([res[c][name] for c in range(8)], axis=0)
    mv = lambda a: np.ascontiguousarray(np.moveaxis(a, 0, 1))
    return (P("y_p"), Sg("y_s"), mv(P("p_kv")), mv(Sg("s_kv")), mv(P("p_pool")), mv(Sg("s_pool")),
            mv(P("p_gconv")), mv(Sg("s_gconv")), mv(P("p_gdn")), mv(Sg("s_gdn")), mv(P("p_fconv")), mv(Sg("s_fconv")))
```
